# Optimizing a Trainium2 kernel written in Bass

```python
import math
import jax, jax.numpy as jnp
from jax import lax
import numpy as np

D_MODEL = 1024
BATCH = 2
SEQ = 8192
DEPTH = 2

N_A_LAYERS = (DEPTH + 1) // 2
N_B_LAYERS = DEPTH // 2
DIFF_HEAD_DIM = 64
DIFF_V_DIM = 2 * DIFF_HEAD_DIM
DIFF_HEADS = D_MODEL // DIFF_V_DIM
MOBA_HEAD_DIM = 64
MOBA_HEADS = D_MODEL // MOBA_HEAD_DIM
MOBA_BLOCK = 256
MOBA_TOPK = 3
D_FF = 4 * D_MODEL
REL_BUCKETS = 32
REL_MAX_DIST = 128
REL_HEADS = 2 * DIFF_HEADS
Q_BLOCK = 128
DEEPNORM_ALPHA = (2 * DEPTH) ** 0.25
DEEPNORM_BETA = (8 * DEPTH) ** -0.25
LN_EPS = 1e-5

kernel_name = 'yoco_diffattn_moba_deepnorm'


def rel_bucket(rel):
    n = jnp.maximum(rel, 0)
    max_exact = REL_BUCKETS // 2
    nf = jnp.maximum(n, max_exact).astype(jnp.float32)
    large = max_exact + (jnp.log(nf / max_exact) / math.log(REL_MAX_DIST / max_exact)
                         * (REL_BUCKETS - max_exact)).astype(jnp.int32)
    large = jnp.minimum(large, REL_BUCKETS - 1)
    return jnp.where(n < max_exact, n, large)


def layer_norm(x, g, b):
    xf = x.astype(jnp.float32)
    mu = jnp.mean(xf, axis=-1, keepdims=True)
    var = jnp.mean(jnp.square(xf - mu), axis=-1, keepdims=True)
    y = (xf - mu) * lax.rsqrt(var + LN_EPS) * g.astype(jnp.float32) + b.astype(jnp.float32)
    return y.astype(x.dtype)


def diff_lambda_init(layer):
    return 0.8 - 0.6 * math.exp(-0.3 * layer)


def squared_relu_mlp(x, w_up, w_down):
    return jnp.square(jax.nn.relu(x @ w_up)) @ w_down


def diff_attention(x, w_in, lam_p, subln_g, w_out, rel_bias, lam_init):
    B, S, _ = x.shape
    H, dh, dv = DIFF_HEADS, DIFF_HEAD_DIM, DIFF_V_DIM
    qkv = x @ w_in
    q, k, v = jnp.split(qkv, [2 * H * dh, 4 * H * dh], axis=-1)
    q = q.reshape(B, S, H, 2, dh) * (dh ** -0.5)
    k = k.reshape(B, S, H, 2, dh)
    v = v.reshape(B, S, H, dv)
    lf = lam_p.astype(jnp.float32)
    lam = jnp.exp(jnp.sum(lf[0] * lf[1])) - jnp.exp(jnp.sum(lf[2] * lf[3])) + lam_init
    kpos = jnp.arange(S)

    def block(i):
        q_blk = lax.dynamic_slice_in_dim(q, i * Q_BLOCK, Q_BLOCK, axis=1)
        qpos = i * Q_BLOCK + jnp.arange(Q_BLOCK)
        rel = qpos[:, None] - kpos[None, :]
        bias = rel_bias[rel_bucket(rel)].astype(jnp.float32)
        bias = bias.reshape(Q_BLOCK, S, H, 2).transpose(2, 3, 0, 1)
        s = jnp.einsum('bqhmd,bkhmd->bhmqk', q_blk, k).astype(jnp.float32) + bias
        s = jnp.where(rel >= 0, s, -jnp.inf)
        p = jax.nn.softmax(s, axis=-1)
        a = p[:, :, 0] - lam * p[:, :, 1]
        return jnp.einsum('bhqk,bkhe->bqhe', a.astype(v.dtype), v)

    o = lax.map(block, jnp.arange(S // Q_BLOCK))
    o = o.transpose(1, 0, 2, 3, 4).reshape(B, S, H, dv).astype(jnp.float32)
    o = o * lax.rsqrt(jnp.mean(o * o, axis=-1, keepdims=True) + LN_EPS)
    o = o * subln_g.astype(jnp.float32) * (1.0 - lam_init)
    return o.reshape(B, S, H * dv).astype(x.dtype) @ w_out


def shared_kv(x, w_kv):
    B, S, _ = x.shape
    H, dh, L = MOBA_HEADS, MOBA_HEAD_DIM, MOBA_BLOCK
    k, v = jnp.split(x @ w_kv, 2, axis=-1)
    k = k.reshape(B, S, H, dh).transpose(0, 2, 1, 3)
    v = v.reshape(B, S, H, dh).transpose(0, 2, 1, 3)
    n_blk = -(-S // L)
    pad = n_blk * L - S
    k = jnp.pad(k, ((0, 0), (0, 0), (0, pad), (0, 0)))
    v = jnp.pad(v, ((0, 0), (0, 0), (0, pad), (0, 0)))
    k_blocks = k.reshape(B, H, n_blk, L, dh)
    v_blocks = v.reshape(B, H, n_blk, L, dh)
    k_mean = jnp.mean(k_blocks.astype(jnp.float32), axis=3)
    return k_blocks, v_blocks, k_mean


def moba_attention(x, w_q, w_out, k_blocks, v_blocks, k_mean, rel_bias):
    B, S, _ = x.shape
    H, dh, L = MOBA_HEADS, MOBA_HEAD_DIM, MOBA_BLOCK
    n_blk = k_blocks.shape[2]
    topk = min(MOBA_TOPK, n_blk)
    scale = dh ** -0.5
    q = (x @ w_q).reshape(B, S, H, dh).transpose(0, 2, 1, 3)
    bi = jnp.arange(B)[:, None, None]
    hi = jnp.arange(H)[None, :, None]
    blk_ids = jnp.arange(n_blk)
    offs = jnp.arange(L)

    def block(i):
        q_c = lax.dynamic_slice_in_dim(q, i * Q_BLOCK, Q_BLOCK, axis=2)
        qpos = i * Q_BLOCK + jnp.arange(Q_BLOCK)
        own = (i * Q_BLOCK) // L
        g = jnp.einsum('bhqd,bhnd->bhqn', q_c.astype(jnp.float32), k_mean)
        past = blk_ids[None, :] < (qpos // L)[:, None]
        g = jnp.where(past, g, -jnp.inf)
        g_val, sel = lax.top_k(g, topk)
        sel_ok = jnp.isfinite(g_val)
        k_own = lax.dynamic_index_in_dim(k_blocks, own, axis=2, keepdims=False)
        v_own = lax.dynamic_index_in_dim(v_blocks, own, axis=2, keepdims=False)
        rel_own = qpos[:, None] - (own * L + offs)[None, :]
        b_own = rel_bias[rel_bucket(rel_own)].astype(jnp.float32).transpose(2, 0, 1)
        s_own = jnp.einsum('bhqd,bhkd->bhqk', q_c, k_own).astype(jnp.float32) * scale + b_own
        logits = [jnp.where(rel_own >= 0, s_own, -jnp.inf)]
        for j in range(topk):
            idx = sel[..., j]
            k_sel = k_blocks[bi, hi, idx]
            rel = qpos[:, None] - (idx[..., None] * L + offs)
            b_sel = rel_bias[rel_bucket(rel), hi[..., None]].astype(jnp.float32)
            s = jnp.einsum('bhqd,bhqkd->bhqk', q_c, k_sel).astype(jnp.float32) * scale + b_sel
            logits.append(jnp.where(sel_ok[..., j, None], s, -jnp.inf))
        p = jax.nn.softmax(jnp.concatenate(logits, axis=-1), axis=-1).astype(v_blocks.dtype)
        o = jnp.einsum('bhqk,bhkd->bhqd', p[..., :L], v_own)
        for j in range(topk):
            v_sel = v_blocks[bi, hi, sel[..., j]]
            o = o + jnp.einsum('bhqk,bhqkd->bhqd', p[..., (j + 1) * L:(j + 2) * L], v_sel)
        return o

    o = lax.map(block, jnp.arange(S // Q_BLOCK))
    o = o.transpose(1, 0, 3, 2, 4).reshape(B, S, H * dh)
    return o @ w_out


def setup_inputs(seed: int = 0) -> dict:
    key = jax.random.key(seed)
    ks = jax.random.split(key, 20)
    f32 = jnp.float32
    D = D_MODEL
    s_in = D ** -0.5
    w_qk_a = jax.random.normal(ks[1], (N_A_LAYERS, D, 4 * D_MODEL // 2), f32) * s_in
    w_v_a = jax.random.normal(ks[2], (N_A_LAYERS, D, DIFF_HEADS * DIFF_V_DIM), f32) * s_in * DEEPNORM_BETA
    w_k_sh = jax.random.normal(ks[5], (D, MOBA_HEADS * MOBA_HEAD_DIM), f32) * s_in
    w_v_sh = jax.random.normal(ks[6], (D, MOBA_HEADS * MOBA_HEAD_DIM), f32) * s_in * DEEPNORM_BETA
    return {
        'x': jax.random.normal(ks[0], (BATCH, SEQ, D), f32),
        'w_in_a': jnp.concatenate([w_qk_a, w_v_a], axis=-1),
        'lam_a': jax.random.normal(ks[3], (N_A_LAYERS, 4, DIFF_HEAD_DIM), f32) * 0.1,
        'subln_a': 1.0 + 0.02 * jax.random.normal(ks[4], (N_A_LAYERS, DIFF_V_DIM), f32),
        'w_out_a': jax.random.normal(ks[7], (N_A_LAYERS, D, D), f32) * s_in * DEEPNORM_BETA,
        'w_kv_shared': jnp.concatenate([w_k_sh, w_v_sh], axis=-1),
        'w_q_b': jax.random.normal(ks[8], (N_B_LAYERS, D, MOBA_HEADS * MOBA_HEAD_DIM), f32) * s_in,
        'w_out_b': jax.random.normal(ks[9], (N_B_LAYERS, D, D), f32) * s_in * DEEPNORM_BETA,
        'rel_bias': jax.random.normal(ks[10], (REL_BUCKETS, REL_HEADS), f32) * 0.5,
        'ln1_g': 1.0 + 0.02 * jax.random.normal(ks[11], (DEPTH, D), f32),
        'ln1_b': 0.02 * jax.random.normal(ks[12], (DEPTH, D), f32),
        'ln2_g': 1.0 + 0.02 * jax.random.normal(ks[13], (DEPTH, D), f32),
        'ln2_b': 0.02 * jax.random.normal(ks[14], (DEPTH, D), f32),
        'w_up': jax.random.normal(ks[15], (DEPTH, D, D_FF), f32) * s_in,
        'w_down': jax.random.normal(ks[16], (DEPTH, D_FF, D), f32) * (D_FF ** -0.5) * DEEPNORM_BETA,
    }


def reference(x, w_in_a, lam_a, subln_a, w_out_a, w_kv_shared, w_q_b, w_out_b, rel_bias,
              ln1_g, ln1_b, ln2_g, ln2_b, w_up, w_down):
    h = x
    kv = None
    for layer in range(DEPTH):
        if layer < N_A_LAYERS:
            mix = diff_attention(h, w_in_a[layer], lam_a[layer], subln_a[layer], w_out_a[layer],
                                 rel_bias, diff_lambda_init(layer))
        else:
            if kv is None:
                kv = shared_kv(h, w_kv_shared)
            j = layer - N_A_LAYERS
            mix = moba_attention(h, w_q_b[j], w_out_b[j], kv[0], kv[1], kv[2], rel_bias)
        h = layer_norm(DEEPNORM_ALPHA * h + mix, ln1_g[layer], ln1_b[layer])
        h = layer_norm(DEEPNORM_ALPHA * h + squared_relu_mlp(h, w_up[layer], w_down[layer]),
                       ln2_g[layer], ln2_b[layer])
    return h
```

```python
import math
from contextlib import ExitStack

import numpy as np
import ml_dtypes
import concourse.bass as bass
import concourse.mybir as mybir
from concourse.bass_utils import run_bass_kernel_spmd

F32 = mybir.dt.float32
BF16 = mybir.dt.bfloat16
AF = mybir.ActivationFunctionType
ALU = mybir.AluOpType
AX = mybir.AxisListType

NCORES = 8
D = 1024
S = 8192
NT = 16
TL = 2048
DFF = 4096
ALPHA = 4.0 ** 0.25
LN_EPS = 1e-5
LAM_INIT0 = 0.8 - 0.6 * math.exp(-0.3 * 0)
NEG = -30000.0

ENGS = ("pe", "act", "dve", "pool", "sp")


class Op:
    __slots__ = ("fn", "deps", "dsem", "seq", "waits")

    def __init__(self, fn, deps, dsem):
        self.fn = fn
        self.deps = deps
        self.dsem = dsem
        self.seq = None
        self.waits = []


class Prog:
    def __init__(self, nc):
        self.nc = nc
        self.ops = {e: [] for e in ENGS}
        self.last_w = {}
        self.readers = {}

    def op(self, eng, fn, reads=(), writes=(), dsem=None):
        deps = set()
        for k in reads:
            w = self.last_w.get(k)
            if w is not None:
                deps.add(w)
        for k in writes:
            w = self.last_w.get(k)
            if w is not None:
                deps.add(w)
            rd = self.readers.get(k)
            if rd:
                for e2, i2 in rd.items():
                    deps.add((e2, i2))
        idx = len(self.ops[eng])
        me = (eng, idx)
        deps.discard(me)
        self.ops[eng].append(Op(fn, deps, dsem))
        for k in reads:
            self.readers.setdefault(k, {})[eng] = idx
        for k in writes:
            self.last_w[k] = me
            self.readers[k] = {}
        return me

    def barrier(self):
        deps = set()
        for e in ENGS:
            ops = self.ops[e]
            last_real = None
            seen_d = set()
            for i in range(len(ops) - 1, -1, -1):
                o = ops[i]
                if o.fn is None:
                    continue
                if o.dsem is not None:
                    if o.dsem not in seen_d:
                        seen_d.add(o.dsem)
                        deps.add((e, i))
                elif last_real is None:
                    last_real = i
                    deps.add((e, i))
        for e in ENGS:
            self.ops[e].append(Op(None, set(deps), None))

    def resolve(self):
        needed = set()
        for e in ENGS:
            for o in self.ops[e]:
                for d in o.deps:
                    if d[0] == e and e == "pe":
                        continue
                    needed.add(d)
        self.sem_names = set()
        cnt = {}
        for e in ENGS:
            for i, o in enumerate(self.ops[e]):
                if o.dsem is not None:
                    name = "d_" + o.dsem
                    cnt[name] = cnt.get(name, 0) + 16
                    o.seq = (name, cnt[name])
                    self.sem_names.add(name)
                elif (e, i) in needed:
                    name = "e_" + e
                    cnt[name] = cnt.get(name, 0) + 1
                    o.seq = (name, cnt[name])
                    self.sem_names.add(name)
        self.finals = dict(cnt)
        for e in ENGS:
            seen = {}
            for o in self.ops[e]:
                req = {}
                for d in o.deps:
                    if d[0] == e and e == "pe":
                        continue
                    name, val = self.ops[d[0]][d[1]].seq
                    if req.get(name, 0) < val:
                        req[name] = val
                for name, val in req.items():
                    if seen.get(name, 0) < val:
                        seen[name] = val
                        o.waits.append((name, val))

    def check(self):
        sem = {}
        pc = {e: 0 for e in ENGS}
        n = {e: len(self.ops[e]) for e in ENGS}
        progress = True
        while progress:
            progress = False
            for e in ENGS:
                while pc[e] < n[e]:
                    o = self.ops[e][pc[e]]
                    if all(sem.get(nm, 0) >= v for nm, v in o.waits):
                        if o.seq is not None:
                            nm = o.seq[0]
                            sem[nm] = sem.get(nm, 0) + (16 if nm.startswith("d_") else 1)
                        pc[e] += 1
                        progress = True
                    else:
                        break
        stuck = {e: (pc[e], n[e]) for e in ENGS if pc[e] < n[e]}
        if stuck:
            msg = []
            for e, (p, m) in stuck.items():
                o = self.ops[e][p]
                msg.append(f"{e}@{p}/{m} waits {[(nm, v, sem.get(nm, 0)) for nm, v in o.waits]}")
            raise RuntimeError("sync deadlock: " + "; ".join(msg))
        return {e: n[e] for e in ENGS}, len(self.sem_names)

    def emit(self):
        nc = self.nc
        self.resolve()
        print("prog ops/sems:", self.check(), flush=True)
        with ExitStack() as st:
            sems = {}
            for name in sorted(self.sem_names):
                sems[name] = st.enter_context(nc.semaphore(name))
            block = st.enter_context(nc.Block())

            def run(engname):
                def body(engine):
                    for o in self.ops[engname]:
                        for (name, val) in o.waits:
                            engine.wait_ge(sems[name], val)
                        if o.fn is None:
                            continue
                        ins = o.fn(engine)
                        if o.seq is not None:
                            name = o.seq[0]
                            ins.then_inc(sems[name], 16 if name.startswith("d_") else 1)
                    if engname == "sp":
                        for name, val in self.finals.items():
                            if name.startswith("d_"):
                                engine.wait_ge(sems[name], val)
                return body

            block.tensor(run("pe"))
            block.scalar(run("act"))
            block.vector(run("dve"))
            block.gpsimd(run("pool"))
            block.sync(run("sp"))


def f_mm(out, lhsT, rhs, start, stop, skip=False):
    return lambda e: e.matmul(out, lhsT=lhsT, rhs=rhs, start=start, stop=stop,
                              skip_group_check=skip)


def f_tr(out, in_, ident):
    return lambda e: e.transpose(out=out, in_=in_, identity=ident)


def f_act(out, in_, func, bias=None, scale=1.0):
    if bias is None:
        return lambda e: e.activation(out=out, in_=in_, func=func, scale=scale)
    return lambda e: e.activation(out=out, in_=in_, func=func, bias=bias, scale=scale)


def f_tt(out, in0, in1, op):
    return lambda e: e.tensor_tensor(out=out, in0=in0, in1=in1, op=op)


def f_ts(out, in0, s1, s2, op0, op1=None):
    if op1 is None:
        return lambda e: e.tensor_scalar(out=out, in0=in0, scalar1=s1, scalar2=None, op0=op0)
    return lambda e: e.tensor_scalar(out=out, in0=in0, scalar1=s1, scalar2=s2, op0=op0, op1=op1)


def f_stt(out, in0, scalar, in1, op0, op1):
    return lambda e: e.scalar_tensor_tensor(out=out, in0=in0, scalar=scalar, in1=in1,
                                            op0=op0, op1=op1)


def f_copy(out, in_):
    return lambda e: e.tensor_copy(out=out, in_=in_)


def f_dma(out, in_):
    return lambda e: e.dma_start(out=out, in_=in_)


def f_memset(ap, v):
    return lambda e: e.memset(ap, v)


def _san(key):
    return "".join(ch for ch in str(key) if ch.isalnum())


class Ctx:
    def __init__(self, nc, st):
        self.nc = nc
        self.st = st
        self.P = Prog(nc)
        self.bank = [st.enter_context(nc.psum_tensor(f"bank{i}", [128, 512], F32))
                     for i in range(8)]
        self.uid = 0
        self.side = "left"
        self.root = st

    def sb(self, name, shape, dt):
        return self.st.enter_context(self.nc.sbuf_tensor("s_" + name, shape, dt, side=self.side))

    def bank_bf(self, i):
        return self.bank[i][:].bitcast(BF16)


def load_weight_bf16(C, wt, w_ap, key, ncols, col0=0, row0=0, nchunks=8):
    P = C.P
    step = 1024
    for c in range(nchunks):
        for n0 in range(0, ncols, step):
            n1 = min(ncols, n0 + step)
            P.op("pool", f_dma(wt[:, c, n0:n1],
                               w_ap[row0 + c * 128: row0 + (c + 1) * 128, col0 + n0: col0 + n1]),
                 writes=[key], dsem=_san(key))


def make_hT_from_dram(C, x_ap, hT, ident, xt, xb, tag):
    P = C.P
    for t in range(NT):
        b = t % 2
        P.op("sp", f_dma(xt[b][:], x_ap[t * 128:(t + 1) * 128, :]),
             writes=[(tag, "xt", b)], dsem=f"{tag}xt{b}")
        P.op("act", f_act(xb[b][:], xt[b][:], AF.Copy), reads=[(tag, "xt", b)],
             writes=[(tag, "xb", b)])
        transpose_tile(C, xb[b], (tag, "xb", b), hT, t, ident, b)


def transpose_tile(C, src_bf, src_key, hT, t, ident, b):
    P = C.P
    pT = C.bank_bf(b).rearrange("p (c n) -> p c n", n=128)
    for c in range(8):
        P.op("pe", f_tr(pT[:, c, :], src_bf[:, c * 128:(c + 1) * 128], ident[:]),
             reads=[src_key, "ident"], writes=[("bank", b)])
    P.op("dve", f_copy(hT[:, :, t * 128:(t + 1) * 128], pT), reads=[("bank", b)],
         writes=[("hT", t)])


def layer_norm_tile(C, src, src_key, dst, dst_key, g_rep, b_rep, gb_key, scr):
    P = C.P
    u = C.uid
    C.uid += 1
    k = u % 2
    stats, mv, rstd, tmp = scr["stats"][k], scr["mv"][k], scr["rstd"][k], scr["tmp"][k]
    kk = ("ln", k)
    kt = ("ln", "tmp")
    P.op("dve", lambda e: e.bn_stats(out=stats[:, 0, :], in_=src[:, 0:512]),
         reads=[src_key], writes=[kk + ("st",)])
    P.op("dve", lambda e: e.bn_stats(out=stats[:, 1, :], in_=src[:, 512:1024]),
         reads=[src_key], writes=[kk + ("st",)])
    P.op("dve", lambda e: e.bn_aggr(out=mv[:], in_=stats[:].rearrange("p a s -> p (a s)")),
         reads=[kk + ("st",)], writes=[kk + ("mv",)])
    P.op("dve", f_ts(rstd[:], mv[:, 1:2], LN_EPS, None, ALU.add),
         reads=[kk + ("mv",)], writes=[kk + ("rs0",)])
    P.op("pool", f_tt(rstd[:], rstd[:], scr["mhalf"][:, 0:1], ALU.pow),
         reads=[kk + ("rs0",), "mhalf"], writes=[kk + ("rs0",)])
    P.op("dve", f_ts(tmp[:], src, mv[:, 0:1], rstd[:, 0:1], ALU.subtract, ALU.mult),
         reads=[src_key, kk + ("mv",), kk + ("rs0",)], writes=[kt])
    P.op("pool", f_tt(tmp[:], tmp[:], g_rep[:], ALU.mult),
         reads=[kt, gb_key], writes=[kt])
    P.op("pool", f_tt(dst, tmp[:], b_rep[:], ALU.add),
         reads=[kt, gb_key], writes=[dst_key])


def ln_scratch(C):
    scr = {
        "stats": [C.sb(f"ln_stats{k}", [128, 2, 6], F32) for k in range(2)],
        "mv": [C.sb(f"ln_mv{k}", [128, 2], F32) for k in range(2)],
        "rstd": [C.sb(f"ln_rstd{k}", [128, 1], F32) for k in range(2)],
        "tmp": [C.sb("ln_tmp", [128, 1024], F32)] * 2,
        "mhalf": C.sb("ln_mhalf", [128, 1], F32),
    }
    C.P.op("dve", f_memset(scr["mhalf"][:], -0.5), writes=["mhalf"])
    return scr


def h_to_hT(C, hs, t, hT, ident, hb):
    P = C.P
    b = t % 2
    P.op("act", f_act(hb[b][:], hs[:, t, :], AF.Copy), reads=[("hs", t)], writes=[("hb", b)])
    transpose_tile(C, hb[b], ("hb", b), hT, t, ident, b)


def mlp(C, hs, hT, w_up_ap, w_down_ap, tagp):
    P = C.P
    NS = 8
    wu = [C.sb(f"{tagp}wu{i}", [128, 8, 512], BF16) for i in range(2)]
    wd = [C.sb(f"{tagp}wd{i}", [128, 4, 1024], BF16) for i in range(2)]
    uT1 = C.sb(f"{tagp}uT", [128, 4, TL], BF16)
    uT = [uT1, uT1]
    rl = [C.sb(f"{tagp}rl{i}", [128, 512], F32) for i in range(2)]
    cnt = 0
    dcnt = 0

    def wload(s):
        b = s % 2
        load_weight_bf16(C, wu[b], w_up_ap, (tagp, "wu", b), 512, col0=s * 512)
        for fc in range(4):
            P.op("pool", f_dma(wd[b][:, fc, :],
                               w_down_ap[s * 512 + fc * 128: s * 512 + (fc + 1) * 128, :]),
                 writes=[(tagp, "wd", b)], dsem=f"{tagp}wd{b}")

    wload(0)
    for s in range(NS):
        b = s % 2
        for g in range(4):
            for fc in range(4):
                bk = 2 + (cnt % 2)
                ps = C.bank[bk]
                for c in range(8):
                    P.op("pe", f_mm(ps[:], wu[b][:, c, fc * 128:(fc + 1) * 128],
                                    hT[:, c, g * 512:(g + 1) * 512], c == 0, c == 7),
                         reads=[(tagp, "wu", b)] + [("hT", 4 * g + a) for a in range(4)],
                         writes=[("bank", bk)])
                r = rl[cnt % 2]
                P.op("act", f_act(r[:], ps[:], AF.Relu), reads=[("bank", bk)],
                     writes=[(tagp, "rl", cnt % 2)])
                P.op("pool", f_tt(uT[b][:, fc, g * 512:(g + 1) * 512], r[:], r[:], ALU.mult),
                     reads=[(tagp, "rl", cnt % 2)], writes=[(tagp, "uT", g)])
                cnt += 1
        if s + 1 < NS:
            wload(s + 1)
        for t in range(NT):
            for half in range(2):
                bk = 4 + (dcnt % 4)
                ps = C.bank[bk]
                for fc in range(4):
                    P.op("pe", f_mm(ps[:], uT[b][:, fc, t * 128:(t + 1) * 128],
                                    wd[b][:, fc, half * 512:(half + 1) * 512], fc == 0, fc == 3),
                         reads=[(tagp, "uT", t // 4), (tagp, "wd", b)], writes=[("bank", bk)])
                dst = hs[:, t, half * 512:(half + 1) * 512]
                if s == 0:
                    P.op("dve", f_stt(dst, dst, ALPHA, ps[:], ALU.mult, ALU.add),
                         reads=[("bank", bk), ("hs", t)], writes=[("hs", t)])
                else:
                    P.op("dve", f_tt(dst, dst, ps[:], ALU.add),
                         reads=[("bank", bk), ("hs", t)], writes=[("hs", t)])
                dcnt += 1


def attention(C, cfg, q_d, k_d, v_d, G_d, b31, epilogue):
    P = C.P
    H, nm, KR, dv = cfg["H"], cfg["nm"], cfg["KR"], cfg["dv"]
    dv1 = dv + 1
    per_bank = 512 // dv1 if nm == 1 else 3
    Kb = [C.sb(f"Kb{i}", [128, S], BF16) for i in range(2)]
    Vb = [C.sb(f"Vb{i}", [128, 64, dv1], BF16) for i in range(2)]
    Qb = [C.sb(f"Qb{i}", [128, TL], BF16) for i in range(2)]
    Gb = [C.sb(f"Gb{i}", [128, nm, 640], F32) for i in range(2)]
    PT = [[C.sb(f"PT{s}_{m}", [128, 512], BF16) for m in range(nm)] for s in range(3)]
    tn = [C.sb(f"tn{i}", [128, 128], F32) for i in range(4)]

    def rows(m):
        return (64 * m, 64 * m + 64) if nm == 2 else (0, KR)

    def mapidx(h, m):
        return 2 * h + m if nm == 2 else h

    def acc_bank(n):
        return 4 + n // per_bank

    def acc_ap(n):
        off = (n % per_bank) * dv1
        return C.bank[acc_bank(n)][:, off:off + dv1]

    def acc_key(n):
        return ("bank", acc_bank(n))

    def load(h):
        p = h % 2
        for r in range(4):
            P.op("sp", f_dma(Kb[p][0:KR, r * TL:(r + 1) * TL], k_d[r, h]),
                 writes=[("K", p)], dsem=f"K{p}")
        for r in range(4):
            P.op("sp", f_dma(Vb[p][:, r * 16:(r + 1) * 16, :], v_d[r, h]),
                 writes=[("V", p)], dsem=f"V{p}")
        P.op("sp", f_dma(Qb[p][0:KR, :], q_d[h]), writes=[("Q", p)], dsem=f"Q{p}")
        m0 = mapidx(h, 0)
        P.op("sp", f_dma(Gb[p][:], G_d[:, m0:m0 + nm, :]), writes=[("G", p)], dsem=f"G{p}")

    def amin_of(u, c):
        return max(0, -((-(c - 16 * u - 3)) // 4))

    def afar_of(u, c):
        return min(4, max(0, -((-(c - 16 * u + 2)) // 4)))

    def kslot(c):
        return (c % 4) * 16 + c // 4

    items = [(h, u, c) for h in range(H) for u in range(4) for c in range(16 * u + 16)]
    tcnt = [0]

    def emit_qk_exp(i):
        h, u, c = items[i]
        p = h % 2
        slot = i % 2
        psl = i % 3
        amin, afar = amin_of(u, c), afar_of(u, c)
        col = kslot(c) * 128
        for m in range(nm):
            r0, r1 = rows(m)
            Sb = C.bank[slot * nm + m]
            P.op("pe", f_mm(Sb[:, amin * 128:512], Kb[p][r0:r1, col:col + 128],
                            Qb[p][r0:r1, (4 * u + amin) * 128:(4 * u + 4) * 128], True, True),
                 reads=[("K", p), ("Q", p)], writes=[("bank", slot * nm + m)])
        for m in range(nm):
            Sb = C.bank[slot * nm + m]
            mi = mapidx(h, m)
            for a in range(amin, afar):
                s = c - (16 * u + 4 * a) + 1
                tb = tcnt[0] % 4
                tcnt[0] += 1
                P.op("dve", f_tt(tn[tb][:], Sb[:, a * 128:(a + 1) * 128],
                                 Gb[p][:, m, s * 128:(s + 1) * 128], ALU.add),
                     reads=[("bank", slot * nm + m), ("G", p)], writes=[("tn", tb)])
                P.op("act", f_act(PT[psl][m][:, a * 128:(a + 1) * 128], tn[tb][:], AF.Exp),
                     reads=[("tn", tb)], writes=[("PT", psl, m, a)])
            if afar < 4:
                P.op("act", f_act(PT[psl][m][:, afar * 128:512], Sb[:, afar * 128:512], AF.Exp,
                                  bias=b31[:, mi:mi + 1]),
                     reads=[("bank", slot * nm + m), "b31"],
                     writes=[("PT", psl, m, a) for a in range(afar, 4)])

    def emit_pv(i):
        h, u, c = items[i]
        p = h % 2
        psl = i % 3
        amin = amin_of(u, c)
        for m in range(nm):
            for a in range(amin, 4):
                n = m * 4 + a
                first = (c == 0) and (n % per_bank == 0)
                last = (c == 16 * u + 4 * a + 3)
                P.op("pe", f_mm(acc_ap(n), PT[psl][m][:, a * 128:(a + 1) * 128],
                                Vb[p][:, kslot(c), :], first, last, skip=True),
                     reads=[("PT", psl, m, a), ("V", p)], writes=[acc_key(n)])

    load(0)
    if H > 1:
        load(1)
    emit_qk_exp(0)
    for i in range(len(items)):
        h, u, c = items[i]
        nxt = items[i + 1] if i + 1 < len(items) else None
        if nxt is not None:
            emit_qk_exp(i + 1)
        emit_pv(i)
        if c == 16 * u + 15:
            epilogue(h, u, acc_ap, acc_key)
        if nxt is not None and nxt[0] != h and h + 2 < H:
            load(h + 2)


def dram_in(nc, name, shape, dt):
    return nc.dram_tensor(name, list(shape), dt, kind="ExternalInput").ap()


def dram_out(nc, name, shape, dt):
    return nc.dram_tensor(name, list(shape), dt, kind="ExternalOutput").ap()


def fm_project(C, w, wkey, col0, hT, scale, tag, store):
    P = C.P
    stg = [C.sb(f"{tag}stg{i}", [128, 8, 512], BF16) for i in range(2)]
    cnt = 0
    for g in range(4):
        b = g % 2
        for f in range(8):
            bk = 2 + (cnt % 2)
            cnt += 1
            ps = C.bank[bk]
            for c in range(8):
                P.op("pe", f_mm(ps[:], w[:, c, col0 + f * 128: col0 + (f + 1) * 128],
                                hT[:, c, g * 512:(g + 1) * 512], c == 0, c == 7),
                     reads=[wkey] + [("hT", 4 * g + a) for a in range(4)], writes=[("bank", bk)])
            P.op("act", f_act(stg[b][:, f, :], ps[:], AF.Copy, scale=scale),
                 reads=[("bank", bk)], writes=[(tag, "stg", b)])
        store(g, stg[b], (tag, "stg", b), b)
    return stg


def v_project(C, wb, wkey, col0, hT, out_d, nh, dv, tag):
    P = C.P
    dv1 = dv + 1
    stg = [C.sb(f"{tag}vstg{i}", [128, nh, 4, dv1], BF16) for i in range(2)]
    for i in range(2):
        P.op("pool", f_memset(stg[i][:], 1.0), writes=[(tag, "vstg", i)])
    hpb = 512 // dv
    nhalf = nh * dv // 512
    cnt = 0
    for g in range(4):
        b = g % 2
        for a in range(4):
            t = 4 * g + a
            for half in range(nhalf):
                bk = 4 + (cnt % 2)
                cnt += 1
                ps = C.bank[bk]
                for c in range(8):
                    P.op("pe", f_mm(ps[:], hT[:, c, t * 128:(t + 1) * 128],
                                    wb[:, c, col0 + half * 512: col0 + (half + 1) * 512],
                                    c == 0, c == 7),
                         reads=[wkey, ("hT", t)], writes=[("bank", bk)])
                P.op("dve", f_copy(stg[b][:, half * hpb:(half + 1) * hpb, a, 0:dv],
                                   ps[:].rearrange("p (h e) -> p h e", e=dv)),
                     reads=[("bank", bk)], writes=[(tag, "vstg", b)])
        P.op("sp", f_dma(out_d.rearrange("h p t e -> p h (t e)")[:, :, 4 * g * dv1:(4 * g + 4) * dv1],
                         stg[b][:].rearrange("p h a e -> p h (a e)")),
             reads=[(tag, "vstg", b)], writes=[(tag, "dram")], dsem=f"{tag}vst{b}")


def out_proj_ln(C, ostore, wo, res_ap, hs, hT, ident, g_rep, b_rep, gb_key, scr, xt, hb, oT):
    P = C.P
    for t in range(NT):
        b = t % 2
        pT = C.bank_bf(b).rearrange("p (c n) -> p c n", n=128)
        for c in range(8):
            P.op("pe", f_tr(pT[:, c, :], ostore[:, t, c * 128:(c + 1) * 128], ident[:]),
                 reads=[("os", t), "ident"], writes=[("bank", b)])
        P.op("dve", f_copy(oT[b][:], pT), reads=[("bank", b)], writes=[("oT", b)])
        if res_ap is not None:
            P.op("sp", f_dma(xt[:], res_ap[t * 128:(t + 1) * 128, :]),
                 writes=[("B", "xt")], dsem="Bxt")
        for half in range(2):
            bk = 2 + half
            ps = C.bank[bk]
            for c in range(8):
                P.op("pe", f_mm(ps[:], oT[b][:, c, :], wo[:, c, half * 512:(half + 1) * 512],
                                c == 0, c == 7),
                     reads=[("oT", b), "wo"], writes=[("bank", bk)])
            dst = hs[:, t, half * 512:(half + 1) * 512]
            if res_ap is None:
                P.op("dve", f_stt(dst, dst, ALPHA, ps[:], ALU.mult, ALU.add),
                     reads=[("bank", bk), ("hs", t)], writes=[("hs", t)])
            else:
                P.op("dve", f_stt(dst, xt[:, half * 512:(half + 1) * 512], ALPHA, ps[:],
                                  ALU.mult, ALU.add),
                     reads=[("bank", bk), ("B", "xt")], writes=[("hs", t)])
        layer_norm_tile(C, hs[:, t, :], ("hs", t), hs[:, t, :], ("hs", t), g_rep, b_rep, gb_key, scr)
        h_to_hT(C, hs, t, hT, ident, hb)


def load_rep(C, tile, ap, key):
    C.P.op("sp", f_dma(tile[:], ap), writes=[key], dsem=_san(key))


class Scope:
    def __init__(self, C):
        self.C = C

    def __enter__(self):
        self.st = ExitStack()
        self.prev = (self.C.st, self.C.side)
        self.C.st, self.C.side = self.st, "right"
        return self

    def __exit__(self, *a):
        self.C.P.barrier()
        self.st.close()
        self.C.st, self.C.side = self.prev
        return False


def build_A():
    nc = bass.Bass("TRN2", target_bir_lowering=False)
    x = dram_in(nc, "x", [TL, D], F32)
    w_in = dram_in(nc, "w_in", [D, 3 * D], F32)
    ident_d = dram_in(nc, "ident", [128, 128], BF16)
    qT = dram_out(nc, "qT", [8, 128, TL], BF16)
    kT = dram_out(nc, "kT", [8, 128, TL], BF16)
    v = dram_out(nc, "v", [8, 128, NT, 129], BF16)
    with ExitStack() as st:
        C = Ctx(nc, st)
        P = C.P
        ident = C.sb("ident", [128, 128], BF16)
        P.op("sp", f_dma(ident[:], ident_d), writes=["ident"], dsem="ident")
        phase_A(C, x, w_in, ident, qT, kT, v)
        P.emit()
    return nc


def phase_A(C, x, w_in, ident, qT, kT, v):
    P = C.P
    with Scope(C):
        wb = C.sb("wb", [128, 8, 3 * D], BF16)
        load_weight_bf16(C, wb, w_in, "wb", 3 * D)
        hT = C.sb("hTA", [128, 8, TL], BF16)
        xt = [C.sb(f"xtA{i}", [128, D], F32) for i in range(2)]
        xb = [C.sb(f"xbA{i}", [128, D], BF16) for i in range(2)]
        make_hT_from_dram(C, x, hT, ident, xt, xb, "A")

        def mkstore(out_d, tag):
            def store(g, stg, key, b):
                P.op("sp", f_dma(out_d.rearrange("h p n -> p h n")[:, :, g * 512:(g + 1) * 512], stg[:]),
                     reads=[key], writes=[(tag, "dram")], dsem=f"{tag}st{b}")
            return store
        fm_project(C, wb, "wb", 0, hT, 0.125, "q", mkstore(qT, "q"))
        fm_project(C, wb, "wb", D, hT, 1.0, "k", mkstore(kT, "k"))
        v_project(C, wb, "wb", 2 * D, hT, v, 8, 128, "v")


def build_B():
    nc = bass.Bass("TRN2", target_bir_lowering=False)
    x = dram_in(nc, "x", [TL, D], F32)
    qT = dram_in(nc, "qT", [8, 128, TL], BF16)
    kT_all = dram_in(nc, "kT_all", [4, 8, 128, TL], BF16)
    v_all = dram_in(nc, "v_all", [4, 8, 128, NT, 129], BF16)
    cst = const_inputs(nc)
    prm = {
        "lam": dram_in(nc, "lam", [128, 256], F32),
        "subg": dram_in(nc, "subg", [128, 128], F32),
        "w_out0": dram_in(nc, "w_out0", [D, D], F32),
        "w_up0": dram_in(nc, "w_up0", [D, DFF], F32),
        "w_down0": dram_in(nc, "w_down0", [DFF, D], F32),
        "w_kv": dram_in(nc, "w_kv", [D, 2 * D], F32),
    }
    for k in ("ln1_g0", "ln1_b0", "ln2_g0", "ln2_b0"):
        prm[k] = dram_in(nc, k, [128, D], F32)
    h1 = dram_out(nc, "h1", [TL, D], F32)
    kT1 = dram_out(nc, "kT1", [16, 96, TL], BF16)
    v1 = dram_out(nc, "v1", [16, 128, NT, 65], BF16)
    with ExitStack() as st:
        C = Ctx(nc, st)
        P = C.P
        cs = load_consts(C, cst)
        hs, hT, hT_stack = phase_B(C, cs, cst, prm, x, qT, kT_all, v_all, kT1, v1)
        st.callback(hT_stack.close)
        for t in range(NT):
            P.op("sp", f_dma(h1[t * 128:(t + 1) * 128, :], hs[:, t, :]), reads=[("hs", t)],
                 dsem=f"h1st{t % 2}")
        P.emit()
    return nc


def const_inputs(nc):
    return {
        "G": dram_in(nc, "G", [128, 16, 640], F32),
        "b31": dram_in(nc, "b31", [128, 16], F32),
        "ident": dram_in(nc, "ident", [128, 128], BF16),
        "e_own": dram_in(nc, "e_own", [32, TL], BF16),
        "past30k": dram_in(nc, "past30k", [128, NT, 32], F32),
    }


def load_consts(C, cst):
    P = C.P
    ident = C.sb("ident", [128, 128], BF16)
    P.op("sp", f_dma(ident[:], cst["ident"]), writes=["ident"], dsem="ident")
    b31 = C.sb("b31", [128, 16], F32)
    P.op("sp", f_dma(b31[:], cst["b31"]), writes=["b31"], dsem="b31")
    scr = ln_scratch(C)
    hb = [C.sb(f"hb{i}", [128, D], BF16) for i in range(2)]
    rep = [C.sb(f"rep{i}", [128, D], F32) for i in range(2)]
    kmT = C.sb("kmT", [128, 8, 32], BF16)
    return {"ident": ident, "b31": b31, "scr": scr, "hb": hb, "rep": rep, "kmT": kmT}


def alloc_hT(C):
    stk = ExitStack()
    hT = stk.enter_context(C.nc.sbuf_tensor("s_hT%d" % C.uid, [128, 8, TL], BF16, side="left"))
    C.uid += 1
    return hT, stk


def phase_B(C, cs, cst, prm, x, qT, kT_all, v_all, kT1, v1):
    P = C.P
    ident, b31, scr, hb, rep = cs["ident"], cs["b31"], cs["scr"], cs["hb"], cs["rep"]
    with Scope(C):
        ostore = C.sb("ostore", [128, NT, D], BF16)
        lam_sb = C.sb("lam_sb", [128, 256], F32)
        lprod = C.sb("lprod", [128, 2, 64], F32)
        lsum = C.sb("lsum", [128, 2], F32)
        lexp = C.sb("lexp", [128, 2], F32)
        nl = C.sb("nl", [128, 2], F32)
        gsub = C.sb("gsub", [128, 128], F32)
        P.op("sp", f_dma(lam_sb[:], prm["lam"]), writes=["lam_sb"], dsem="lam")
        P.op("sp", f_dma(gsub[:], prm["subg"]), writes=["gsub"], dsem="gsub")
        lv = lam_sb[:].rearrange("p (a b d) -> p a b d", a=2, b=2)
        P.op("dve", f_tt(lprod[:], lv[:, :, 0, :], lv[:, :, 1, :], ALU.mult),
             reads=["lam_sb"], writes=["lprod"])
        P.op("dve", lambda e: e.reduce_sum(out=lsum[:], in_=lprod[:], axis=AX.X),
             reads=["lprod"], writes=["lsum"])
        P.op("act", f_act(lexp[:], lsum[:], AF.Exp), reads=["lsum"], writes=["lexp"])
        P.op("dve", f_tt(nl[:, 0:1], lexp[:, 1:2], lexp[:, 0:1], ALU.subtract),
             reads=["lexp"], writes=["nl0"])
        P.op("dve", f_ts(nl[:, 1:2], nl[:, 0:1], -LAM_INIT0, None, ALU.add),
             reads=["nl0"], writes=["neglam"])
        neglam = nl[:, 1:2]
        P.op("dve", f_ts(gsub[:], gsub[:], 1.0 - LAM_INIT0, None, ALU.mult),
             reads=["gsub"], writes=["gsub"])
        ep = {k: [C.sb(f"ep_{k}{i}", shp, F32) for i in range(2)]
              for k, shp in (("rl", [128, 2]), ("c1", [128, 1]), ("t1", [128, 128]),
                             ("o", [128, 128]), ("sq", [128, 128]), ("ss", [128, 1]))}
        ecnt = [0]

        def epilogue(h, u, acc_ap, acc_key):
            for a in range(4):
                t = 4 * u + a
                k = ecnt[0] % 2
                ecnt[0] += 1
                rl, c1, t1, o, sq, ss = (ep[x_][k] for x_ in ("rl", "c1", "t1", "o", "sq", "ss"))
                a0, a1 = acc_ap(a), acc_ap(4 + a)
                k0, k1 = acc_key(a), acc_key(4 + a)
                P.op("dve", lambda e, rl=rl, a0=a0: e.reciprocal(out=rl[:, 0:1], in_=a0[:, 128:129]),
                     reads=[k0], writes=[("ep", k, "rl0")])
                P.op("dve", lambda e, rl=rl, a1=a1: e.reciprocal(out=rl[:, 1:2], in_=a1[:, 128:129]),
                     reads=[k1], writes=[("ep", k, "rl1")])
                P.op("dve", f_tt(c1[:], rl[:, 1:2], neglam, ALU.mult),
                     reads=[("ep", k, "rl1"), "neglam"], writes=[("ep", k, "c1")])
                P.op("dve", f_ts(t1[:], a1[:, 0:128], c1[:, 0:1], None, ALU.mult),
                     reads=[k1, ("ep", k, "c1")], writes=[("ep", k, "t1")])
                P.op("dve", f_stt(o[:], a0[:, 0:128], rl[:, 0:1], t1[:], ALU.mult, ALU.add),
                     reads=[k0, ("ep", k, "rl0"), ("ep", k, "t1")], writes=[("ep", k, "o")])
                P.op("pool", f_tt(sq[:], o[:], o[:], ALU.mult), reads=[("ep", k, "o")],
                     writes=[("ep", k, "sq")])
                P.op("dve", lambda e, ss=ss, sq=sq: e.reduce_sum(out=ss[:], in_=sq[:], axis=AX.X),
                     reads=[("ep", k, "sq")], writes=[("ep", k, "ss")])
                P.op("dve", f_ts(ss[:], ss[:], 1.0 / 128.0, LN_EPS, ALU.mult, ALU.add),
                     reads=[("ep", k, "ss")], writes=[("ep", k, "ss")])
                P.op("pool", f_tt(ss[:], ss[:], scr["mhalf"][:, 0:1], ALU.pow),
                     reads=[("ep", k, "ss"), "mhalf"], writes=[("ep", k, "ss")])
                P.op("dve", f_stt(ostore[:, t, h * 128:(h + 1) * 128], o[:], ss[:, 0:1], gsub[:],
                                  ALU.mult, ALU.mult),
                     reads=[("ep", k, "o"), ("ep", k, "ss"), "gsub"], writes=[("os", t)])

        with Scope(C):
            attention(C, dict(H=8, nm=2, KR=128, dv=128), qT, kT_all, v_all, cst["G"], b31, epilogue)
        C.side = "left"
        hs = C.nc.sbuf_tensor("s_hs", [128, NT, D], F32, side="left")
        hs = C.root.enter_context(hs)
        hT, hT_stack = alloc_hT(C)
        C.side = "right"
        with Scope(C):
            wo = C.sb("wo", [128, 8, D], BF16)
            load_weight_bf16(C, wo, prm["w_out0"], "wo", D)
            xt = C.sb("xtB", [128, D], F32)
            oT = [C.sb(f"oT{i}", [128, 8, 128], BF16) for i in range(2)]
            load_rep(C, rep[0], prm["ln1_g0"], "rep0")
            load_rep(C, rep[1], prm["ln1_b0"], "rep1")
            out_proj_ln(C, ostore, wo, x, hs, hT, ident, rep[0], rep[1], "rep0", scr, xt, hb, oT)
    with Scope(C):
        mlp(C, hs, hT, prm["w_up0"], prm["w_down0"], "m0")
    load_rep(C, rep[0], prm["ln2_g0"], "rep0")
    load_rep(C, rep[1], prm["ln2_b0"], "rep1")
    for t in range(NT):
        layer_norm_tile(C, hs[:, t, :], ("hs", t), hs[:, t, :], ("hs", t), rep[0], rep[1], "rep0", scr)
        h_to_hT(C, hs, t, hT, ident, hb)
    with Scope(C):
        wkv = C.sb("wk1", [128, 8, D], BF16)
        load_weight_bf16(C, wkv, prm["w_kv"], "wkv", D)

        def store(g, stg, key, b):
            kv = kT1.rearrange("(f two) r n -> two r f n", two=2)
            for hh in range(2):
                P.op("sp", f_dma(kv[hh, 0:64, :, g * 512:(g + 1) * 512], stg[hh * 64:(hh + 1) * 64, :, :]),
                     reads=[key], writes=[("k1dram", hh)], dsem=f"k1st{b}")
        fm_project(C, wkv, "wkv", 0, hT, 1.0, "k1", store)
        for h in range(16):
            P.op("sp", f_dma(kT1[h, 64:96, :], cst["e_own"]), writes=[("k1dramE",)], dsem="k1e")
    with Scope(C):
        wv1 = C.sb("wv1", [128, 8, D], BF16)
        load_weight_bf16(C, wv1, prm["w_kv"], "wv1", D, col0=D)
        v_project(C, wv1, "wv1", 0, hT, v1, 16, 64, "v1")
    return hs, hT, hT_stack


def build_C(stop_after=None):
    nc = bass.Bass("TRN2", target_bir_lowering=False)
    h1 = dram_in(nc, "h1", [TL, D], F32)
    kT1_all = dram_in(nc, "kT1_all", [4, 16, 96, TL], BF16)
    v1_all = dram_in(nc, "v1_all", [4, 16, 128, NT, 65], BF16)
    cst = const_inputs(nc)
    prm = {
        "w_q": dram_in(nc, "w_q", [D, D], F32),
        "w_out1": dram_in(nc, "w_out1", [D, D], F32),
        "w_up1": dram_in(nc, "w_up1", [D, DFF], F32),
        "w_down1": dram_in(nc, "w_down1", [DFF, D], F32),
    }
    for k in ("ln1_g1", "ln1_b1", "ln2_g1", "ln2_b1"):
        prm[k] = dram_in(nc, k, [128, D], F32)
    out = dram_out(nc, "out", [TL, D], F32)
    qaug = nc.dram_tensor("qaug", [16, 96, TL], BF16,
                          kind="ExternalOutput" if stop_after else "Internal").ap()
    with ExitStack() as st:
        C = Ctx(nc, st)
        C.stop_after = stop_after
        P = C.P
        cs = load_consts(C, cst)
        hs = C.sb("hs", [128, NT, D], F32)
        hT, hT_stack = alloc_hT(C)
        for t in range(NT):
            P.op("sp", f_dma(hs[:, t, :], h1[t * 128:(t + 1) * 128, :]), writes=[("hs", t)],
                 dsem=f"hsld{t}")
            h_to_hT(C, hs, t, hT, cs["ident"], cs["hb"])
        phase_C(C, cs, cst, prm, kT1_all, v1_all, qaug, hs, hT, hT_stack, out)
        P.emit()
    return nc


def phase_C(C, cs, cst, prm, kT1_all, v1_all, qaug, hs, hT, hT_stack, out):
    P = C.P
    ident, b31, scr, hb, rep = cs["ident"], cs["b31"], cs["scr"], cs["hb"], cs["rep"]
    kmT = cs["kmT"]
    with Scope(C):
        moba_kmean(C, kT1_all, kmT)
    if getattr(C, "stop_after", None) == "kmean":
        P.op("sp", f_dma(qaug.rearrange("h r n -> r h n")[0:64, 0:8, 0:32], kmT[0:64, :, :]),
             reads=[("kmT", f) for f in range(8)], dsem="dbg1")
        P.op("sp", f_dma(qaug.rearrange("h r n -> r h n")[0:64, 8:16, 0:32], kmT[64:128, :, :]),
             reads=[("kmT", f) for f in range(8)], dsem="dbg2")
        for t in range(NT):
            P.op("sp", f_dma(out[t * 128:(t + 1) * 128, :], hs[:, t, :]), reads=[("hs", t)],
                 dsem=f"outst{t % 2}")
        hT_stack.close()
        return
    with Scope(C):
        moba_gate_qproj(C, hT, prm["w_q"], kmT, cst["past30k"], qaug, ident)
    hT_stack.close()
    hT = None
    if getattr(C, "stop_after", None) == "gate":
        for t in range(NT):
            P.op("sp", f_dma(out[t * 128:(t + 1) * 128, :], hs[:, t, :]), reads=[("hs", t)],
                 dsem=f"outst{t % 2}")
        return
    with Scope(C):
        ostore = C.sb("ostore1", [128, NT, D], BF16)
        eprl = [C.sb(f"eprl{i}", [128, 1], F32) for i in range(8)]

        def epilogue(h, u, acc_ap, acc_key):
            for a in range(4):
                t = 4 * u + a
                k = (h * 4 + u) % 2
                rl = eprl[4 * k + a]
                acc = acc_ap(a)
                P.op("dve", lambda e, rl=rl, acc=acc: e.reciprocal(out=rl[:], in_=acc[:, 64:65]),
                     reads=[acc_key(a)], writes=[("eprl", 4 * k + a)])
                P.op("dve", f_ts(ostore[:, t, h * 64:(h + 1) * 64], acc[:, 0:64], rl[:, 0:1], None,
                                 ALU.mult),
                     reads=[acc_key(a), ("eprl", 4 * k + a)], writes=[("os", t)])

        with Scope(C):
            attention(C, dict(H=16, nm=1, KR=96, dv=64), qaug, kT1_all, v1_all, cst["G"], b31, epilogue)
        hT, hT_stack2 = alloc_hT(C)
        C.root.callback(hT_stack2.close)
        with Scope(C):
            wo = C.sb("wo1", [128, 8, D], BF16)
            load_weight_bf16(C, wo, prm["w_out1"], "wo", D)
            oT = [C.sb(f"oT1{i}", [128, 8, 128], BF16) for i in range(2)]
            load_rep(C, rep[0], prm["ln1_g1"], "rep0")
            load_rep(C, rep[1], prm["ln1_b1"], "rep1")
            out_proj_ln(C, ostore, wo, None, hs, hT, ident, rep[0], rep[1], "rep0", scr, None, hb, oT)
    with Scope(C):
        mlp(C, hs, hT, prm["w_up1"], prm["w_down1"], "m1")
    load_rep(C, rep[0], prm["ln2_g1"], "rep0")
    load_rep(C, rep[1], prm["ln2_b1"], "rep1")
    for t in range(NT):
        layer_norm_tile(C, hs[:, t, :], ("hs", t), hs[:, t, :], ("hs", t), rep[0], rep[1], "rep0", scr)
        P.op("sp", f_dma(out[t * 128:(t + 1) * 128, :], hs[:, t, :]), reads=[("hs", t)],
             dsem=f"outst{t % 2}")


def moba_kmean(C, kT1_all, kmT):
    P = C.P
    Kp = [C.sb(f"Kp{i}", [128, S], BF16) for i in range(2)]
    ksum = [C.sb(f"ksum{i}", [128, 64], F32) for i in range(2)]
    kmf = [C.sb(f"kmf{i}", [128, 16, 2], F32) for i in range(2)]
    for f in range(8):
        b = f % 2
        for r in range(4):
            for hh in range(2):
                P.op("sp", f_dma(Kp[b][hh * 64:(hh + 1) * 64, r * TL:(r + 1) * TL],
                                 kT1_all[r, 2 * f + hh, 0:64, :]),
                     writes=[("Kp", b)], dsem=f"Kp{b}")
        P.op("dve", lambda e, b=b: e.reduce_sum(out=ksum[b][:],
                                                in_=Kp[b][:].rearrange("p (s q) -> p s q", q=128),
                                                axis=AX.X),
             reads=[("Kp", b)], writes=[("ksum", b)])
        kv = ksum[b][:].rearrange("p (r t) -> p r t", t=16)
        for par in range(2):
            P.op("dve", f_tt(kmf[b][:, :, par], kv[:, 2 * par, :], kv[:, 2 * par + 1, :], ALU.add),
                 reads=[("ksum", b)], writes=[("kmf", b, par)])
        P.op("dve", f_ts(kmT[:, f, :], kmf[b][:].rearrange("p t two -> p (t two)"), 1.0 / 256.0,
                         None, ALU.mult),
             reads=[("kmf", b, 0), ("kmf", b, 1)], writes=[("kmT", f)])


def moba_gate_qproj(C, hT, w_q, kmT, past_d, qaug, ident):
    P = C.P
    wq = C.sb("wq", [128, 8, D], BF16)
    load_weight_bf16(C, wq, w_q, "wq", D)
    past = C.sb("past", [128, NT, 32], F32)
    padd = C.sb("padd", [128, NT, 32], F32)
    P.op("sp", f_dma(past[:], past_d), writes=["past"], dsem="past")
    P.op("dve", f_ts(padd[:], past[:], NEG, None, ALU.add), reads=["past"], writes=["padd"])
    mstg = [C.sb(f"mstg{i}", [128, 16, 128], BF16) for i in range(2)]
    gm = [C.sb(f"gm{i}", [128, 16, 32], F32) for i in range(2)]
    top8 = [C.sb(f"top8{i}", [128, 16, 8], F32) for i in range(2)]
    sel = [C.sb(f"sel{i}", [128, 16, 32], F32) for i in range(2)]
    negpad = [C.sb(f"negpad{i}", [128, 16, 96], BF16) for i in range(2)]
    for i in range(2):
        P.op("pool", f_memset(negpad[i][:], 0.0), writes=[("negpad", i)])

    def store(g, stg, key, b):
        qv = qaug.rearrange("(f two) r n -> two r f n", two=2)
        for hh in range(2):
            P.op("sp", f_dma(qv[hh, 0:64, :, g * 512:(g + 1) * 512], stg[hh * 64:(hh + 1) * 64, :, :]),
                 reads=[key], writes=[("qaug", "q")], dsem=f"qst{b}")
        for a in range(4):
            t = 4 * g + a
            k = t % 2
            gbk = (4, 5) if k == 0 else (0, 1)
            for h in range(16):
                r0 = (h % 2) * 64
                gps = C.bank[gbk[h % 2]]
                P.op("pe", f_mm(gps[:, (h // 2) * 32:(h // 2 + 1) * 32],
                                stg[r0:r0 + 64, h // 2, a * 128:(a + 1) * 128],
                                kmT[r0:r0 + 64, h // 2, :], True, True),
                     reads=[key, ("kmT", h // 2)], writes=[("bank", gbk[h % 2])])
            gmv = gm[k][:].rearrange("p (f two) n -> p two f n", two=2)
            for par in range(2):
                P.op("dve", f_tt(gmv[:, par], C.bank[gbk[par]][:, 0:256].rearrange("p (f n) -> p f n", n=32),
                                 padd[:, t, :].unsqueeze(1).broadcast_to([128, 8, 32]), ALU.add),
                     reads=[("bank", gbk[par]), "padd"], writes=[("gm", k)])
            for h in range(16):
                P.op("dve", lambda e, k=k, h=h: e.max(out=top8[k][:, h, :], in_=gm[k][:, h, :]),
                     reads=[("gm", k)], writes=[("top8", k)])
            P.op("dve", f_tt(sel[k][:], gm[k][:], top8[k][:, :, 2:3].broadcast_to([128, 16, 32]),
                             ALU.is_ge),
                 reads=[("gm", k), ("top8", k)], writes=[("sel", k)])
            P.op("dve", f_ts(sel[k][:], sel[k][:], -1.0, None, ALU.add),
                 reads=[("sel", k)], writes=[("sel", k)])
            P.op("dve", f_tt(negpad[k][:, :, 64:96], sel[k][:],
                             past[:, t, :].unsqueeze(1).broadcast_to([128, 16, 32]), ALU.mult),
                 reads=[("sel", k), "past"], writes=[("negpad", k)])
            for hq in range(2):
                bk = 6 + hq
                tp = C.bank_bf(bk).rearrange("p (h n) -> p h n", n=128)
                for h8 in range(8):
                    h = hq * 8 + h8
                    P.op("pe", f_tr(tp[0:96, h8, :], negpad[k][:, h, :], ident[:]),
                         reads=[("negpad", k), "ident"], writes=[("bank", bk)])
                P.op("act", f_act(mstg[k][64:96, hq * 8:(hq + 1) * 8, :], tp[64:96, :, :], AF.Copy),
                     reads=[("bank", bk)], writes=[("mstg", k)])
            P.op("sp", f_dma(qaug.rearrange("h r n -> r h n")[64:96, :, t * 128:(t + 1) * 128],
                             mstg[k][64:96, :, :]),
                 reads=[("mstg", k)], writes=[("qaug", "m")], dsem=f"mst{k}")

    fm_project(C, wq, "wq", 0, hT, 0.125, "qb", store)


def _bucket_np(n):
    n = np.maximum(n, 0)
    nf = np.maximum(n, 16).astype(np.float32)
    large = 16 + (np.log(nf / np.float32(16)) / np.float32(math.log(8.0)) * np.float32(16)).astype(np.int32)
    large = np.minimum(large, 31)
    return np.where(n < 16, n, large)


def _g_index(j):
    k = np.arange(128)[:, None, None]
    q = np.arange(128)[None, None, :]
    e = (np.arange(5) - 1)[None, :, None]
    delta = e - j
    rel = q - k - 128 * delta
    idx = _bucket_np(rel)
    idx = np.where(rel >= 0, idx, 32)
    return np.broadcast_to(idx, (128, 5, 128)).astype(np.int64)


_CACHE = {}


def _get(name, fn):
    if name not in _CACHE:
        _CACHE[name] = fn()
    return _CACHE[name]


def _host_prep(x, rel_bias):
    f32 = np.float32
    bf = ml_dtypes.bfloat16
    table_ext = np.concatenate([rel_bias, np.full((1, 16), NEG, f32)], axis=0)
    per = []
    for c in range(NCORES):
        b, j = c // 4, c % 4
        xs = np.ascontiguousarray(x[b].reshape(64, 128, D)[j::4].reshape(TL, D))
        G = table_ext[_g_index(j)]
        G = np.ascontiguousarray(G.transpose(0, 3, 1, 2).reshape(128, 16, 640))
        blk = (4 * np.arange(NT) + j) // 2
        e = np.zeros((32, NT, 128), f32)
        e[blk, np.arange(NT), :] = 1.0
        past = (np.arange(32)[None, :] < blk[:, None]).astype(f32) * f32(30000.0)
        per.append({
            "x": xs, "G": G, "e_own": e.reshape(32, TL).astype(bf),
            "past30k": np.ascontiguousarray(np.broadcast_to(past[None], (128, NT, 32))),
            "b31": np.ascontiguousarray(np.broadcast_to(rel_bias[31:32, :], (128, 16))),
            "ident": np.eye(128, dtype=f32).astype(bf),
        })
    return per


def _rep(v, n=128):
    v = np.asarray(v, np.float32).reshape(1, -1)
    return np.ascontiguousarray(np.broadcast_to(v, (n, v.shape[1])))


CONST_KEYS = ("G", "b31", "ident", "e_own", "past30k")


def kernel(x, w_in_a, lam_a, subln_a, w_out_a, w_kv_shared, w_q_b, w_out_b, rel_bias,
           ln1_g, ln1_b, ln2_g, ln2_b, w_up, w_down):
    f32 = np.float32
    A = lambda v: np.ascontiguousarray(np.asarray(v, f32))
    x = A(x)
    rel_bias = A(rel_bias)
    cores = list(range(NCORES))
    per = _host_prep(x, rel_bias)
    ncA = _get("A", build_A)
    inA = [{"x": per[c]["x"], "w_in": A(w_in_a[0]), "ident": per[c]["ident"]} for c in cores]
    rA = run_bass_kernel_spmd(ncA, inA, core_ids=cores).results
    ncB = _get("B", build_B)
    prmB = {
        "lam": _rep(np.asarray(lam_a[0], f32).reshape(-1)), "subg": _rep(subln_a[0]),
        "w_out0": A(w_out_a[0]), "w_up0": A(w_up[0]), "w_down0": A(w_down[0]),
        "w_kv": A(w_kv_shared),
        "ln1_g0": _rep(ln1_g[0]), "ln1_b0": _rep(ln1_b[0]),
        "ln2_g0": _rep(ln2_g[0]), "ln2_b0": _rep(ln2_b[0]),
    }
    inB = []
    for c in cores:
        grp = [4 * (c // 4) + r for r in range(4)]
        d = {"x": per[c]["x"], "qT": rA[c]["qT"],
             "kT_all": np.stack([rA[g]["kT"] for g in grp]),
             "v_all": np.stack([rA[g]["v"] for g in grp])}
        d.update({k: per[c][k] for k in CONST_KEYS})
        d.update(prmB)
        inB.append(d)
    rB = run_bass_kernel_spmd(ncB, inB, core_ids=cores).results
    ncC = _get("C", build_C)
    prmC = {
        "w_q": A(w_q_b[0]), "w_out1": A(w_out_b[0]), "w_up1": A(w_up[1]), "w_down1": A(w_down[1]),
        "ln1_g1": _rep(ln1_g[1]), "ln1_b1": _rep(ln1_b[1]),
        "ln2_g1": _rep(ln2_g[1]), "ln2_b1": _rep(ln2_b[1]),
    }
    inC = []
    for c in cores:
        grp = [4 * (c // 4) + r for r in range(4)]
        d = {"h1": rB[c]["h1"],
             "kT1_all": np.stack([rB[g]["kT1"] for g in grp]),
             "v1_all": np.stack([rB[g]["v1"] for g in grp])}
        d.update({k: per[c][k] for k in CONST_KEYS})
        d.update(prmC)
        inC.append(d)
    rC = run_bass_kernel_spmd(ncC, inC, core_ids=cores).results
    out = np.empty((2, 64, 128, D), f32)
    for c in cores:
        b, j = c // 4, c % 4
        out[b, j::4] = np.asarray(rC[c]["out"], f32).reshape(NT, 128, D)
    return out.reshape(2, S, D)
```

```python
import math
from contextlib import ExitStack

import numpy as np
import ml_dtypes
import concourse.bass as bass
import concourse.mybir as mybir
from concourse.bass_utils import run_bass_kernel_spmd

F32 = mybir.dt.float32
BF16 = mybir.dt.bfloat16
AF = mybir.ActivationFunctionType
ALU = mybir.AluOpType
AX = mybir.AxisListType

NCORES = 8
D = 1024
S = 8192
NT = 16
TL = 2048
DFF = 4096
ALPHA = 4.0 ** 0.25
LN_EPS = 1e-5
LAM_INIT0 = 0.8 - 0.6 * math.exp(-0.3 * 0)
NEG = -30000.0

ENGS = ("pe", "act", "dve", "pool", "sp")


class Op:
    __slots__ = ("fn", "deps", "dsem", "seq", "waits")

    def __init__(self, fn, deps, dsem):
        self.fn = fn
        self.deps = deps
        self.dsem = dsem
        self.seq = None
        self.waits = []


class Prog:
    def __init__(self, nc):
        self.nc = nc
        self.ops = {e: [] for e in ENGS}
        self.last_w = {}
        self.readers = {}

    def op(self, eng, fn, reads=(), writes=(), dsem=None):
        deps = set()
        for k in reads:
            w = self.last_w.get(k)
            if w is not None:
                deps.add(w)
        for k in writes:
            w = self.last_w.get(k)
            if w is not None:
                deps.add(w)
            rd = self.readers.get(k)
            if rd:
                for e2, i2 in rd.items():
                    deps.add((e2, i2))
        idx = len(self.ops[eng])
        me = (eng, idx)
        deps.discard(me)
        self.ops[eng].append(Op(fn, deps, dsem))
        for k in reads:
            self.readers.setdefault(k, {})[eng] = idx
        for k in writes:
            self.last_w[k] = me
            self.readers[k] = {}
        return me

    def barrier(self):
        deps = set()
        for e in ENGS:
            ops = self.ops[e]
            last_real = None
            seen_d = set()
            for i in range(len(ops) - 1, -1, -1):
                o = ops[i]
                if o.fn is None:
                    continue
                if o.dsem is not None:
                    if o.dsem not in seen_d:
                        seen_d.add(o.dsem)
                        deps.add((e, i))
                elif last_real is None:
                    last_real = i
                    deps.add((e, i))
        for e in ENGS:
            self.ops[e].append(Op(None, set(deps), None))

    def resolve(self):
        needed = set()
        for e in ENGS:
            for o in self.ops[e]:
                for d in o.deps:
                    if d[0] == e and e == "pe":
                        continue
                    needed.add(d)
        self.sem_names = set()
        cnt = {}
        for e in ENGS:
            for i, o in enumerate(self.ops[e]):
                if o.dsem is not None:
                    name = "d_" + o.dsem
                    cnt[name] = cnt.get(name, 0) + 16
                    o.seq = (name, cnt[name])
                    self.sem_names.add(name)
                elif (e, i) in needed:
                    name = "e_" + e
                    cnt[name] = cnt.get(name, 0) + 1
                    o.seq = (name, cnt[name])
                    self.sem_names.add(name)
        self.finals = dict(cnt)
        for e in ENGS:
            seen = {}
            for o in self.ops[e]:
                req = {}
                for d in o.deps:
                    if d[0] == e and e == "pe":
                        continue
                    name, val = self.ops[d[0]][d[1]].seq
                    if req.get(name, 0) < val:
                        req[name] = val
                for name, val in req.items():
                    if seen.get(name, 0) < val:
                        seen[name] = val
                        o.waits.append((name, val))

    def check(self):
        sem = {}
        pc = {e: 0 for e in ENGS}
        n = {e: len(self.ops[e]) for e in ENGS}
        progress = True
        while progress:
            progress = False
            for e in ENGS:
                while pc[e] < n[e]:
                    o = self.ops[e][pc[e]]
                    if all(sem.get(nm, 0) >= v for nm, v in o.waits):
                        if o.seq is not None:
                            nm = o.seq[0]
                            sem[nm] = sem.get(nm, 0) + (16 if nm.startswith("d_") else 1)
                        pc[e] += 1
                        progress = True
                    else:
                        break
        stuck = {e: (pc[e], n[e]) for e in ENGS if pc[e] < n[e]}
        if stuck:
            msg = []
            for e, (p, m) in stuck.items():
                o = self.ops[e][p]
                msg.append(f"{e}@{p}/{m} waits {[(nm, v, sem.get(nm, 0)) for nm, v in o.waits]}")
            raise RuntimeError("sync deadlock: " + "; ".join(msg))
        return {e: n[e] for e in ENGS}, len(self.sem_names)

    def emit(self):
        nc = self.nc
        self.resolve()
        print("prog ops/sems:", self.check(), flush=True)
        with ExitStack() as st:
            sems = {}
            for name in sorted(self.sem_names):
                sems[name] = st.enter_context(nc.semaphore(name))
            block = st.enter_context(nc.Block())

            def run(engname):
                def body(engine):
                    for o in self.ops[engname]:
                        if o.fn is None:
                            for (name, val) in o.waits:
                                engine.wait_ge(sems[name], val)
                            continue
                        for (name, val) in o.waits[1:]:
                            engine.wait_ge(sems[name], val)
                        ins = o.fn(engine)
                        if o.waits:
                            ins._wait_ge(sems[o.waits[0][0]], o.waits[0][1])
                        if o.seq is not None:
                            name = o.seq[0]
                            ins.then_inc(sems[name], 16 if name.startswith("d_") else 1)
                    if engname == "sp":
                        for name, val in self.finals.items():
                            if name.startswith("d_"):
                                engine.wait_ge(sems[name], val)
                return body

            block.tensor(run("pe"))
            block.scalar(run("act"))
            block.vector(run("dve"))
            block.gpsimd(run("pool"))
            block.sync(run("sp"))


def f_mm(out, lhsT, rhs, start, stop, skip=False):
    return lambda e: e.matmul(out, lhsT=lhsT, rhs=rhs, start=start, stop=stop,
                              skip_group_check=skip)


def f_tr(out, in_, ident):
    return lambda e: e.transpose(out=out, in_=in_, identity=ident)


def f_act(out, in_, func, bias=None, scale=1.0):
    if bias is None:
        return lambda e: e.activation(out=out, in_=in_, func=func, scale=scale)
    return lambda e: e.activation(out=out, in_=in_, func=func, bias=bias, scale=scale)


def f_tt(out, in0, in1, op):
    return lambda e: e.tensor_tensor(out=out, in0=in0, in1=in1, op=op)


def f_ts(out, in0, s1, s2, op0, op1=None):
    if op1 is None:
        return lambda e: e.tensor_scalar(out=out, in0=in0, scalar1=s1, scalar2=None, op0=op0)
    return lambda e: e.tensor_scalar(out=out, in0=in0, scalar1=s1, scalar2=s2, op0=op0, op1=op1)


def f_stt(out, in0, scalar, in1, op0, op1):
    return lambda e: e.scalar_tensor_tensor(out=out, in0=in0, scalar=scalar, in1=in1,
                                            op0=op0, op1=op1)


def f_copy(out, in_):
    return lambda e: e.tensor_copy(out=out, in_=in_)


def f_dma(out, in_):
    return lambda e: e.dma_start(out=out, in_=in_)


def f_memset(ap, v):
    return lambda e: e.memset(ap, v)


def _san(key):
    return "".join(ch for ch in str(key) if ch.isalnum())


class Ctx:
    def __init__(self, nc, st):
        self.nc = nc
        self.st = st
        self.P = Prog(nc)
        self.psum = st.enter_context(nc.psum_tensor("psum_all", [128, 8, 512], F32))
        self.bank = [self.psum[:, i, :] for i in range(8)]
        self.uid = 0
        self.side = "left"
        self.root = st

    def sb(self, name, shape, dt):
        return self.st.enter_context(self.nc.sbuf_tensor("s_" + name, shape, dt, side=self.side))

    def bank_bf(self, i):
        return self.bank[i].bitcast(BF16)


def load_weight_bf16(C, wt, w_ap, key, ncols, col0=0, row0=0, nchunks=8):
    P = C.P
    step = 1024
    for c in range(nchunks):
        for n0 in range(0, ncols, step):
            n1 = min(ncols, n0 + step)
            P.op("pool", f_dma(wt[:, c, n0:n1],
                               w_ap[row0 + c * 128: row0 + (c + 1) * 128, col0 + n0: col0 + n1]),
                 writes=[key], dsem=_san(key))


def make_hT_from_dram(C, x_ap, hT, ident, xt, xb, tag):
    P = C.P
    for t in range(NT):
        b = t % 2
        P.op("sp", f_dma(xt[b][:], x_ap[t * 128:(t + 1) * 128, :]),
             writes=[(tag, "xt", b)], dsem=f"{tag}xt{b}")
        P.op("act", f_act(xb[b][:], xt[b][:], AF.Copy), reads=[(tag, "xt", b)],
             writes=[(tag, "xb", b)])
        transpose_tile(C, xb[b], (tag, "xb", b), hT, t, ident, b)


def transpose_tile(C, src_bf, src_key, hT, t, ident, b):
    P = C.P
    pT = C.bank_bf(b).rearrange("p (c n) -> p c n", n=128)
    for c in range(8):
        P.op("pe", f_tr(pT[:, c, :], src_bf[:, c * 128:(c + 1) * 128], ident[:]),
             reads=[src_key, "ident"], writes=[("bank", b)])
    P.op("dve", f_copy(hT[:, :, t * 128:(t + 1) * 128], pT), reads=[("bank", b)],
         writes=[("hT", t)])


def layer_norm_tile(C, src, src_key, dst, dst_key, g_rep, b_rep, gb_key, scr):
    P = C.P
    u = C.uid
    C.uid += 1
    k = u % 2
    stats, mv, rstd, tmp = scr["stats"][k], scr["mv"][k], scr["rstd"][k], scr["tmp"][k]
    kk = ("ln", k)
    kt = ("ln", "tmp")
    P.op("dve", lambda e: e.bn_stats(out=stats[:, 0, :], in_=src[:, 0:512]),
         reads=[src_key], writes=[kk + ("st",)])
    P.op("dve", lambda e: e.bn_stats(out=stats[:, 1, :], in_=src[:, 512:1024]),
         reads=[src_key], writes=[kk + ("st",)])
    P.op("dve", lambda e: e.bn_aggr(out=mv[:], in_=stats[:].rearrange("p a s -> p (a s)")),
         reads=[kk + ("st",)], writes=[kk + ("mv",)])
    P.op("dve", f_ts(rstd[:], mv[:, 1:2], LN_EPS, None, ALU.add),
         reads=[kk + ("mv",)], writes=[kk + ("rs0",)])
    P.op("pool", f_tt(rstd[:], rstd[:], scr["mhalf"][:, 0:1], ALU.pow),
         reads=[kk + ("rs0",), "mhalf"], writes=[kk + ("rs0",)])
    P.op("dve", f_ts(tmp[:], src, mv[:, 0:1], rstd[:, 0:1], ALU.subtract, ALU.mult),
         reads=[src_key, kk + ("mv",), kk + ("rs0",)], writes=[kt])
    P.op("pool", f_tt(tmp[:], tmp[:], g_rep[:], ALU.mult),
         reads=[kt, gb_key], writes=[kt])
    P.op("pool", f_tt(dst, tmp[:], b_rep[:], ALU.add),
         reads=[kt, gb_key], writes=[dst_key])


def ln_scratch(C):
    scr = {
        "stats": [C.sb(f"ln_stats{k}", [128, 2, 6], F32) for k in range(2)],
        "mv": [C.sb(f"ln_mv{k}", [128, 2], F32) for k in range(2)],
        "rstd": [C.sb(f"ln_rstd{k}", [128, 1], F32) for k in range(2)],
        "tmp": [C.sb("ln_tmp", [128, 1024], F32)] * 2,
        "mhalf": C.sb("ln_mhalf", [128, 1], F32),
    }
    C.P.op("dve", f_memset(scr["mhalf"][:], -0.5), writes=["mhalf"])
    return scr


def h_to_hT(C, hs, t, hT, ident, hb):
    P = C.P
    b = t % 2
    P.op("act", f_act(hb[b][:], hs[:, t, :], AF.Copy), reads=[("hs", t)], writes=[("hb", b)])
    transpose_tile(C, hb[b], ("hb", b), hT, t, ident, b)


def mlp(C, hs, hT, w_up_ap, w_down_ap, tagp):
    P = C.P
    NS = 8
    wu = [C.sb(f"{tagp}wu{i}", [128, 8, 512], BF16) for i in range(2)]
    wd = [C.sb(f"{tagp}wd{i}", [128, 4, 1024], BF16) for i in range(2)]
    uT1 = C.sb(f"{tagp}uT", [128, 4, TL], BF16)
    uT = [uT1, uT1]
    rl = [C.sb(f"{tagp}rl{i}", [128, 512], F32) for i in range(2)]
    cnt = 0
    dcnt = 0

    def wload(s):
        b = s % 2
        load_weight_bf16(C, wu[b], w_up_ap, (tagp, "wu", b), 512, col0=s * 512)
        for fc in range(4):
            P.op("pool", f_dma(wd[b][:, fc, :],
                               w_down_ap[s * 512 + fc * 128: s * 512 + (fc + 1) * 128, :]),
                 writes=[(tagp, "wd", b)], dsem=f"{tagp}wd{b}")

    wload(0)
    for s in range(NS):
        b = s % 2
        for g in range(4):
            for fc in range(4):
                bk = 2 + (cnt % 2)
                ps = C.bank[bk]
                for c in range(8):
                    P.op("pe", f_mm(ps[:], wu[b][:, c, fc * 128:(fc + 1) * 128],
                                    hT[:, c, g * 512:(g + 1) * 512], c == 0, c == 7),
                         reads=[(tagp, "wu", b)] + [("hT", 4 * g + a) for a in range(4)],
                         writes=[("bank", bk)])
                r = rl[cnt % 2]
                P.op("act", f_act(r[:], ps[:], AF.Relu), reads=[("bank", bk)],
                     writes=[(tagp, "rl", cnt % 2)])
                P.op("pool", f_tt(uT[b][:, fc, g * 512:(g + 1) * 512], r[:], r[:], ALU.mult),
                     reads=[(tagp, "rl", cnt % 2)], writes=[(tagp, "uT", g)])
                cnt += 1
        if s + 1 < NS:
            wload(s + 1)
        for t in range(NT):
            for half in range(2):
                bk = 4 + (dcnt % 4)
                ps = C.bank[bk]
                for fc in range(4):
                    P.op("pe", f_mm(ps[:], uT[b][:, fc, t * 128:(t + 1) * 128],
                                    wd[b][:, fc, half * 512:(half + 1) * 512], fc == 0, fc == 3),
                         reads=[(tagp, "uT", t // 4), (tagp, "wd", b)], writes=[("bank", bk)])
                dst = hs[:, t, half * 512:(half + 1) * 512]
                if s == 0:
                    P.op("dve", f_stt(dst, dst, ALPHA, ps[:], ALU.mult, ALU.add),
                         reads=[("bank", bk), ("hs", t)], writes=[("hs", t)])
                else:
                    P.op("dve", f_tt(dst, dst, ps[:], ALU.add),
                         reads=[("bank", bk), ("hs", t)], writes=[("hs", t)])
                dcnt += 1


def attention(C, cfg, q_d, k_d, v_d, G_d, b31, epilogue):
    P = C.P
    H, nm, KR, dv = cfg["H"], cfg["nm"], cfg["KR"], cfg["dv"]
    dv1 = dv + 1
    per_bank = 512 // dv1 if nm == 1 else 3
    Kb = [C.sb(f"Kb{i}", [128, S], BF16) for i in range(2)]
    Vb = [C.sb(f"Vb{i}", [128, 64, dv1], BF16) for i in range(2)]
    Qb = [C.sb(f"Qb{i}", [128, TL], BF16) for i in range(2)]
    Gb = [C.sb(f"Gb{i}", [128, nm, 640], F32) for i in range(2)]
    PT = [C.sb(f"PT{s_}", [128, 2, 512], BF16) for s_ in range(3)]

    def rows(m):
        return (64 * m, 64 * m + 64) if nm == 2 else (0, KR)

    def mapidx(h, m):
        return 2 * h + m if nm == 2 else h

    def acc_bank(n):
        return 4 + n // per_bank

    def acc_ap(n):
        off = (n % per_bank) * dv1
        return C.bank[acc_bank(n)][:, off:off + dv1]

    def acc_key(n):
        return ("bank", acc_bank(n))

    def load(h):
        p = h % 2
        for r in range(4):
            P.op("sp", f_dma(Kb[p][0:KR, r * TL:(r + 1) * TL], k_d[r, h]),
                 writes=[("K", p)], dsem=f"K{p}")
        for r in range(4):
            P.op("sp", f_dma(Vb[p][:, r * 16:(r + 1) * 16, :], v_d[r, h]),
                 writes=[("V", p)], dsem=f"V{p}")
        P.op("sp", f_dma(Qb[p][0:KR, :], q_d[h]), writes=[("Q", p)], dsem=f"Q{p}")
        m0 = mapidx(h, 0)
        P.op("sp", f_dma(Gb[p][:], G_d[:, m0:m0 + nm, :]), writes=[("G", p)], dsem=f"G{p}")
        for m in range(nm):
            mi = mapidx(h, m)
            P.op("pool", f_ts(Gb[p][:, m, :], Gb[p][:, m, :], b31[:, mi:mi + 1], None, ALU.subtract),
                 reads=[("G", p), "b31"], writes=[("G", p)])

    def amin_of(u, c):
        return max(0, -((-(c - 16 * u - 3)) // 4))

    def afar_of(u, c):
        return min(4, max(0, -((-(c - 16 * u + 2)) // 4)))

    def kslot(c):
        return (c % 4) * 16 + c // 4

    items = []
    for h in range(H):
        for u in range(4):
            if nm == 2:
                for c in range(16 * u + 16):
                    items.append((h, u, ((c, 0), (c, 1))))
            else:
                for cp in range(8 * u + 8):
                    items.append((h, u, ((2 * cp, 0), (2 * cp + 1, 0))))

    def emit_qk_exp(i):
        h, u, subs = items[i]
        p = h % 2
        slot = i % 2
        psl = i % 3
        bks = (2 * slot, 2 * slot + 1)
        amins = [amin_of(u, c) for c, m in subs]
        afars = [afar_of(u, c) for c, m in subs]
        for j, (c, m) in enumerate(subs):
            r0, r1 = rows(m)
            col = kslot(c) * 128
            amin = amins[j]
            P.op("pe", f_mm(C.bank[bks[j]][:, amin * 128:512], Kb[p][r0:r1, col:col + 128],
                            Qb[p][r0:r1, (4 * u + amin) * 128:(4 * u + 4) * 128], True, True),
                 reads=[("K", p), ("Q", p)], writes=[("bank", bks[j])])
        if nm == 2:
            c = subs[0][0]
            for a in range(amins[0], afars[0]):
                s_ = c - (16 * u + 4 * a) + 1
                P.op("dve", f_tt(C.psum[:, bks[0]:bks[0] + 2, a * 128:(a + 1) * 128],
                                 C.psum[:, bks[0]:bks[0] + 2, a * 128:(a + 1) * 128],
                                 Gb[p][:, :, s_ * 128:(s_ + 1) * 128], ALU.add),
                     reads=[("bank", bks[0]), ("bank", bks[1]), ("G", p)],
                     writes=[("bank", bks[0]), ("bank", bks[1])])
        else:
            for j, (c, m) in enumerate(subs):
                for a in range(amins[j], afars[j]):
                    s_ = c - (16 * u + 4 * a) + 1
                    P.op("dve", f_tt(C.bank[bks[j]][:, a * 128:(a + 1) * 128],
                                     C.bank[bks[j]][:, a * 128:(a + 1) * 128],
                                     Gb[p][:, 0, s_ * 128:(s_ + 1) * 128], ALU.add),
                         reads=[("bank", bks[j]), ("G", p)], writes=[("bank", bks[j])])
        am = min(amins)
        P.op("act", f_act(PT[psl][:, :, am * 128:512], C.psum[:, bks[0]:bks[0] + 2, am * 128:512], AF.Exp),
             reads=[("bank", bks[0]), ("bank", bks[1])],
             writes=[("PT", psl, j, a) for j in range(2) for a in range(am, 4)])

    def emit_pv(i):
        h, u, subs = items[i]
        p = h % 2
        psl = i % 3
        for j, (c, m) in enumerate(subs):
            amin = amin_of(u, c)
            for a in range(amin, 4):
                n = m * 4 + a
                first = (c == 0) and (n % per_bank == 0)
                last = (c == 16 * u + 4 * a + 3)
                P.op("pe", f_mm(acc_ap(n), PT[psl][:, j, a * 128:(a + 1) * 128],
                                Vb[p][:, kslot(c), :], first, last, skip=True),
                     reads=[("PT", psl, j, a), ("V", p)], writes=[acc_key(n)])

    load(0)
    if H > 1:
        load(1)
    emit_qk_exp(0)
    for i in range(len(items)):
        h, u, subs = items[i]
        nxt = items[i + 1] if i + 1 < len(items) else None
        if nxt is not None:
            emit_qk_exp(i + 1)
        emit_pv(i)
        if subs[1][0] == 16 * u + 15:
            epilogue(h, u, acc_ap, acc_key)
        if nxt is not None and nxt[0] != h and h + 2 < H:
            load(h + 2)


def dram_in(nc, name, shape, dt):
    return nc.dram_tensor(name, list(shape), dt, kind="ExternalInput").ap()


def dram_out(nc, name, shape, dt):
    return nc.dram_tensor(name, list(shape), dt, kind="ExternalOutput").ap()


def fm_project(C, w, wkey, col0, hT, scale, tag, store):
    P = C.P
    stg = [C.sb(f"{tag}stg{i}", [128, 8, 512], BF16) for i in range(2)]
    cnt = 0
    for g in range(4):
        b = g % 2
        for f in range(8):
            bk = 2 + (cnt % 2)
            cnt += 1
            ps = C.bank[bk]
            for c in range(8):
                P.op("pe", f_mm(ps[:], w[:, c, col0 + f * 128: col0 + (f + 1) * 128],
                                hT[:, c, g * 512:(g + 1) * 512], c == 0, c == 7),
                     reads=[wkey] + [("hT", 4 * g + a) for a in range(4)], writes=[("bank", bk)])
            P.op("act", f_act(stg[b][:, f, :], ps[:], AF.Copy, scale=scale),
                 reads=[("bank", bk)], writes=[(tag, "stg", b)])
        store(g, stg[b], (tag, "stg", b), b)
    return stg


def v_project(C, wb, wkey, col0, hT, out_d, nh, dv, tag):
    P = C.P
    dv1 = dv + 1
    stg = [C.sb(f"{tag}vstg{i}", [128, nh, 4, dv1], BF16) for i in range(2)]
    for i in range(2):
        P.op("pool", f_memset(stg[i][:], 1.0), writes=[(tag, "vstg", i)])
    hpb = 512 // dv
    nhalf = nh * dv // 512
    cnt = 0
    for g in range(4):
        b = g % 2
        for a in range(4):
            t = 4 * g + a
            for half in range(nhalf):
                bk = 4 + (cnt % 2)
                cnt += 1
                ps = C.bank[bk]
                for c in range(8):
                    P.op("pe", f_mm(ps[:], hT[:, c, t * 128:(t + 1) * 128],
                                    wb[:, c, col0 + half * 512: col0 + (half + 1) * 512],
                                    c == 0, c == 7),
                         reads=[wkey, ("hT", t)], writes=[("bank", bk)])
                P.op("dve", f_copy(stg[b][:, half * hpb:(half + 1) * hpb, a, 0:dv],
                                   ps[:].rearrange("p (h e) -> p h e", e=dv)),
                     reads=[("bank", bk)], writes=[(tag, "vstg", b)])
        P.op("sp", f_dma(out_d.rearrange("h p t e -> p h (t e)")[:, :, 4 * g * dv1:(4 * g + 4) * dv1],
                         stg[b][:].rearrange("p h a e -> p h (a e)")),
             reads=[(tag, "vstg", b)], writes=[(tag, "dram")], dsem=f"{tag}vst{b}")


def out_proj_ln(C, ostore, wo, res_ap, hs, hT, ident, g_rep, b_rep, gb_key, scr, xt, hb, oT):
    P = C.P
    for t in range(NT):
        b = t % 2
        pT = C.bank_bf(b).rearrange("p (c n) -> p c n", n=128)
        for c in range(8):
            P.op("pe", f_tr(pT[:, c, :], ostore[:, t, c * 128:(c + 1) * 128], ident[:]),
                 reads=[("os", t), "ident"], writes=[("bank", b)])
        P.op("dve", f_copy(oT[b][:], pT), reads=[("bank", b)], writes=[("oT", b)])
        if res_ap is not None:
            P.op("sp", f_dma(xt[:], res_ap[t * 128:(t + 1) * 128, :]),
                 writes=[("B", "xt")], dsem="Bxt")
        for half in range(2):
            bk = 2 + half
            ps = C.bank[bk]
            for c in range(8):
                P.op("pe", f_mm(ps[:], oT[b][:, c, :], wo[:, c, half * 512:(half + 1) * 512],
                                c == 0, c == 7),
                     reads=[("oT", b), "wo"], writes=[("bank", bk)])
            dst = hs[:, t, half * 512:(half + 1) * 512]
            if res_ap is None:
                P.op("dve", f_stt(dst, dst, ALPHA, ps[:], ALU.mult, ALU.add),
                     reads=[("bank", bk), ("hs", t)], writes=[("hs", t)])
            else:
                P.op("dve", f_stt(dst, xt[:, half * 512:(half + 1) * 512], ALPHA, ps[:],
                                  ALU.mult, ALU.add),
                     reads=[("bank", bk), ("B", "xt")], writes=[("hs", t)])
        layer_norm_tile(C, hs[:, t, :], ("hs", t), hs[:, t, :], ("hs", t), g_rep, b_rep, gb_key, scr)
        h_to_hT(C, hs, t, hT, ident, hb)


def load_rep(C, tile, ap, key):
    C.P.op("sp", f_dma(tile[:], ap), writes=[key], dsem=_san(key))


class Scope:
    def __init__(self, C):
        self.C = C

    def __enter__(self):
        self.st = ExitStack()
        self.prev = (self.C.st, self.C.side)
        self.C.st, self.C.side = self.st, "right"
        return self

    def __exit__(self, *a):
        self.C.P.barrier()
        self.st.close()
        self.C.st, self.C.side = self.prev
        return False


def build_A():
    nc = bass.Bass("TRN2", target_bir_lowering=False)
    x = dram_in(nc, "x", [TL, D], F32)
    w_in = dram_in(nc, "w_in", [D, 3 * D], F32)
    ident_d = dram_in(nc, "ident", [128, 128], BF16)
    qT = dram_out(nc, "qT", [8, 128, TL], BF16)
    kT = dram_out(nc, "kT", [8, 128, TL], BF16)
    v = dram_out(nc, "v", [8, 128, NT, 129], BF16)
    with ExitStack() as st:
        C = Ctx(nc, st)
        P = C.P
        ident = C.sb("ident", [128, 128], BF16)
        P.op("sp", f_dma(ident[:], ident_d), writes=["ident"], dsem="ident")
        phase_A(C, x, w_in, ident, qT, kT, v)
        P.emit()
    return nc


def phase_A(C, x, w_in, ident, qT, kT, v):
    P = C.P
    with Scope(C):
        wb = C.sb("wb", [128, 8, 3 * D], BF16)
        load_weight_bf16(C, wb, w_in, "wb", 3 * D)
        hT = C.sb("hTA", [128, 8, TL], BF16)
        xt = [C.sb(f"xtA{i}", [128, D], F32) for i in range(2)]
        xb = [C.sb(f"xbA{i}", [128, D], BF16) for i in range(2)]
        make_hT_from_dram(C, x, hT, ident, xt, xb, "A")

        def mkstore(out_d, tag):
            def store(g, stg, key, b):
                P.op("sp", f_dma(out_d.rearrange("h p n -> p h n")[:, :, g * 512:(g + 1) * 512], stg[:]),
                     reads=[key], writes=[(tag, "dram")], dsem=f"{tag}st{b}")
            return store
        fm_project(C, wb, "wb", 0, hT, 0.125, "q", mkstore(qT, "q"))
        fm_project(C, wb, "wb", D, hT, 1.0, "k", mkstore(kT, "k"))
        v_project(C, wb, "wb", 2 * D, hT, v, 8, 128, "v")


def build_B():
    nc = bass.Bass("TRN2", target_bir_lowering=False)
    x = dram_in(nc, "x", [TL, D], F32)
    qT = dram_in(nc, "qT", [8, 128, TL], BF16)
    kT_all = dram_in(nc, "kT_all", [4, 8, 128, TL], BF16)
    v_all = dram_in(nc, "v_all", [4, 8, 128, NT, 129], BF16)
    cst = const_inputs(nc)
    prm = {
        "lam": dram_in(nc, "lam", [128, 256], F32),
        "subg": dram_in(nc, "subg", [128, 128], F32),
        "w_out0": dram_in(nc, "w_out0", [D, D], F32),
        "w_up0": dram_in(nc, "w_up0", [D, DFF], F32),
        "w_down0": dram_in(nc, "w_down0", [DFF, D], F32),
        "w_kv": dram_in(nc, "w_kv", [D, 2 * D], F32),
    }
    for k in ("ln1_g0", "ln1_b0", "ln2_g0", "ln2_b0"):
        prm[k] = dram_in(nc, k, [128, D], F32)
    h1 = dram_out(nc, "h1", [TL, D], F32)
    kT1 = dram_out(nc, "kT1", [16, 96, TL], BF16)
    v1 = dram_out(nc, "v1", [16, 128, NT, 65], BF16)
    with ExitStack() as st:
        C = Ctx(nc, st)
        P = C.P
        cs = load_consts(C, cst)
        hs, hT, hT_stack = phase_B(C, cs, cst, prm, x, qT, kT_all, v_all, kT1, v1)
        st.callback(hT_stack.close)
        for t in range(NT):
            P.op("sp", f_dma(h1[t * 128:(t + 1) * 128, :], hs[:, t, :]), reads=[("hs", t)],
                 dsem=f"h1st{t % 2}")
        P.emit()
    return nc


def const_inputs(nc):
    return {
        "G": dram_in(nc, "G", [128, 16, 640], F32),
        "b31": dram_in(nc, "b31", [128, 16], F32),
        "ident": dram_in(nc, "ident", [128, 128], BF16),
        "e_own": dram_in(nc, "e_own", [32, TL], BF16),
        "past30k": dram_in(nc, "past30k", [128, NT, 32], F32),
    }


def load_consts(C, cst):
    P = C.P
    ident = C.sb("ident", [128, 128], BF16)
    P.op("sp", f_dma(ident[:], cst["ident"]), writes=["ident"], dsem="ident")
    b31 = C.sb("b31", [128, 16], F32)
    P.op("sp", f_dma(b31[:], cst["b31"]), writes=["b31"], dsem="b31")
    scr = ln_scratch(C)
    hb = [C.sb(f"hb{i}", [128, D], BF16) for i in range(2)]
    rep = [C.sb(f"rep{i}", [128, D], F32) for i in range(2)]
    kmT = C.sb("kmT", [128, 8, 32], BF16)
    return {"ident": ident, "b31": b31, "scr": scr, "hb": hb, "rep": rep, "kmT": kmT}


def alloc_hT(C):
    stk = ExitStack()
    hT = stk.enter_context(C.nc.sbuf_tensor("s_hT%d" % C.uid, [128, 8, TL], BF16, side="left"))
    C.uid += 1
    return hT, stk


def phase_B(C, cs, cst, prm, x, qT, kT_all, v_all, kT1, v1):
    P = C.P
    ident, b31, scr, hb, rep = cs["ident"], cs["b31"], cs["scr"], cs["hb"], cs["rep"]
    with Scope(C):
        ostore = C.sb("ostore", [128, NT, D], BF16)
        lam_sb = C.sb("lam_sb", [128, 256], F32)
        lprod = C.sb("lprod", [128, 2, 64], F32)
        lsum = C.sb("lsum", [128, 2], F32)
        lexp = C.sb("lexp", [128, 2], F32)
        nl = C.sb("nl", [128, 2], F32)
        gsub = C.sb("gsub", [128, 128], F32)
        P.op("sp", f_dma(lam_sb[:], prm["lam"]), writes=["lam_sb"], dsem="lam")
        P.op("sp", f_dma(gsub[:], prm["subg"]), writes=["gsub"], dsem="gsub")
        lv = lam_sb[:].rearrange("p (a b d) -> p a b d", a=2, b=2)
        P.op("dve", f_tt(lprod[:], lv[:, :, 0, :], lv[:, :, 1, :], ALU.mult),
             reads=["lam_sb"], writes=["lprod"])
        P.op("dve", lambda e: e.reduce_sum(out=lsum[:], in_=lprod[:], axis=AX.X),
             reads=["lprod"], writes=["lsum"])
        P.op("act", f_act(lexp[:], lsum[:], AF.Exp), reads=["lsum"], writes=["lexp"])
        P.op("dve", f_tt(nl[:, 0:1], lexp[:, 1:2], lexp[:, 0:1], ALU.subtract),
             reads=["lexp"], writes=["nl0"])
        P.op("dve", f_ts(nl[:, 1:2], nl[:, 0:1], -LAM_INIT0, None, ALU.add),
             reads=["nl0"], writes=["neglam"])
        neglam = nl[:, 1:2]
        P.op("dve", f_ts(gsub[:], gsub[:], 1.0 - LAM_INIT0, None, ALU.mult),
             reads=["gsub"], writes=["gsub"])
        ep = {k: [C.sb(f"ep_{k}{i}", shp, F32) for i in range(2)]
              for k, shp in (("rl", [128, 2]), ("c1", [128, 1]), ("t1", [128, 128]),
                             ("o", [128, 128]), ("sq", [128, 128]), ("ss", [128, 1]))}
        ecnt = [0]

        def epilogue(h, u, acc_ap, acc_key):
            for a in range(4):
                t = 4 * u + a
                k = ecnt[0] % 2
                ecnt[0] += 1
                rl, c1, t1, o, sq, ss = (ep[x_][k] for x_ in ("rl", "c1", "t1", "o", "sq", "ss"))
                a0, a1 = acc_ap(a), acc_ap(4 + a)
                k0, k1 = acc_key(a), acc_key(4 + a)
                P.op("dve", lambda e, rl=rl, a0=a0: e.reciprocal(out=rl[:, 0:1], in_=a0[:, 128:129]),
                     reads=[k0], writes=[("ep", k, "rl0")])
                P.op("dve", lambda e, rl=rl, a1=a1: e.reciprocal(out=rl[:, 1:2], in_=a1[:, 128:129]),
                     reads=[k1], writes=[("ep", k, "rl1")])
                P.op("dve", f_tt(c1[:], rl[:, 1:2], neglam, ALU.mult),
                     reads=[("ep", k, "rl1"), "neglam"], writes=[("ep", k, "c1")])
                P.op("dve", f_ts(t1[:], a1[:, 0:128], c1[:, 0:1], None, ALU.mult),
                     reads=[k1, ("ep", k, "c1")], writes=[("ep", k, "t1")])
                P.op("dve", f_stt(o[:], a0[:, 0:128], rl[:, 0:1], t1[:], ALU.mult, ALU.add),
                     reads=[k0, ("ep", k, "rl0"), ("ep", k, "t1")], writes=[("ep", k, "o")])
                P.op("pool", f_tt(sq[:], o[:], o[:], ALU.mult), reads=[("ep", k, "o")],
                     writes=[("ep", k, "sq")])
                P.op("dve", lambda e, ss=ss, sq=sq: e.reduce_sum(out=ss[:], in_=sq[:], axis=AX.X),
                     reads=[("ep", k, "sq")], writes=[("ep", k, "ss")])
                P.op("dve", f_ts(ss[:], ss[:], 1.0 / 128.0, LN_EPS, ALU.mult, ALU.add),
                     reads=[("ep", k, "ss")], writes=[("ep", k, "ss")])
                P.op("pool", f_tt(ss[:], ss[:], scr["mhalf"][:, 0:1], ALU.pow),
                     reads=[("ep", k, "ss"), "mhalf"], writes=[("ep", k, "ss")])
                P.op("dve", f_stt(ostore[:, t, h * 128:(h + 1) * 128], o[:], ss[:, 0:1], gsub[:],
                                  ALU.mult, ALU.mult),
                     reads=[("ep", k, "o"), ("ep", k, "ss"), "gsub"], writes=[("os", t)])

        with Scope(C):
            attention(C, dict(H=8, nm=2, KR=128, dv=128), qT, kT_all, v_all, cst["G"], b31, epilogue)
        C.side = "left"
        hs = C.nc.sbuf_tensor("s_hs", [128, NT, D], F32, side="left")
        hs = C.root.enter_context(hs)
        hT, hT_stack = alloc_hT(C)
        C.side = "right"
        with Scope(C):
            wo = C.sb("wo", [128, 8, D], BF16)
            load_weight_bf16(C, wo, prm["w_out0"], "wo", D)
            xt = C.sb("xtB", [128, D], F32)
            oT = [C.sb(f"oT{i}", [128, 8, 128], BF16) for i in range(2)]
            load_rep(C, rep[0], prm["ln1_g0"], "rep0")
            load_rep(C, rep[1], prm["ln1_b0"], "rep1")
            out_proj_ln(C, ostore, wo, x, hs, hT, ident, rep[0], rep[1], "rep0", scr, xt, hb, oT)
    with Scope(C):
        mlp(C, hs, hT, prm["w_up0"], prm["w_down0"], "m0")
    load_rep(C, rep[0], prm["ln2_g0"], "rep0")
    load_rep(C, rep[1], prm["ln2_b0"], "rep1")
    for t in range(NT):
        layer_norm_tile(C, hs[:, t, :], ("hs", t), hs[:, t, :], ("hs", t), rep[0], rep[1], "rep0", scr)
        h_to_hT(C, hs, t, hT, ident, hb)
    with Scope(C):
        wkv = C.sb("wk1", [128, 8, D], BF16)
        load_weight_bf16(C, wkv, prm["w_kv"], "wkv", D)

        def store(g, stg, key, b):
            kv = kT1.rearrange("(f two) r n -> two r f n", two=2)
            for hh in range(2):
                P.op("sp", f_dma(kv[hh, 0:64, :, g * 512:(g + 1) * 512], stg[hh * 64:(hh + 1) * 64, :, :]),
                     reads=[key], writes=[("k1dram", hh)], dsem=f"k1st{b}")
        fm_project(C, wkv, "wkv", 0, hT, 1.0, "k1", store)
        for h in range(16):
            P.op("sp", f_dma(kT1[h, 64:96, :], cst["e_own"]), writes=[("k1dramE",)], dsem="k1e")
    with Scope(C):
        wv1 = C.sb("wv1", [128, 8, D], BF16)
        load_weight_bf16(C, wv1, prm["w_kv"], "wv1", D, col0=D)
        v_project(C, wv1, "wv1", 0, hT, v1, 16, 64, "v1")
    return hs, hT, hT_stack


def build_C(stop_after=None):
    nc = bass.Bass("TRN2", target_bir_lowering=False)
    h1 = dram_in(nc, "h1", [TL, D], F32)
    kT1_all = dram_in(nc, "kT1_all", [4, 16, 96, TL], BF16)
    v1_all = dram_in(nc, "v1_all", [4, 16, 128, NT, 65], BF16)
    cst = const_inputs(nc)
    prm = {
        "w_q": dram_in(nc, "w_q", [D, D], F32),
        "w_out1": dram_in(nc, "w_out1", [D, D], F32),
        "w_up1": dram_in(nc, "w_up1", [D, DFF], F32),
        "w_down1": dram_in(nc, "w_down1", [DFF, D], F32),
    }
    for k in ("ln1_g1", "ln1_b1", "ln2_g1", "ln2_b1"):
        prm[k] = dram_in(nc, k, [128, D], F32)
    out = dram_out(nc, "out", [TL, D], F32)
    qaug = nc.dram_tensor("qaug", [16, 96, TL], BF16,
                          kind="ExternalOutput" if stop_after else "Internal").ap()
    with ExitStack() as st:
        C = Ctx(nc, st)
        C.stop_after = stop_after
        P = C.P
        cs = load_consts(C, cst)
        hs = C.sb("hs", [128, NT, D], F32)
        hT, hT_stack = alloc_hT(C)
        for t in range(NT):
            P.op("sp", f_dma(hs[:, t, :], h1[t * 128:(t + 1) * 128, :]), writes=[("hs", t)],
                 dsem=f"hsld{t}")
            h_to_hT(C, hs, t, hT, cs["ident"], cs["hb"])
        phase_C(C, cs, cst, prm, kT1_all, v1_all, qaug, hs, hT, hT_stack, out)
        P.emit()
    return nc


def phase_C(C, cs, cst, prm, kT1_all, v1_all, qaug, hs, hT, hT_stack, out):
    P = C.P
    ident, b31, scr, hb, rep = cs["ident"], cs["b31"], cs["scr"], cs["hb"], cs["rep"]
    kmT = cs["kmT"]
    with Scope(C):
        moba_kmean(C, kT1_all, kmT)
    if getattr(C, "stop_after", None) == "kmean":
        P.op("sp", f_dma(qaug.rearrange("h r n -> r h n")[0:64, 0:8, 0:32], kmT[0:64, :, :]),
             reads=[("kmT", f) for f in range(8)], dsem="dbg1")
        P.op("sp", f_dma(qaug.rearrange("h r n -> r h n")[0:64, 8:16, 0:32], kmT[64:128, :, :]),
             reads=[("kmT", f) for f in range(8)], dsem="dbg2")
        for t in range(NT):
            P.op("sp", f_dma(out[t * 128:(t + 1) * 128, :], hs[:, t, :]), reads=[("hs", t)],
                 dsem=f"outst{t % 2}")
        hT_stack.close()
        return
    with Scope(C):
        moba_gate_qproj(C, hT, prm["w_q"], kmT, cst["past30k"], qaug, ident)
    hT_stack.close()
    hT = None
    if getattr(C, "stop_after", None) == "gate":
        for t in range(NT):
            P.op("sp", f_dma(out[t * 128:(t + 1) * 128, :], hs[:, t, :]), reads=[("hs", t)],
                 dsem=f"outst{t % 2}")
        return
    with Scope(C):
        ostore = C.sb("ostore1", [128, NT, D], BF16)
        eprl = [C.sb(f"eprl{i}", [128, 1], F32) for i in range(8)]

        def epilogue(h, u, acc_ap, acc_key):
            for a in range(4):
                t = 4 * u + a
                k = (h * 4 + u) % 2
                rl = eprl[4 * k + a]
                acc = acc_ap(a)
                P.op("dve", lambda e, rl=rl, acc=acc: e.reciprocal(out=rl[:], in_=acc[:, 64:65]),
                     reads=[acc_key(a)], writes=[("eprl", 4 * k + a)])
                P.op("dve", f_ts(ostore[:, t, h * 64:(h + 1) * 64], acc[:, 0:64], rl[:, 0:1], None,
                                 ALU.mult),
                     reads=[acc_key(a), ("eprl", 4 * k + a)], writes=[("os", t)])

        with Scope(C):
            attention(C, dict(H=16, nm=1, KR=96, dv=64), qaug, kT1_all, v1_all, cst["G"], b31, epilogue)
        hT, hT_stack2 = alloc_hT(C)
        C.root.callback(hT_stack2.close)
        with Scope(C):
            wo = C.sb("wo1", [128, 8, D], BF16)
            load_weight_bf16(C, wo, prm["w_out1"], "wo", D)
            oT = [C.sb(f"oT1{i}", [128, 8, 128], BF16) for i in range(2)]
            load_rep(C, rep[0], prm["ln1_g1"], "rep0")
            load_rep(C, rep[1], prm["ln1_b1"], "rep1")
            out_proj_ln(C, ostore, wo, None, hs, hT, ident, rep[0], rep[1], "rep0", scr, None, hb, oT)
    with Scope(C):
        mlp(C, hs, hT, prm["w_up1"], prm["w_down1"], "m1")
    load_rep(C, rep[0], prm["ln2_g1"], "rep0")
    load_rep(C, rep[1], prm["ln2_b1"], "rep1")
    for t in range(NT):
        layer_norm_tile(C, hs[:, t, :], ("hs", t), hs[:, t, :], ("hs", t), rep[0], rep[1], "rep0", scr)
        P.op("sp", f_dma(out[t * 128:(t + 1) * 128, :], hs[:, t, :]), reads=[("hs", t)],
             dsem=f"outst{t % 2}")


def moba_kmean(C, kT1_all, kmT):
    P = C.P
    Kp = [C.sb(f"Kp{i}", [128, S], BF16) for i in range(2)]
    ksum = [C.sb(f"ksum{i}", [128, 64], F32) for i in range(2)]
    kmf = [C.sb(f"kmf{i}", [128, 16, 2], F32) for i in range(2)]
    for f in range(8):
        b = f % 2
        for r in range(4):
            for hh in range(2):
                P.op("sp", f_dma(Kp[b][hh * 64:(hh + 1) * 64, r * TL:(r + 1) * TL],
                                 kT1_all[r, 2 * f + hh, 0:64, :]),
                     writes=[("Kp", b)], dsem=f"Kp{b}")
        P.op("dve", lambda e, b=b: e.reduce_sum(out=ksum[b][:],
                                                in_=Kp[b][:].rearrange("p (s q) -> p s q", q=128),
                                                axis=AX.X),
             reads=[("Kp", b)], writes=[("ksum", b)])
        kv = ksum[b][:].rearrange("p (r t) -> p r t", t=16)
        for par in range(2):
            P.op("dve", f_tt(kmf[b][:, :, par], kv[:, 2 * par, :], kv[:, 2 * par + 1, :], ALU.add),
                 reads=[("ksum", b)], writes=[("kmf", b, par)])
        P.op("dve", f_ts(kmT[:, f, :], kmf[b][:].rearrange("p t two -> p (t two)"), 1.0 / 256.0,
                         None, ALU.mult),
             reads=[("kmf", b, 0), ("kmf", b, 1)], writes=[("kmT", f)])


def moba_gate_qproj(C, hT, w_q, kmT, past_d, qaug, ident):
    P = C.P
    wq = C.sb("wq", [128, 8, D], BF16)
    load_weight_bf16(C, wq, w_q, "wq", D)
    past = C.sb("past", [128, NT, 32], F32)
    padd = C.sb("padd", [128, NT, 32], F32)
    P.op("sp", f_dma(past[:], past_d), writes=["past"], dsem="past")
    P.op("dve", f_ts(padd[:], past[:], NEG, None, ALU.add), reads=["past"], writes=["padd"])
    mstg = [C.sb(f"mstg{i}", [128, 16, 128], BF16) for i in range(2)]
    gm = [C.sb(f"gm{i}", [128, 16, 32], F32) for i in range(2)]
    top8 = [C.sb(f"top8{i}", [128, 16, 8], F32) for i in range(2)]
    sel = [C.sb(f"sel{i}", [128, 16, 32], F32) for i in range(2)]
    negpad = [C.sb(f"negpad{i}", [128, 16, 96], BF16) for i in range(2)]
    for i in range(2):
        P.op("pool", f_memset(negpad[i][:], 0.0), writes=[("negpad", i)])

    def store(g, stg, key, b):
        qv = qaug.rearrange("(f two) r n -> two r f n", two=2)
        for hh in range(2):
            P.op("sp", f_dma(qv[hh, 0:64, :, g * 512:(g + 1) * 512], stg[hh * 64:(hh + 1) * 64, :, :]),
                 reads=[key], writes=[("qaug", "q")], dsem=f"qst{b}")
        for a in range(4):
            t = 4 * g + a
            k = t % 2
            gbk = (4, 5) if k == 0 else (0, 1)
            for h in range(16):
                r0 = (h % 2) * 64
                gps = C.bank[gbk[h % 2]]
                P.op("pe", f_mm(gps[:, (h // 2) * 32:(h // 2 + 1) * 32],
                                stg[r0:r0 + 64, h // 2, a * 128:(a + 1) * 128],
                                kmT[r0:r0 + 64, h // 2, :], True, True),
                     reads=[key, ("kmT", h // 2)], writes=[("bank", gbk[h % 2])])
            gmv = gm[k][:].rearrange("p (f two) n -> p two f n", two=2)
            for par in range(2):
                P.op("dve", f_tt(gmv[:, par], C.bank[gbk[par]][:, 0:256].rearrange("p (f n) -> p f n", n=32),
                                 padd[:, t, :].unsqueeze(1).broadcast_to([128, 8, 32]), ALU.add),
                     reads=[("bank", gbk[par]), "padd"], writes=[("gm", k)])
            for h in range(16):
                P.op("dve", lambda e, k=k, h=h: e.max(out=top8[k][:, h, :], in_=gm[k][:, h, :]),
                     reads=[("gm", k)], writes=[("top8", k)])
            P.op("dve", f_tt(sel[k][:], gm[k][:], top8[k][:, :, 2:3].broadcast_to([128, 16, 32]),
                             ALU.is_ge),
                 reads=[("gm", k), ("top8", k)], writes=[("sel", k)])
            P.op("dve", f_ts(sel[k][:], sel[k][:], -1.0, None, ALU.add),
                 reads=[("sel", k)], writes=[("sel", k)])
            P.op("dve", f_tt(negpad[k][:, :, 64:96], sel[k][:],
                             past[:, t, :].unsqueeze(1).broadcast_to([128, 16, 32]), ALU.mult),
                 reads=[("sel", k), "past"], writes=[("negpad", k)])
            for hq in range(2):
                bk = 6 + hq
                tp = C.bank_bf(bk).rearrange("p (h n) -> p h n", n=128)
                for h8 in range(8):
                    h = hq * 8 + h8
                    P.op("pe", f_tr(tp[0:96, h8, :], negpad[k][:, h, :], ident[:]),
                         reads=[("negpad", k), "ident"], writes=[("bank", bk)])
                P.op("act", f_act(mstg[k][64:96, hq * 8:(hq + 1) * 8, :], tp[64:96, :, :], AF.Copy),
                     reads=[("bank", bk)], writes=[("mstg", k)])
            P.op("sp", f_dma(qaug.rearrange("h r n -> r h n")[64:96, :, t * 128:(t + 1) * 128],
                             mstg[k][64:96, :, :]),
                 reads=[("mstg", k)], writes=[("qaug", "m")], dsem=f"mst{k}")

    fm_project(C, wq, "wq", 0, hT, 0.125, "qb", store)


def _bucket_np(n):
    n = np.maximum(n, 0)
    nf = np.maximum(n, 16).astype(np.float32)
    large = 16 + (np.log(nf / np.float32(16)) / np.float32(math.log(8.0)) * np.float32(16)).astype(np.int32)
    large = np.minimum(large, 31)
    return np.where(n < 16, n, large)


def _g_index(j):
    k = np.arange(128)[:, None, None]
    q = np.arange(128)[None, None, :]
    e = (np.arange(5) - 1)[None, :, None]
    delta = e - j
    rel = q - k - 128 * delta
    idx = _bucket_np(rel)
    idx = np.where(rel >= 0, idx, 32)
    return np.broadcast_to(idx, (128, 5, 128)).astype(np.int64)


_CACHE = {}


def _get(name, fn):
    if name not in _CACHE:
        _CACHE[name] = fn()
    return _CACHE[name]


def _host_prep(x, rel_bias):
    f32 = np.float32
    bf = ml_dtypes.bfloat16
    table_ext = np.concatenate([rel_bias, np.full((1, 16), NEG, f32)], axis=0)
    per = []
    for c in range(NCORES):
        b, j = c // 4, c % 4
        xs = np.ascontiguousarray(x[b].reshape(64, 128, D)[j::4].reshape(TL, D))
        G = table_ext[_g_index(j)]
        G = np.ascontiguousarray(G.transpose(0, 3, 1, 2).reshape(128, 16, 640))
        blk = (4 * np.arange(NT) + j) // 2
        e = np.zeros((32, NT, 128), f32)
        e[blk, np.arange(NT), :] = 1.0
        past = (np.arange(32)[None, :] < blk[:, None]).astype(f32) * f32(30000.0)
        per.append({
            "x": xs, "G": G, "e_own": e.reshape(32, TL).astype(bf),
            "past30k": np.ascontiguousarray(np.broadcast_to(past[None], (128, NT, 32))),
            "b31": np.ascontiguousarray(np.broadcast_to(rel_bias[31:32, :], (128, 16))),
            "ident": np.eye(128, dtype=f32).astype(bf),
        })
    return per


def _rep(v, n=128):
    v = np.asarray(v, np.float32).reshape(1, -1)
    return np.ascontiguousarray(np.broadcast_to(v, (n, v.shape[1])))


CONST_KEYS = ("G", "b31", "ident", "e_own", "past30k")


def kernel(x, w_in_a, lam_a, subln_a, w_out_a, w_kv_shared, w_q_b, w_out_b, rel_bias,
           ln1_g, ln1_b, ln2_g, ln2_b, w_up, w_down):
    f32 = np.float32
    A = lambda v: np.ascontiguousarray(np.asarray(v, f32))
    x = A(x)
    rel_bias = A(rel_bias)
    cores = list(range(NCORES))
    per = _host_prep(x, rel_bias)
    ncA = _get("A", build_A)
    inA = [{"x": per[c]["x"], "w_in": A(w_in_a[0]), "ident": per[c]["ident"]} for c in cores]
    rA = run_bass_kernel_spmd(ncA, inA, core_ids=cores).results
    ncB = _get("B", build_B)
    prmB = {
        "lam": _rep(np.asarray(lam_a[0], f32).reshape(-1)), "subg": _rep(subln_a[0]),
        "w_out0": A(w_out_a[0]), "w_up0": A(w_up[0]), "w_down0": A(w_down[0]),
        "w_kv": A(w_kv_shared),
        "ln1_g0": _rep(ln1_g[0]), "ln1_b0": _rep(ln1_b[0]),
        "ln2_g0": _rep(ln2_g[0]), "ln2_b0": _rep(ln2_b[0]),
    }
    inB = []
    for c in cores:
        grp = [4 * (c // 4) + r for r in range(4)]
        d = {"x": per[c]["x"], "qT": rA[c]["qT"],
             "kT_all": np.stack([rA[g]["kT"] for g in grp]),
             "v_all": np.stack([rA[g]["v"] for g in grp])}
        d.update({k: per[c][k] for k in CONST_KEYS})
        d.update(prmB)
        inB.append(d)
    rB = run_bass_kernel_spmd(ncB, inB, core_ids=cores).results
    ncC = _get("C", build_C)
    prmC = {
        "w_q": A(w_q_b[0]), "w_out1": A(w_out_b[0]), "w_up1": A(w_up[1]), "w_down1": A(w_down[1]),
        "ln1_g1": _rep(ln1_g[1]), "ln1_b1": _rep(ln1_b[1]),
        "ln2_g1": _rep(ln2_g[1]), "ln2_b1": _rep(ln2_b[1]),
    }
    inC = []
    for c in cores:
        grp = [4 * (c // 4) + r for r in range(4)]
        d = {"h1": rB[c]["h1"],
             "kT1_all": np.stack([rB[g]["kT1"] for g in grp]),
             "v1_all": np.stack([rB[g]["v1"] for g in grp])}
        d.update({k: per[c][k] for k in CONST_KEYS})
        d.update(prmC)
        inC.append(d)
    rC = run_bass_kernel_spmd(ncC, inC, core_ids=cores).results
    out = np.empty((2, 64, 128, D), f32)
    for c in cores:
        b, j = c // 4, c % 4
        out[b, j::4] = np.asarray(rC[c]["out"], f32).reshape(NT, 128, D)
    return out.reshape(2, S, D)
```

```python
import math
from contextlib import ExitStack

import numpy as np
import ml_dtypes
import concourse.bass as bass
import concourse.mybir as mybir
from concourse.bass_utils import run_bass_kernel_spmd

F32 = mybir.dt.float32
BF16 = mybir.dt.bfloat16
AF = mybir.ActivationFunctionType
ALU = mybir.AluOpType
AX = mybir.AxisListType

NCORES = 8
D = 1024
S = 8192
NT = 16
TL = 2048
DFF = 4096
ALPHA = 4.0 ** 0.25
LN_EPS = 1e-5
LAM_INIT0 = 0.8 - 0.6 * math.exp(-0.3 * 0)
NEG = -30000.0

ENGS = ("pe", "act", "dve", "pool", "sp")


class Op:
    __slots__ = ("fn", "deps", "dsem", "seq", "waits")

    def __init__(self, fn, deps, dsem):
        self.fn = fn
        self.deps = deps
        self.dsem = dsem
        self.seq = None
        self.waits = []


class Prog:
    def __init__(self, nc):
        self.nc = nc
        self.ops = {e: [] for e in ENGS}
        self.last_w = {}
        self.readers = {}

    def op(self, eng, fn, reads=(), writes=(), dsem=None):
        deps = set()
        for k in reads:
            w = self.last_w.get(k)
            if w is not None:
                deps.add(w)
        for k in writes:
            w = self.last_w.get(k)
            if w is not None:
                deps.add(w)
            rd = self.readers.get(k)
            if rd:
                for e2, i2 in rd.items():
                    deps.add((e2, i2))
        idx = len(self.ops[eng])
        me = (eng, idx)
        deps.discard(me)
        self.ops[eng].append(Op(fn, deps, dsem))
        for k in reads:
            self.readers.setdefault(k, {})[eng] = idx
        for k in writes:
            self.last_w[k] = me
            self.readers[k] = {}
        return me

    def barrier(self):
        deps = set()
        for e in ENGS:
            ops = self.ops[e]
            last_real = None
            seen_d = set()
            for i in range(len(ops) - 1, -1, -1):
                o = ops[i]
                if o.fn is None:
                    continue
                if o.dsem is not None:
                    if o.dsem not in seen_d:
                        seen_d.add(o.dsem)
                        deps.add((e, i))
                elif last_real is None:
                    last_real = i
                    deps.add((e, i))
        for e in ENGS:
            self.ops[e].append(Op(None, set(deps), None))

    def resolve(self):
        needed = set()
        for e in ENGS:
            for o in self.ops[e]:
                for d in o.deps:
                    if d[0] == e and e == "pe":
                        continue
                    needed.add(d)
        self.sem_names = set()
        cnt = {}
        for e in ENGS:
            for i, o in enumerate(self.ops[e]):
                if o.dsem is not None:
                    name = "d_" + o.dsem
                    cnt[name] = cnt.get(name, 0) + 16
                    o.seq = (name, cnt[name])
                    self.sem_names.add(name)
                elif (e, i) in needed:
                    name = "e_" + e
                    cnt[name] = cnt.get(name, 0) + 1
                    o.seq = (name, cnt[name])
                    self.sem_names.add(name)
        self.finals = dict(cnt)
        for e in ENGS:
            seen = {}
            for o in self.ops[e]:
                req = {}
                for d in o.deps:
                    if d[0] == e and e == "pe":
                        continue
                    name, val = self.ops[d[0]][d[1]].seq
                    if req.get(name, 0) < val:
                        req[name] = val
                for name, val in req.items():
                    if seen.get(name, 0) < val:
                        seen[name] = val
                        o.waits.append((name, val))

    def check(self):
        sem = {}
        pc = {e: 0 for e in ENGS}
        n = {e: len(self.ops[e]) for e in ENGS}
        progress = True
        while progress:
            progress = False
            for e in ENGS:
                while pc[e] < n[e]:
                    o = self.ops[e][pc[e]]
                    if all(sem.get(nm, 0) >= v for nm, v in o.waits):
                        if o.seq is not None:
                            nm = o.seq[0]
                            sem[nm] = sem.get(nm, 0) + (16 if nm.startswith("d_") else 1)
                        pc[e] += 1
                        progress = True
                    else:
                        break
        stuck = {e: (pc[e], n[e]) for e in ENGS if pc[e] < n[e]}
        if stuck:
            msg = []
            for e, (p, m) in stuck.items():
                o = self.ops[e][p]
                msg.append(f"{e}@{p}/{m} waits {[(nm, v, sem.get(nm, 0)) for nm, v in o.waits]}")
            raise RuntimeError("sync deadlock: " + "; ".join(msg))
        return {e: n[e] for e in ENGS}, len(self.sem_names)

    def emit(self):
        nc = self.nc
        self.resolve()
        print("prog ops/sems:", self.check(), flush=True)
        with ExitStack() as st:
            sems = {}
            for name in sorted(self.sem_names):
                sems[name] = st.enter_context(nc.semaphore(name))
            block = st.enter_context(nc.Block())

            def run(engname):
                def body(engine):
                    for o in self.ops[engname]:
                        if o.fn is None:
                            for (name, val) in o.waits:
                                engine.wait_ge(sems[name], val)
                            continue
                        for (name, val) in o.waits[1:]:
                            engine.wait_ge(sems[name], val)
                        ins = o.fn(engine)
                        if o.waits:
                            ins._wait_ge(sems[o.waits[0][0]], o.waits[0][1])
                        if o.seq is not None:
                            name = o.seq[0]
                            ins.then_inc(sems[name], 16 if name.startswith("d_") else 1)
                    if engname == "sp":
                        for name, val in self.finals.items():
                            if name.startswith("d_"):
                                engine.wait_ge(sems[name], val)
                return body

            block.tensor(run("pe"))
            block.scalar(run("act"))
            block.vector(run("dve"))
            block.gpsimd(run("pool"))
            block.sync(run("sp"))


def f_mm(out, lhsT, rhs, start, stop, skip=False):
    return lambda e: e.matmul(out, lhsT=lhsT, rhs=rhs, start=start, stop=stop,
                              skip_group_check=skip)


def f_tr(out, in_, ident):
    return lambda e: e.transpose(out=out, in_=in_, identity=ident)


def f_act(out, in_, func, bias=None, scale=1.0):
    if bias is None:
        return lambda e: e.activation(out=out, in_=in_, func=func, scale=scale)
    return lambda e: e.activation(out=out, in_=in_, func=func, bias=bias, scale=scale)


def f_tt(out, in0, in1, op):
    return lambda e: e.tensor_tensor(out=out, in0=in0, in1=in1, op=op)


def f_ts(out, in0, s1, s2, op0, op1=None):
    if op1 is None:
        return lambda e: e.tensor_scalar(out=out, in0=in0, scalar1=s1, scalar2=None, op0=op0)
    return lambda e: e.tensor_scalar(out=out, in0=in0, scalar1=s1, scalar2=s2, op0=op0, op1=op1)


def f_stt(out, in0, scalar, in1, op0, op1):
    return lambda e: e.scalar_tensor_tensor(out=out, in0=in0, scalar=scalar, in1=in1,
                                            op0=op0, op1=op1)


def f_copy(out, in_):
    return lambda e: e.tensor_copy(out=out, in_=in_)


def f_dma(out, in_):
    return lambda e: e.dma_start(out=out, in_=in_)


def f_memset(ap, v):
    return lambda e: e.memset(ap, v)


def _san(key):
    return "".join(ch for ch in str(key) if ch.isalnum())


class Ctx:
    def __init__(self, nc, st):
        self.nc = nc
        self.st = st
        self.P = Prog(nc)
        self.psum = st.enter_context(nc.psum_tensor("psum_all", [128, 8, 512], F32))
        self.bank = [self.psum[:, i, :] for i in range(8)]
        self.uid = 0
        self.side = "left"
        self.root = st

    def sb(self, name, shape, dt):
        return self.st.enter_context(self.nc.sbuf_tensor("s_" + name, shape, dt, side=self.side))

    def bank_bf(self, i):
        return self.bank[i].bitcast(BF16)


def load_weight_bf16(C, wt, w_ap, key, ncols, col0=0, row0=0, nchunks=8):
    P = C.P
    step = 1024
    for c in range(nchunks):
        for n0 in range(0, ncols, step):
            n1 = min(ncols, n0 + step)
            P.op("pool", f_dma(wt[:, c, n0:n1],
                               w_ap[row0 + c * 128: row0 + (c + 1) * 128, col0 + n0: col0 + n1]),
                 writes=[key], dsem=_san(key))


def make_hT_from_dram(C, x_ap, hT, ident, xt, xb, tag):
    P = C.P
    for t in range(NT):
        b = t % 2
        P.op("sp", f_dma(xt[b][:], x_ap[t * 128:(t + 1) * 128, :]),
             writes=[(tag, "xt", b)], dsem=f"{tag}xt{b}")
        P.op("act", f_act(xb[b][:], xt[b][:], AF.Copy), reads=[(tag, "xt", b)],
             writes=[(tag, "xb", b)])
        transpose_tile(C, xb[b], (tag, "xb", b), hT, t, ident, b)


def transpose_tile(C, src_bf, src_key, hT, t, ident, b):
    P = C.P
    pT = C.bank_bf(b).rearrange("p (c n) -> p c n", n=128)
    for c in range(8):
        P.op("pe", f_tr(pT[:, c, :], src_bf[:, c * 128:(c + 1) * 128], ident[:]),
             reads=[src_key, "ident"], writes=[("bank", b)])
    P.op("dve", f_copy(hT[:, :, t * 128:(t + 1) * 128], pT), reads=[("bank", b)],
         writes=[("hT", t)])


def layer_norm_tile(C, src, src_key, dst, dst_key, g_rep, b_rep, gb_key, scr):
    P = C.P
    u = C.uid
    C.uid += 1
    k = u % 2
    stats, mv, rstd, tmp = scr["stats"][k], scr["mv"][k], scr["rstd"][k], scr["tmp"][k]
    kk = ("ln", k)
    kt = ("ln", "tmp")
    P.op("dve", lambda e: e.bn_stats(out=stats[:, 0, :], in_=src[:, 0:512]),
         reads=[src_key], writes=[kk + ("st",)])
    P.op("dve", lambda e: e.bn_stats(out=stats[:, 1, :], in_=src[:, 512:1024]),
         reads=[src_key], writes=[kk + ("st",)])
    P.op("dve", lambda e: e.bn_aggr(out=mv[:], in_=stats[:].rearrange("p a s -> p (a s)")),
         reads=[kk + ("st",)], writes=[kk + ("mv",)])
    P.op("dve", f_ts(rstd[:], mv[:, 1:2], LN_EPS, None, ALU.add),
         reads=[kk + ("mv",)], writes=[kk + ("rs0",)])
    P.op("pool", f_tt(rstd[:], rstd[:], scr["mhalf"][:, 0:1], ALU.pow),
         reads=[kk + ("rs0",), "mhalf"], writes=[kk + ("rs0",)])
    P.op("dve", f_ts(tmp[:], src, mv[:, 0:1], rstd[:, 0:1], ALU.subtract, ALU.mult),
         reads=[src_key, kk + ("mv",), kk + ("rs0",)], writes=[kt])
    P.op("pool", f_tt(tmp[:], tmp[:], g_rep[:], ALU.mult),
         reads=[kt, gb_key], writes=[kt])
    P.op("pool", f_tt(dst, tmp[:], b_rep[:], ALU.add),
         reads=[kt, gb_key], writes=[dst_key])


def ln_scratch(C):
    scr = {
        "stats": [C.sb(f"ln_stats{k}", [128, 2, 6], F32) for k in range(2)],
        "mv": [C.sb(f"ln_mv{k}", [128, 2], F32) for k in range(2)],
        "rstd": [C.sb(f"ln_rstd{k}", [128, 1], F32) for k in range(2)],
        "tmp": [C.sb("ln_tmp", [128, 1024], F32)] * 2,
        "mhalf": C.sb("ln_mhalf", [128, 1], F32),
    }
    C.P.op("dve", f_memset(scr["mhalf"][:], -0.5), writes=["mhalf"])
    return scr


def h_to_hT(C, hs, t, hT, ident, hb):
    P = C.P
    b = t % 2
    P.op("act", f_act(hb[b][:], hs[:, t, :], AF.Copy), reads=[("hs", t)], writes=[("hb", b)])
    transpose_tile(C, hb[b], ("hb", b), hT, t, ident, b)


def mlp(C, hs, hT, w_up_ap, w_down_ap, tagp):
    P = C.P
    NS = 8
    wu = [C.sb(f"{tagp}wu{i}", [128, 8, 512], BF16) for i in range(2)]
    wd = [C.sb(f"{tagp}wd{i}", [128, 4, 1024], BF16) for i in range(2)]
    uT1 = C.sb(f"{tagp}uT", [128, 4, TL], BF16)
    uT = [uT1, uT1]
    rl = [C.sb(f"{tagp}rl{i}", [128, 512], F32) for i in range(2)]
    cnt = 0
    dcnt = 0

    def wload(s):
        b = s % 2
        load_weight_bf16(C, wu[b], w_up_ap, (tagp, "wu", b), 512, col0=s * 512)
        for fc in range(4):
            P.op("pool", f_dma(wd[b][:, fc, :],
                               w_down_ap[s * 512 + fc * 128: s * 512 + (fc + 1) * 128, :]),
                 writes=[(tagp, "wd", b)], dsem=f"{tagp}wd{b}")

    wload(0)
    for s in range(NS):
        b = s % 2
        for g in range(4):
            for fc in range(4):
                bk = 2 + (cnt % 2)
                ps = C.bank[bk]
                for c in range(8):
                    P.op("pe", f_mm(ps[:], wu[b][:, c, fc * 128:(fc + 1) * 128],
                                    hT[:, c, g * 512:(g + 1) * 512], c == 0, c == 7),
                         reads=[(tagp, "wu", b)] + [("hT", 4 * g + a) for a in range(4)],
                         writes=[("bank", bk)])
                r = rl[cnt % 2]
                P.op("act", f_act(r[:], ps[:], AF.Relu), reads=[("bank", bk)],
                     writes=[(tagp, "rl", cnt % 2)])
                P.op("pool", f_tt(uT[b][:, fc, g * 512:(g + 1) * 512], r[:], r[:], ALU.mult),
                     reads=[(tagp, "rl", cnt % 2)], writes=[(tagp, "uT", g)])
                cnt += 1
        if s + 1 < NS:
            wload(s + 1)
        for t in range(NT):
            for half in range(2):
                bk = 4 + (dcnt % 4)
                ps = C.bank[bk]
                for fc in range(4):
                    P.op("pe", f_mm(ps[:], uT[b][:, fc, t * 128:(t + 1) * 128],
                                    wd[b][:, fc, half * 512:(half + 1) * 512], fc == 0, fc == 3),
                         reads=[(tagp, "uT", t // 4), (tagp, "wd", b)], writes=[("bank", bk)])
                dst = hs[:, t, half * 512:(half + 1) * 512]
                if s == 0:
                    P.op("dve", f_stt(dst, dst, ALPHA, ps[:], ALU.mult, ALU.add),
                         reads=[("bank", bk), ("hs", t)], writes=[("hs", t)])
                else:
                    P.op("dve", f_tt(dst, dst, ps[:], ALU.add),
                         reads=[("bank", bk), ("hs", t)], writes=[("hs", t)])
                dcnt += 1


def attention(C, cfg, q_d, k_d, v_d, G_d, b31, epilogue):
    P = C.P
    H, nm, KR, dv = cfg["H"], cfg["nm"], cfg["KR"], cfg["dv"]
    dv1 = dv + 1
    per_bank = 512 // dv1 if nm == 1 else 3
    Kb = [C.sb(f"Kb{i}", [128, S], BF16) for i in range(2)]
    Vb = [C.sb(f"Vb{i}", [128, 64, dv1], BF16) for i in range(2)]
    Qb = [C.sb(f"Qb{i}", [128, TL], BF16) for i in range(2)]
    Gb = [C.sb(f"Gb{i}", [128, nm, 640], F32) for i in range(2)]
    nslots = 3 if nm == 1 else 2
    depth = nslots - 1
    nPT = depth + 2
    acc0 = 2 * nslots
    nacc_banks = -(-(4 * nm) // per_bank)
    PT = [C.sb(f"PT{s_}", [128, 2, 512], BF16) for s_ in range(nPT)]
    accsb = [C.sb(f"accsb{i}", [128, nacc_banks, 512], F32) for i in range(2)]

    def rows(m):
        return (64 * m, 64 * m + 64) if nm == 2 else (0, KR)

    def mapidx(h, m):
        return 2 * h + m if nm == 2 else h

    def acc_bank(n):
        return acc0 + n // per_bank

    def acc_ap(n):
        off = (n % per_bank) * dv1
        return C.bank[acc_bank(n)][:, off:off + dv1]

    def acc_key(n):
        return ("bank", acc_bank(n))

    def load(h):
        p = h % 2
        for r in range(4):
            P.op("sp", f_dma(Kb[p][0:KR, r * TL:(r + 1) * TL], k_d[r, h]),
                 writes=[("K", p)], dsem=f"K{p}")
        for r in range(4):
            P.op("sp", f_dma(Vb[p][:, r * 16:(r + 1) * 16, :], v_d[r, h]),
                 writes=[("V", p)], dsem=f"V{p}")
        P.op("sp", f_dma(Qb[p][0:KR, :], q_d[h]), writes=[("Q", p)], dsem=f"Q{p}")
        m0 = mapidx(h, 0)
        P.op("sp", f_dma(Gb[p][:], G_d[:, m0:m0 + nm, :]), writes=[("G", p)], dsem=f"G{p}")
        for m in range(nm):
            mi = mapidx(h, m)
            P.op("dve", f_ts(Gb[p][:, m, :], Gb[p][:, m, :], b31[:, mi:mi + 1], None, ALU.subtract),
                 reads=[("G", p), "b31"], writes=[("G", p)])

    def amin_of(u, c):
        return max(0, -((-(c - 16 * u - 3)) // 4))

    def afar_of(u, c):
        return min(4, max(0, -((-(c - 16 * u + 2)) // 4)))

    def kslot(c):
        return (c % 4) * 16 + c // 4

    items = []
    for h in range(H):
        for u in range(4):
            if nm == 2:
                for c in range(16 * u + 16):
                    items.append((h, u, ((c, 0), (c, 1))))
            else:
                for cp in range(8 * u + 8):
                    items.append((h, u, ((2 * cp, 0), (2 * cp + 1, 0))))

    def emit_qk_exp(i):
        h, u, subs = items[i]
        p = h % 2
        slot = i % nslots
        psl = i % nPT
        bks = (2 * slot, 2 * slot + 1)
        amins = [amin_of(u, c) for c, m in subs]
        afars = [afar_of(u, c) for c, m in subs]
        for j, (c, m) in enumerate(subs):
            r0, r1 = rows(m)
            col = kslot(c) * 128
            amin = amins[j]
            P.op("pe", f_mm(C.bank[bks[j]][:, amin * 128:512], Kb[p][r0:r1, col:col + 128],
                            Qb[p][r0:r1, (4 * u + amin) * 128:(4 * u + 4) * 128], True, True),
                 reads=[("K", p), ("Q", p)], writes=[("bank", bks[j])])
        if nm == 2:
            c = subs[0][0]
            for a in range(amins[0], afars[0]):
                s_ = c - (16 * u + 4 * a) + 1
                P.op("dve", f_tt(C.psum[:, bks[0]:bks[0] + 2, a * 128:(a + 1) * 128],
                                 C.psum[:, bks[0]:bks[0] + 2, a * 128:(a + 1) * 128],
                                 Gb[p][:, :, s_ * 128:(s_ + 1) * 128], ALU.add),
                     reads=[("bank", bks[0]), ("bank", bks[1]), ("G", p)],
                     writes=[("bank", bks[0]), ("bank", bks[1])])
        else:
            for j, (c, m) in enumerate(subs):
                for a in range(amins[j], afars[j]):
                    s_ = c - (16 * u + 4 * a) + 1
                    P.op("dve", f_tt(C.bank[bks[j]][:, a * 128:(a + 1) * 128],
                                     C.bank[bks[j]][:, a * 128:(a + 1) * 128],
                                     Gb[p][:, 0, s_ * 128:(s_ + 1) * 128], ALU.add),
                         reads=[("bank", bks[j]), ("G", p)], writes=[("bank", bks[j])])
        am = min(amins)
        P.op("act", f_act(PT[psl][:, :, am * 128:512], C.psum[:, bks[0]:bks[0] + 2, am * 128:512], AF.Exp),
             reads=[("bank", bks[0]), ("bank", bks[1])],
             writes=[("PT", psl, j, a) for j in range(2) for a in range(am, 4)])

    def emit_pv(i):
        h, u, subs = items[i]
        p = h % 2
        psl = i % nPT
        for j, (c, m) in enumerate(subs):
            amin = amin_of(u, c)
            for a in range(amin, 4):
                n = m * 4 + a
                first = (c == 0) and (n % per_bank == 0)
                last = (c == 16 * u + 4 * a + 3)
                P.op("pe", f_mm(acc_ap(n), PT[psl][:, j, a * 128:(a + 1) * 128],
                                Vb[p][:, kslot(c), :], first, last, skip=True),
                     reads=[("PT", psl, j, a), ("V", p)], writes=[acc_key(n)])

    gcnt = [0]

    def evacuate_and_epilogue(h, u):
        k = gcnt[0] % 2
        gcnt[0] += 1
        for bi in range(nacc_banks):
            nin = min(per_bank, 4 * nm - bi * per_bank)
            P.op("dve", f_copy(accsb[k][:, bi, 0:nin * dv1], C.bank[acc0 + bi][:, 0:nin * dv1]),
                 reads=[("bank", acc0 + bi)], writes=[("accsb", k, bi)])

        def sb_ap(n):
            off = (n % per_bank) * dv1
            return accsb[k][:, n // per_bank, off:off + dv1]

        def sb_key(n):
            return ("accsb", k, n // per_bank)
        epilogue(h, u, sb_ap, sb_key)

    load(0)
    if H > 1:
        load(1)
    for i0 in range(min(depth, len(items))):
        emit_qk_exp(i0)
    for i in range(len(items)):
        h, u, subs = items[i]
        nxt = items[i + 1] if i + 1 < len(items) else None
        if i + depth < len(items):
            emit_qk_exp(i + depth)
        emit_pv(i)
        if subs[1][0] == 16 * u + 15:
            evacuate_and_epilogue(h, u)
        if nxt is not None and nxt[0] != h and h + 2 < H:
            load(h + 2)


def dram_in(nc, name, shape, dt):
    return nc.dram_tensor(name, list(shape), dt, kind="ExternalInput").ap()


def dram_out(nc, name, shape, dt):
    return nc.dram_tensor(name, list(shape), dt, kind="ExternalOutput").ap()


def fm_project(C, w, wkey, col0, hT, scale, tag, store):
    P = C.P
    stg = [C.sb(f"{tag}stg{i}", [128, 8, 512], BF16) for i in range(2)]
    cnt = 0
    for g in range(4):
        b = g % 2
        for f in range(8):
            bk = 2 + (cnt % 2)
            cnt += 1
            ps = C.bank[bk]
            for c in range(8):
                P.op("pe", f_mm(ps[:], w[:, c, col0 + f * 128: col0 + (f + 1) * 128],
                                hT[:, c, g * 512:(g + 1) * 512], c == 0, c == 7),
                     reads=[wkey] + [("hT", 4 * g + a) for a in range(4)], writes=[("bank", bk)])
            P.op("act", f_act(stg[b][:, f, :], ps[:], AF.Copy, scale=scale),
                 reads=[("bank", bk)], writes=[(tag, "stg", b)])
        store(g, stg[b], (tag, "stg", b), b)
    return stg


def v_project(C, wb, wkey, col0, hT, out_d, nh, dv, tag):
    P = C.P
    dv1 = dv + 1
    stg = [C.sb(f"{tag}vstg{i}", [128, nh, 4, dv1], BF16) for i in range(2)]
    for i in range(2):
        P.op("pool", f_memset(stg[i][:], 1.0), writes=[(tag, "vstg", i)])
    hpb = 512 // dv
    nhalf = nh * dv // 512
    cnt = 0
    for g in range(4):
        b = g % 2
        for a in range(4):
            t = 4 * g + a
            for half in range(nhalf):
                bk = 4 + (cnt % 2)
                cnt += 1
                ps = C.bank[bk]
                for c in range(8):
                    P.op("pe", f_mm(ps[:], hT[:, c, t * 128:(t + 1) * 128],
                                    wb[:, c, col0 + half * 512: col0 + (half + 1) * 512],
                                    c == 0, c == 7),
                         reads=[wkey, ("hT", t)], writes=[("bank", bk)])
                P.op("dve", f_copy(stg[b][:, half * hpb:(half + 1) * hpb, a, 0:dv],
                                   ps[:].rearrange("p (h e) -> p h e", e=dv)),
                     reads=[("bank", bk)], writes=[(tag, "vstg", b)])
        P.op("sp", f_dma(out_d.rearrange("h p t e -> p h (t e)")[:, :, 4 * g * dv1:(4 * g + 4) * dv1],
                         stg[b][:].rearrange("p h a e -> p h (a e)")),
             reads=[(tag, "vstg", b)], writes=[(tag, "dram")], dsem=f"{tag}vst{b}")


def out_proj_ln(C, ostore, wo, res_ap, hs, hT, ident, g_rep, b_rep, gb_key, scr, xt, hb, oT):
    P = C.P
    for t in range(NT):
        b = t % 2
        pT = C.bank_bf(b).rearrange("p (c n) -> p c n", n=128)
        for c in range(8):
            P.op("pe", f_tr(pT[:, c, :], ostore[:, t, c * 128:(c + 1) * 128], ident[:]),
                 reads=[("os", t), "ident"], writes=[("bank", b)])
        P.op("dve", f_copy(oT[b][:], pT), reads=[("bank", b)], writes=[("oT", b)])
        if res_ap is not None:
            P.op("sp", f_dma(xt[:], res_ap[t * 128:(t + 1) * 128, :]),
                 writes=[("B", "xt")], dsem="Bxt")
        for half in range(2):
            bk = 2 + half
            ps = C.bank[bk]
            for c in range(8):
                P.op("pe", f_mm(ps[:], oT[b][:, c, :], wo[:, c, half * 512:(half + 1) * 512],
                                c == 0, c == 7),
                     reads=[("oT", b), "wo"], writes=[("bank", bk)])
            dst = hs[:, t, half * 512:(half + 1) * 512]
            if res_ap is None:
                P.op("dve", f_stt(dst, dst, ALPHA, ps[:], ALU.mult, ALU.add),
                     reads=[("bank", bk), ("hs", t)], writes=[("hs", t)])
            else:
                P.op("dve", f_stt(dst, xt[:, half * 512:(half + 1) * 512], ALPHA, ps[:],
                                  ALU.mult, ALU.add),
                     reads=[("bank", bk), ("B", "xt")], writes=[("hs", t)])
        layer_norm_tile(C, hs[:, t, :], ("hs", t), hs[:, t, :], ("hs", t), g_rep, b_rep, gb_key, scr)
        h_to_hT(C, hs, t, hT, ident, hb)


def load_rep(C, tile, ap, key):
    C.P.op("sp", f_dma(tile[:], ap), writes=[key], dsem=_san(key))


class Scope:
    def __init__(self, C):
        self.C = C

    def __enter__(self):
        self.st = ExitStack()
        self.prev = (self.C.st, self.C.side)
        self.C.st, self.C.side = self.st, "right"
        return self

    def __exit__(self, *a):
        self.C.P.barrier()
        self.st.close()
        self.C.st, self.C.side = self.prev
        return False


def build_A():
    nc = bass.Bass("TRN2", target_bir_lowering=False)
    x = dram_in(nc, "x", [TL, D], F32)
    w_in = dram_in(nc, "w_in", [D, 3 * D], F32)
    ident_d = dram_in(nc, "ident", [128, 128], BF16)
    qT = dram_out(nc, "qT", [8, 128, TL], BF16)
    kT = dram_out(nc, "kT", [8, 128, TL], BF16)
    v = dram_out(nc, "v", [8, 128, NT, 129], BF16)
    with ExitStack() as st:
        C = Ctx(nc, st)
        P = C.P
        ident = C.sb("ident", [128, 128], BF16)
        P.op("sp", f_dma(ident[:], ident_d), writes=["ident"], dsem="ident")
        phase_A(C, x, w_in, ident, qT, kT, v)
        P.emit()
    return nc


def phase_A(C, x, w_in, ident, qT, kT, v):
    P = C.P
    with Scope(C):
        wb = C.sb("wb", [128, 8, 3 * D], BF16)
        load_weight_bf16(C, wb, w_in, "wb", 3 * D)
        hT = C.sb("hTA", [128, 8, TL], BF16)
        xt = [C.sb(f"xtA{i}", [128, D], F32) for i in range(2)]
        xb = [C.sb(f"xbA{i}", [128, D], BF16) for i in range(2)]
        make_hT_from_dram(C, x, hT, ident, xt, xb, "A")

        def mkstore(out_d, tag):
            def store(g, stg, key, b):
                P.op("sp", f_dma(out_d.rearrange("h p n -> p h n")[:, :, g * 512:(g + 1) * 512], stg[:]),
                     reads=[key], writes=[(tag, "dram")], dsem=f"{tag}st{b}")
            return store
        fm_project(C, wb, "wb", 0, hT, 0.125, "q", mkstore(qT, "q"))
        fm_project(C, wb, "wb", D, hT, 1.0, "k", mkstore(kT, "k"))
        v_project(C, wb, "wb", 2 * D, hT, v, 8, 128, "v")


def build_B():
    nc = bass.Bass("TRN2", target_bir_lowering=False)
    x = dram_in(nc, "x", [TL, D], F32)
    qT = dram_in(nc, "qT", [8, 128, TL], BF16)
    kT_all = dram_in(nc, "kT_all", [4, 8, 128, TL], BF16)
    v_all = dram_in(nc, "v_all", [4, 8, 128, NT, 129], BF16)
    cst = const_inputs(nc)
    prm = {
        "lam": dram_in(nc, "lam", [128, 256], F32),
        "subg": dram_in(nc, "subg", [128, 128], F32),
        "w_out0": dram_in(nc, "w_out0", [D, D], F32),
        "w_up0": dram_in(nc, "w_up0", [D, DFF], F32),
        "w_down0": dram_in(nc, "w_down0", [DFF, D], F32),
        "w_kv": dram_in(nc, "w_kv", [D, 2 * D], F32),
    }
    for k in ("ln1_g0", "ln1_b0", "ln2_g0", "ln2_b0"):
        prm[k] = dram_in(nc, k, [128, D], F32)
    h1 = dram_out(nc, "h1", [TL, D], F32)
    kT1 = dram_out(nc, "kT1", [16, 96, TL], BF16)
    v1 = dram_out(nc, "v1", [16, 128, NT, 65], BF16)
    with ExitStack() as st:
        C = Ctx(nc, st)
        P = C.P
        cs = load_consts(C, cst)
        hs, hT, hT_stack = phase_B(C, cs, cst, prm, x, qT, kT_all, v_all, kT1, v1)
        st.callback(hT_stack.close)
        for t in range(NT):
            P.op("sp", f_dma(h1[t * 128:(t + 1) * 128, :], hs[:, t, :]), reads=[("hs", t)],
                 dsem=f"h1st{t % 2}")
        P.emit()
    return nc


def const_inputs(nc):
    return {
        "G": dram_in(nc, "G", [128, 16, 640], F32),
        "b31": dram_in(nc, "b31", [128, 16], F32),
        "ident": dram_in(nc, "ident", [128, 128], BF16),
        "e_own": dram_in(nc, "e_own", [32, TL], BF16),
        "past30k": dram_in(nc, "past30k", [128, NT, 32], F32),
    }


def load_consts(C, cst):
    P = C.P
    ident = C.sb("ident", [128, 128], BF16)
    P.op("sp", f_dma(ident[:], cst["ident"]), writes=["ident"], dsem="ident")
    b31 = C.sb("b31", [128, 16], F32)
    P.op("sp", f_dma(b31[:], cst["b31"]), writes=["b31"], dsem="b31")
    scr = ln_scratch(C)
    hb = [C.sb(f"hb{i}", [128, D], BF16) for i in range(2)]
    rep = [C.sb(f"rep{i}", [128, D], F32) for i in range(2)]
    kmT = C.sb("kmT", [128, 8, 32], BF16)
    return {"ident": ident, "b31": b31, "scr": scr, "hb": hb, "rep": rep, "kmT": kmT}


def alloc_hT(C):
    stk = ExitStack()
    hT = stk.enter_context(C.nc.sbuf_tensor("s_hT%d" % C.uid, [128, 8, TL], BF16, side="left"))
    C.uid += 1
    return hT, stk


def phase_B(C, cs, cst, prm, x, qT, kT_all, v_all, kT1, v1):
    P = C.P
    ident, b31, scr, hb, rep = cs["ident"], cs["b31"], cs["scr"], cs["hb"], cs["rep"]
    with Scope(C):
        ostore = C.sb("ostore", [128, NT, D], BF16)
        lam_sb = C.sb("lam_sb", [128, 256], F32)
        lprod = C.sb("lprod", [128, 2, 64], F32)
        lsum = C.sb("lsum", [128, 2], F32)
        lexp = C.sb("lexp", [128, 2], F32)
        nl = C.sb("nl", [128, 2], F32)
        gsub = C.sb("gsub", [128, 128], F32)
        P.op("sp", f_dma(lam_sb[:], prm["lam"]), writes=["lam_sb"], dsem="lam")
        P.op("sp", f_dma(gsub[:], prm["subg"]), writes=["gsub"], dsem="gsub")
        lv = lam_sb[:].rearrange("p (a b d) -> p a b d", a=2, b=2)
        P.op("dve", f_tt(lprod[:], lv[:, :, 0, :], lv[:, :, 1, :], ALU.mult),
             reads=["lam_sb"], writes=["lprod"])
        P.op("dve", lambda e: e.reduce_sum(out=lsum[:], in_=lprod[:], axis=AX.X),
             reads=["lprod"], writes=["lsum"])
        P.op("act", f_act(lexp[:], lsum[:], AF.Exp), reads=["lsum"], writes=["lexp"])
        P.op("dve", f_tt(nl[:, 0:1], lexp[:, 1:2], lexp[:, 0:1], ALU.subtract),
             reads=["lexp"], writes=["nl0"])
        P.op("dve", f_ts(nl[:, 1:2], nl[:, 0:1], -LAM_INIT0, None, ALU.add),
             reads=["nl0"], writes=["neglam"])
        neglam = nl[:, 1:2]
        P.op("dve", f_ts(gsub[:], gsub[:], 1.0 - LAM_INIT0, None, ALU.mult),
             reads=["gsub"], writes=["gsub"])
        ep = {k: [C.sb(f"ep_{k}{i}", shp, F32) for i in range(2)]
              for k, shp in (("rl", [128, 2]), ("c1", [128, 1]), ("t1", [128, 128]),
                             ("o", [128, 128]), ("sq", [128, 128]), ("ss", [128, 1]))}
        ecnt = [0]

        def epilogue(h, u, acc_ap, acc_key):
            for a in range(4):
                t = 4 * u + a
                k = ecnt[0] % 2
                ecnt[0] += 1
                rl, c1, t1, o, sq, ss = (ep[x_][k] for x_ in ("rl", "c1", "t1", "o", "sq", "ss"))
                a0, a1 = acc_ap(a), acc_ap(4 + a)
                k0, k1 = acc_key(a), acc_key(4 + a)
                P.op("dve", lambda e, rl=rl, a0=a0: e.reciprocal(out=rl[:, 0:1], in_=a0[:, 128:129]),
                     reads=[k0], writes=[("ep", k, "rl0")])
                P.op("dve", lambda e, rl=rl, a1=a1: e.reciprocal(out=rl[:, 1:2], in_=a1[:, 128:129]),
                     reads=[k1], writes=[("ep", k, "rl1")])
                P.op("dve", f_tt(c1[:], rl[:, 1:2], neglam, ALU.mult),
                     reads=[("ep", k, "rl1"), "neglam"], writes=[("ep", k, "c1")])
                P.op("dve", f_ts(t1[:], a1[:, 0:128], c1[:, 0:1], None, ALU.mult),
                     reads=[k1, ("ep", k, "c1")], writes=[("ep", k, "t1")])
                P.op("dve", f_stt(o[:], a0[:, 0:128], rl[:, 0:1], t1[:], ALU.mult, ALU.add),
                     reads=[k0, ("ep", k, "rl0"), ("ep", k, "t1")], writes=[("ep", k, "o")])
                P.op("pool", f_tt(sq[:], o[:], o[:], ALU.mult), reads=[("ep", k, "o")],
                     writes=[("ep", k, "sq")])
                P.op("dve", lambda e, ss=ss, sq=sq: e.reduce_sum(out=ss[:], in_=sq[:], axis=AX.X),
                     reads=[("ep", k, "sq")], writes=[("ep", k, "ss")])
                P.op("dve", f_ts(ss[:], ss[:], 1.0 / 128.0, LN_EPS, ALU.mult, ALU.add),
                     reads=[("ep", k, "ss")], writes=[("ep", k, "ss")])
                P.op("pool", f_tt(ss[:], ss[:], scr["mhalf"][:, 0:1], ALU.pow),
                     reads=[("ep", k, "ss"), "mhalf"], writes=[("ep", k, "ss")])
                P.op("dve", f_stt(ostore[:, t, h * 128:(h + 1) * 128], o[:], ss[:, 0:1], gsub[:],
                                  ALU.mult, ALU.mult),
                     reads=[("ep", k, "o"), ("ep", k, "ss"), "gsub"], writes=[("os", t)])

        with Scope(C):
            attention(C, dict(H=8, nm=2, KR=128, dv=128), qT, kT_all, v_all, cst["G"], b31, epilogue)
        C.side = "left"
        hs = C.nc.sbuf_tensor("s_hs", [128, NT, D], F32, side="left")
        hs = C.root.enter_context(hs)
        hT, hT_stack = alloc_hT(C)
        C.side = "right"
        with Scope(C):
            wo = C.sb("wo", [128, 8, D], BF16)
            load_weight_bf16(C, wo, prm["w_out0"], "wo", D)
            xt = C.sb("xtB", [128, D], F32)
            oT = [C.sb(f"oT{i}", [128, 8, 128], BF16) for i in range(2)]
            load_rep(C, rep[0], prm["ln1_g0"], "rep0")
            load_rep(C, rep[1], prm["ln1_b0"], "rep1")
            out_proj_ln(C, ostore, wo, x, hs, hT, ident, rep[0], rep[1], "rep0", scr, xt, hb, oT)
    with Scope(C):
        mlp(C, hs, hT, prm["w_up0"], prm["w_down0"], "m0")
    load_rep(C, rep[0], prm["ln2_g0"], "rep0")
    load_rep(C, rep[1], prm["ln2_b0"], "rep1")
    for t in range(NT):
        layer_norm_tile(C, hs[:, t, :], ("hs", t), hs[:, t, :], ("hs", t), rep[0], rep[1], "rep0", scr)
        h_to_hT(C, hs, t, hT, ident, hb)
    with Scope(C):
        wkv = C.sb("wk1", [128, 8, D], BF16)
        load_weight_bf16(C, wkv, prm["w_kv"], "wkv", D)

        def store(g, stg, key, b):
            kv = kT1.rearrange("(f two) r n -> two r f n", two=2)
            for hh in range(2):
                P.op("sp", f_dma(kv[hh, 0:64, :, g * 512:(g + 1) * 512], stg[hh * 64:(hh + 1) * 64, :, :]),
                     reads=[key], writes=[("k1dram", hh)], dsem=f"k1st{b}")
        fm_project(C, wkv, "wkv", 0, hT, 1.0, "k1", store)
        for h in range(16):
            P.op("sp", f_dma(kT1[h, 64:96, :], cst["e_own"]), writes=[("k1dramE",)], dsem="k1e")
    with Scope(C):
        wv1 = C.sb("wv1", [128, 8, D], BF16)
        load_weight_bf16(C, wv1, prm["w_kv"], "wv1", D, col0=D)
        v_project(C, wv1, "wv1", 0, hT, v1, 16, 64, "v1")
    return hs, hT, hT_stack


def build_C(stop_after=None):
    nc = bass.Bass("TRN2", target_bir_lowering=False)
    h1 = dram_in(nc, "h1", [TL, D], F32)
    kT1_all = dram_in(nc, "kT1_all", [4, 16, 96, TL], BF16)
    v1_all = dram_in(nc, "v1_all", [4, 16, 128, NT, 65], BF16)
    cst = const_inputs(nc)
    prm = {
        "w_q": dram_in(nc, "w_q", [D, D], F32),
        "w_out1": dram_in(nc, "w_out1", [D, D], F32),
        "w_up1": dram_in(nc, "w_up1", [D, DFF], F32),
        "w_down1": dram_in(nc, "w_down1", [DFF, D], F32),
    }
    for k in ("ln1_g1", "ln1_b1", "ln2_g1", "ln2_b1"):
        prm[k] = dram_in(nc, k, [128, D], F32)
    out = dram_out(nc, "out", [TL, D], F32)
    qaug = nc.dram_tensor("qaug", [16, 96, TL], BF16,
                          kind="ExternalOutput" if stop_after else "Internal").ap()
    with ExitStack() as st:
        C = Ctx(nc, st)
        C.stop_after = stop_after
        P = C.P
        cs = load_consts(C, cst)
        hs = C.sb("hs", [128, NT, D], F32)
        hT, hT_stack = alloc_hT(C)
        for t in range(NT):
            P.op("sp", f_dma(hs[:, t, :], h1[t * 128:(t + 1) * 128, :]), writes=[("hs", t)],
                 dsem=f"hsld{t}")
            h_to_hT(C, hs, t, hT, cs["ident"], cs["hb"])
        phase_C(C, cs, cst, prm, kT1_all, v1_all, qaug, hs, hT, hT_stack, out)
        P.emit()
    return nc


def phase_C(C, cs, cst, prm, kT1_all, v1_all, qaug, hs, hT, hT_stack, out):
    P = C.P
    ident, b31, scr, hb, rep = cs["ident"], cs["b31"], cs["scr"], cs["hb"], cs["rep"]
    kmT = cs["kmT"]
    with Scope(C):
        moba_kmean(C, kT1_all, kmT)
    if getattr(C, "stop_after", None) == "kmean":
        P.op("sp", f_dma(qaug.rearrange("h r n -> r h n")[0:64, 0:8, 0:32], kmT[0:64, :, :]),
             reads=[("kmT", f) for f in range(8)], dsem="dbg1")
        P.op("sp", f_dma(qaug.rearrange("h r n -> r h n")[0:64, 8:16, 0:32], kmT[64:128, :, :]),
             reads=[("kmT", f) for f in range(8)], dsem="dbg2")
        for t in range(NT):
            P.op("sp", f_dma(out[t * 128:(t + 1) * 128, :], hs[:, t, :]), reads=[("hs", t)],
                 dsem=f"outst{t % 2}")
        hT_stack.close()
        return
    with Scope(C):
        moba_gate_qproj(C, hT, prm["w_q"], kmT, cst["past30k"], qaug, ident)
    hT_stack.close()
    hT = None
    if getattr(C, "stop_after", None) == "gate":
        for t in range(NT):
            P.op("sp", f_dma(out[t * 128:(t + 1) * 128, :], hs[:, t, :]), reads=[("hs", t)],
                 dsem=f"outst{t % 2}")
        return
    with Scope(C):
        ostore = C.sb("ostore1", [128, NT, D], BF16)
        eprl = [C.sb(f"eprl{i}", [128, 1], F32) for i in range(8)]

        def epilogue(h, u, acc_ap, acc_key):
            for a in range(4):
                t = 4 * u + a
                k = (h * 4 + u) % 2
                rl = eprl[4 * k + a]
                acc = acc_ap(a)
                P.op("dve", lambda e, rl=rl, acc=acc: e.reciprocal(out=rl[:], in_=acc[:, 64:65]),
                     reads=[acc_key(a)], writes=[("eprl", 4 * k + a)])
                P.op("dve", f_ts(ostore[:, t, h * 64:(h + 1) * 64], acc[:, 0:64], rl[:, 0:1], None,
                                 ALU.mult),
                     reads=[acc_key(a), ("eprl", 4 * k + a)], writes=[("os", t)])

        with Scope(C):
            attention(C, dict(H=16, nm=1, KR=96, dv=64), qaug, kT1_all, v1_all, cst["G"], b31, epilogue)
        hT, hT_stack2 = alloc_hT(C)
        C.root.callback(hT_stack2.close)
        with Scope(C):
            wo = C.sb("wo1", [128, 8, D], BF16)
            load_weight_bf16(C, wo, prm["w_out1"], "wo", D)
            oT = [C.sb(f"oT1{i}", [128, 8, 128], BF16) for i in range(2)]
            load_rep(C, rep[0], prm["ln1_g1"], "rep0")
            load_rep(C, rep[1], prm["ln1_b1"], "rep1")
            out_proj_ln(C, ostore, wo, None, hs, hT, ident, rep[0], rep[1], "rep0", scr, None, hb, oT)
    with Scope(C):
        mlp(C, hs, hT, prm["w_up1"], prm["w_down1"], "m1")
    load_rep(C, rep[0], prm["ln2_g1"], "rep0")
    load_rep(C, rep[1], prm["ln2_b1"], "rep1")
    for t in range(NT):
        layer_norm_tile(C, hs[:, t, :], ("hs", t), hs[:, t, :], ("hs", t), rep[0], rep[1], "rep0", scr)
        P.op("sp", f_dma(out[t * 128:(t + 1) * 128, :], hs[:, t, :]), reads=[("hs", t)],
             dsem=f"outst{t % 2}")


def moba_kmean(C, kT1_all, kmT):
    P = C.P
    Kp = [C.sb(f"Kp{i}", [128, S], BF16) for i in range(2)]
    ksum = [C.sb(f"ksum{i}", [128, 64], F32) for i in range(2)]
    kmf = [C.sb(f"kmf{i}", [128, 16, 2], F32) for i in range(2)]
    for f in range(8):
        b = f % 2
        for r in range(4):
            for hh in range(2):
                P.op("sp", f_dma(Kp[b][hh * 64:(hh + 1) * 64, r * TL:(r + 1) * TL],
                                 kT1_all[r, 2 * f + hh, 0:64, :]),
                     writes=[("Kp", b)], dsem=f"Kp{b}")
        P.op("dve", lambda e, b=b: e.reduce_sum(out=ksum[b][:],
                                                in_=Kp[b][:].rearrange("p (s q) -> p s q", q=128),
                                                axis=AX.X),
             reads=[("Kp", b)], writes=[("ksum", b)])
        kv = ksum[b][:].rearrange("p (r t) -> p r t", t=16)
        for par in range(2):
            P.op("dve", f_tt(kmf[b][:, :, par], kv[:, 2 * par, :], kv[:, 2 * par + 1, :], ALU.add),
                 reads=[("ksum", b)], writes=[("kmf", b, par)])
        P.op("dve", f_ts(kmT[:, f, :], kmf[b][:].rearrange("p t two -> p (t two)"), 1.0 / 256.0,
                         None, ALU.mult),
             reads=[("kmf", b, 0), ("kmf", b, 1)], writes=[("kmT", f)])


def moba_gate_qproj(C, hT, w_q, kmT, past_d, qaug, ident):
    P = C.P
    wq = C.sb("wq", [128, 8, D], BF16)
    load_weight_bf16(C, wq, w_q, "wq", D)
    past = C.sb("past", [128, NT, 32], F32)
    padd = C.sb("padd", [128, NT, 32], F32)
    P.op("sp", f_dma(past[:], past_d), writes=["past"], dsem="past")
    P.op("dve", f_ts(padd[:], past[:], NEG, None, ALU.add), reads=["past"], writes=["padd"])
    mstg = [C.sb(f"mstg{i}", [128, 16, 128], BF16) for i in range(2)]
    gm = [C.sb(f"gm{i}", [128, 16, 32], F32) for i in range(2)]
    top8 = [C.sb(f"top8{i}", [128, 16, 8], F32) for i in range(2)]
    sel = [C.sb(f"sel{i}", [128, 16, 32], F32) for i in range(2)]
    negpad = [C.sb(f"negpad{i}", [128, 16, 96], BF16) for i in range(2)]
    for i in range(2):
        P.op("pool", f_memset(negpad[i][:], 0.0), writes=[("negpad", i)])

    def store(g, stg, key, b):
        qv = qaug.rearrange("(f two) r n -> two r f n", two=2)
        for hh in range(2):
            P.op("sp", f_dma(qv[hh, 0:64, :, g * 512:(g + 1) * 512], stg[hh * 64:(hh + 1) * 64, :, :]),
                 reads=[key], writes=[("qaug", "q")], dsem=f"qst{b}")
        for a in range(4):
            t = 4 * g + a
            k = t % 2
            gbk = (4, 5) if k == 0 else (0, 1)
            for h in range(16):
                r0 = (h % 2) * 64
                gps = C.bank[gbk[h % 2]]
                P.op("pe", f_mm(gps[:, (h // 2) * 32:(h // 2 + 1) * 32],
                                stg[r0:r0 + 64, h // 2, a * 128:(a + 1) * 128],
                                kmT[r0:r0 + 64, h // 2, :], True, True),
                     reads=[key, ("kmT", h // 2)], writes=[("bank", gbk[h % 2])])
            gmv = gm[k][:].rearrange("p (f two) n -> p two f n", two=2)
            for par in range(2):
                P.op("dve", f_tt(gmv[:, par], C.bank[gbk[par]][:, 0:256].rearrange("p (f n) -> p f n", n=32),
                                 padd[:, t, :].unsqueeze(1).broadcast_to([128, 8, 32]), ALU.add),
                     reads=[("bank", gbk[par]), "padd"], writes=[("gm", k)])
            for h in range(16):
                P.op("dve", lambda e, k=k, h=h: e.max(out=top8[k][:, h, :], in_=gm[k][:, h, :]),
                     reads=[("gm", k)], writes=[("top8", k)])
            P.op("dve", f_tt(sel[k][:], gm[k][:], top8[k][:, :, 2:3].broadcast_to([128, 16, 32]),
                             ALU.is_ge),
                 reads=[("gm", k), ("top8", k)], writes=[("sel", k)])
            P.op("dve", f_ts(sel[k][:], sel[k][:], -1.0, None, ALU.add),
                 reads=[("sel", k)], writes=[("sel", k)])
            P.op("dve", f_tt(negpad[k][:, :, 64:96], sel[k][:],
                             past[:, t, :].unsqueeze(1).broadcast_to([128, 16, 32]), ALU.mult),
                 reads=[("sel", k), "past"], writes=[("negpad", k)])
            for hq in range(2):
                bk = 6 + hq
                tp = C.bank_bf(bk).rearrange("p (h n) -> p h n", n=128)
                for h8 in range(8):
                    h = hq * 8 + h8
                    P.op("pe", f_tr(tp[0:96, h8, :], negpad[k][:, h, :], ident[:]),
                         reads=[("negpad", k), "ident"], writes=[("bank", bk)])
                P.op("act", f_act(mstg[k][64:96, hq * 8:(hq + 1) * 8, :], tp[64:96, :, :], AF.Copy),
                     reads=[("bank", bk)], writes=[("mstg", k)])
            P.op("sp", f_dma(qaug.rearrange("h r n -> r h n")[64:96, :, t * 128:(t + 1) * 128],
                             mstg[k][64:96, :, :]),
                 reads=[("mstg", k)], writes=[("qaug", "m")], dsem=f"mst{k}")

    fm_project(C, wq, "wq", 0, hT, 0.125, "qb", store)


def _bucket_np(n):
    n = np.maximum(n, 0)
    nf = np.maximum(n, 16).astype(np.float32)
    large = 16 + (np.log(nf / np.float32(16)) / np.float32(math.log(8.0)) * np.float32(16)).astype(np.int32)
    large = np.minimum(large, 31)
    return np.where(n < 16, n, large)


def _g_index(j):
    k = np.arange(128)[:, None, None]
    q = np.arange(128)[None, None, :]
    e = (np.arange(5) - 1)[None, :, None]
    delta = e - j
    rel = q - k - 128 * delta
    idx = _bucket_np(rel)
    idx = np.where(rel >= 0, idx, 32)
    return np.broadcast_to(idx, (128, 5, 128)).astype(np.int64)


_CACHE = {}


def _get(name, fn):
    if name not in _CACHE:
        _CACHE[name] = fn()
    return _CACHE[name]


def _host_prep(x, rel_bias):
    f32 = np.float32
    bf = ml_dtypes.bfloat16
    table_ext = np.concatenate([rel_bias, np.full((1, 16), NEG, f32)], axis=0)
    per = []
    for c in range(NCORES):
        b, j = c // 4, c % 4
        xs = np.ascontiguousarray(x[b].reshape(64, 128, D)[j::4].reshape(TL, D))
        G = table_ext[_g_index(j)]
        G = np.ascontiguousarray(G.transpose(0, 3, 1, 2).reshape(128, 16, 640))
        blk = (4 * np.arange(NT) + j) // 2
        e = np.zeros((32, NT, 128), f32)
        e[blk, np.arange(NT), :] = 1.0
        past = (np.arange(32)[None, :] < blk[:, None]).astype(f32) * f32(30000.0)
        per.append({
            "x": xs, "G": G, "e_own": e.reshape(32, TL).astype(bf),
            "past30k": np.ascontiguousarray(np.broadcast_to(past[None], (128, NT, 32))),
            "b31": np.ascontiguousarray(np.broadcast_to(rel_bias[31:32, :], (128, 16))),
            "ident": np.eye(128, dtype=f32).astype(bf),
        })
    return per


def _rep(v, n=128):
    v = np.asarray(v, np.float32).reshape(1, -1)
    return np.ascontiguousarray(np.broadcast_to(v, (n, v.shape[1])))


CONST_KEYS = ("G", "b31", "ident", "e_own", "past30k")


def kernel(x, w_in_a, lam_a, subln_a, w_out_a, w_kv_shared, w_q_b, w_out_b, rel_bias,
           ln1_g, ln1_b, ln2_g, ln2_b, w_up, w_down):
    f32 = np.float32
    A = lambda v: np.ascontiguousarray(np.asarray(v, f32))
    x = A(x)
    rel_bias = A(rel_bias)
    cores = list(range(NCORES))
    per = _host_prep(x, rel_bias)
    ncA = _get("A", build_A)
    inA = [{"x": per[c]["x"], "w_in": A(w_in_a[0]), "ident": per[c]["ident"]} for c in cores]
    rA = run_bass_kernel_spmd(ncA, inA, core_ids=cores).results
    ncB = _get("B", build_B)
    prmB = {
        "lam": _rep(np.asarray(lam_a[0], f32).reshape(-1)), "subg": _rep(subln_a[0]),
        "w_out0": A(w_out_a[0]), "w_up0": A(w_up[0]), "w_down0": A(w_down[0]),
        "w_kv": A(w_kv_shared),
        "ln1_g0": _rep(ln1_g[0]), "ln1_b0": _rep(ln1_b[0]),
        "ln2_g0": _rep(ln2_g[0]), "ln2_b0": _rep(ln2_b[0]),
    }
    inB = []
    for c in cores:
        grp = [4 * (c // 4) + r for r in range(4)]
        d = {"x": per[c]["x"], "qT": rA[c]["qT"],
             "kT_all": np.stack([rA[g]["kT"] for g in grp]),
             "v_all": np.stack([rA[g]["v"] for g in grp])}
        d.update({k: per[c][k] for k in CONST_KEYS})
        d.update(prmB)
        inB.append(d)
    rB = run_bass_kernel_spmd(ncB, inB, core_ids=cores).results
    ncC = _get("C", build_C)
    prmC = {
        "w_q": A(w_q_b[0]), "w_out1": A(w_out_b[0]), "w_up1": A(w_up[1]), "w_down1": A(w_down[1]),
        "ln1_g1": _rep(ln1_g[1]), "ln1_b1": _rep(ln1_b[1]),
        "ln2_g1": _rep(ln2_g[1]), "ln2_b1": _rep(ln2_b[1]),
    }
    inC = []
    for c in cores:
        grp = [4 * (c // 4) + r for r in range(4)]
        d = {"h1": rB[c]["h1"],
             "kT1_all": np.stack([rB[g]["kT1"] for g in grp]),
             "v1_all": np.stack([rB[g]["v1"] for g in grp])}
        d.update({k: per[c][k] for k in CONST_KEYS})
        d.update(prmC)
        inC.append(d)
    rC = run_bass_kernel_spmd(ncC, inC, core_ids=cores).results
    out = np.empty((2, 64, 128, D), f32)
    for c in cores:
        b, j = c // 4, c % 4
        out[b, j::4] = np.asarray(rC[c]["out"], f32).reshape(NT, 128, D)
    return out.reshape(2, S, D)
```

```python
import math
from contextlib import ExitStack

import numpy as np
import ml_dtypes
import concourse.bass as bass
import concourse.mybir as mybir
from concourse.bass_utils import run_bass_kernel_spmd

F32 = mybir.dt.float32
BF16 = mybir.dt.bfloat16
AF = mybir.ActivationFunctionType
ALU = mybir.AluOpType
AX = mybir.AxisListType

NCORES = 8
D = 1024
S = 8192
NT = 16
TL = 2048
DFF = 4096
ALPHA = 4.0 ** 0.25
LN_EPS = 1e-5
LAM_INIT0 = 0.8 - 0.6 * math.exp(-0.3 * 0)
NEG = -30000.0

ENGS = ("pe", "act", "dve", "pool", "sp")


class Op:
    __slots__ = ("fn", "deps", "dsem", "seq", "waits")

    def __init__(self, fn, deps, dsem):
        self.fn = fn
        self.deps = deps
        self.dsem = dsem
        self.seq = None
        self.waits = []


class Prog:
    def __init__(self, nc):
        self.nc = nc
        self.ops = {e: [] for e in ENGS}
        self.last_w = {}
        self.readers = {}

    def op(self, eng, fn, reads=(), writes=(), dsem=None):
        deps = set()
        for k in reads:
            w = self.last_w.get(k)
            if w is not None:
                deps.add(w)
        for k in writes:
            w = self.last_w.get(k)
            if w is not None:
                deps.add(w)
            rd = self.readers.get(k)
            if rd:
                for e2, i2 in rd.items():
                    deps.add((e2, i2))
        idx = len(self.ops[eng])
        me = (eng, idx)
        deps.discard(me)
        self.ops[eng].append(Op(fn, deps, dsem))
        for k in reads:
            self.readers.setdefault(k, {})[eng] = idx
        for k in writes:
            self.last_w[k] = me
            self.readers[k] = {}
        return me

    def barrier(self):
        deps = set()
        for e in ENGS:
            ops = self.ops[e]
            last_real = None
            seen_d = set()
            for i in range(len(ops) - 1, -1, -1):
                o = ops[i]
                if o.fn is None:
                    continue
                if o.dsem is not None:
                    if o.dsem not in seen_d:
                        seen_d.add(o.dsem)
                        deps.add((e, i))
                elif last_real is None:
                    last_real = i
                    deps.add((e, i))
        for e in ENGS:
            self.ops[e].append(Op(None, set(deps), None))

    def resolve(self):
        needed = set()
        for e in ENGS:
            for o in self.ops[e]:
                for d in o.deps:
                    if d[0] == e and e == "pe":
                        continue
                    needed.add(d)
        self.sem_names = set()
        cnt = {}
        for e in ENGS:
            for i, o in enumerate(self.ops[e]):
                if o.dsem is not None:
                    name = "d_" + o.dsem
                    cnt[name] = cnt.get(name, 0) + 16
                    o.seq = (name, cnt[name])
                    self.sem_names.add(name)
                elif (e, i) in needed:
                    name = "e_" + e
                    cnt[name] = cnt.get(name, 0) + 1
                    o.seq = (name, cnt[name])
                    self.sem_names.add(name)
        self.finals = dict(cnt)
        for e in ENGS:
            seen = {}
            for o in self.ops[e]:
                req = {}
                for d in o.deps:
                    if d[0] == e and e == "pe":
                        continue
                    name, val = self.ops[d[0]][d[1]].seq
                    if req.get(name, 0) < val:
                        req[name] = val
                for name, val in req.items():
                    if seen.get(name, 0) < val:
                        seen[name] = val
                        o.waits.append((name, val))

    def check(self):
        sem = {}
        pc = {e: 0 for e in ENGS}
        n = {e: len(self.ops[e]) for e in ENGS}
        progress = True
        while progress:
            progress = False
            for e in ENGS:
                while pc[e] < n[e]:
                    o = self.ops[e][pc[e]]
                    if all(sem.get(nm, 0) >= v for nm, v in o.waits):
                        if o.seq is not None:
                            nm = o.seq[0]
                            sem[nm] = sem.get(nm, 0) + (16 if nm.startswith("d_") else 1)
                        pc[e] += 1
                        progress = True
                    else:
                        break
        stuck = {e: (pc[e], n[e]) for e in ENGS if pc[e] < n[e]}
        if stuck:
            msg = []
            for e, (p, m) in stuck.items():
                o = self.ops[e][p]
                msg.append(f"{e}@{p}/{m} waits {[(nm, v, sem.get(nm, 0)) for nm, v in o.waits]}")
            raise RuntimeError("sync deadlock: " + "; ".join(msg))
        return {e: n[e] for e in ENGS}, len(self.sem_names)

    def emit(self):
        nc = self.nc
        self.resolve()
        print("prog ops/sems:", self.check(), flush=True)
        with ExitStack() as st:
            sems = {}
            for name in sorted(self.sem_names):
                sems[name] = st.enter_context(nc.semaphore(name))
            block = st.enter_context(nc.Block())

            def run(engname):
                def body(engine):
                    for o in self.ops[engname]:
                        if o.fn is None:
                            for (name, val) in o.waits:
                                engine.wait_ge(sems[name], val)
                            continue
                        for (name, val) in o.waits[1:]:
                            engine.wait_ge(sems[name], val)
                        ins = o.fn(engine)
                        if o.waits:
                            ins._wait_ge(sems[o.waits[0][0]], o.waits[0][1])
                        if o.seq is not None:
                            name = o.seq[0]
                            ins.then_inc(sems[name], 16 if name.startswith("d_") else 1)
                    if engname == "sp":
                        for name, val in self.finals.items():
                            if name.startswith("d_"):
                                engine.wait_ge(sems[name], val)
                return body

            block.tensor(run("pe"))
            block.scalar(run("act"))
            block.vector(run("dve"))
            block.gpsimd(run("pool"))
            block.sync(run("sp"))


def f_mm(out, lhsT, rhs, start, stop, skip=False):
    return lambda e: e.matmul(out, lhsT=lhsT, rhs=rhs, start=start, stop=stop,
                              skip_group_check=skip)


def f_tr(out, in_, ident):
    return lambda e: e.transpose(out=out, in_=in_, identity=ident)


def f_act(out, in_, func, bias=None, scale=1.0):
    if bias is None:
        return lambda e: e.activation(out=out, in_=in_, func=func, scale=scale)
    return lambda e: e.activation(out=out, in_=in_, func=func, bias=bias, scale=scale)


def f_tt(out, in0, in1, op):
    return lambda e: e.tensor_tensor(out=out, in0=in0, in1=in1, op=op)


def f_ts(out, in0, s1, s2, op0, op1=None):
    if op1 is None:
        return lambda e: e.tensor_scalar(out=out, in0=in0, scalar1=s1, scalar2=None, op0=op0)
    return lambda e: e.tensor_scalar(out=out, in0=in0, scalar1=s1, scalar2=s2, op0=op0, op1=op1)


def f_stt(out, in0, scalar, in1, op0, op1):
    return lambda e: e.scalar_tensor_tensor(out=out, in0=in0, scalar=scalar, in1=in1,
                                            op0=op0, op1=op1)


def f_copy(out, in_):
    return lambda e: e.tensor_copy(out=out, in_=in_)


def f_dma(out, in_):
    return lambda e: e.dma_start(out=out, in_=in_)


def f_memset(ap, v):
    return lambda e: e.memset(ap, v)


def _san(key):
    return "".join(ch for ch in str(key) if ch.isalnum())


class Ctx:
    def __init__(self, nc, st):
        self.nc = nc
        self.st = st
        self.P = Prog(nc)
        self.psum = st.enter_context(nc.psum_tensor("psum_all", [128, 8, 512], F32))
        self.bank = [self.psum[:, i, :] for i in range(8)]
        self.uid = 0
        self.side = "left"
        self.root = st

    def sb(self, name, shape, dt):
        return self.st.enter_context(self.nc.sbuf_tensor("s_" + name, shape, dt, side=self.side))

    def bank_bf(self, i):
        return self.bank[i].bitcast(BF16)


def load_weight_bf16(C, wt, w_ap, key, ncols, col0=0, row0=0, nchunks=8):
    P = C.P
    step = 1024
    for c in range(nchunks):
        for n0 in range(0, ncols, step):
            n1 = min(ncols, n0 + step)
            P.op("pool", f_dma(wt[:, c, n0:n1],
                               w_ap[row0 + c * 128: row0 + (c + 1) * 128, col0 + n0: col0 + n1]),
                 writes=[key], dsem=_san(key))


def make_hT_from_dram(C, x_ap, hT, ident, xt, xb, tag):
    P = C.P
    for t in range(NT):
        b = t % 2
        P.op("sp", f_dma(xt[b][:], x_ap[t * 128:(t + 1) * 128, :]),
             writes=[(tag, "xt", b)], dsem=f"{tag}xt{b}")
        P.op("act", f_act(xb[b][:], xt[b][:], AF.Copy), reads=[(tag, "xt", b)],
             writes=[(tag, "xb", b)])
        transpose_tile(C, xb[b], (tag, "xb", b), hT, t, ident, b)


def transpose_tile(C, src_bf, src_key, hT, t, ident, b):
    P = C.P
    pT = C.bank_bf(b).rearrange("p (c n) -> p c n", n=128)
    for c in range(8):
        P.op("pe", f_tr(pT[:, c, :], src_bf[:, c * 128:(c + 1) * 128], ident[:]),
             reads=[src_key, "ident"], writes=[("bank", b)])
    P.op("dve", f_copy(hT[:, :, t * 128:(t + 1) * 128], pT), reads=[("bank", b)],
         writes=[("hT", t)])


def layer_norm_tile(C, src, src_key, dst, dst_key, g_rep, b_rep, gb_key, scr):
    P = C.P
    u = C.uid
    C.uid += 1
    k = u % 2
    stats, mv, rstd, tmp = scr["stats"][k], scr["mv"][k], scr["rstd"][k], scr["tmp"][k]
    kk = ("ln", k)
    kt = ("ln", "tmp")
    P.op("dve", lambda e: e.bn_stats(out=stats[:, 0, :], in_=src[:, 0:512]),
         reads=[src_key], writes=[kk + ("st",)])
    P.op("dve", lambda e: e.bn_stats(out=stats[:, 1, :], in_=src[:, 512:1024]),
         reads=[src_key], writes=[kk + ("st",)])
    P.op("dve", lambda e: e.bn_aggr(out=mv[:], in_=stats[:].rearrange("p a s -> p (a s)")),
         reads=[kk + ("st",)], writes=[kk + ("mv",)])
    P.op("dve", f_ts(rstd[:], mv[:, 1:2], LN_EPS, None, ALU.add),
         reads=[kk + ("mv",)], writes=[kk + ("rs0",)])
    P.op("pool", f_tt(rstd[:], rstd[:], scr["mhalf"][:, 0:1], ALU.pow),
         reads=[kk + ("rs0",), "mhalf"], writes=[kk + ("rs0",)])
    P.op("dve", f_ts(tmp[:], src, mv[:, 0:1], rstd[:, 0:1], ALU.subtract, ALU.mult),
         reads=[src_key, kk + ("mv",), kk + ("rs0",)], writes=[kt])
    P.op("pool", f_tt(tmp[:], tmp[:], g_rep[:], ALU.mult),
         reads=[kt, gb_key], writes=[kt])
    P.op("pool", f_tt(dst, tmp[:], b_rep[:], ALU.add),
         reads=[kt, gb_key], writes=[dst_key])


def ln_scratch(C):
    scr = {
        "stats": [C.sb(f"ln_stats{k}", [128, 2, 6], F32) for k in range(2)],
        "mv": [C.sb(f"ln_mv{k}", [128, 2], F32) for k in range(2)],
        "rstd": [C.sb(f"ln_rstd{k}", [128, 1], F32) for k in range(2)],
        "tmp": [C.sb("ln_tmp", [128, 1024], F32)] * 2,
        "mhalf": C.sb("ln_mhalf", [128, 1], F32),
    }
    C.P.op("dve", f_memset(scr["mhalf"][:], -0.5), writes=["mhalf"])
    return scr


def h_to_hT(C, hs, t, hT, ident, hb):
    P = C.P
    b = t % 2
    P.op("act", f_act(hb[b][:], hs[:, t, :], AF.Copy), reads=[("hs", t)], writes=[("hb", b)])
    transpose_tile(C, hb[b], ("hb", b), hT, t, ident, b)


def mlp(C, hs, hT, w_up_ap, w_down_ap, tagp):
    P = C.P
    NS = 8
    wu = [C.sb(f"{tagp}wu{i}", [128, 8, 512], BF16) for i in range(2)]
    wd = [C.sb(f"{tagp}wd{i}", [128, 4, 1024], BF16) for i in range(2)]
    uT1 = C.sb(f"{tagp}uT", [128, 4, TL], BF16)
    uT = [uT1, uT1]
    rl = [C.sb(f"{tagp}rl{i}", [128, 512], F32) for i in range(2)]
    cnt = 0
    dcnt = 0

    def wload(s):
        b = s % 2
        load_weight_bf16(C, wu[b], w_up_ap, (tagp, "wu", b), 512, col0=s * 512)
        for fc in range(4):
            P.op("pool", f_dma(wd[b][:, fc, :],
                               w_down_ap[s * 512 + fc * 128: s * 512 + (fc + 1) * 128, :]),
                 writes=[(tagp, "wd", b)], dsem=f"{tagp}wd{b}")

    wload(0)
    for s in range(NS):
        b = s % 2
        for g in range(4):
            for fc in range(4):
                bk = 2 + (cnt % 2)
                ps = C.bank[bk]
                for c in range(8):
                    P.op("pe", f_mm(ps[:], wu[b][:, c, fc * 128:(fc + 1) * 128],
                                    hT[:, c, g * 512:(g + 1) * 512], c == 0, c == 7),
                         reads=[(tagp, "wu", b)] + [("hT", 4 * g + a) for a in range(4)],
                         writes=[("bank", bk)])
                r = rl[cnt % 2]
                P.op("act", f_act(r[:], ps[:], AF.Relu), reads=[("bank", bk)],
                     writes=[(tagp, "rl", cnt % 2)])
                P.op("pool", f_tt(uT[b][:, fc, g * 512:(g + 1) * 512], r[:], r[:], ALU.mult),
                     reads=[(tagp, "rl", cnt % 2)], writes=[(tagp, "uT", g)])
                cnt += 1
        if s + 1 < NS:
            wload(s + 1)
        for t in range(NT):
            for half in range(2):
                bk = 4 + (dcnt % 4)
                ps = C.bank[bk]
                for fc in range(4):
                    P.op("pe", f_mm(ps[:], uT[b][:, fc, t * 128:(t + 1) * 128],
                                    wd[b][:, fc, half * 512:(half + 1) * 512], fc == 0, fc == 3),
                         reads=[(tagp, "uT", t // 4), (tagp, "wd", b)], writes=[("bank", bk)])
                dst = hs[:, t, half * 512:(half + 1) * 512]
                if s == 0:
                    P.op("dve", f_stt(dst, dst, ALPHA, ps[:], ALU.mult, ALU.add),
                         reads=[("bank", bk), ("hs", t)], writes=[("hs", t)])
                else:
                    P.op("dve", f_tt(dst, dst, ps[:], ALU.add),
                         reads=[("bank", bk), ("hs", t)], writes=[("hs", t)])
                dcnt += 1


def attention(C, cfg, q_d, k_d, v_d, G_d, b31, epilogue):
    P = C.P
    H, nm, KR, dv = cfg["H"], cfg["nm"], cfg["KR"], cfg["dv"]
    dv1 = dv + 1
    per_bank = 512 // dv1 if nm == 1 else 3
    Kb = [C.sb(f"Kb{i}", [128, S], BF16) for i in range(2)]
    Vb = [C.sb(f"Vb{i}", [128, 64, dv1], BF16) for i in range(2)]
    Qb = [C.sb(f"Qb{i}", [128, TL], BF16) for i in range(2)]
    Gb = [C.sb(f"Gb{i}", [128, nm, 640], F32) for i in range(2)]
    nslots = 3 if nm == 1 else 2
    depth = nslots - 1
    nPT = depth + 2
    acc0 = 2 * nslots
    nacc_banks = -(-(4 * nm) // per_bank)
    PT = [C.sb(f"PT{s_}", [128, 2, 512], BF16) for s_ in range(nPT)]
    accsb = [C.sb(f"accsb{i}", [128, nacc_banks, 512], F32) for i in range(2)]

    def rows(m):
        return (64 * m, 64 * m + 64) if nm == 2 else (0, KR)

    def mapidx(h, m):
        return 2 * h + m if nm == 2 else h

    def acc_bank(n):
        return acc0 + n // per_bank

    def acc_ap(n):
        off = (n % per_bank) * dv1
        return C.bank[acc_bank(n)][:, off:off + dv1]

    def acc_key(n):
        return ("bank", acc_bank(n))

    def load(h):
        p = h % 2
        m0 = mapidx(h, 0)
        P.op("sp", f_dma(Gb[p][:], G_d[:, m0:m0 + nm, :]), writes=[("G", p)], dsem=f"G{p}")
        P.op("sp", f_dma(Qb[p][0:KR, :], q_d[h]), writes=[("Q", p)], dsem=f"Q{p}")
        for r in range(4):
            P.op("sp", f_dma(Kb[p][0:KR, r * TL:(r + 1) * TL], k_d[r, h]),
                 writes=[("K", p)], dsem=f"K{p}")
        for r in range(4):
            P.op("sp", f_dma(Vb[p][:, r * 16:(r + 1) * 16, :], v_d[r, h]),
                 writes=[("V", p)], dsem=f"V{p}")

    def fold_bias(h):
        p = h % 2
        for m in range(nm):
            mi = mapidx(h, m)
            P.op("dve", f_ts(Gb[p][:, m, :], Gb[p][:, m, :], b31[:, mi:mi + 1], None, ALU.subtract),
                 reads=[("G", p), "b31"], writes=[("G", p)])

    def amin_of(u, c):
        return max(0, -((-(c - 16 * u - 3)) // 4))

    def afar_of(u, c):
        return min(4, max(0, -((-(c - 16 * u + 2)) // 4)))

    def kslot(c):
        return (c % 4) * 16 + c // 4

    items = []
    for h in range(H):
        for u in range(4):
            if nm == 2:
                for c in range(16 * u + 16):
                    items.append((h, u, ((c, 0), (c, 1))))
            else:
                for cp in range(8 * u + 8):
                    items.append((h, u, ((2 * cp, 0), (2 * cp + 1, 0))))

    def emit_qk_exp(i):
        h, u, subs = items[i]
        if i == 0 or items[i - 1][0] != h:
            fold_bias(h)
        p = h % 2
        slot = i % nslots
        psl = i % nPT
        bks = (2 * slot, 2 * slot + 1)
        amins = [amin_of(u, c) for c, m in subs]
        afars = [afar_of(u, c) for c, m in subs]
        for j, (c, m) in enumerate(subs):
            r0, r1 = rows(m)
            col = kslot(c) * 128
            amin = amins[j]
            P.op("pe", f_mm(C.bank[bks[j]][:, amin * 128:512], Kb[p][r0:r1, col:col + 128],
                            Qb[p][r0:r1, (4 * u + amin) * 128:(4 * u + 4) * 128], True, True),
                 reads=[("K", p), ("Q", p)], writes=[("bank", bks[j])])
        if nm == 2:
            c = subs[0][0]
            for a in range(amins[0], afars[0]):
                s_ = c - (16 * u + 4 * a) + 1
                P.op("dve", f_tt(C.psum[:, bks[0]:bks[0] + 2, a * 128:(a + 1) * 128],
                                 C.psum[:, bks[0]:bks[0] + 2, a * 128:(a + 1) * 128],
                                 Gb[p][:, :, s_ * 128:(s_ + 1) * 128], ALU.add),
                     reads=[("bank", bks[0]), ("bank", bks[1]), ("G", p)],
                     writes=[("bank", bks[0]), ("bank", bks[1])])
        else:
            for j, (c, m) in enumerate(subs):
                for a in range(amins[j], afars[j]):
                    s_ = c - (16 * u + 4 * a) + 1
                    P.op("dve", f_tt(C.bank[bks[j]][:, a * 128:(a + 1) * 128],
                                     C.bank[bks[j]][:, a * 128:(a + 1) * 128],
                                     Gb[p][:, 0, s_ * 128:(s_ + 1) * 128], ALU.add),
                         reads=[("bank", bks[j]), ("G", p)], writes=[("bank", bks[j])])
        am = min(amins)
        P.op("act", f_act(PT[psl][:, :, am * 128:512], C.psum[:, bks[0]:bks[0] + 2, am * 128:512], AF.Exp),
             reads=[("bank", bks[0]), ("bank", bks[1])],
             writes=[("PT", psl, j, a) for j in range(2) for a in range(am, 4)])

    def emit_pv(i):
        h, u, subs = items[i]
        p = h % 2
        psl = i % nPT
        for j, (c, m) in enumerate(subs):
            amin = amin_of(u, c)
            for a in range(amin, 4):
                n = m * 4 + a
                first = (c == 0) and (n % per_bank == 0)
                last = (c == 16 * u + 4 * a + 3)
                P.op("pe", f_mm(acc_ap(n), PT[psl][:, j, a * 128:(a + 1) * 128],
                                Vb[p][:, kslot(c), :], first, last, skip=True),
                     reads=[("PT", psl, j, a), ("V", p)], writes=[acc_key(n)])

    gcnt = [0]

    def evacuate_and_epilogue(h, u):
        k = gcnt[0] % 2
        gcnt[0] += 1
        for bi in range(nacc_banks):
            nin = min(per_bank, 4 * nm - bi * per_bank)
            P.op("dve", f_copy(accsb[k][:, bi, 0:nin * dv1], C.bank[acc0 + bi][:, 0:nin * dv1]),
                 reads=[("bank", acc0 + bi)], writes=[("accsb", k, bi)])

        def sb_ap(n):
            off = (n % per_bank) * dv1
            return accsb[k][:, n // per_bank, off:off + dv1]

        def sb_key(n):
            return ("accsb", k, n // per_bank)
        epilogue(h, u, sb_ap, sb_key)

    load(0)
    if H > 1:
        load(1)
    for i0 in range(min(depth, len(items))):
        emit_qk_exp(i0)
    for i in range(len(items)):
        h, u, subs = items[i]
        nxt = items[i + 1] if i + 1 < len(items) else None
        if i + depth < len(items):
            emit_qk_exp(i + depth)
        emit_pv(i)
        if subs[1][0] == 16 * u + 15:
            evacuate_and_epilogue(h, u)
        if nxt is not None and nxt[0] != h and h + 2 < H:
            load(h + 2)


def dram_in(nc, name, shape, dt):
    return nc.dram_tensor(name, list(shape), dt, kind="ExternalInput").ap()


def dram_out(nc, name, shape, dt):
    return nc.dram_tensor(name, list(shape), dt, kind="ExternalOutput").ap()


def fm_project(C, w, wkey, col0, hT, scale, tag, store):
    P = C.P
    stg = [C.sb(f"{tag}stg{i}", [128, 8, 512], BF16) for i in range(2)]
    cnt = 0
    for g in range(4):
        b = g % 2
        for f in range(8):
            bk = 2 + (cnt % 2)
            cnt += 1
            ps = C.bank[bk]
            for c in range(8):
                P.op("pe", f_mm(ps[:], w[:, c, col0 + f * 128: col0 + (f + 1) * 128],
                                hT[:, c, g * 512:(g + 1) * 512], c == 0, c == 7),
                     reads=[wkey] + [("hT", 4 * g + a) for a in range(4)], writes=[("bank", bk)])
            P.op("act", f_act(stg[b][:, f, :], ps[:], AF.Copy, scale=scale),
                 reads=[("bank", bk)], writes=[(tag, "stg", b)])
        store(g, stg[b], (tag, "stg", b), b)
    return stg


def v_project(C, wb, wkey, col0, hT, out_d, nh, dv, tag):
    P = C.P
    dv1 = dv + 1
    stg = [C.sb(f"{tag}vstg{i}", [128, nh, 4, dv1], BF16) for i in range(2)]
    for i in range(2):
        P.op("pool", f_memset(stg[i][:], 1.0), writes=[(tag, "vstg", i)])
    hpb = 512 // dv
    nhalf = nh * dv // 512
    cnt = 0
    for g in range(4):
        b = g % 2
        for a in range(4):
            t = 4 * g + a
            for half in range(nhalf):
                bk = 4 + (cnt % 2)
                cnt += 1
                ps = C.bank[bk]
                for c in range(8):
                    P.op("pe", f_mm(ps[:], hT[:, c, t * 128:(t + 1) * 128],
                                    wb[:, c, col0 + half * 512: col0 + (half + 1) * 512],
                                    c == 0, c == 7),
                         reads=[wkey, ("hT", t)], writes=[("bank", bk)])
                P.op("dve", f_copy(stg[b][:, half * hpb:(half + 1) * hpb, a, 0:dv],
                                   ps[:].rearrange("p (h e) -> p h e", e=dv)),
                     reads=[("bank", bk)], writes=[(tag, "vstg", b)])
        P.op("sp", f_dma(out_d.rearrange("h p t e -> p h (t e)")[:, :, 4 * g * dv1:(4 * g + 4) * dv1],
                         stg[b][:].rearrange("p h a e -> p h (a e)")),
             reads=[(tag, "vstg", b)], writes=[(tag, "dram")], dsem=f"{tag}vst{b}")


def out_proj_ln(C, ostore, wo, res_ap, hs, hT, ident, g_rep, b_rep, gb_key, scr, xt, hb, oT):
    P = C.P
    for t in range(NT):
        b = t % 2
        pT = C.bank_bf(b).rearrange("p (c n) -> p c n", n=128)
        for c in range(8):
            P.op("pe", f_tr(pT[:, c, :], ostore[:, t, c * 128:(c + 1) * 128], ident[:]),
                 reads=[("os", t), "ident"], writes=[("bank", b)])
        P.op("dve", f_copy(oT[b][:], pT), reads=[("bank", b)], writes=[("oT", b)])
        if res_ap is not None:
            P.op("sp", f_dma(xt[:], res_ap[t * 128:(t + 1) * 128, :]),
                 writes=[("B", "xt")], dsem="Bxt")
        for half in range(2):
            bk = 2 + half
            ps = C.bank[bk]
            for c in range(8):
                P.op("pe", f_mm(ps[:], oT[b][:, c, :], wo[:, c, half * 512:(half + 1) * 512],
                                c == 0, c == 7),
                     reads=[("oT", b), "wo"], writes=[("bank", bk)])
            dst = hs[:, t, half * 512:(half + 1) * 512]
            if res_ap is None:
                P.op("dve", f_stt(dst, dst, ALPHA, ps[:], ALU.mult, ALU.add),
                     reads=[("bank", bk), ("hs", t)], writes=[("hs", t)])
            else:
                P.op("dve", f_stt(dst, xt[:, half * 512:(half + 1) * 512], ALPHA, ps[:],
                                  ALU.mult, ALU.add),
                     reads=[("bank", bk), ("B", "xt")], writes=[("hs", t)])
        layer_norm_tile(C, hs[:, t, :], ("hs", t), hs[:, t, :], ("hs", t), g_rep, b_rep, gb_key, scr)
        h_to_hT(C, hs, t, hT, ident, hb)


def load_rep(C, tile, ap, key):
    C.P.op("sp", f_dma(tile[:], ap), writes=[key], dsem=_san(key))


class Scope:
    def __init__(self, C):
        self.C = C

    def __enter__(self):
        self.st = ExitStack()
        self.prev = (self.C.st, self.C.side)
        self.C.st, self.C.side = self.st, "right"
        return self

    def __exit__(self, *a):
        self.C.P.barrier()
        self.st.close()
        self.C.st, self.C.side = self.prev
        return False


def build_A():
    nc = bass.Bass("TRN2", target_bir_lowering=False)
    x = dram_in(nc, "x", [TL, D], F32)
    w_in = dram_in(nc, "w_in", [D, 3 * D], F32)
    ident_d = dram_in(nc, "ident", [128, 128], BF16)
    qT = dram_out(nc, "qT", [8, 128, TL], BF16)
    kT = dram_out(nc, "kT", [8, 128, TL], BF16)
    v = dram_out(nc, "v", [8, 128, NT, 129], BF16)
    with ExitStack() as st:
        C = Ctx(nc, st)
        P = C.P
        ident = C.sb("ident", [128, 128], BF16)
        P.op("sp", f_dma(ident[:], ident_d), writes=["ident"], dsem="ident")
        phase_A(C, x, w_in, ident, qT, kT, v)
        P.emit()
    return nc


def phase_A(C, x, w_in, ident, qT, kT, v):
    P = C.P
    with Scope(C):
        wb = C.sb("wb", [128, 8, 3 * D], BF16)
        load_weight_bf16(C, wb, w_in, "wb", 3 * D)
        hT = C.sb("hTA", [128, 8, TL], BF16)
        xt = [C.sb(f"xtA{i}", [128, D], F32) for i in range(2)]
        xb = [C.sb(f"xbA{i}", [128, D], BF16) for i in range(2)]
        make_hT_from_dram(C, x, hT, ident, xt, xb, "A")

        def mkstore(out_d, tag):
            def store(g, stg, key, b):
                P.op("sp", f_dma(out_d.rearrange("h p n -> p h n")[:, :, g * 512:(g + 1) * 512], stg[:]),
                     reads=[key], writes=[(tag, "dram")], dsem=f"{tag}st{b}")
            return store
        fm_project(C, wb, "wb", 0, hT, 0.125, "q", mkstore(qT, "q"))
        fm_project(C, wb, "wb", D, hT, 1.0, "k", mkstore(kT, "k"))
        v_project(C, wb, "wb", 2 * D, hT, v, 8, 128, "v")


def build_B():
    nc = bass.Bass("TRN2", target_bir_lowering=False)
    x = dram_in(nc, "x", [TL, D], F32)
    qT = dram_in(nc, "qT", [8, 128, TL], BF16)
    kT_all = dram_in(nc, "kT_all", [4, 8, 128, TL], BF16)
    v_all = dram_in(nc, "v_all", [4, 8, 128, NT, 129], BF16)
    cst = const_inputs(nc)
    prm = {
        "lam": dram_in(nc, "lam", [128, 256], F32),
        "subg": dram_in(nc, "subg", [128, 128], F32),
        "w_out0": dram_in(nc, "w_out0", [D, D], F32),
        "w_up0": dram_in(nc, "w_up0", [D, DFF], F32),
        "w_down0": dram_in(nc, "w_down0", [DFF, D], F32),
        "w_kv": dram_in(nc, "w_kv", [D, 2 * D], F32),
    }
    for k in ("ln1_g0", "ln1_b0", "ln2_g0", "ln2_b0"):
        prm[k] = dram_in(nc, k, [128, D], F32)
    h1 = dram_out(nc, "h1", [TL, D], F32)
    kT1 = dram_out(nc, "kT1", [16, 96, TL], BF16)
    v1 = dram_out(nc, "v1", [16, 128, NT, 65], BF16)
    with ExitStack() as st:
        C = Ctx(nc, st)
        P = C.P
        cs = load_consts(C, cst)
        hs, hT, hT_stack = phase_B(C, cs, cst, prm, x, qT, kT_all, v_all, kT1, v1)
        st.callback(hT_stack.close)
        for t in range(NT):
            P.op("sp", f_dma(h1[t * 128:(t + 1) * 128, :], hs[:, t, :]), reads=[("hs", t)],
                 dsem=f"h1st{t % 2}")
        P.emit()
    return nc


def const_inputs(nc):
    return {
        "G": dram_in(nc, "G", [128, 16, 640], F32),
        "b31": dram_in(nc, "b31", [128, 16], F32),
        "ident": dram_in(nc, "ident", [128, 128], BF16),
        "e_own": dram_in(nc, "e_own", [32, TL], BF16),
        "past30k": dram_in(nc, "past30k", [128, NT, 32], F32),
    }


def load_consts(C, cst):
    P = C.P
    ident = C.sb("ident", [128, 128], BF16)
    P.op("sp", f_dma(ident[:], cst["ident"]), writes=["ident"], dsem="ident")
    b31 = C.sb("b31", [128, 16], F32)
    P.op("sp", f_dma(b31[:], cst["b31"]), writes=["b31"], dsem="b31")
    scr = ln_scratch(C)
    hb = [C.sb(f"hb{i}", [128, D], BF16) for i in range(2)]
    rep = [C.sb(f"rep{i}", [128, D], F32) for i in range(2)]
    kmT = C.sb("kmT", [128, 8, 32], BF16)
    return {"ident": ident, "b31": b31, "scr": scr, "hb": hb, "rep": rep, "kmT": kmT}


def alloc_hT(C):
    stk = ExitStack()
    hT = stk.enter_context(C.nc.sbuf_tensor("s_hT%d" % C.uid, [128, 8, TL], BF16, side="left"))
    C.uid += 1
    return hT, stk


def phase_B(C, cs, cst, prm, x, qT, kT_all, v_all, kT1, v1):
    P = C.P
    ident, b31, scr, hb, rep = cs["ident"], cs["b31"], cs["scr"], cs["hb"], cs["rep"]
    with Scope(C):
        ostore = C.sb("ostore", [128, NT, D], BF16)
        lam_sb = C.sb("lam_sb", [128, 256], F32)
        lprod = C.sb("lprod", [128, 2, 64], F32)
        lsum = C.sb("lsum", [128, 2], F32)
        lexp = C.sb("lexp", [128, 2], F32)
        nl = C.sb("nl", [128, 2], F32)
        gsub = C.sb("gsub", [128, 128], F32)
        P.op("sp", f_dma(lam_sb[:], prm["lam"]), writes=["lam_sb"], dsem="lam")
        P.op("sp", f_dma(gsub[:], prm["subg"]), writes=["gsub"], dsem="gsub")
        lv = lam_sb[:].rearrange("p (a b d) -> p a b d", a=2, b=2)
        P.op("dve", f_tt(lprod[:], lv[:, :, 0, :], lv[:, :, 1, :], ALU.mult),
             reads=["lam_sb"], writes=["lprod"])
        P.op("dve", lambda e: e.reduce_sum(out=lsum[:], in_=lprod[:], axis=AX.X),
             reads=["lprod"], writes=["lsum"])
        P.op("act", f_act(lexp[:], lsum[:], AF.Exp), reads=["lsum"], writes=["lexp"])
        P.op("dve", f_tt(nl[:, 0:1], lexp[:, 1:2], lexp[:, 0:1], ALU.subtract),
             reads=["lexp"], writes=["nl0"])
        P.op("dve", f_ts(nl[:, 1:2], nl[:, 0:1], -LAM_INIT0, None, ALU.add),
             reads=["nl0"], writes=["neglam"])
        neglam = nl[:, 1:2]
        P.op("dve", f_ts(gsub[:], gsub[:], 1.0 - LAM_INIT0, None, ALU.mult),
             reads=["gsub"], writes=["gsub"])
        ep = {k: [C.sb(f"ep_{k}{i}", shp, F32) for i in range(2)]
              for k, shp in (("rl", [128, 2]), ("c1", [128, 1]), ("t1", [128, 128]),
                             ("o", [128, 128]), ("sq", [128, 128]), ("ss", [128, 1]))}
        ecnt = [0]

        def epilogue(h, u, acc_ap, acc_key):
            for a in range(4):
                t = 4 * u + a
                k = ecnt[0] % 2
                ecnt[0] += 1
                rl, c1, t1, o, sq, ss = (ep[x_][k] for x_ in ("rl", "c1", "t1", "o", "sq", "ss"))
                a0, a1 = acc_ap(a), acc_ap(4 + a)
                k0, k1 = acc_key(a), acc_key(4 + a)
                P.op("dve", lambda e, rl=rl, a0=a0: e.reciprocal(out=rl[:, 0:1], in_=a0[:, 128:129]),
                     reads=[k0], writes=[("ep", k, "rl0")])
                P.op("dve", lambda e, rl=rl, a1=a1: e.reciprocal(out=rl[:, 1:2], in_=a1[:, 128:129]),
                     reads=[k1], writes=[("ep", k, "rl1")])
                P.op("dve", f_tt(c1[:], rl[:, 1:2], neglam, ALU.mult),
                     reads=[("ep", k, "rl1"), "neglam"], writes=[("ep", k, "c1")])
                P.op("dve", f_ts(t1[:], a1[:, 0:128], c1[:, 0:1], None, ALU.mult),
                     reads=[k1, ("ep", k, "c1")], writes=[("ep", k, "t1")])
                P.op("dve", f_stt(o[:], a0[:, 0:128], rl[:, 0:1], t1[:], ALU.mult, ALU.add),
                     reads=[k0, ("ep", k, "rl0"), ("ep", k, "t1")], writes=[("ep", k, "o")])
                P.op("pool", f_tt(sq[:], o[:], o[:], ALU.mult), reads=[("ep", k, "o")],
                     writes=[("ep", k, "sq")])
                P.op("dve", lambda e, ss=ss, sq=sq: e.reduce_sum(out=ss[:], in_=sq[:], axis=AX.X),
                     reads=[("ep", k, "sq")], writes=[("ep", k, "ss")])
                P.op("dve", f_ts(ss[:], ss[:], 1.0 / 128.0, LN_EPS, ALU.mult, ALU.add),
                     reads=[("ep", k, "ss")], writes=[("ep", k, "ss")])
                P.op("pool", f_tt(ss[:], ss[:], scr["mhalf"][:, 0:1], ALU.pow),
                     reads=[("ep", k, "ss"), "mhalf"], writes=[("ep", k, "ss")])
                P.op("dve", f_stt(ostore[:, t, h * 128:(h + 1) * 128], o[:], ss[:, 0:1], gsub[:],
                                  ALU.mult, ALU.mult),
                     reads=[("ep", k, "o"), ("ep", k, "ss"), "gsub"], writes=[("os", t)])

        with Scope(C):
            attention(C, dict(H=8, nm=2, KR=128, dv=128), qT, kT_all, v_all, cst["G"], b31, epilogue)
        C.side = "left"
        hs = C.nc.sbuf_tensor("s_hs", [128, NT, D], F32, side="left")
        hs = C.root.enter_context(hs)
        hT, hT_stack = alloc_hT(C)
        C.side = "right"
        with Scope(C):
            wo = C.sb("wo", [128, 8, D], BF16)
            load_weight_bf16(C, wo, prm["w_out0"], "wo", D)
            xt = C.sb("xtB", [128, D], F32)
            oT = [C.sb(f"oT{i}", [128, 8, 128], BF16) for i in range(2)]
            load_rep(C, rep[0], prm["ln1_g0"], "rep0")
            load_rep(C, rep[1], prm["ln1_b0"], "rep1")
            out_proj_ln(C, ostore, wo, x, hs, hT, ident, rep[0], rep[1], "rep0", scr, xt, hb, oT)
    with Scope(C):
        mlp(C, hs, hT, prm["w_up0"], prm["w_down0"], "m0")
    load_rep(C, rep[0], prm["ln2_g0"], "rep0")
    load_rep(C, rep[1], prm["ln2_b0"], "rep1")
    for t in range(NT):
        layer_norm_tile(C, hs[:, t, :], ("hs", t), hs[:, t, :], ("hs", t), rep[0], rep[1], "rep0", scr)
        h_to_hT(C, hs, t, hT, ident, hb)
    with Scope(C):
        wkv = C.sb("wk1", [128, 8, D], BF16)
        load_weight_bf16(C, wkv, prm["w_kv"], "wkv", D)

        def store(g, stg, key, b):
            kv = kT1.rearrange("(f two) r n -> two r f n", two=2)
            for hh in range(2):
                P.op("sp", f_dma(kv[hh, 0:64, :, g * 512:(g + 1) * 512], stg[hh * 64:(hh + 1) * 64, :, :]),
                     reads=[key], writes=[("k1dram", hh)], dsem=f"k1st{b}")
        fm_project(C, wkv, "wkv", 0, hT, 1.0, "k1", store)
        for h in range(16):
            P.op("sp", f_dma(kT1[h, 64:96, :], cst["e_own"]), writes=[("k1dramE",)], dsem="k1e")
    with Scope(C):
        wv1 = C.sb("wv1", [128, 8, D], BF16)
        load_weight_bf16(C, wv1, prm["w_kv"], "wv1", D, col0=D)
        v_project(C, wv1, "wv1", 0, hT, v1, 16, 64, "v1")
    return hs, hT, hT_stack


def build_C(stop_after=None):
    nc = bass.Bass("TRN2", target_bir_lowering=False)
    h1 = dram_in(nc, "h1", [TL, D], F32)
    kT1_all = dram_in(nc, "kT1_all", [4, 16, 96, TL], BF16)
    v1_all = dram_in(nc, "v1_all", [4, 16, 128, NT, 65], BF16)
    cst = const_inputs(nc)
    prm = {
        "w_q": dram_in(nc, "w_q", [D, D], F32),
        "w_out1": dram_in(nc, "w_out1", [D, D], F32),
        "w_up1": dram_in(nc, "w_up1", [D, DFF], F32),
        "w_down1": dram_in(nc, "w_down1", [DFF, D], F32),
    }
    for k in ("ln1_g1", "ln1_b1", "ln2_g1", "ln2_b1"):
        prm[k] = dram_in(nc, k, [128, D], F32)
    out = dram_out(nc, "out", [TL, D], F32)
    qaug = nc.dram_tensor("qaug", [16, 96, TL], BF16,
                          kind="ExternalOutput" if stop_after else "Internal").ap()
    with ExitStack() as st:
        C = Ctx(nc, st)
        C.stop_after = stop_after
        P = C.P
        cs = load_consts(C, cst)
        hs = C.sb("hs", [128, NT, D], F32)
        hT, hT_stack = alloc_hT(C)
        for t in range(NT):
            P.op("sp", f_dma(hs[:, t, :], h1[t * 128:(t + 1) * 128, :]), writes=[("hs", t)],
                 dsem=f"hsld{t}")
            h_to_hT(C, hs, t, hT, cs["ident"], cs["hb"])
        phase_C(C, cs, cst, prm, kT1_all, v1_all, qaug, hs, hT, hT_stack, out)
        P.emit()
    return nc


def phase_C(C, cs, cst, prm, kT1_all, v1_all, qaug, hs, hT, hT_stack, out):
    P = C.P
    ident, b31, scr, hb, rep = cs["ident"], cs["b31"], cs["scr"], cs["hb"], cs["rep"]
    kmT = cs["kmT"]
    with Scope(C):
        moba_kmean(C, kT1_all, kmT)
    if getattr(C, "stop_after", None) == "kmean":
        P.op("sp", f_dma(qaug.rearrange("h r n -> r h n")[0:64, 0:8, 0:32], kmT[0:64, :, :]),
             reads=[("kmT", f) for f in range(8)], dsem="dbg1")
        P.op("sp", f_dma(qaug.rearrange("h r n -> r h n")[0:64, 8:16, 0:32], kmT[64:128, :, :]),
             reads=[("kmT", f) for f in range(8)], dsem="dbg2")
        for t in range(NT):
            P.op("sp", f_dma(out[t * 128:(t + 1) * 128, :], hs[:, t, :]), reads=[("hs", t)],
                 dsem=f"outst{t % 2}")
        hT_stack.close()
        return
    with Scope(C):
        moba_gate_qproj(C, hT, prm["w_q"], kmT, cst["past30k"], qaug, ident)
    hT_stack.close()
    hT = None
    if getattr(C, "stop_after", None) == "gate":
        for t in range(NT):
            P.op("sp", f_dma(out[t * 128:(t + 1) * 128, :], hs[:, t, :]), reads=[("hs", t)],
                 dsem=f"outst{t % 2}")
        return
    with Scope(C):
        ostore = C.sb("ostore1", [128, NT, D], BF16)
        eprl = [C.sb(f"eprl{i}", [128, 1], F32) for i in range(8)]

        def epilogue(h, u, acc_ap, acc_key):
            for a in range(4):
                t = 4 * u + a
                k = (h * 4 + u) % 2
                rl = eprl[4 * k + a]
                acc = acc_ap(a)
                P.op("dve", lambda e, rl=rl, acc=acc: e.reciprocal(out=rl[:], in_=acc[:, 64:65]),
                     reads=[acc_key(a)], writes=[("eprl", 4 * k + a)])
                P.op("dve", f_ts(ostore[:, t, h * 64:(h + 1) * 64], acc[:, 0:64], rl[:, 0:1], None,
                                 ALU.mult),
                     reads=[acc_key(a), ("eprl", 4 * k + a)], writes=[("os", t)])

        with Scope(C):
            attention(C, dict(H=16, nm=1, KR=96, dv=64), qaug, kT1_all, v1_all, cst["G"], b31, epilogue)
        hT, hT_stack2 = alloc_hT(C)
        C.root.callback(hT_stack2.close)
        with Scope(C):
            wo = C.sb("wo1", [128, 8, D], BF16)
            load_weight_bf16(C, wo, prm["w_out1"], "wo", D)
            oT = [C.sb(f"oT1{i}", [128, 8, 128], BF16) for i in range(2)]
            load_rep(C, rep[0], prm["ln1_g1"], "rep0")
            load_rep(C, rep[1], prm["ln1_b1"], "rep1")
            out_proj_ln(C, ostore, wo, None, hs, hT, ident, rep[0], rep[1], "rep0", scr, None, hb, oT)
    with Scope(C):
        mlp(C, hs, hT, prm["w_up1"], prm["w_down1"], "m1")
    load_rep(C, rep[0], prm["ln2_g1"], "rep0")
    load_rep(C, rep[1], prm["ln2_b1"], "rep1")
    for t in range(NT):
        layer_norm_tile(C, hs[:, t, :], ("hs", t), hs[:, t, :], ("hs", t), rep[0], rep[1], "rep0", scr)
        P.op("sp", f_dma(out[t * 128:(t + 1) * 128, :], hs[:, t, :]), reads=[("hs", t)],
             dsem=f"outst{t % 2}")


def moba_kmean(C, kT1_all, kmT):
    P = C.P
    Kp = [C.sb(f"Kp{i}", [128, S], BF16) for i in range(2)]
    ksum = [C.sb(f"ksum{i}", [128, 64], F32) for i in range(2)]
    kmf = [C.sb(f"kmf{i}", [128, 16, 2], F32) for i in range(2)]
    for f in range(8):
        b = f % 2
        for r in range(4):
            for hh in range(2):
                P.op("sp", f_dma(Kp[b][hh * 64:(hh + 1) * 64, r * TL:(r + 1) * TL],
                                 kT1_all[r, 2 * f + hh, 0:64, :]),
                     writes=[("Kp", b)], dsem=f"Kp{b}")
        P.op("dve", lambda e, b=b: e.reduce_sum(out=ksum[b][:],
                                                in_=Kp[b][:].rearrange("p (s q) -> p s q", q=128),
                                                axis=AX.X),
             reads=[("Kp", b)], writes=[("ksum", b)])
        kv = ksum[b][:].rearrange("p (r t) -> p r t", t=16)
        for par in range(2):
            P.op("dve", f_tt(kmf[b][:, :, par], kv[:, 2 * par, :], kv[:, 2 * par + 1, :], ALU.add),
                 reads=[("ksum", b)], writes=[("kmf", b, par)])
        P.op("dve", f_ts(kmT[:, f, :], kmf[b][:].rearrange("p t two -> p (t two)"), 1.0 / 256.0,
                         None, ALU.mult),
             reads=[("kmf", b, 0), ("kmf", b, 1)], writes=[("kmT", f)])


def moba_gate_qproj(C, hT, w_q, kmT, past_d, qaug, ident):
    P = C.P
    wq = C.sb("wq", [128, 8, D], BF16)
    load_weight_bf16(C, wq, w_q, "wq", D)
    past = C.sb("past", [128, NT, 32], F32)
    padd = C.sb("padd", [128, NT, 32], F32)
    P.op("sp", f_dma(past[:], past_d), writes=["past"], dsem="past")
    P.op("dve", f_ts(padd[:], past[:], NEG, None, ALU.add), reads=["past"], writes=["padd"])
    mstg = [C.sb(f"mstg{i}", [128, 16, 128], BF16) for i in range(2)]
    gm = [C.sb(f"gm{i}", [128, 16, 32], F32) for i in range(2)]
    top8 = [C.sb(f"top8{i}", [128, 16, 8], F32) for i in range(2)]
    sel = [C.sb(f"sel{i}", [128, 16, 32], F32) for i in range(2)]
    negpad = [C.sb(f"negpad{i}", [128, 16, 96], BF16) for i in range(2)]
    for i in range(2):
        P.op("pool", f_memset(negpad[i][:], 0.0), writes=[("negpad", i)])

    def store(g, stg, key, b):
        qv = qaug.rearrange("(f two) r n -> two r f n", two=2)
        for hh in range(2):
            P.op("sp", f_dma(qv[hh, 0:64, :, g * 512:(g + 1) * 512], stg[hh * 64:(hh + 1) * 64, :, :]),
                 reads=[key], writes=[("qaug", "q")], dsem=f"qst{b}")
        for a in range(4):
            t = 4 * g + a
            k = t % 2
            gbk = (4, 5) if k == 0 else (0, 1)
            for h in range(16):
                r0 = (h % 2) * 64
                gps = C.bank[gbk[h % 2]]
                P.op("pe", f_mm(gps[:, (h // 2) * 32:(h // 2 + 1) * 32],
                                stg[r0:r0 + 64, h // 2, a * 128:(a + 1) * 128],
                                kmT[r0:r0 + 64, h // 2, :], True, True),
                     reads=[key, ("kmT", h // 2)], writes=[("bank", gbk[h % 2])])
            gmv = gm[k][:].rearrange("p (f two) n -> p two f n", two=2)
            for par in range(2):
                P.op("dve", f_tt(gmv[:, par], C.bank[gbk[par]][:, 0:256].rearrange("p (f n) -> p f n", n=32),
                                 padd[:, t, :].unsqueeze(1).broadcast_to([128, 8, 32]), ALU.add),
                     reads=[("bank", gbk[par]), "padd"], writes=[("gm", k)])
            for h in range(16):
                P.op("dve", lambda e, k=k, h=h: e.max(out=top8[k][:, h, :], in_=gm[k][:, h, :]),
                     reads=[("gm", k)], writes=[("top8", k)])
            P.op("dve", f_tt(sel[k][:], gm[k][:], top8[k][:, :, 2:3].broadcast_to([128, 16, 32]),
                             ALU.is_ge),
                 reads=[("gm", k), ("top8", k)], writes=[("sel", k)])
            P.op("dve", f_ts(sel[k][:], sel[k][:], -1.0, None, ALU.add),
                 reads=[("sel", k)], writes=[("sel", k)])
            P.op("dve", f_tt(negpad[k][:, :, 64:96], sel[k][:],
                             past[:, t, :].unsqueeze(1).broadcast_to([128, 16, 32]), ALU.mult),
                 reads=[("sel", k), "past"], writes=[("negpad", k)])
            for hq in range(2):
                bk = 6 + hq
                tp = C.bank_bf(bk).rearrange("p (h n) -> p h n", n=128)
                for h8 in range(8):
                    h = hq * 8 + h8
                    P.op("pe", f_tr(tp[0:96, h8, :], negpad[k][:, h, :], ident[:]),
                         reads=[("negpad", k), "ident"], writes=[("bank", bk)])
                P.op("act", f_act(mstg[k][64:96, hq * 8:(hq + 1) * 8, :], tp[64:96, :, :], AF.Copy),
                     reads=[("bank", bk)], writes=[("mstg", k)])
            P.op("sp", f_dma(qaug.rearrange("h r n -> r h n")[64:96, :, t * 128:(t + 1) * 128],
                             mstg[k][64:96, :, :]),
                 reads=[("mstg", k)], writes=[("qaug", "m")], dsem=f"mst{k}")

    fm_project(C, wq, "wq", 0, hT, 0.125, "qb", store)


def _bucket_np(n):
    n = np.maximum(n, 0)
    nf = np.maximum(n, 16).astype(np.float32)
    large = 16 + (np.log(nf / np.float32(16)) / np.float32(math.log(8.0)) * np.float32(16)).astype(np.int32)
    large = np.minimum(large, 31)
    return np.where(n < 16, n, large)


def _g_index(j):
    k = np.arange(128)[:, None, None]
    q = np.arange(128)[None, None, :]
    e = (np.arange(5) - 1)[None, :, None]
    delta = e - j
    rel = q - k - 128 * delta
    idx = _bucket_np(rel)
    idx = np.where(rel >= 0, idx, 32)
    return np.broadcast_to(idx, (128, 5, 128)).astype(np.int64)


_CACHE = {}


def _get(name, fn):
    if name not in _CACHE:
        _CACHE[name] = fn()
    return _CACHE[name]


def _host_prep(x, rel_bias):
    f32 = np.float32
    bf = ml_dtypes.bfloat16
    table_ext = np.concatenate([rel_bias, np.full((1, 16), NEG, f32)], axis=0)
    per = []
    for c in range(NCORES):
        b, j = c // 4, c % 4
        xs = np.ascontiguousarray(x[b].reshape(64, 128, D)[j::4].reshape(TL, D))
        G = table_ext[_g_index(j)]
        G = np.ascontiguousarray(G.transpose(0, 3, 1, 2).reshape(128, 16, 640))
        blk = (4 * np.arange(NT) + j) // 2
        e = np.zeros((32, NT, 128), f32)
        e[blk, np.arange(NT), :] = 1.0
        past = (np.arange(32)[None, :] < blk[:, None]).astype(f32) * f32(30000.0)
        per.append({
            "x": xs, "G": G, "e_own": e.reshape(32, TL).astype(bf),
            "past30k": np.ascontiguousarray(np.broadcast_to(past[None], (128, NT, 32))),
            "b31": np.ascontiguousarray(np.broadcast_to(rel_bias[31:32, :], (128, 16))),
            "ident": np.eye(128, dtype=f32).astype(bf),
        })
    return per


def _rep(v, n=128):
    v = np.asarray(v, np.float32).reshape(1, -1)
    return np.ascontiguousarray(np.broadcast_to(v, (n, v.shape[1])))


CONST_KEYS = ("G", "b31", "ident", "e_own", "past30k")


def kernel(x, w_in_a, lam_a, subln_a, w_out_a, w_kv_shared, w_q_b, w_out_b, rel_bias,
           ln1_g, ln1_b, ln2_g, ln2_b, w_up, w_down):
    f32 = np.float32
    A = lambda v: np.ascontiguousarray(np.asarray(v, f32))
    x = A(x)
    rel_bias = A(rel_bias)
    cores = list(range(NCORES))
    per = _host_prep(x, rel_bias)
    ncA = _get("A", build_A)
    inA = [{"x": per[c]["x"], "w_in": A(w_in_a[0]), "ident": per[c]["ident"]} for c in cores]
    rA = run_bass_kernel_spmd(ncA, inA, core_ids=cores).results
    ncB = _get("B", build_B)
    prmB = {
        "lam": _rep(np.asarray(lam_a[0], f32).reshape(-1)), "subg": _rep(subln_a[0]),
        "w_out0": A(w_out_a[0]), "w_up0": A(w_up[0]), "w_down0": A(w_down[0]),
        "w_kv": A(w_kv_shared),
        "ln1_g0": _rep(ln1_g[0]), "ln1_b0": _rep(ln1_b[0]),
        "ln2_g0": _rep(ln2_g[0]), "ln2_b0": _rep(ln2_b[0]),
    }
    inB = []
    for c in cores:
        grp = [4 * (c // 4) + r for r in range(4)]
        d = {"x": per[c]["x"], "qT": rA[c]["qT"],
             "kT_all": np.stack([rA[g]["kT"] for g in grp]),
             "v_all": np.stack([rA[g]["v"] for g in grp])}
        d.update({k: per[c][k] for k in CONST_KEYS})
        d.update(prmB)
        inB.append(d)
    rB = run_bass_kernel_spmd(ncB, inB, core_ids=cores).results
    ncC = _get("C", build_C)
    prmC = {
        "w_q": A(w_q_b[0]), "w_out1": A(w_out_b[0]), "w_up1": A(w_up[1]), "w_down1": A(w_down[1]),
        "ln1_g1": _rep(ln1_g[1]), "ln1_b1": _rep(ln1_b[1]),
        "ln2_g1": _rep(ln2_g[1]), "ln2_b1": _rep(ln2_b[1]),
    }
    inC = []
    for c in cores:
        grp = [4 * (c // 4) + r for r in range(4)]
        d = {"h1": rB[c]["h1"],
             "kT1_all": np.stack([rB[g]["kT1"] for g in grp]),
             "v1_all": np.stack([rB[g]["v1"] for g in grp])}
        d.update({k: per[c][k] for k in CONST_KEYS})
        d.update(prmC)
        inC.append(d)
    rC = run_bass_kernel_spmd(ncC, inC, core_ids=cores).results
    out = np.empty((2, 64, 128, D), f32)
    for c in cores:
        b, j = c // 4, c % 4
        out[b, j::4] = np.asarray(rC[c]["out"], f32).reshape(NT, 128, D)
    return out.reshape(2, S, D)
```

```python
import math
from contextlib import ExitStack

import numpy as np
import ml_dtypes
import concourse.bass as bass
import concourse.mybir as mybir
from concourse.bass_utils import run_bass_kernel_spmd

F32 = mybir.dt.float32
BF16 = mybir.dt.bfloat16
AF = mybir.ActivationFunctionType
ALU = mybir.AluOpType
AX = mybir.AxisListType

NCORES = 8
D = 1024
S = 8192
NT = 16
TL = 2048
DFF = 4096
ALPHA = 4.0 ** 0.25
LN_EPS = 1e-5
LAM_INIT0 = 0.8 - 0.6 * math.exp(-0.3 * 0)
NEG = -30000.0

ENGS = ("pe", "act", "dve", "pool", "sp")


class Op:
    __slots__ = ("fn", "deps", "dsem", "seq", "waits")

    def __init__(self, fn, deps, dsem):
        self.fn = fn
        self.deps = deps
        self.dsem = dsem
        self.seq = None
        self.waits = []


class Prog:
    def __init__(self, nc):
        self.nc = nc
        self.ops = {e: [] for e in ENGS}
        self.last_w = {}
        self.readers = {}

    def op(self, eng, fn, reads=(), writes=(), dsem=None):
        deps = set()
        for k in reads:
            w = self.last_w.get(k)
            if w is not None:
                deps.add(w)
        for k in writes:
            w = self.last_w.get(k)
            if w is not None:
                deps.add(w)
            rd = self.readers.get(k)
            if rd:
                for e2, i2 in rd.items():
                    deps.add((e2, i2))
        idx = len(self.ops[eng])
        me = (eng, idx)
        deps.discard(me)
        self.ops[eng].append(Op(fn, deps, dsem))
        for k in reads:
            self.readers.setdefault(k, {})[eng] = idx
        for k in writes:
            self.last_w[k] = me
            self.readers[k] = {}
        return me

    def barrier(self):
        deps = set()
        for e in ENGS:
            ops = self.ops[e]
            last_real = None
            seen_d = set()
            for i in range(len(ops) - 1, -1, -1):
                o = ops[i]
                if o.fn is None:
                    continue
                if o.dsem is not None:
                    if o.dsem not in seen_d:
                        seen_d.add(o.dsem)
                        deps.add((e, i))
                elif last_real is None:
                    last_real = i
                    deps.add((e, i))
        for e in ENGS:
            self.ops[e].append(Op(None, set(deps), None))

    def resolve(self):
        needed = set()
        for e in ENGS:
            for o in self.ops[e]:
                for d in o.deps:
                    if d[0] == e and e == "pe":
                        continue
                    needed.add(d)
        self.sem_names = set()
        cnt = {}
        for e in ENGS:
            for i, o in enumerate(self.ops[e]):
                if o.dsem is not None:
                    name = "d_" + o.dsem
                    cnt[name] = cnt.get(name, 0) + 16
                    o.seq = (name, cnt[name])
                    self.sem_names.add(name)
                elif (e, i) in needed:
                    name = "e_" + e
                    cnt[name] = cnt.get(name, 0) + 1
                    o.seq = (name, cnt[name])
                    self.sem_names.add(name)
        self.finals = dict(cnt)
        for e in ENGS:
            seen = {}
            for o in self.ops[e]:
                req = {}
                for d in o.deps:
                    if d[0] == e and e == "pe":
                        continue
                    name, val = self.ops[d[0]][d[1]].seq
                    if req.get(name, 0) < val:
                        req[name] = val
                for name, val in req.items():
                    if seen.get(name, 0) < val:
                        seen[name] = val
                        o.waits.append((name, val))

    def check(self):
        sem = {}
        pc = {e: 0 for e in ENGS}
        n = {e: len(self.ops[e]) for e in ENGS}
        progress = True
        while progress:
            progress = False
            for e in ENGS:
                while pc[e] < n[e]:
                    o = self.ops[e][pc[e]]
                    if all(sem.get(nm, 0) >= v for nm, v in o.waits):
                        if o.seq is not None:
                            nm = o.seq[0]
                            sem[nm] = sem.get(nm, 0) + (16 if nm.startswith("d_") else 1)
                        pc[e] += 1
                        progress = True
                    else:
                        break
        stuck = {e: (pc[e], n[e]) for e in ENGS if pc[e] < n[e]}
        if stuck:
            msg = []
            for e, (p, m) in stuck.items():
                o = self.ops[e][p]
                msg.append(f"{e}@{p}/{m} waits {[(nm, v, sem.get(nm, 0)) for nm, v in o.waits]}")
            raise RuntimeError("sync deadlock: " + "; ".join(msg))
        return {e: n[e] for e in ENGS}, len(self.sem_names)

    def emit(self):
        nc = self.nc
        self.resolve()
        print("prog ops/sems:", self.check(), flush=True)
        with ExitStack() as st:
            sems = {}
            for name in sorted(self.sem_names):
                sems[name] = st.enter_context(nc.semaphore(name))
            block = st.enter_context(nc.Block())

            def run(engname):
                def body(engine):
                    for o in self.ops[engname]:
                        if o.fn is None:
                            for (name, val) in o.waits:
                                engine.wait_ge(sems[name], val)
                            continue
                        for (name, val) in o.waits[1:]:
                            engine.wait_ge(sems[name], val)
                        ins = o.fn(engine)
                        if o.waits:
                            ins._wait_ge(sems[o.waits[0][0]], o.waits[0][1])
                        if o.seq is not None:
                            name = o.seq[0]
                            ins.then_inc(sems[name], 16 if name.startswith("d_") else 1)
                    if engname == "sp":
                        for name, val in self.finals.items():
                            if name.startswith("d_"):
                                engine.wait_ge(sems[name], val)
                return body

            block.tensor(run("pe"))
            block.scalar(run("act"))
            block.vector(run("dve"))
            block.gpsimd(run("pool"))
            block.sync(run("sp"))


def f_mm(out, lhsT, rhs, start, stop, skip=False):
    return lambda e: e.matmul(out, lhsT=lhsT, rhs=rhs, start=start, stop=stop,
                              skip_group_check=skip)


def f_tr(out, in_, ident):
    return lambda e: e.transpose(out=out, in_=in_, identity=ident)


def f_act(out, in_, func, bias=None, scale=1.0):
    if bias is None:
        return lambda e: e.activation(out=out, in_=in_, func=func, scale=scale)
    return lambda e: e.activation(out=out, in_=in_, func=func, bias=bias, scale=scale)


def f_tt(out, in0, in1, op):
    return lambda e: e.tensor_tensor(out=out, in0=in0, in1=in1, op=op)


def f_ts(out, in0, s1, s2, op0, op1=None):
    if op1 is None:
        return lambda e: e.tensor_scalar(out=out, in0=in0, scalar1=s1, scalar2=None, op0=op0)
    return lambda e: e.tensor_scalar(out=out, in0=in0, scalar1=s1, scalar2=s2, op0=op0, op1=op1)


def f_stt(out, in0, scalar, in1, op0, op1):
    return lambda e: e.scalar_tensor_tensor(out=out, in0=in0, scalar=scalar, in1=in1,
                                            op0=op0, op1=op1)


def f_copy(out, in_):
    return lambda e: e.tensor_copy(out=out, in_=in_)


def f_dma(out, in_):
    return lambda e: e.dma_start(out=out, in_=in_)


def f_memset(ap, v):
    return lambda e: e.memset(ap, v)


def _san(key):
    return "".join(ch for ch in str(key) if ch.isalnum())


class Ctx:
    def __init__(self, nc, st):
        self.nc = nc
        self.st = st
        self.P = Prog(nc)
        self.psum = st.enter_context(nc.psum_tensor("psum_all", [128, 8, 512], F32))
        self.bank = [self.psum[:, i, :] for i in range(8)]
        self.uid = 0
        self.side = "left"
        self.root = st

    def sb(self, name, shape, dt):
        return self.st.enter_context(self.nc.sbuf_tensor("s_" + name, shape, dt, side=self.side))

    def bank_bf(self, i):
        return self.bank[i].bitcast(BF16)


def load_weight_bf16(C, wt, w_ap, key, ncols, col0=0, row0=0, nchunks=8):
    P = C.P
    step = 1024
    for c in range(nchunks):
        for n0 in range(0, ncols, step):
            n1 = min(ncols, n0 + step)
            P.op("pool", f_dma(wt[:, c, n0:n1],
                               w_ap[row0 + c * 128: row0 + (c + 1) * 128, col0 + n0: col0 + n1]),
                 writes=[key], dsem=_san(key))


def make_hT_from_dram(C, x_ap, hT, ident, xt, xb, tag):
    P = C.P
    for t in range(NT):
        b = t % 2
        P.op("sp", f_dma(xt[b][:], x_ap[t * 128:(t + 1) * 128, :]),
             writes=[(tag, "xt", b)], dsem=f"{tag}xt{b}")
        P.op("act", f_act(xb[b][:], xt[b][:], AF.Copy), reads=[(tag, "xt", b)],
             writes=[(tag, "xb", b)])
        transpose_tile(C, xb[b], (tag, "xb", b), hT, t, ident, b)


def transpose_tile(C, src_bf, src_key, hT, t, ident, b):
    P = C.P
    pT = C.bank_bf(b).rearrange("p (c n) -> p c n", n=128)
    for c in range(8):
        P.op("pe", f_tr(pT[:, c, :], src_bf[:, c * 128:(c + 1) * 128], ident[:]),
             reads=[src_key, "ident"], writes=[("bank", b)])
    P.op("dve", f_copy(hT[:, :, t * 128:(t + 1) * 128], pT), reads=[("bank", b)],
         writes=[("hT", t)])


def ln_stats(C, src, src_key, scr):
    P = C.P
    k = C.uid % 2
    C.uid += 1
    stats, mv, rstd = scr["stats"][k], scr["mv"][k], scr["rstd"][k]
    kk = ("ln", k)
    P.op("dve", lambda e: e.bn_stats(out=stats[:, 0, :], in_=src[:, 0:512]),
         reads=[src_key], writes=[kk + ("st",)])
    P.op("dve", lambda e: e.bn_stats(out=stats[:, 1, :], in_=src[:, 512:1024]),
         reads=[src_key], writes=[kk + ("st",)])
    P.op("dve", lambda e: e.bn_aggr(out=mv[:], in_=stats[:].rearrange("p a s -> p (a s)")),
         reads=[kk + ("st",)], writes=[kk + ("mv",)])
    P.op("dve", f_ts(rstd[:], mv[:, 1:2], LN_EPS, None, ALU.add),
         reads=[kk + ("mv",)], writes=[kk + ("rs0",)])
    P.op("pool", f_tt(rstd[:], rstd[:], scr["mhalf"][:, 0:1], ALU.pow),
         reads=[kk + ("rs0",), "mhalf"], writes=[kk + ("rs0",)])
    return k


def ln_apply(C, k, src, src_key, dst, dst_key, g_rep, b_rep, gb_key, scr):
    P = C.P
    mv, rstd, tmp = scr["mv"][k], scr["rstd"][k], scr["tmp"][k]
    kk = ("ln", k)
    kt = ("ln", "tmp")
    P.op("dve", f_ts(tmp[:], src, mv[:, 0:1], rstd[:, 0:1], ALU.subtract, ALU.mult),
         reads=[src_key, kk + ("mv",), kk + ("rs0",)], writes=[kt])
    P.op("pool", f_tt(tmp[:], tmp[:], g_rep[:], ALU.mult),
         reads=[kt, gb_key], writes=[kt])
    P.op("pool", f_tt(dst, tmp[:], b_rep[:], ALU.add),
         reads=[kt, gb_key], writes=[dst_key])


def layer_norm_tile(C, src, src_key, dst, dst_key, g_rep, b_rep, gb_key, scr):
    k = ln_stats(C, src, src_key, scr)
    ln_apply(C, k, src, src_key, dst, dst_key, g_rep, b_rep, gb_key, scr)


def ln_scratch(C):
    scr = {
        "stats": [C.sb(f"ln_stats{k}", [128, 2, 6], F32) for k in range(2)],
        "mv": [C.sb(f"ln_mv{k}", [128, 2], F32) for k in range(2)],
        "rstd": [C.sb(f"ln_rstd{k}", [128, 1], F32) for k in range(2)],
        "tmp": [C.sb("ln_tmp", [128, 1024], F32)] * 2,
        "mhalf": C.sb("ln_mhalf", [128, 1], F32),
    }
    C.P.op("dve", f_memset(scr["mhalf"][:], -0.5), writes=["mhalf"])
    return scr


def h_to_hT(C, hs, t, hT, ident, hb):
    P = C.P
    b = t % 2
    P.op("act", f_act(hb[b][:], hs[:, t, :], AF.Copy), reads=[("hs", t)], writes=[("hb", b)])
    transpose_tile(C, hb[b], ("hb", b), hT, t, ident, b)


def mlp(C, hs, hT, w_up_ap, w_down_ap, tagp):
    P = C.P
    NS = 8
    wu = [C.sb(f"{tagp}wu{i}", [128, 8, 512], BF16) for i in range(2)]
    wd = [C.sb(f"{tagp}wd{i}", [128, 4, 1024], BF16) for i in range(2)]
    uT1 = C.sb(f"{tagp}uT", [128, 4, TL], BF16)
    uT = [uT1, uT1]
    rl = [C.sb(f"{tagp}rl{i}", [128, 512], F32) for i in range(2)]
    cnt = 0
    dcnt = 0

    def wload(s):
        b = s % 2
        load_weight_bf16(C, wu[b], w_up_ap, (tagp, "wu", b), 512, col0=s * 512)
        for fc in range(4):
            P.op("pool", f_dma(wd[b][:, fc, :],
                               w_down_ap[s * 512 + fc * 128: s * 512 + (fc + 1) * 128, :]),
                 writes=[(tagp, "wd", b)], dsem=f"{tagp}wd{b}")

    wload(0)
    for s in range(NS):
        b = s % 2
        for g in range(4):
            for fc in range(4):
                bk = 2 + (cnt % 2)
                ps = C.bank[bk]
                for c in range(8):
                    P.op("pe", f_mm(ps[:], wu[b][:, c, fc * 128:(fc + 1) * 128],
                                    hT[:, c, g * 512:(g + 1) * 512], c == 0, c == 7),
                         reads=[(tagp, "wu", b)] + [("hT", 4 * g + a) for a in range(4)],
                         writes=[("bank", bk)])
                r = rl[cnt % 2]
                P.op("act", f_act(r[:], ps[:], AF.Relu), reads=[("bank", bk)],
                     writes=[(tagp, "rl", cnt % 2)])
                P.op("pool", f_tt(uT[b][:, fc, g * 512:(g + 1) * 512], r[:], r[:], ALU.mult),
                     reads=[(tagp, "rl", cnt % 2)], writes=[(tagp, "uT", g)])
                cnt += 1
        if s + 1 < NS:
            wload(s + 1)
        for t in range(NT):
            for half in range(2):
                bk = 4 + (dcnt % 4)
                ps = C.bank[bk]
                for fc in range(4):
                    P.op("pe", f_mm(ps[:], uT[b][:, fc, t * 128:(t + 1) * 128],
                                    wd[b][:, fc, half * 512:(half + 1) * 512], fc == 0, fc == 3),
                         reads=[(tagp, "uT", t // 4), (tagp, "wd", b)], writes=[("bank", bk)])
                dst = hs[:, t, half * 512:(half + 1) * 512]
                if s == 0:
                    P.op("dve", f_stt(dst, dst, ALPHA, ps[:], ALU.mult, ALU.add),
                         reads=[("bank", bk), ("hs", t)], writes=[("hs", t)])
                else:
                    P.op("dve", f_tt(dst, dst, ps[:], ALU.add),
                         reads=[("bank", bk), ("hs", t)], writes=[("hs", t)])
                dcnt += 1


def attention(C, cfg, q_d, k_d, v_d, G_d, b31, epilogue):
    P = C.P
    H, nm, KR, dv = cfg["H"], cfg["nm"], cfg["KR"], cfg["dv"]
    dv1 = dv + 1
    per_bank = 512 // dv1 if nm == 1 else 3
    Kb = [C.sb(f"Kb{i}", [128, S], BF16) for i in range(2)]
    Vb = [C.sb(f"Vb{i}", [128, 64, dv1], BF16) for i in range(2)]
    Qb = [C.sb(f"Qb{i}", [128, TL], BF16) for i in range(2)]
    Gb = [C.sb(f"Gb{i}", [128, nm, 640], F32) for i in range(2)]
    nslots = 3 if nm == 1 else 2
    depth = nslots - 1
    nPT = depth + 2
    acc0 = 2 * nslots
    nacc_banks = -(-(4 * nm) // per_bank)
    PT = [C.sb(f"PT{s_}", [128, 2, 512], BF16) for s_ in range(nPT)]
    accsb = [C.sb(f"accsb{i}", [128, nacc_banks, 512], F32) for i in range(2)]

    def rows(m):
        return (64 * m, 64 * m + 64) if nm == 2 else (0, KR)

    def mapidx(h, m):
        return 2 * h + m if nm == 2 else h

    def acc_bank(n):
        return acc0 + n // per_bank

    def acc_ap(n):
        off = (n % per_bank) * dv1
        return C.bank[acc_bank(n)][:, off:off + dv1]

    def acc_key(n):
        return ("bank", acc_bank(n))

    def load(h):
        p = h % 2
        m0 = mapidx(h, 0)
        P.op("sp", f_dma(Gb[p][:], G_d[:, m0:m0 + nm, :]), writes=[("G", p)], dsem=f"G{p}")
        P.op("sp", f_dma(Qb[p][0:KR, :], q_d[h]), writes=[("Q", p)], dsem=f"Q{p}")
        for r in range(4):
            P.op("sp", f_dma(Kb[p][0:KR, r * TL:(r + 1) * TL], k_d[r, h]),
                 writes=[("K", p)], dsem=f"K{p}")
        for r in range(4):
            P.op("sp", f_dma(Vb[p][:, r * 16:(r + 1) * 16, :], v_d[r, h]),
                 writes=[("V", p)], dsem=f"V{p}")

    def fold_bias(h):
        p = h % 2
        for m in range(nm):
            mi = mapidx(h, m)
            P.op("dve", f_ts(Gb[p][:, m, :], Gb[p][:, m, :], b31[:, mi:mi + 1], None, ALU.subtract),
                 reads=[("G", p), "b31"], writes=[("G", p)])

    def amin_of(u, c):
        return max(0, -((-(c - 16 * u - 3)) // 4))

    def afar_of(u, c):
        return min(4, max(0, -((-(c - 16 * u + 2)) // 4)))

    def kslot(c):
        return (c % 4) * 16 + c // 4

    items = []
    for h in range(H):
        for u in range(4):
            if nm == 2:
                for c in range(16 * u + 16):
                    items.append((h, u, ((c, 0), (c, 1))))
            else:
                for cp in range(8 * u + 8):
                    items.append((h, u, ((2 * cp, 0), (2 * cp + 1, 0))))

    def emit_qk_exp(i):
        h, u, subs = items[i]
        if i == 0 or items[i - 1][0] != h:
            fold_bias(h)
        p = h % 2
        slot = i % nslots
        psl = i % nPT
        bks = (2 * slot, 2 * slot + 1)
        amins = [amin_of(u, c) for c, m in subs]
        afars = [afar_of(u, c) for c, m in subs]
        for j, (c, m) in enumerate(subs):
            r0, r1 = rows(m)
            col = kslot(c) * 128
            amin = amins[j]
            P.op("pe", f_mm(C.bank[bks[j]][:, amin * 128:512], Kb[p][r0:r1, col:col + 128],
                            Qb[p][r0:r1, (4 * u + amin) * 128:(4 * u + 4) * 128], True, True),
                 reads=[("K", p), ("Q", p)], writes=[("bank", bks[j])])
        if nm == 2:
            c = subs[0][0]
            for a in range(amins[0], afars[0]):
                s_ = c - (16 * u + 4 * a) + 1
                P.op("dve", f_tt(C.psum[:, bks[0]:bks[0] + 2, a * 128:(a + 1) * 128],
                                 C.psum[:, bks[0]:bks[0] + 2, a * 128:(a + 1) * 128],
                                 Gb[p][:, :, s_ * 128:(s_ + 1) * 128], ALU.add),
                     reads=[("bank", bks[0]), ("bank", bks[1]), ("G", p)],
                     writes=[("bank", bks[0]), ("bank", bks[1])])
        else:
            for j, (c, m) in enumerate(subs):
                for a in range(amins[j], afars[j]):
                    s_ = c - (16 * u + 4 * a) + 1
                    P.op("dve", f_tt(C.bank[bks[j]][:, a * 128:(a + 1) * 128],
                                     C.bank[bks[j]][:, a * 128:(a + 1) * 128],
                                     Gb[p][:, 0, s_ * 128:(s_ + 1) * 128], ALU.add),
                         reads=[("bank", bks[j]), ("G", p)], writes=[("bank", bks[j])])
        am = min(amins)
        P.op("act", f_act(PT[psl][:, :, am * 128:512], C.psum[:, bks[0]:bks[0] + 2, am * 128:512], AF.Exp),
             reads=[("bank", bks[0]), ("bank", bks[1])],
             writes=[("PT", psl, j, a) for j in range(2) for a in range(am, 4)])

    def emit_pv(i):
        h, u, subs = items[i]
        p = h % 2
        psl = i % nPT
        for j, (c, m) in enumerate(subs):
            amin = amin_of(u, c)
            for a in range(amin, 4):
                n = m * 4 + a
                first = (c == 0) and (n % per_bank == 0)
                last = (c == 16 * u + 4 * a + 3)
                P.op("pe", f_mm(acc_ap(n), PT[psl][:, j, a * 128:(a + 1) * 128],
                                Vb[p][:, kslot(c), :], first, last, skip=True),
                     reads=[("PT", psl, j, a), ("V", p)], writes=[acc_key(n)])

    gcnt = [0]

    def evacuate_and_epilogue(h, u):
        k = gcnt[0] % 2
        gcnt[0] += 1
        for bi in range(nacc_banks):
            nin = min(per_bank, 4 * nm - bi * per_bank)
            P.op("dve", f_copy(accsb[k][:, bi, 0:nin * dv1], C.bank[acc0 + bi][:, 0:nin * dv1]),
                 reads=[("bank", acc0 + bi)], writes=[("accsb", k, bi)])

        def sb_ap(n):
            off = (n % per_bank) * dv1
            return accsb[k][:, n // per_bank, off:off + dv1]

        def sb_key(n):
            return ("accsb", k, n // per_bank)
        epilogue(h, u, sb_ap, sb_key)

    load(0)
    if H > 1:
        load(1)
    for i0 in range(min(depth, len(items))):
        emit_qk_exp(i0)
    for i in range(len(items)):
        h, u, subs = items[i]
        nxt = items[i + 1] if i + 1 < len(items) else None
        if i + depth < len(items):
            emit_qk_exp(i + depth)
        emit_pv(i)
        if subs[1][0] == 16 * u + 15:
            evacuate_and_epilogue(h, u)
        if nxt is not None and nxt[0] != h and h + 2 < H:
            load(h + 2)


def dram_in(nc, name, shape, dt):
    return nc.dram_tensor(name, list(shape), dt, kind="ExternalInput").ap()


def dram_out(nc, name, shape, dt):
    return nc.dram_tensor(name, list(shape), dt, kind="ExternalOutput").ap()


def fm_project(C, w, wkey, col0, hT, scale, tag, store):
    P = C.P
    stg = [C.sb(f"{tag}stg{i}", [128, 8, 512], BF16) for i in range(2)]
    cnt = 0
    for g in range(4):
        b = g % 2
        for f in range(8):
            bk = 2 + (cnt % 2)
            cnt += 1
            ps = C.bank[bk]
            for c in range(8):
                P.op("pe", f_mm(ps[:], w[:, c, col0 + f * 128: col0 + (f + 1) * 128],
                                hT[:, c, g * 512:(g + 1) * 512], c == 0, c == 7),
                     reads=[wkey] + [("hT", 4 * g + a) for a in range(4)], writes=[("bank", bk)])
            P.op("act", f_act(stg[b][:, f, :], ps[:], AF.Copy, scale=scale),
                 reads=[("bank", bk)], writes=[(tag, "stg", b)])
        store(g, stg[b], (tag, "stg", b), b)
    return stg


def v_project(C, wb, wkey, col0, hT, out_d, nh, dv, tag):
    P = C.P
    dv1 = dv + 1
    stg = [C.sb(f"{tag}vstg{i}", [128, nh, 4, dv1], BF16) for i in range(2)]
    for i in range(2):
        P.op("pool", f_memset(stg[i][:], 1.0), writes=[(tag, "vstg", i)])
    hpb = 512 // dv
    nhalf = nh * dv // 512
    cnt = 0
    for g in range(4):
        b = g % 2
        for a in range(4):
            t = 4 * g + a
            for half in range(nhalf):
                bk = 4 + (cnt % 2)
                cnt += 1
                ps = C.bank[bk]
                for c in range(8):
                    P.op("pe", f_mm(ps[:], hT[:, c, t * 128:(t + 1) * 128],
                                    wb[:, c, col0 + half * 512: col0 + (half + 1) * 512],
                                    c == 0, c == 7),
                         reads=[wkey, ("hT", t)], writes=[("bank", bk)])
                P.op("dve", f_copy(stg[b][:, half * hpb:(half + 1) * hpb, a, 0:dv],
                                   ps[:].rearrange("p (h e) -> p h e", e=dv)),
                     reads=[("bank", bk)], writes=[(tag, "vstg", b)])
        P.op("sp", f_dma(out_d.rearrange("h p t e -> p h (t e)")[:, :, 4 * g * dv1:(4 * g + 4) * dv1],
                         stg[b][:].rearrange("p h a e -> p h (a e)")),
             reads=[(tag, "vstg", b)], writes=[(tag, "dram")], dsem=f"{tag}vst{b}")


def out_proj_ln(C, ostore, wo, res_ap, hs, hT, ident, g_rep, b_rep, gb_key, scr, xt, hb, oT):
    P = C.P
    prev = None
    for t in range(NT):
        b = t % 2
        pT = C.bank_bf(b).rearrange("p (c n) -> p c n", n=128)
        for c in range(8):
            P.op("pe", f_tr(pT[:, c, :], ostore[:, t, c * 128:(c + 1) * 128], ident[:]),
                 reads=[("os", t), "ident"], writes=[("bank", b)])
        P.op("dve", f_copy(oT[b][:], pT), reads=[("bank", b)], writes=[("oT", b)])
        if res_ap is not None:
            P.op("sp", f_dma(xt[:], res_ap[t * 128:(t + 1) * 128, :]),
                 writes=[("B", "xt")], dsem="Bxt")
        for half in range(2):
            bk = 2 + half
            ps = C.bank[bk]
            for c in range(8):
                P.op("pe", f_mm(ps[:], oT[b][:, c, :], wo[:, c, half * 512:(half + 1) * 512],
                                c == 0, c == 7),
                     reads=[("oT", b), "wo"], writes=[("bank", bk)])
            dst = hs[:, t, half * 512:(half + 1) * 512]
            if res_ap is None:
                P.op("dve", f_stt(dst, dst, ALPHA, ps[:], ALU.mult, ALU.add),
                     reads=[("bank", bk), ("hs", t)], writes=[("hs", t)])
            else:
                P.op("dve", f_stt(dst, xt[:, half * 512:(half + 1) * 512], ALPHA, ps[:],
                                  ALU.mult, ALU.add),
                     reads=[("bank", bk), ("B", "xt")], writes=[("hs", t)])
        kslot_ln = ln_stats(C, hs[:, t, :], ("hs", t), scr)
        if prev is not None:
            pt, pk = prev
            ln_apply(C, pk, hs[:, pt, :], ("hs", pt), hs[:, pt, :], ("hs", pt), g_rep, b_rep, gb_key, scr)
            h_to_hT(C, hs, pt, hT, ident, hb)
        prev = (t, kslot_ln)
    pt, pk = prev
    ln_apply(C, pk, hs[:, pt, :], ("hs", pt), hs[:, pt, :], ("hs", pt), g_rep, b_rep, gb_key, scr)
    h_to_hT(C, hs, pt, hT, ident, hb)


def ln_all_tiles(C, hs, g_rep, b_rep, gb_key, scr, after):
    prev = None
    for t in range(NT):
        k = ln_stats(C, hs[:, t, :], ("hs", t), scr)
        if prev is not None:
            ln_apply(C, prev[1], hs[:, prev[0], :], ("hs", prev[0]), hs[:, prev[0], :], ("hs", prev[0]),
                     g_rep, b_rep, gb_key, scr)
            after(prev[0])
        prev = (t, k)
    ln_apply(C, prev[1], hs[:, prev[0], :], ("hs", prev[0]), hs[:, prev[0], :], ("hs", prev[0]),
             g_rep, b_rep, gb_key, scr)
    after(prev[0])


def load_rep(C, tile, ap, key):
    C.P.op("sp", f_dma(tile[:], ap), writes=[key], dsem=_san(key))


class Scope:
    def __init__(self, C):
        self.C = C

    def __enter__(self):
        self.st = ExitStack()
        self.prev = (self.C.st, self.C.side)
        self.C.st, self.C.side = self.st, "right"
        return self

    def __exit__(self, *a):
        self.C.P.barrier()
        self.st.close()
        self.C.st, self.C.side = self.prev
        return False


def build_A():
    nc = bass.Bass("TRN2", target_bir_lowering=False)
    x = dram_in(nc, "x", [TL, D], F32)
    w_in = dram_in(nc, "w_in", [D, 3 * D], F32)
    ident_d = dram_in(nc, "ident", [128, 128], BF16)
    qT = dram_out(nc, "qT", [8, 128, TL], BF16)
    kT = dram_out(nc, "kT", [8, 128, TL], BF16)
    v = dram_out(nc, "v", [8, 128, NT, 129], BF16)
    with ExitStack() as st:
        C = Ctx(nc, st)
        P = C.P
        ident = C.sb("ident", [128, 128], BF16)
        P.op("sp", f_dma(ident[:], ident_d), writes=["ident"], dsem="ident")
        phase_A(C, x, w_in, ident, qT, kT, v)
        P.emit()
    return nc


def phase_A(C, x, w_in, ident, qT, kT, v):
    P = C.P
    with Scope(C):
        wb = C.sb("wb", [128, 8, 3 * D], BF16)
        load_weight_bf16(C, wb, w_in, "wb", 3 * D)
        hT = C.sb("hTA", [128, 8, TL], BF16)
        xt = [C.sb(f"xtA{i}", [128, D], F32) for i in range(2)]
        xb = [C.sb(f"xbA{i}", [128, D], BF16) for i in range(2)]
        make_hT_from_dram(C, x, hT, ident, xt, xb, "A")

        def mkstore(out_d, tag):
            def store(g, stg, key, b):
                P.op("sp", f_dma(out_d.rearrange("h p n -> p h n")[:, :, g * 512:(g + 1) * 512], stg[:]),
                     reads=[key], writes=[(tag, "dram")], dsem=f"{tag}st{b}")
            return store
        fm_project(C, wb, "wb", 0, hT, 0.125, "q", mkstore(qT, "q"))
        fm_project(C, wb, "wb", D, hT, 1.0, "k", mkstore(kT, "k"))
        v_project(C, wb, "wb", 2 * D, hT, v, 8, 128, "v")


def build_B():
    nc = bass.Bass("TRN2", target_bir_lowering=False)
    x = dram_in(nc, "x", [TL, D], F32)
    qT = dram_in(nc, "qT", [8, 128, TL], BF16)
    kT_all = dram_in(nc, "kT_all", [4, 8, 128, TL], BF16)
    v_all = dram_in(nc, "v_all", [4, 8, 128, NT, 129], BF16)
    cst = const_inputs(nc)
    prm = {
        "lam": dram_in(nc, "lam", [128, 256], F32),
        "subg": dram_in(nc, "subg", [128, 128], F32),
        "w_out0": dram_in(nc, "w_out0", [D, D], F32),
        "w_up0": dram_in(nc, "w_up0", [D, DFF], F32),
        "w_down0": dram_in(nc, "w_down0", [DFF, D], F32),
        "w_kv": dram_in(nc, "w_kv", [D, 2 * D], F32),
    }
    for k in ("ln1_g0", "ln1_b0", "ln2_g0", "ln2_b0"):
        prm[k] = dram_in(nc, k, [128, D], F32)
    h1 = dram_out(nc, "h1", [TL, D], F32)
    kT1 = dram_out(nc, "kT1", [16, 96, TL], BF16)
    v1 = dram_out(nc, "v1", [16, 128, NT, 65], BF16)
    with ExitStack() as st:
        C = Ctx(nc, st)
        P = C.P
        cs = load_consts(C, cst)
        hs, hT, hT_stack = phase_B(C, cs, cst, prm, x, qT, kT_all, v_all, kT1, v1)
        st.callback(hT_stack.close)
        for t in range(NT):
            P.op("sp", f_dma(h1[t * 128:(t + 1) * 128, :], hs[:, t, :]), reads=[("hs", t)],
                 dsem=f"h1st{t % 2}")
        P.emit()
    return nc


def const_inputs(nc):
    return {
        "G": dram_in(nc, "G", [128, 16, 640], F32),
        "b31": dram_in(nc, "b31", [128, 16], F32),
        "ident": dram_in(nc, "ident", [128, 128], BF16),
        "e_own": dram_in(nc, "e_own", [32, TL], BF16),
        "past30k": dram_in(nc, "past30k", [128, NT, 32], F32),
    }


def load_consts(C, cst):
    P = C.P
    ident = C.sb("ident", [128, 128], BF16)
    P.op("sp", f_dma(ident[:], cst["ident"]), writes=["ident"], dsem="ident")
    b31 = C.sb("b31", [128, 16], F32)
    P.op("sp", f_dma(b31[:], cst["b31"]), writes=["b31"], dsem="b31")
    scr = ln_scratch(C)
    hb = [C.sb(f"hb{i}", [128, D], BF16) for i in range(2)]
    rep = [C.sb(f"rep{i}", [128, D], F32) for i in range(2)]
    kmT = C.sb("kmT", [128, 8, 32], BF16)
    return {"ident": ident, "b31": b31, "scr": scr, "hb": hb, "rep": rep, "kmT": kmT}


def alloc_hT(C):
    stk = ExitStack()
    hT = stk.enter_context(C.nc.sbuf_tensor("s_hT%d" % C.uid, [128, 8, TL], BF16, side="left"))
    C.uid += 1
    return hT, stk


def phase_B(C, cs, cst, prm, x, qT, kT_all, v_all, kT1, v1):
    P = C.P
    ident, b31, scr, hb, rep = cs["ident"], cs["b31"], cs["scr"], cs["hb"], cs["rep"]
    with Scope(C):
        ostore = C.sb("ostore", [128, NT, D], BF16)
        lam_sb = C.sb("lam_sb", [128, 256], F32)
        lprod = C.sb("lprod", [128, 2, 64], F32)
        lsum = C.sb("lsum", [128, 2], F32)
        lexp = C.sb("lexp", [128, 2], F32)
        nl = C.sb("nl", [128, 2], F32)
        gsub = C.sb("gsub", [128, 128], F32)
        P.op("sp", f_dma(lam_sb[:], prm["lam"]), writes=["lam_sb"], dsem="lam")
        P.op("sp", f_dma(gsub[:], prm["subg"]), writes=["gsub"], dsem="gsub")
        lv = lam_sb[:].rearrange("p (a b d) -> p a b d", a=2, b=2)
        P.op("dve", f_tt(lprod[:], lv[:, :, 0, :], lv[:, :, 1, :], ALU.mult),
             reads=["lam_sb"], writes=["lprod"])
        P.op("dve", lambda e: e.reduce_sum(out=lsum[:], in_=lprod[:], axis=AX.X),
             reads=["lprod"], writes=["lsum"])
        P.op("act", f_act(lexp[:], lsum[:], AF.Exp), reads=["lsum"], writes=["lexp"])
        P.op("dve", f_tt(nl[:, 0:1], lexp[:, 1:2], lexp[:, 0:1], ALU.subtract),
             reads=["lexp"], writes=["nl0"])
        P.op("dve", f_ts(nl[:, 1:2], nl[:, 0:1], -LAM_INIT0, None, ALU.add),
             reads=["nl0"], writes=["neglam"])
        neglam = nl[:, 1:2]
        P.op("dve", f_ts(gsub[:], gsub[:], 1.0 - LAM_INIT0, None, ALU.mult),
             reads=["gsub"], writes=["gsub"])
        ep = {k: [C.sb(f"ep_{k}{i}", shp, F32) for i in range(2)]
              for k, shp in (("rl", [128, 2]), ("c1", [128, 1]), ("t1", [128, 128]),
                             ("o", [128, 128]), ("sq", [128, 128]), ("ss", [128, 1]))}
        ecnt = [0]

        def epilogue(h, u, acc_ap, acc_key):
            for a in range(4):
                t = 4 * u + a
                k = ecnt[0] % 2
                ecnt[0] += 1
                rl, c1, t1, o, sq, ss = (ep[x_][k] for x_ in ("rl", "c1", "t1", "o", "sq", "ss"))
                a0, a1 = acc_ap(a), acc_ap(4 + a)
                k0, k1 = acc_key(a), acc_key(4 + a)
                P.op("dve", lambda e, rl=rl, a0=a0: e.reciprocal(out=rl[:, 0:1], in_=a0[:, 128:129]),
                     reads=[k0], writes=[("ep", k, "rl0")])
                P.op("dve", lambda e, rl=rl, a1=a1: e.reciprocal(out=rl[:, 1:2], in_=a1[:, 128:129]),
                     reads=[k1], writes=[("ep", k, "rl1")])
                P.op("dve", f_tt(c1[:], rl[:, 1:2], neglam, ALU.mult),
                     reads=[("ep", k, "rl1"), "neglam"], writes=[("ep", k, "c1")])
                P.op("dve", f_ts(t1[:], a1[:, 0:128], c1[:, 0:1], None, ALU.mult),
                     reads=[k1, ("ep", k, "c1")], writes=[("ep", k, "t1")])
                P.op("dve", f_stt(o[:], a0[:, 0:128], rl[:, 0:1], t1[:], ALU.mult, ALU.add),
                     reads=[k0, ("ep", k, "rl0"), ("ep", k, "t1")], writes=[("ep", k, "o")])
                P.op("pool", f_tt(sq[:], o[:], o[:], ALU.mult), reads=[("ep", k, "o")],
                     writes=[("ep", k, "sq")])
                P.op("dve", lambda e, ss=ss, sq=sq: e.reduce_sum(out=ss[:], in_=sq[:], axis=AX.X),
                     reads=[("ep", k, "sq")], writes=[("ep", k, "ss")])
                P.op("dve", f_ts(ss[:], ss[:], 1.0 / 128.0, LN_EPS, ALU.mult, ALU.add),
                     reads=[("ep", k, "ss")], writes=[("ep", k, "ss")])
                P.op("pool", f_tt(ss[:], ss[:], scr["mhalf"][:, 0:1], ALU.pow),
                     reads=[("ep", k, "ss"), "mhalf"], writes=[("ep", k, "ss")])
                P.op("dve", f_stt(ostore[:, t, h * 128:(h + 1) * 128], o[:], ss[:, 0:1], gsub[:],
                                  ALU.mult, ALU.mult),
                     reads=[("ep", k, "o"), ("ep", k, "ss"), "gsub"], writes=[("os", t)])

        with Scope(C):
            attention(C, dict(H=8, nm=2, KR=128, dv=128), qT, kT_all, v_all, cst["G"], b31, epilogue)
        C.side = "left"
        hs = C.nc.sbuf_tensor("s_hs", [128, NT, D], F32, side="left")
        hs = C.root.enter_context(hs)
        hT, hT_stack = alloc_hT(C)
        C.side = "right"
        with Scope(C):
            wo = C.sb("wo", [128, 8, D], BF16)
            load_weight_bf16(C, wo, prm["w_out0"], "wo", D)
            xt = C.sb("xtB", [128, D], F32)
            oT = [C.sb(f"oT{i}", [128, 8, 128], BF16) for i in range(2)]
            load_rep(C, rep[0], prm["ln1_g0"], "rep0")
            load_rep(C, rep[1], prm["ln1_b0"], "rep1")
            out_proj_ln(C, ostore, wo, x, hs, hT, ident, rep[0], rep[1], "rep0", scr, xt, hb, oT)
    with Scope(C):
        mlp(C, hs, hT, prm["w_up0"], prm["w_down0"], "m0")
    load_rep(C, rep[0], prm["ln2_g0"], "rep0")
    load_rep(C, rep[1], prm["ln2_b0"], "rep1")
    ln_all_tiles(C, hs, rep[0], rep[1], "rep0", scr, lambda t: h_to_hT(C, hs, t, hT, ident, hb))
    with Scope(C):
        wkv = C.sb("wk1", [128, 8, D], BF16)
        load_weight_bf16(C, wkv, prm["w_kv"], "wkv", D)

        def store(g, stg, key, b):
            kv = kT1.rearrange("(f two) r n -> two r f n", two=2)
            for hh in range(2):
                P.op("sp", f_dma(kv[hh, 0:64, :, g * 512:(g + 1) * 512], stg[hh * 64:(hh + 1) * 64, :, :]),
                     reads=[key], writes=[("k1dram", hh)], dsem=f"k1st{b}")
        fm_project(C, wkv, "wkv", 0, hT, 1.0, "k1", store)
        for h in range(16):
            P.op("sp", f_dma(kT1[h, 64:96, :], cst["e_own"]), writes=[("k1dramE",)], dsem="k1e")
    with Scope(C):
        wv1 = C.sb("wv1", [128, 8, D], BF16)
        load_weight_bf16(C, wv1, prm["w_kv"], "wv1", D, col0=D)
        v_project(C, wv1, "wv1", 0, hT, v1, 16, 64, "v1")
    return hs, hT, hT_stack


def build_C(stop_after=None):
    nc = bass.Bass("TRN2", target_bir_lowering=False)
    h1 = dram_in(nc, "h1", [TL, D], F32)
    kT1_all = dram_in(nc, "kT1_all", [4, 16, 96, TL], BF16)
    v1_all = dram_in(nc, "v1_all", [4, 16, 128, NT, 65], BF16)
    cst = const_inputs(nc)
    prm = {
        "w_q": dram_in(nc, "w_q", [D, D], F32),
        "w_out1": dram_in(nc, "w_out1", [D, D], F32),
        "w_up1": dram_in(nc, "w_up1", [D, DFF], F32),
        "w_down1": dram_in(nc, "w_down1", [DFF, D], F32),
    }
    for k in ("ln1_g1", "ln1_b1", "ln2_g1", "ln2_b1"):
        prm[k] = dram_in(nc, k, [128, D], F32)
    out = dram_out(nc, "out", [TL, D], F32)
    qaug = nc.dram_tensor("qaug", [16, 96, TL], BF16,
                          kind="ExternalOutput" if stop_after else "Internal").ap()
    with ExitStack() as st:
        C = Ctx(nc, st)
        C.stop_after = stop_after
        P = C.P
        cs = load_consts(C, cst)
        hs = C.sb("hs", [128, NT, D], F32)
        hT, hT_stack = alloc_hT(C)
        for t in range(NT):
            P.op("sp", f_dma(hs[:, t, :], h1[t * 128:(t + 1) * 128, :]), writes=[("hs", t)],
                 dsem=f"hsld{t}")
            h_to_hT(C, hs, t, hT, cs["ident"], cs["hb"])
        phase_C(C, cs, cst, prm, kT1_all, v1_all, qaug, hs, hT, hT_stack, out)
        P.emit()
    return nc


def phase_C(C, cs, cst, prm, kT1_all, v1_all, qaug, hs, hT, hT_stack, out):
    P = C.P
    ident, b31, scr, hb, rep = cs["ident"], cs["b31"], cs["scr"], cs["hb"], cs["rep"]
    kmT = cs["kmT"]
    with Scope(C):
        moba_kmean(C, kT1_all, kmT)
    if getattr(C, "stop_after", None) == "kmean":
        P.op("sp", f_dma(qaug.rearrange("h r n -> r h n")[0:64, 0:8, 0:32], kmT[0:64, :, :]),
             reads=[("kmT", f) for f in range(8)], dsem="dbg1")
        P.op("sp", f_dma(qaug.rearrange("h r n -> r h n")[0:64, 8:16, 0:32], kmT[64:128, :, :]),
             reads=[("kmT", f) for f in range(8)], dsem="dbg2")
        for t in range(NT):
            P.op("sp", f_dma(out[t * 128:(t + 1) * 128, :], hs[:, t, :]), reads=[("hs", t)],
                 dsem=f"outst{t % 2}")
        hT_stack.close()
        return
    with Scope(C):
        moba_gate_qproj(C, hT, prm["w_q"], kmT, cst["past30k"], qaug, ident)
    hT_stack.close()
    hT = None
    if getattr(C, "stop_after", None) == "gate":
        for t in range(NT):
            P.op("sp", f_dma(out[t * 128:(t + 1) * 128, :], hs[:, t, :]), reads=[("hs", t)],
                 dsem=f"outst{t % 2}")
        return
    with Scope(C):
        ostore = C.sb("ostore1", [128, NT, D], BF16)
        eprl = [C.sb(f"eprl{i}", [128, 1], F32) for i in range(8)]

        def epilogue(h, u, acc_ap, acc_key):
            for a in range(4):
                t = 4 * u + a
                k = (h * 4 + u) % 2
                rl = eprl[4 * k + a]
                acc = acc_ap(a)
                P.op("dve", lambda e, rl=rl, acc=acc: e.reciprocal(out=rl[:], in_=acc[:, 64:65]),
                     reads=[acc_key(a)], writes=[("eprl", 4 * k + a)])
                P.op("dve", f_ts(ostore[:, t, h * 64:(h + 1) * 64], acc[:, 0:64], rl[:, 0:1], None,
                                 ALU.mult),
                     reads=[acc_key(a), ("eprl", 4 * k + a)], writes=[("os", t)])

        with Scope(C):
            attention(C, dict(H=16, nm=1, KR=96, dv=64), qaug, kT1_all, v1_all, cst["G"], b31, epilogue)
        hT, hT_stack2 = alloc_hT(C)
        C.root.callback(hT_stack2.close)
        with Scope(C):
            wo = C.sb("wo1", [128, 8, D], BF16)
            load_weight_bf16(C, wo, prm["w_out1"], "wo", D)
            oT = [C.sb(f"oT1{i}", [128, 8, 128], BF16) for i in range(2)]
            load_rep(C, rep[0], prm["ln1_g1"], "rep0")
            load_rep(C, rep[1], prm["ln1_b1"], "rep1")
            out_proj_ln(C, ostore, wo, None, hs, hT, ident, rep[0], rep[1], "rep0", scr, None, hb, oT)
    with Scope(C):
        mlp(C, hs, hT, prm["w_up1"], prm["w_down1"], "m1")
    load_rep(C, rep[0], prm["ln2_g1"], "rep0")
    load_rep(C, rep[1], prm["ln2_b1"], "rep1")
    ln_all_tiles(C, hs, rep[0], rep[1], "rep0", scr,
                 lambda t: P.op("sp", f_dma(out[t * 128:(t + 1) * 128, :], hs[:, t, :]),
                                reads=[("hs", t)], dsem=f"outst{t % 2}"))


def moba_kmean(C, kT1_all, kmT):
    P = C.P
    Kp = [C.sb(f"Kp{i}", [128, S], BF16) for i in range(2)]
    ksum = [C.sb(f"ksum{i}", [128, 64], F32) for i in range(2)]
    kmf = [C.sb(f"kmf{i}", [128, 16, 2], F32) for i in range(2)]
    for f in range(8):
        b = f % 2
        for r in range(4):
            for hh in range(2):
                P.op("sp", f_dma(Kp[b][hh * 64:(hh + 1) * 64, r * TL:(r + 1) * TL],
                                 kT1_all[r, 2 * f + hh, 0:64, :]),
                     writes=[("Kp", b)], dsem=f"Kp{b}")
        P.op("dve", lambda e, b=b: e.reduce_sum(out=ksum[b][:],
                                                in_=Kp[b][:].rearrange("p (s q) -> p s q", q=128),
                                                axis=AX.X),
             reads=[("Kp", b)], writes=[("ksum", b)])
        kv = ksum[b][:].rearrange("p (r t) -> p r t", t=16)
        for par in range(2):
            P.op("dve", f_tt(kmf[b][:, :, par], kv[:, 2 * par, :], kv[:, 2 * par + 1, :], ALU.add),
                 reads=[("ksum", b)], writes=[("kmf", b, par)])
        P.op("dve", f_ts(kmT[:, f, :], kmf[b][:].rearrange("p t two -> p (t two)"), 1.0 / 256.0,
                         None, ALU.mult),
             reads=[("kmf", b, 0), ("kmf", b, 1)], writes=[("kmT", f)])


def moba_gate_qproj(C, hT, w_q, kmT, past_d, qaug, ident):
    P = C.P
    wq = C.sb("wq", [128, 8, D], BF16)
    load_weight_bf16(C, wq, w_q, "wq", D)
    past = C.sb("past", [128, NT, 32], F32)
    padd = C.sb("padd", [128, NT, 32], F32)
    P.op("sp", f_dma(past[:], past_d), writes=["past"], dsem="past")
    P.op("dve", f_ts(padd[:], past[:], NEG, None, ALU.add), reads=["past"], writes=["padd"])
    mstg = [C.sb(f"mstg{i}", [128, 16, 128], BF16) for i in range(2)]
    gm = [C.sb(f"gm{i}", [128, 16, 32], F32) for i in range(2)]
    top8 = [C.sb(f"top8{i}", [128, 16, 8], F32) for i in range(2)]
    sel = [C.sb(f"sel{i}", [128, 16, 32], F32) for i in range(2)]
    negpad = [C.sb(f"negpad{i}", [128, 16, 96], BF16) for i in range(2)]
    for i in range(2):
        P.op("pool", f_memset(negpad[i][:], 0.0), writes=[("negpad", i)])

    def store(g, stg, key, b):
        qv = qaug.rearrange("(f two) r n -> two r f n", two=2)
        for hh in range(2):
            P.op("sp", f_dma(qv[hh, 0:64, :, g * 512:(g + 1) * 512], stg[hh * 64:(hh + 1) * 64, :, :]),
                 reads=[key], writes=[("qaug", "q")], dsem=f"qst{b}")
        for a in range(4):
            t = 4 * g + a
            k = t % 2
            gbk = (4, 5) if k == 0 else (0, 1)
            for h in range(16):
                r0 = (h % 2) * 64
                gps = C.bank[gbk[h % 2]]
                P.op("pe", f_mm(gps[:, (h // 2) * 32:(h // 2 + 1) * 32],
                                stg[r0:r0 + 64, h // 2, a * 128:(a + 1) * 128],
                                kmT[r0:r0 + 64, h // 2, :], True, True),
                     reads=[key, ("kmT", h // 2)], writes=[("bank", gbk[h % 2])])
            gmv = gm[k][:].rearrange("p (f two) n -> p two f n", two=2)
            for par in range(2):
                P.op("dve", f_tt(gmv[:, par], C.bank[gbk[par]][:, 0:256].rearrange("p (f n) -> p f n", n=32),
                                 padd[:, t, :].unsqueeze(1).broadcast_to([128, 8, 32]), ALU.add),
                     reads=[("bank", gbk[par]), "padd"], writes=[("gm", k)])
            for h in range(16):
                P.op("dve", lambda e, k=k, h=h: e.max(out=top8[k][:, h, :], in_=gm[k][:, h, :]),
                     reads=[("gm", k)], writes=[("top8", k)])
            P.op("dve", f_tt(sel[k][:], gm[k][:], top8[k][:, :, 2:3].broadcast_to([128, 16, 32]),
                             ALU.is_ge),
                 reads=[("gm", k), ("top8", k)], writes=[("sel", k)])
            P.op("dve", f_ts(sel[k][:], sel[k][:], -1.0, None, ALU.add),
                 reads=[("sel", k)], writes=[("sel", k)])
            P.op("dve", f_tt(negpad[k][:, :, 64:96], sel[k][:],
                             past[:, t, :].unsqueeze(1).broadcast_to([128, 16, 32]), ALU.mult),
                 reads=[("sel", k), "past"], writes=[("negpad", k)])
            for hq in range(2):
                bk = 6 + hq
                tp = C.bank_bf(bk).rearrange("p (h n) -> p h n", n=128)
                for h8 in range(8):
                    h = hq * 8 + h8
                    P.op("pe", f_tr(tp[0:96, h8, :], negpad[k][:, h, :], ident[:]),
                         reads=[("negpad", k), "ident"], writes=[("bank", bk)])
                P.op("act", f_act(mstg[k][64:96, hq * 8:(hq + 1) * 8, :], tp[64:96, :, :], AF.Copy),
                     reads=[("bank", bk)], writes=[("mstg", k)])
            P.op("sp", f_dma(qaug.rearrange("h r n -> r h n")[64:96, :, t * 128:(t + 1) * 128],
                             mstg[k][64:96, :, :]),
                 reads=[("mstg", k)], writes=[("qaug", "m")], dsem=f"mst{k}")

    fm_project(C, wq, "wq", 0, hT, 0.125, "qb", store)


def _bucket_np(n):
    n = np.maximum(n, 0)
    nf = np.maximum(n, 16).astype(np.float32)
    large = 16 + (np.log(nf / np.float32(16)) / np.float32(math.log(8.0)) * np.float32(16)).astype(np.int32)
    large = np.minimum(large, 31)
    return np.where(n < 16, n, large)


def _g_index(j):
    k = np.arange(128)[:, None, None]
    q = np.arange(128)[None, None, :]
    e = (np.arange(5) - 1)[None, :, None]
    delta = e - j
    rel = q - k - 128 * delta
    idx = _bucket_np(rel)
    idx = np.where(rel >= 0, idx, 32)
    return np.broadcast_to(idx, (128, 5, 128)).astype(np.int64)


_CACHE = {}


def _get(name, fn):
    if name not in _CACHE:
        _CACHE[name] = fn()
    return _CACHE[name]


def _host_prep(x, rel_bias):
    f32 = np.float32
    bf = ml_dtypes.bfloat16
    table_ext = np.concatenate([rel_bias, np.full((1, 16), NEG, f32)], axis=0)
    per = []
    for c in range(NCORES):
        b, j = c // 4, c % 4
        xs = np.ascontiguousarray(x[b].reshape(64, 128, D)[j::4].reshape(TL, D))
        G = table_ext[_g_index(j)]
        G = np.ascontiguousarray(G.transpose(0, 3, 1, 2).reshape(128, 16, 640))
        blk = (4 * np.arange(NT) + j) // 2
        e = np.zeros((32, NT, 128), f32)
        e[blk, np.arange(NT), :] = 1.0
        past = (np.arange(32)[None, :] < blk[:, None]).astype(f32) * f32(30000.0)
        per.append({
            "x": xs, "G": G, "e_own": e.reshape(32, TL).astype(bf),
            "past30k": np.ascontiguousarray(np.broadcast_to(past[None], (128, NT, 32))),
            "b31": np.ascontiguousarray(np.broadcast_to(rel_bias[31:32, :], (128, 16))),
            "ident": np.eye(128, dtype=f32).astype(bf),
        })
    return per


def _rep(v, n=128):
    v = np.asarray(v, np.float32).reshape(1, -1)
    return np.ascontiguousarray(np.broadcast_to(v, (n, v.shape[1])))


CONST_KEYS = ("G", "b31", "ident", "e_own", "past30k")


def kernel(x, w_in_a, lam_a, subln_a, w_out_a, w_kv_shared, w_q_b, w_out_b, rel_bias,
           ln1_g, ln1_b, ln2_g, ln2_b, w_up, w_down):
    f32 = np.float32
    A = lambda v: np.ascontiguousarray(np.asarray(v, f32))
    x = A(x)
    rel_bias = A(rel_bias)
    cores = list(range(NCORES))
    per = _host_prep(x, rel_bias)
    ncA = _get("A", build_A)
    inA = [{"x": per[c]["x"], "w_in": A(w_in_a[0]), "ident": per[c]["ident"]} for c in cores]
    rA = run_bass_kernel_spmd(ncA, inA, core_ids=cores).results
    ncB = _get("B", build_B)
    prmB = {
        "lam": _rep(np.asarray(lam_a[0], f32).reshape(-1)), "subg": _rep(subln_a[0]),
        "w_out0": A(w_out_a[0]), "w_up0": A(w_up[0]), "w_down0": A(w_down[0]),
        "w_kv": A(w_kv_shared),
        "ln1_g0": _rep(ln1_g[0]), "ln1_b0": _rep(ln1_b[0]),
        "ln2_g0": _rep(ln2_g[0]), "ln2_b0": _rep(ln2_b[0]),
    }
    inB = []
    for c in cores:
        grp = [4 * (c // 4) + r for r in range(4)]
        d = {"x": per[c]["x"], "qT": rA[c]["qT"],
             "kT_all": np.stack([rA[g]["kT"] for g in grp]),
             "v_all": np.stack([rA[g]["v"] for g in grp])}
        d.update({k: per[c][k] for k in CONST_KEYS})
        d.update(prmB)
        inB.append(d)
    rB = run_bass_kernel_spmd(ncB, inB, core_ids=cores).results
    ncC = _get("C", build_C)
    prmC = {
        "w_q": A(w_q_b[0]), "w_out1": A(w_out_b[0]), "w_up1": A(w_up[1]), "w_down1": A(w_down[1]),
        "ln1_g1": _rep(ln1_g[1]), "ln1_b1": _rep(ln1_b[1]),
        "ln2_g1": _rep(ln2_g[1]), "ln2_b1": _rep(ln2_b[1]),
    }
    inC = []
    for c in cores:
        grp = [4 * (c // 4) + r for r in range(4)]
        d = {"h1": rB[c]["h1"],
             "kT1_all": np.stack([rB[g]["kT1"] for g in grp]),
             "v1_all": np.stack([rB[g]["v1"] for g in grp])}
        d.update({k: per[c][k] for k in CONST_KEYS})
        d.update(prmC)
        inC.append(d)
    rC = run_bass_kernel_spmd(ncC, inC, core_ids=cores).results
    out = np.empty((2, 64, 128, D), f32)
    for c in cores:
        b, j = c // 4, c % 4
        out[b, j::4] = np.asarray(rC[c]["out"], f32).reshape(NT, 128, D)
    return out.reshape(2, S, D)
```

```python
import math
from contextlib import ExitStack

import numpy as np
import ml_dtypes
import concourse.bass as bass
import concourse.mybir as mybir
from concourse.bass_utils import run_bass_kernel_spmd

F32 = mybir.dt.float32
BF16 = mybir.dt.bfloat16
AF = mybir.ActivationFunctionType
ALU = mybir.AluOpType
AX = mybir.AxisListType

NCORES = 8
D = 1024
S = 8192
NT = 16
TL = 2048
DFF = 4096
ALPHA = 4.0 ** 0.25
LN_EPS = 1e-5
LAM_INIT0 = 0.8 - 0.6 * math.exp(-0.3 * 0)
NEG = -30000.0
LN_SKEW = 3

ENGS = ("pe", "act", "dve", "pool", "sp")


class Op:
    __slots__ = ("fn", "deps", "dsem", "seq", "waits")

    def __init__(self, fn, deps, dsem):
        self.fn = fn
        self.deps = deps
        self.dsem = dsem
        self.seq = None
        self.waits = []


class Prog:
    def __init__(self, nc):
        self.nc = nc
        self.ops = {e: [] for e in ENGS}
        self.last_w = {}
        self.readers = {}

    def op(self, eng, fn, reads=(), writes=(), dsem=None):
        deps = set()
        for k in reads:
            w = self.last_w.get(k)
            if w is not None:
                deps.add(w)
        for k in writes:
            w = self.last_w.get(k)
            if w is not None:
                deps.add(w)
            rd = self.readers.get(k)
            if rd:
                for e2, i2 in rd.items():
                    deps.add((e2, i2))
        idx = len(self.ops[eng])
        me = (eng, idx)
        deps.discard(me)
        self.ops[eng].append(Op(fn, deps, dsem))
        for k in reads:
            self.readers.setdefault(k, {})[eng] = idx
        for k in writes:
            self.last_w[k] = me
            self.readers[k] = {}
        return me

    def barrier(self):
        deps = set()
        for e in ENGS:
            ops = self.ops[e]
            last_real = None
            seen_d = set()
            for i in range(len(ops) - 1, -1, -1):
                o = ops[i]
                if o.fn is None:
                    continue
                if o.dsem is not None:
                    if o.dsem not in seen_d:
                        seen_d.add(o.dsem)
                        deps.add((e, i))
                elif last_real is None:
                    last_real = i
                    deps.add((e, i))
        for e in ENGS:
            self.ops[e].append(Op(None, set(deps), None))

    def resolve(self):
        needed = set()
        for e in ENGS:
            for o in self.ops[e]:
                for d in o.deps:
                    if d[0] == e and e == "pe":
                        continue
                    needed.add(d)
        self.sem_names = set()
        cnt = {}
        for e in ENGS:
            for i, o in enumerate(self.ops[e]):
                if o.dsem is not None:
                    name = "d_" + o.dsem
                    cnt[name] = cnt.get(name, 0) + 16
                    o.seq = (name, cnt[name])
                    self.sem_names.add(name)
                elif (e, i) in needed:
                    name = "e_" + e
                    cnt[name] = cnt.get(name, 0) + 1
                    o.seq = (name, cnt[name])
                    self.sem_names.add(name)
        self.finals = dict(cnt)
        for e in ENGS:
            seen = {}
            for o in self.ops[e]:
                req = {}
                for d in o.deps:
                    if d[0] == e and e == "pe":
                        continue
                    name, val = self.ops[d[0]][d[1]].seq
                    if req.get(name, 0) < val:
                        req[name] = val
                for name, val in req.items():
                    if seen.get(name, 0) < val:
                        seen[name] = val
                        o.waits.append((name, val))

    def check(self):
        sem = {}
        pc = {e: 0 for e in ENGS}
        n = {e: len(self.ops[e]) for e in ENGS}
        progress = True
        while progress:
            progress = False
            for e in ENGS:
                while pc[e] < n[e]:
                    o = self.ops[e][pc[e]]
                    if all(sem.get(nm, 0) >= v for nm, v in o.waits):
                        if o.seq is not None:
                            nm = o.seq[0]
                            sem[nm] = sem.get(nm, 0) + (16 if nm.startswith("d_") else 1)
                        pc[e] += 1
                        progress = True
                    else:
                        break
        stuck = {e: (pc[e], n[e]) for e in ENGS if pc[e] < n[e]}
        if stuck:
            msg = []
            for e, (p, m) in stuck.items():
                o = self.ops[e][p]
                msg.append(f"{e}@{p}/{m} waits {[(nm, v, sem.get(nm, 0)) for nm, v in o.waits]}")
            raise RuntimeError("sync deadlock: " + "; ".join(msg))
        return {e: n[e] for e in ENGS}, len(self.sem_names)

    def emit(self):
        nc = self.nc
        self.resolve()
        print("prog ops/sems:", self.check(), flush=True)
        with ExitStack() as st:
            sems = {}
            for name in sorted(self.sem_names):
                sems[name] = st.enter_context(nc.semaphore(name))
            block = st.enter_context(nc.Block())

            def run(engname):
                def body(engine):
                    for o in self.ops[engname]:
                        if o.fn is None:
                            for (name, val) in o.waits:
                                engine.wait_ge(sems[name], val)
                            continue
                        for (name, val) in o.waits[1:]:
                            engine.wait_ge(sems[name], val)
                        ins = o.fn(engine)
                        if o.waits:
                            ins._wait_ge(sems[o.waits[0][0]], o.waits[0][1])
                        if o.seq is not None:
                            name = o.seq[0]
                            ins.then_inc(sems[name], 16 if name.startswith("d_") else 1)
                    if engname == "sp":
                        for name, val in self.finals.items():
                            if name.startswith("d_"):
                                engine.wait_ge(sems[name], val)
                return body

            block.tensor(run("pe"))
            block.scalar(run("act"))
            block.vector(run("dve"))
            block.gpsimd(run("pool"))
            block.sync(run("sp"))


def f_mm(out, lhsT, rhs, start, stop, skip=False):
    return lambda e: e.matmul(out, lhsT=lhsT, rhs=rhs, start=start, stop=stop,
                              skip_group_check=skip)


def f_tr(out, in_, ident):
    return lambda e: e.transpose(out=out, in_=in_, identity=ident)


def f_act(out, in_, func, bias=None, scale=1.0):
    if bias is None:
        return lambda e: e.activation(out=out, in_=in_, func=func, scale=scale)
    return lambda e: e.activation(out=out, in_=in_, func=func, bias=bias, scale=scale)


def f_tt(out, in0, in1, op):
    return lambda e: e.tensor_tensor(out=out, in0=in0, in1=in1, op=op)


def f_ts(out, in0, s1, s2, op0, op1=None):
    if op1 is None:
        return lambda e: e.tensor_scalar(out=out, in0=in0, scalar1=s1, scalar2=None, op0=op0)
    return lambda e: e.tensor_scalar(out=out, in0=in0, scalar1=s1, scalar2=s2, op0=op0, op1=op1)


def f_stt(out, in0, scalar, in1, op0, op1):
    return lambda e: e.scalar_tensor_tensor(out=out, in0=in0, scalar=scalar, in1=in1,
                                            op0=op0, op1=op1)


def f_copy(out, in_):
    return lambda e: e.tensor_copy(out=out, in_=in_)


def f_dma(out, in_):
    return lambda e: e.dma_start(out=out, in_=in_)


def f_memset(ap, v):
    return lambda e: e.memset(ap, v)


def _san(key):
    return "".join(ch for ch in str(key) if ch.isalnum())


class Ctx:
    def __init__(self, nc, st):
        self.nc = nc
        self.st = st
        self.P = Prog(nc)
        self.psum = st.enter_context(nc.psum_tensor("psum_all", [128, 8, 512], F32))
        self.bank = [self.psum[:, i, :] for i in range(8)]
        self.uid = 0
        self.side = "left"
        self.root = st

    def sb(self, name, shape, dt):
        return self.st.enter_context(self.nc.sbuf_tensor("s_" + name, shape, dt, side=self.side))

    def bank_bf(self, i):
        return self.bank[i].bitcast(BF16)


def load_weight_bf16(C, wt, w_ap, key, ncols, col0=0, row0=0, nchunks=8):
    P = C.P
    step = 1024
    for c in range(nchunks):
        for n0 in range(0, ncols, step):
            n1 = min(ncols, n0 + step)
            P.op("pool", f_dma(wt[:, c, n0:n1],
                               w_ap[row0 + c * 128: row0 + (c + 1) * 128, col0 + n0: col0 + n1]),
                 writes=[key], dsem=_san(key))


def make_hT_from_dram(C, x_ap, hT, ident, xt, xb, tag):
    P = C.P
    for t in range(NT):
        b = t % 2
        P.op("sp", f_dma(xt[b][:], x_ap[t * 128:(t + 1) * 128, :]),
             writes=[(tag, "xt", b)], dsem=f"{tag}xt{b}")
        P.op("act", f_act(xb[b][:], xt[b][:], AF.Copy), reads=[(tag, "xt", b)],
             writes=[(tag, "xb", b)])
        transpose_tile(C, xb[b], (tag, "xb", b), hT, t, ident, b)


def transpose_tile(C, src_bf, src_key, hT, t, ident, b, bank0=0):
    P = C.P
    bk = bank0 + b
    pT = C.bank_bf(bk).rearrange("p (c n) -> p c n", n=128)
    for c in range(8):
        P.op("pe", f_tr(pT[:, c, :], src_bf[:, c * 128:(c + 1) * 128], ident[:]),
             reads=[src_key, "ident"], writes=[("bank", bk)])
    P.op("dve", f_copy(hT[:, :, t * 128:(t + 1) * 128], pT), reads=[("bank", bk)],
         writes=[("hT", t)])


def ln_stats(C, src, src_key, scr):
    P = C.P
    k = C.uid % 4
    C.uid += 1
    stats, mv, rstd = scr["stats"][k], scr["mv"][k], scr["rstd"][k]
    kk = ("ln", k)
    P.op("dve", lambda e: e.bn_stats(out=stats[:, 0, :], in_=src[:, 0:512]),
         reads=[src_key], writes=[kk + ("st",)])
    P.op("dve", lambda e: e.bn_stats(out=stats[:, 1, :], in_=src[:, 512:1024]),
         reads=[src_key], writes=[kk + ("st",)])
    P.op("dve", lambda e: e.bn_aggr(out=mv[:], in_=stats[:].rearrange("p a s -> p (a s)")),
         reads=[kk + ("st",)], writes=[kk + ("mv",)])
    P.op("dve", f_ts(rstd[:], mv[:, 1:2], LN_EPS, None, ALU.add),
         reads=[kk + ("mv",)], writes=[kk + ("rs0",)])
    P.op("pool", f_tt(rstd[:], rstd[:], scr["mhalf"][:, 0:1], ALU.pow),
         reads=[kk + ("rs0",), "mhalf"], writes=[kk + ("rs0",)])
    return k


def ln_apply(C, k, src, src_key, dst, dst_key, g_rep, b_rep, gb_key, scr):
    P = C.P
    mv, rstd, tmp = scr["mv"][k], scr["rstd"][k], scr["tmp"][k % 2]
    kk = ("ln", k)
    kt = ("ln", "tmp", k % 2)
    P.op("dve", f_ts(tmp[:], src, mv[:, 0:1], rstd[:, 0:1], ALU.subtract, ALU.mult),
         reads=[src_key, kk + ("mv",), kk + ("rs0",)], writes=[kt])
    P.op("pool", f_tt(tmp[:], tmp[:], g_rep[:], ALU.mult),
         reads=[kt, gb_key], writes=[kt])
    P.op("pool", f_tt(dst, tmp[:], b_rep[:], ALU.add),
         reads=[kt, gb_key], writes=[dst_key])


def layer_norm_tile(C, src, src_key, dst, dst_key, g_rep, b_rep, gb_key, scr):
    k = ln_stats(C, src, src_key, scr)
    ln_apply(C, k, src, src_key, dst, dst_key, g_rep, b_rep, gb_key, scr)


def ln_scratch(C):
    scr = {
        "stats": [C.sb(f"ln_stats{k}", [128, 2, 6], F32) for k in range(4)],
        "mv": [C.sb(f"ln_mv{k}", [128, 2], F32) for k in range(4)],
        "rstd": [C.sb(f"ln_rstd{k}", [128, 1], F32) for k in range(4)],
        "tmp": [C.sb(f"ln_tmp{k}", [128, 1024], F32) for k in range(2)],
        "mhalf": C.sb("ln_mhalf", [128, 1], F32),
    }
    C.P.op("dve", f_memset(scr["mhalf"][:], -0.5), writes=["mhalf"])
    return scr


def h_to_hT(C, hs, t, hT, ident, hb):
    P = C.P
    b = t % 2
    P.op("act", f_act(hb[b][:], hs[:, t, :], AF.Copy), reads=[("hs", t)], writes=[("hb", b)])
    transpose_tile(C, hb[b], ("hb", b), hT, t, ident, b, bank0=4)


def mlp(C, hs, hT, w_up_ap, w_down_ap, tagp):
    P = C.P
    NS = 8
    wu = [C.sb(f"{tagp}wu{i}", [128, 8, 512], BF16) for i in range(2)]
    wd = [C.sb(f"{tagp}wd{i}", [128, 4, 1024], BF16) for i in range(2)]
    uT1 = C.sb(f"{tagp}uT", [128, 4, TL], BF16)
    uT = [uT1, uT1]
    rl = [C.sb(f"{tagp}rl{i}", [128, 512], F32) for i in range(2)]
    cnt = 0
    dcnt = 0

    def wload(s):
        b = s % 2
        load_weight_bf16(C, wu[b], w_up_ap, (tagp, "wu", b), 512, col0=s * 512)
        for fc in range(4):
            P.op("pool", f_dma(wd[b][:, fc, :],
                               w_down_ap[s * 512 + fc * 128: s * 512 + (fc + 1) * 128, :]),
                 writes=[(tagp, "wd", b)], dsem=f"{tagp}wd{b}")

    wload(0)
    for s in range(NS):
        b = s % 2
        for g in range(4):
            for fc in range(4):
                bk = 2 + (cnt % 2)
                ps = C.bank[bk]
                for c in range(8):
                    P.op("pe", f_mm(ps[:], wu[b][:, c, fc * 128:(fc + 1) * 128],
                                    hT[:, c, g * 512:(g + 1) * 512], c == 0, c == 7),
                         reads=[(tagp, "wu", b)] + [("hT", 4 * g + a) for a in range(4)],
                         writes=[("bank", bk)])
                r = rl[cnt % 2]
                P.op("act", f_act(r[:], ps[:], AF.Relu), reads=[("bank", bk)],
                     writes=[(tagp, "rl", cnt % 2)])
                P.op("pool", f_tt(uT[b][:, fc, g * 512:(g + 1) * 512], r[:], r[:], ALU.mult),
                     reads=[(tagp, "rl", cnt % 2)], writes=[(tagp, "uT", g)])
                cnt += 1
        if s + 1 < NS:
            wload(s + 1)
        for t in range(NT):
            for half in range(2):
                bk = 4 + (dcnt % 4)
                ps = C.bank[bk]
                for fc in range(4):
                    P.op("pe", f_mm(ps[:], uT[b][:, fc, t * 128:(t + 1) * 128],
                                    wd[b][:, fc, half * 512:(half + 1) * 512], fc == 0, fc == 3),
                         reads=[(tagp, "uT", t // 4), (tagp, "wd", b)], writes=[("bank", bk)])
                dst = hs[:, t, half * 512:(half + 1) * 512]
                if s == 0:
                    P.op("dve", f_stt(dst, dst, ALPHA, ps[:], ALU.mult, ALU.add),
                         reads=[("bank", bk), ("hs", t)], writes=[("hs", t)])
                else:
                    P.op("dve", f_tt(dst, dst, ps[:], ALU.add),
                         reads=[("bank", bk), ("hs", t)], writes=[("hs", t)])
                dcnt += 1


def attention(C, cfg, q_d, k_d, v_d, G_d, b31, epilogue):
    P = C.P
    H, nm, KR, dv = cfg["H"], cfg["nm"], cfg["KR"], cfg["dv"]
    dv1 = dv + 1
    per_bank = 512 // dv1 if nm == 1 else 3
    Kb = [C.sb(f"Kb{i}", [128, S], BF16) for i in range(2)]
    Vb = [C.sb(f"Vb{i}", [128, 64, dv1], BF16) for i in range(2)]
    Qb = [C.sb(f"Qb{i}", [128, TL], BF16) for i in range(2)]
    Gb = [C.sb(f"Gb{i}", [128, nm, 640], F32) for i in range(2)]
    nslots = 3 if nm == 1 else 2
    depth = nslots - 1
    nPT = depth + 2
    acc0 = 2 * nslots
    nacc_banks = -(-(4 * nm) // per_bank)
    PT = [C.sb(f"PT{s_}", [128, 2, 512], BF16) for s_ in range(nPT)]
    accsb = [C.sb(f"accsb{i}", [128, nacc_banks, 512], F32) for i in range(2)]

    def rows(m):
        return (64 * m, 64 * m + 64) if nm == 2 else (0, KR)

    def mapidx(h, m):
        return 2 * h + m if nm == 2 else h

    def acc_bank(n):
        return acc0 + n // per_bank

    def acc_ap(n):
        off = (n % per_bank) * dv1
        return C.bank[acc_bank(n)][:, off:off + dv1]

    def acc_key(n):
        return ("bank", acc_bank(n))

    def load(h):
        p = h % 2
        m0 = mapidx(h, 0)
        P.op("sp", f_dma(Gb[p][:], G_d[:, m0:m0 + nm, :]), writes=[("G", p)], dsem=f"G{p}")
        P.op("sp", f_dma(Qb[p][0:KR, :], q_d[h]), writes=[("Q", p)], dsem=f"Q{p}")
        for r in range(4):
            P.op("sp", f_dma(Kb[p][0:KR, r * TL:(r + 1) * TL], k_d[r, h]),
                 writes=[("K", p)], dsem=f"K{p}")
        for r in range(4):
            P.op("sp", f_dma(Vb[p][:, r * 16:(r + 1) * 16, :], v_d[r, h]),
                 writes=[("V", p)], dsem=f"V{p}")

    def fold_bias(h):
        p = h % 2
        for m in range(nm):
            mi = mapidx(h, m)
            P.op("dve", f_ts(Gb[p][:, m, :], Gb[p][:, m, :], b31[:, mi:mi + 1], None, ALU.subtract),
                 reads=[("G", p), "b31"], writes=[("G", p)])

    def amin_of(u, c):
        return max(0, -((-(c - 16 * u - 3)) // 4))

    def afar_of(u, c):
        return min(4, max(0, -((-(c - 16 * u + 2)) // 4)))

    def kslot(c):
        return (c % 4) * 16 + c // 4

    items = []
    for h in range(H):
        for u in range(4):
            if nm == 2:
                for c in range(16 * u + 16):
                    items.append((h, u, ((c, 0), (c, 1))))
            else:
                for cp in range(8 * u + 8):
                    items.append((h, u, ((2 * cp, 0), (2 * cp + 1, 0))))

    def emit_qk_exp(i):
        h, u, subs = items[i]
        if i == 0 or items[i - 1][0] != h:
            fold_bias(h)
        p = h % 2
        slot = i % nslots
        psl = i % nPT
        bks = (2 * slot, 2 * slot + 1)
        amins = [amin_of(u, c) for c, m in subs]
        afars = [afar_of(u, c) for c, m in subs]
        for j, (c, m) in enumerate(subs):
            r0, r1 = rows(m)
            col = kslot(c) * 128
            amin = amins[j]
            P.op("pe", f_mm(C.bank[bks[j]][:, amin * 128:512], Kb[p][r0:r1, col:col + 128],
                            Qb[p][r0:r1, (4 * u + amin) * 128:(4 * u + 4) * 128], True, True),
                 reads=[("K", p), ("Q", p)], writes=[("bank", bks[j])])
        if nm == 2:
            c = subs[0][0]
            for a in range(amins[0], afars[0]):
                s_ = c - (16 * u + 4 * a) + 1
                P.op("dve", f_tt(C.psum[:, bks[0]:bks[0] + 2, a * 128:(a + 1) * 128],
                                 C.psum[:, bks[0]:bks[0] + 2, a * 128:(a + 1) * 128],
                                 Gb[p][:, :, s_ * 128:(s_ + 1) * 128], ALU.add),
                     reads=[("bank", bks[0]), ("bank", bks[1]), ("G", p)],
                     writes=[("bank", bks[0]), ("bank", bks[1])])
        else:
            for j, (c, m) in enumerate(subs):
                for a in range(amins[j], afars[j]):
                    s_ = c - (16 * u + 4 * a) + 1
                    P.op("dve", f_tt(C.bank[bks[j]][:, a * 128:(a + 1) * 128],
                                     C.bank[bks[j]][:, a * 128:(a + 1) * 128],
                                     Gb[p][:, 0, s_ * 128:(s_ + 1) * 128], ALU.add),
                         reads=[("bank", bks[j]), ("G", p)], writes=[("bank", bks[j])])
        am = min(amins)
        P.op("act", f_act(PT[psl][:, :, am * 128:512], C.psum[:, bks[0]:bks[0] + 2, am * 128:512], AF.Exp),
             reads=[("bank", bks[0]), ("bank", bks[1])],
             writes=[("PT", psl, j, a) for j in range(2) for a in range(am, 4)])

    def emit_pv(i):
        h, u, subs = items[i]
        p = h % 2
        psl = i % nPT
        for j, (c, m) in enumerate(subs):
            amin = amin_of(u, c)
            for a in range(amin, 4):
                n = m * 4 + a
                first = (c == 0) and (n % per_bank == 0)
                last = (c == 16 * u + 4 * a + 3)
                P.op("pe", f_mm(acc_ap(n), PT[psl][:, j, a * 128:(a + 1) * 128],
                                Vb[p][:, kslot(c), :], first, last, skip=True),
                     reads=[("PT", psl, j, a), ("V", p)], writes=[acc_key(n)])

    gcnt = [0]

    def evacuate_and_epilogue(h, u):
        k = gcnt[0] % 2
        gcnt[0] += 1
        for bi in range(nacc_banks):
            nin = min(per_bank, 4 * nm - bi * per_bank)
            P.op("dve", f_copy(accsb[k][:, bi, 0:nin * dv1], C.bank[acc0 + bi][:, 0:nin * dv1]),
                 reads=[("bank", acc0 + bi)], writes=[("accsb", k, bi)])

        def sb_ap(n):
            off = (n % per_bank) * dv1
            return accsb[k][:, n // per_bank, off:off + dv1]

        def sb_key(n):
            return ("accsb", k, n // per_bank)
        epilogue(h, u, sb_ap, sb_key)

    load(0)
    if H > 1:
        load(1)
    for i0 in range(min(depth, len(items))):
        emit_qk_exp(i0)
    for i in range(len(items)):
        h, u, subs = items[i]
        nxt = items[i + 1] if i + 1 < len(items) else None
        if i + depth < len(items):
            emit_qk_exp(i + depth)
        emit_pv(i)
        if subs[1][0] == 16 * u + 15:
            evacuate_and_epilogue(h, u)
        if nxt is not None and nxt[0] != h and h + 2 < H:
            load(h + 2)


def dram_in(nc, name, shape, dt):
    return nc.dram_tensor(name, list(shape), dt, kind="ExternalInput").ap()


def dram_out(nc, name, shape, dt):
    return nc.dram_tensor(name, list(shape), dt, kind="ExternalOutput").ap()


def fm_project(C, w, wkey, col0, hT, scale, tag, store):
    P = C.P
    stg = [C.sb(f"{tag}stg{i}", [128, 8, 512], BF16) for i in range(2)]
    cnt = 0
    for g in range(4):
        b = g % 2
        for f in range(8):
            bk = 2 + (cnt % 2)
            cnt += 1
            ps = C.bank[bk]
            for c in range(8):
                P.op("pe", f_mm(ps[:], w[:, c, col0 + f * 128: col0 + (f + 1) * 128],
                                hT[:, c, g * 512:(g + 1) * 512], c == 0, c == 7),
                     reads=[wkey] + [("hT", 4 * g + a) for a in range(4)], writes=[("bank", bk)])
            P.op("act", f_act(stg[b][:, f, :], ps[:], AF.Copy, scale=scale),
                 reads=[("bank", bk)], writes=[(tag, "stg", b)])
        store(g, stg[b], (tag, "stg", b), b)
    return stg


def v_project(C, wb, wkey, col0, hT, out_d, nh, dv, tag):
    P = C.P
    dv1 = dv + 1
    stg = [C.sb(f"{tag}vstg{i}", [128, nh, 4, dv1], BF16) for i in range(2)]
    for i in range(2):
        P.op("pool", f_memset(stg[i][:], 1.0), writes=[(tag, "vstg", i)])
    hpb = 512 // dv
    nhalf = nh * dv // 512
    cnt = 0
    for g in range(4):
        b = g % 2
        for a in range(4):
            t = 4 * g + a
            for half in range(nhalf):
                bk = 4 + (cnt % 2)
                cnt += 1
                ps = C.bank[bk]
                for c in range(8):
                    P.op("pe", f_mm(ps[:], hT[:, c, t * 128:(t + 1) * 128],
                                    wb[:, c, col0 + half * 512: col0 + (half + 1) * 512],
                                    c == 0, c == 7),
                         reads=[wkey, ("hT", t)], writes=[("bank", bk)])
                P.op("dve", f_copy(stg[b][:, half * hpb:(half + 1) * hpb, a, 0:dv],
                                   ps[:].rearrange("p (h e) -> p h e", e=dv)),
                     reads=[("bank", bk)], writes=[(tag, "vstg", b)])
        P.op("sp", f_dma(out_d.rearrange("h p t e -> p h (t e)")[:, :, 4 * g * dv1:(4 * g + 4) * dv1],
                         stg[b][:].rearrange("p h a e -> p h (a e)")),
             reads=[(tag, "vstg", b)], writes=[(tag, "dram")], dsem=f"{tag}vst{b}")


def out_proj_ln(C, ostore, wo, res_ap, hs, hT, ident, g_rep, b_rep, gb_key, scr, xt, hb, oT):
    P = C.P
    pend = []
    for t in range(NT):
        b = t % 2
        pT = C.bank_bf(b).rearrange("p (c n) -> p c n", n=128)
        for c in range(8):
            P.op("pe", f_tr(pT[:, c, :], ostore[:, t, c * 128:(c + 1) * 128], ident[:]),
                 reads=[("os", t), "ident"], writes=[("bank", b)])
        P.op("dve", f_copy(oT[b][:], pT), reads=[("bank", b)], writes=[("oT", b)])
        if res_ap is not None:
            P.op("sp", f_dma(xt[:], res_ap[t * 128:(t + 1) * 128, :]),
                 writes=[("B", "xt")], dsem="Bxt")
        for half in range(2):
            bk = 2 + half
            ps = C.bank[bk]
            for c in range(8):
                P.op("pe", f_mm(ps[:], oT[b][:, c, :], wo[:, c, half * 512:(half + 1) * 512],
                                c == 0, c == 7),
                     reads=[("oT", b), "wo"], writes=[("bank", bk)])
            dst = hs[:, t, half * 512:(half + 1) * 512]
            if res_ap is None:
                P.op("dve", f_stt(dst, dst, ALPHA, ps[:], ALU.mult, ALU.add),
                     reads=[("bank", bk), ("hs", t)], writes=[("hs", t)])
            else:
                P.op("dve", f_stt(dst, xt[:, half * 512:(half + 1) * 512], ALPHA, ps[:],
                                  ALU.mult, ALU.add),
                     reads=[("bank", bk), ("B", "xt")], writes=[("hs", t)])
        pend.append((t, ln_stats(C, hs[:, t, :], ("hs", t), scr)))
        if len(pend) > LN_SKEW:
            pt, pk = pend.pop(0)
            ln_apply(C, pk, hs[:, pt, :], ("hs", pt), hs[:, pt, :], ("hs", pt), g_rep, b_rep, gb_key, scr)
            h_to_hT(C, hs, pt, hT, ident, hb)
    for pt, pk in pend:
        ln_apply(C, pk, hs[:, pt, :], ("hs", pt), hs[:, pt, :], ("hs", pt), g_rep, b_rep, gb_key, scr)
        h_to_hT(C, hs, pt, hT, ident, hb)


def ln_all_tiles(C, hs, g_rep, b_rep, gb_key, scr, after):
    pend = []
    for t in range(NT):
        pend.append((t, ln_stats(C, hs[:, t, :], ("hs", t), scr)))
        if len(pend) > LN_SKEW:
            pt, pk = pend.pop(0)
            ln_apply(C, pk, hs[:, pt, :], ("hs", pt), hs[:, pt, :], ("hs", pt), g_rep, b_rep, gb_key, scr)
            after(pt)
    for pt, pk in pend:
        ln_apply(C, pk, hs[:, pt, :], ("hs", pt), hs[:, pt, :], ("hs", pt), g_rep, b_rep, gb_key, scr)
        after(pt)


def load_rep(C, tile, ap, key):
    C.P.op("sp", f_dma(tile[:], ap), writes=[key], dsem=_san(key))


class Scope:
    def __init__(self, C):
        self.C = C

    def __enter__(self):
        self.st = ExitStack()
        self.prev = (self.C.st, self.C.side)
        self.C.st, self.C.side = self.st, "right"
        return self

    def __exit__(self, *a):
        self.C.P.barrier()
        self.st.close()
        self.C.st, self.C.side = self.prev
        return False


def build_A():
    nc = bass.Bass("TRN2", target_bir_lowering=False)
    x = dram_in(nc, "x", [TL, D], F32)
    w_in = dram_in(nc, "w_in", [D, 3 * D], F32)
    ident_d = dram_in(nc, "ident", [128, 128], BF16)
    qT = dram_out(nc, "qT", [8, 128, TL], BF16)
    kT = dram_out(nc, "kT", [8, 128, TL], BF16)
    v = dram_out(nc, "v", [8, 128, NT, 129], BF16)
    with ExitStack() as st:
        C = Ctx(nc, st)
        P = C.P
        ident = C.sb("ident", [128, 128], BF16)
        P.op("sp", f_dma(ident[:], ident_d), writes=["ident"], dsem="ident")
        phase_A(C, x, w_in, ident, qT, kT, v)
        P.emit()
    return nc


def phase_A(C, x, w_in, ident, qT, kT, v):
    P = C.P
    with Scope(C):
        wb = C.sb("wb", [128, 8, 3 * D], BF16)
        load_weight_bf16(C, wb, w_in, "wb", 3 * D)
        hT = C.sb("hTA", [128, 8, TL], BF16)
        xt = [C.sb(f"xtA{i}", [128, D], F32) for i in range(2)]
        xb = [C.sb(f"xbA{i}", [128, D], BF16) for i in range(2)]
        make_hT_from_dram(C, x, hT, ident, xt, xb, "A")

        def mkstore(out_d, tag):
            def store(g, stg, key, b):
                P.op("sp", f_dma(out_d.rearrange("h p n -> p h n")[:, :, g * 512:(g + 1) * 512], stg[:]),
                     reads=[key], writes=[(tag, "dram")], dsem=f"{tag}st{b}")
            return store
        fm_project(C, wb, "wb", 0, hT, 0.125, "q", mkstore(qT, "q"))
        fm_project(C, wb, "wb", D, hT, 1.0, "k", mkstore(kT, "k"))
        v_project(C, wb, "wb", 2 * D, hT, v, 8, 128, "v")


def build_B():
    nc = bass.Bass("TRN2", target_bir_lowering=False)
    x = dram_in(nc, "x", [TL, D], F32)
    qT = dram_in(nc, "qT", [8, 128, TL], BF16)
    kT_all = dram_in(nc, "kT_all", [4, 8, 128, TL], BF16)
    v_all = dram_in(nc, "v_all", [4, 8, 128, NT, 129], BF16)
    cst = const_inputs(nc)
    prm = {
        "lam": dram_in(nc, "lam", [128, 256], F32),
        "subg": dram_in(nc, "subg", [128, 128], F32),
        "w_out0": dram_in(nc, "w_out0", [D, D], F32),
        "w_up0": dram_in(nc, "w_up0", [D, DFF], F32),
        "w_down0": dram_in(nc, "w_down0", [DFF, D], F32),
        "w_kv": dram_in(nc, "w_kv", [D, 2 * D], F32),
    }
    for k in ("ln1_g0", "ln1_b0", "ln2_g0", "ln2_b0"):
        prm[k] = dram_in(nc, k, [128, D], F32)
    h1 = dram_out(nc, "h1", [TL, D], F32)
    kT1 = dram_out(nc, "kT1", [16, 96, TL], BF16)
    v1 = dram_out(nc, "v1", [16, 128, NT, 65], BF16)
    with ExitStack() as st:
        C = Ctx(nc, st)
        P = C.P
        cs = load_consts(C, cst)
        hs, hT, hT_stack = phase_B(C, cs, cst, prm, x, qT, kT_all, v_all, kT1, v1)
        st.callback(hT_stack.close)
        for t in range(NT):
            P.op("sp", f_dma(h1[t * 128:(t + 1) * 128, :], hs[:, t, :]), reads=[("hs", t)],
                 dsem=f"h1st{t % 2}")
        P.emit()
    return nc


def const_inputs(nc):
    return {
        "G": dram_in(nc, "G", [128, 16, 640], F32),
        "b31": dram_in(nc, "b31", [128, 16], F32),
        "ident": dram_in(nc, "ident", [128, 128], BF16),
        "e_own": dram_in(nc, "e_own", [32, TL], BF16),
        "past30k": dram_in(nc, "past30k", [128, NT, 32], F32),
    }


def load_consts(C, cst):
    P = C.P
    ident = C.sb("ident", [128, 128], BF16)
    P.op("sp", f_dma(ident[:], cst["ident"]), writes=["ident"], dsem="ident")
    b31 = C.sb("b31", [128, 16], F32)
    P.op("sp", f_dma(b31[:], cst["b31"]), writes=["b31"], dsem="b31")
    scr = ln_scratch(C)
    hb = [C.sb(f"hb{i}", [128, D], BF16) for i in range(2)]
    rep = [C.sb(f"rep{i}", [128, D], F32) for i in range(2)]
    kmT = C.sb("kmT", [128, 8, 32], BF16)
    return {"ident": ident, "b31": b31, "scr": scr, "hb": hb, "rep": rep, "kmT": kmT}


def alloc_hT(C):
    stk = ExitStack()
    hT = stk.enter_context(C.nc.sbuf_tensor("s_hT%d" % C.uid, [128, 8, TL], BF16, side="left"))
    C.uid += 1
    return hT, stk


def phase_B(C, cs, cst, prm, x, qT, kT_all, v_all, kT1, v1):
    P = C.P
    ident, b31, scr, hb, rep = cs["ident"], cs["b31"], cs["scr"], cs["hb"], cs["rep"]
    with Scope(C):
        ostore = C.sb("ostore", [128, NT, D], BF16)
        lam_sb = C.sb("lam_sb", [128, 256], F32)
        lprod = C.sb("lprod", [128, 2, 64], F32)
        lsum = C.sb("lsum", [128, 2], F32)
        lexp = C.sb("lexp", [128, 2], F32)
        nl = C.sb("nl", [128, 2], F32)
        gsub = C.sb("gsub", [128, 128], F32)
        P.op("sp", f_dma(lam_sb[:], prm["lam"]), writes=["lam_sb"], dsem="lam")
        P.op("sp", f_dma(gsub[:], prm["subg"]), writes=["gsub"], dsem="gsub")
        lv = lam_sb[:].rearrange("p (a b d) -> p a b d", a=2, b=2)
        P.op("dve", f_tt(lprod[:], lv[:, :, 0, :], lv[:, :, 1, :], ALU.mult),
             reads=["lam_sb"], writes=["lprod"])
        P.op("dve", lambda e: e.reduce_sum(out=lsum[:], in_=lprod[:], axis=AX.X),
             reads=["lprod"], writes=["lsum"])
        P.op("act", f_act(lexp[:], lsum[:], AF.Exp), reads=["lsum"], writes=["lexp"])
        P.op("dve", f_tt(nl[:, 0:1], lexp[:, 1:2], lexp[:, 0:1], ALU.subtract),
             reads=["lexp"], writes=["nl0"])
        P.op("dve", f_ts(nl[:, 1:2], nl[:, 0:1], -LAM_INIT0, None, ALU.add),
             reads=["nl0"], writes=["neglam"])
        neglam = nl[:, 1:2]
        P.op("dve", f_ts(gsub[:], gsub[:], 1.0 - LAM_INIT0, None, ALU.mult),
             reads=["gsub"], writes=["gsub"])
        ep = {k: [C.sb(f"ep_{k}{i}", shp, F32) for i in range(2)]
              for k, shp in (("rl", [128, 2]), ("c1", [128, 1]), ("t1", [128, 128]),
                             ("o", [128, 128]), ("sq", [128, 128]), ("ss", [128, 1]))}
        ecnt = [0]

        def epilogue(h, u, acc_ap, acc_key):
            for a in range(4):
                t = 4 * u + a
                k = ecnt[0] % 2
                ecnt[0] += 1
                rl, c1, t1, o, sq, ss = (ep[x_][k] for x_ in ("rl", "c1", "t1", "o", "sq", "ss"))
                a0, a1 = acc_ap(a), acc_ap(4 + a)
                k0, k1 = acc_key(a), acc_key(4 + a)
                P.op("dve", lambda e, rl=rl, a0=a0: e.reciprocal(out=rl[:, 0:1], in_=a0[:, 128:129]),
                     reads=[k0], writes=[("ep", k, "rl0")])
                P.op("dve", lambda e, rl=rl, a1=a1: e.reciprocal(out=rl[:, 1:2], in_=a1[:, 128:129]),
                     reads=[k1], writes=[("ep", k, "rl1")])
                P.op("dve", f_tt(c1[:], rl[:, 1:2], neglam, ALU.mult),
                     reads=[("ep", k, "rl1"), "neglam"], writes=[("ep", k, "c1")])
                P.op("dve", f_ts(t1[:], a1[:, 0:128], c1[:, 0:1], None, ALU.mult),
                     reads=[k1, ("ep", k, "c1")], writes=[("ep", k, "t1")])
                P.op("dve", f_stt(o[:], a0[:, 0:128], rl[:, 0:1], t1[:], ALU.mult, ALU.add),
                     reads=[k0, ("ep", k, "rl0"), ("ep", k, "t1")], writes=[("ep", k, "o")])
                P.op("pool", f_tt(sq[:], o[:], o[:], ALU.mult), reads=[("ep", k, "o")],
                     writes=[("ep", k, "sq")])
                P.op("dve", lambda e, ss=ss, sq=sq: e.reduce_sum(out=ss[:], in_=sq[:], axis=AX.X),
                     reads=[("ep", k, "sq")], writes=[("ep", k, "ss")])
                P.op("dve", f_ts(ss[:], ss[:], 1.0 / 128.0, LN_EPS, ALU.mult, ALU.add),
                     reads=[("ep", k, "ss")], writes=[("ep", k, "ss")])
                P.op("pool", f_tt(ss[:], ss[:], scr["mhalf"][:, 0:1], ALU.pow),
                     reads=[("ep", k, "ss"), "mhalf"], writes=[("ep", k, "ss")])
                P.op("dve", f_stt(ostore[:, t, h * 128:(h + 1) * 128], o[:], ss[:, 0:1], gsub[:],
                                  ALU.mult, ALU.mult),
                     reads=[("ep", k, "o"), ("ep", k, "ss"), "gsub"], writes=[("os", t)])

        with Scope(C):
            attention(C, dict(H=8, nm=2, KR=128, dv=128), qT, kT_all, v_all, cst["G"], b31, epilogue)
        C.side = "left"
        hs = C.nc.sbuf_tensor("s_hs", [128, NT, D], F32, side="left")
        hs = C.root.enter_context(hs)
        hT, hT_stack = alloc_hT(C)
        C.side = "right"
        with Scope(C):
            wo = C.sb("wo", [128, 8, D], BF16)
            load_weight_bf16(C, wo, prm["w_out0"], "wo", D)
            xt = C.sb("xtB", [128, D], F32)
            oT = [C.sb(f"oT{i}", [128, 8, 128], BF16) for i in range(2)]
            load_rep(C, rep[0], prm["ln1_g0"], "rep0")
            load_rep(C, rep[1], prm["ln1_b0"], "rep1")
            out_proj_ln(C, ostore, wo, x, hs, hT, ident, rep[0], rep[1], "rep0", scr, xt, hb, oT)
    with Scope(C):
        mlp(C, hs, hT, prm["w_up0"], prm["w_down0"], "m0")
    load_rep(C, rep[0], prm["ln2_g0"], "rep0")
    load_rep(C, rep[1], prm["ln2_b0"], "rep1")
    with Scope(C):
        wkv = C.sb("wk1", [128, 8, D], BF16)
        load_weight_bf16(C, wkv, prm["w_kv"], "wkv", D)
        wv1 = C.sb("wv1", [128, 8, D], BF16)
        load_weight_bf16(C, wv1, prm["w_kv"], "wv1", D, col0=D)
        ln_all_tiles(C, hs, rep[0], rep[1], "rep0", scr, lambda t: h_to_hT(C, hs, t, hT, ident, hb))

        def store(g, stg, key, b):
            kv = kT1.rearrange("(f two) r n -> two r f n", two=2)
            for hh in range(2):
                P.op("sp", f_dma(kv[hh, 0:64, :, g * 512:(g + 1) * 512], stg[hh * 64:(hh + 1) * 64, :, :]),
                     reads=[key], writes=[("k1dram", hh)], dsem=f"k1st{b}")
        fm_project(C, wkv, "wkv", 0, hT, 1.0, "k1", store)
        for h in range(16):
            P.op("sp", f_dma(kT1[h, 64:96, :], cst["e_own"]), writes=[("k1dramE",)], dsem="k1e")
        v_project(C, wv1, "wv1", 0, hT, v1, 16, 64, "v1")
    return hs, hT, hT_stack


def build_C(stop_after=None):
    nc = bass.Bass("TRN2", target_bir_lowering=False)
    h1 = dram_in(nc, "h1", [TL, D], F32)
    kT1_all = dram_in(nc, "kT1_all", [4, 16, 96, TL], BF16)
    v1_all = dram_in(nc, "v1_all", [4, 16, 128, NT, 65], BF16)
    cst = const_inputs(nc)
    prm = {
        "w_q": dram_in(nc, "w_q", [D, D], F32),
        "w_out1": dram_in(nc, "w_out1", [D, D], F32),
        "w_up1": dram_in(nc, "w_up1", [D, DFF], F32),
        "w_down1": dram_in(nc, "w_down1", [DFF, D], F32),
    }
    for k in ("ln1_g1", "ln1_b1", "ln2_g1", "ln2_b1"):
        prm[k] = dram_in(nc, k, [128, D], F32)
    out = dram_out(nc, "out", [TL, D], F32)
    qaug = nc.dram_tensor("qaug", [16, 96, TL], BF16,
                          kind="ExternalOutput" if stop_after else "Internal").ap()
    with ExitStack() as st:
        C = Ctx(nc, st)
        C.stop_after = stop_after
        P = C.P
        cs = load_consts(C, cst)
        hs = C.sb("hs", [128, NT, D], F32)
        hT, hT_stack = alloc_hT(C)
        for t in range(NT):
            P.op("sp", f_dma(hs[:, t, :], h1[t * 128:(t + 1) * 128, :]), writes=[("hs", t)],
                 dsem=f"hsld{t}")
            h_to_hT(C, hs, t, hT, cs["ident"], cs["hb"])
        phase_C(C, cs, cst, prm, kT1_all, v1_all, qaug, hs, hT, hT_stack, out)
        P.emit()
    return nc


def phase_C(C, cs, cst, prm, kT1_all, v1_all, qaug, hs, hT, hT_stack, out):
    P = C.P
    ident, b31, scr, hb, rep = cs["ident"], cs["b31"], cs["scr"], cs["hb"], cs["rep"]
    kmT = cs["kmT"]
    with Scope(C):
        moba_kmean(C, kT1_all, kmT)
    if getattr(C, "stop_after", None) == "kmean":
        P.op("sp", f_dma(qaug.rearrange("h r n -> r h n")[0:64, 0:8, 0:32], kmT[0:64, :, :]),
             reads=[("kmT", f) for f in range(8)], dsem="dbg1")
        P.op("sp", f_dma(qaug.rearrange("h r n -> r h n")[0:64, 8:16, 0:32], kmT[64:128, :, :]),
             reads=[("kmT", f) for f in range(8)], dsem="dbg2")
        for t in range(NT):
            P.op("sp", f_dma(out[t * 128:(t + 1) * 128, :], hs[:, t, :]), reads=[("hs", t)],
                 dsem=f"outst{t % 2}")
        hT_stack.close()
        return
    with Scope(C):
        moba_gate_qproj(C, hT, prm["w_q"], kmT, cst["past30k"], qaug, ident)
    hT_stack.close()
    hT = None
    if getattr(C, "stop_after", None) == "gate":
        for t in range(NT):
            P.op("sp", f_dma(out[t * 128:(t + 1) * 128, :], hs[:, t, :]), reads=[("hs", t)],
                 dsem=f"outst{t % 2}")
        return
    with Scope(C):
        ostore = C.sb("ostore1", [128, NT, D], BF16)
        eprl = [C.sb(f"eprl{i}", [128, 1], F32) for i in range(8)]

        def epilogue(h, u, acc_ap, acc_key):
            for a in range(4):
                t = 4 * u + a
                k = (h * 4 + u) % 2
                rl = eprl[4 * k + a]
                acc = acc_ap(a)
                P.op("dve", lambda e, rl=rl, acc=acc: e.reciprocal(out=rl[:], in_=acc[:, 64:65]),
                     reads=[acc_key(a)], writes=[("eprl", 4 * k + a)])
                P.op("dve", f_ts(ostore[:, t, h * 64:(h + 1) * 64], acc[:, 0:64], rl[:, 0:1], None,
                                 ALU.mult),
                     reads=[acc_key(a), ("eprl", 4 * k + a)], writes=[("os", t)])

        with Scope(C):
            attention(C, dict(H=16, nm=1, KR=96, dv=64), qaug, kT1_all, v1_all, cst["G"], b31, epilogue)
        hT, hT_stack2 = alloc_hT(C)
        C.root.callback(hT_stack2.close)
        with Scope(C):
            wo = C.sb("wo1", [128, 8, D], BF16)
            load_weight_bf16(C, wo, prm["w_out1"], "wo", D)
            oT = [C.sb(f"oT1{i}", [128, 8, 128], BF16) for i in range(2)]
            load_rep(C, rep[0], prm["ln1_g1"], "rep0")
            load_rep(C, rep[1], prm["ln1_b1"], "rep1")
            out_proj_ln(C, ostore, wo, None, hs, hT, ident, rep[0], rep[1], "rep0", scr, None, hb, oT)
    with Scope(C):
        mlp(C, hs, hT, prm["w_up1"], prm["w_down1"], "m1")
    load_rep(C, rep[0], prm["ln2_g1"], "rep0")
    load_rep(C, rep[1], prm["ln2_b1"], "rep1")
    ln_all_tiles(C, hs, rep[0], rep[1], "rep0", scr,
                 lambda t: P.op("sp", f_dma(out[t * 128:(t + 1) * 128, :], hs[:, t, :]),
                                reads=[("hs", t)], dsem=f"outst{t % 2}"))


def moba_kmean(C, kT1_all, kmT):
    P = C.P
    Kp = [C.sb(f"Kp{i}", [128, S], BF16) for i in range(2)]
    ksum = [C.sb(f"ksum{i}", [128, 64], F32) for i in range(2)]
    kmf = [C.sb(f"kmf{i}", [128, 16, 2], F32) for i in range(2)]
    for f in range(8):
        b = f % 2
        for r in range(4):
            for hh in range(2):
                P.op("sp", f_dma(Kp[b][hh * 64:(hh + 1) * 64, r * TL:(r + 1) * TL],
                                 kT1_all[r, 2 * f + hh, 0:64, :]),
                     writes=[("Kp", b)], dsem=f"Kp{b}")
        P.op("dve", lambda e, b=b: e.reduce_sum(out=ksum[b][:],
                                                in_=Kp[b][:].rearrange("p (s q) -> p s q", q=128),
                                                axis=AX.X),
             reads=[("Kp", b)], writes=[("ksum", b)])
        kv = ksum[b][:].rearrange("p (r t) -> p r t", t=16)
        for par in range(2):
            P.op("dve", f_tt(kmf[b][:, :, par], kv[:, 2 * par, :], kv[:, 2 * par + 1, :], ALU.add),
                 reads=[("ksum", b)], writes=[("kmf", b, par)])
        P.op("dve", f_ts(kmT[:, f, :], kmf[b][:].rearrange("p t two -> p (t two)"), 1.0 / 256.0,
                         None, ALU.mult),
             reads=[("kmf", b, 0), ("kmf", b, 1)], writes=[("kmT", f)])


def moba_gate_qproj(C, hT, w_q, kmT, past_d, qaug, ident):
    P = C.P
    wq = C.sb("wq", [128, 8, D], BF16)
    load_weight_bf16(C, wq, w_q, "wq", D)
    past = C.sb("past", [128, NT, 32], F32)
    padd = C.sb("padd", [128, NT, 32], F32)
    P.op("sp", f_dma(past[:], past_d), writes=["past"], dsem="past")
    P.op("dve", f_ts(padd[:], past[:], NEG, None, ALU.add), reads=["past"], writes=["padd"])
    mstg = [C.sb(f"mstg{i}", [128, 16, 128], BF16) for i in range(2)]
    gm = [C.sb(f"gm{i}", [128, 16, 32], F32) for i in range(2)]
    top8 = [C.sb(f"top8{i}", [128, 16, 8], F32) for i in range(2)]
    sel = [C.sb(f"sel{i}", [128, 16, 32], F32) for i in range(2)]
    negpad = [C.sb(f"negpad{i}", [128, 16, 96], BF16) for i in range(2)]
    for i in range(2):
        P.op("pool", f_memset(negpad[i][:], 0.0), writes=[("negpad", i)])

    def store(g, stg, key, b):
        qv = qaug.rearrange("(f two) r n -> two r f n", two=2)
        for hh in range(2):
            P.op("sp", f_dma(qv[hh, 0:64, :, g * 512:(g + 1) * 512], stg[hh * 64:(hh + 1) * 64, :, :]),
                 reads=[key], writes=[("qaug", "q")], dsem=f"qst{b}")
        for a in range(4):
            t = 4 * g + a
            k = t % 2
            gbk = (4, 5) if k == 0 else (0, 1)
            for h in range(16):
                r0 = (h % 2) * 64
                gps = C.bank[gbk[h % 2]]
                P.op("pe", f_mm(gps[:, (h // 2) * 32:(h // 2 + 1) * 32],
                                stg[r0:r0 + 64, h // 2, a * 128:(a + 1) * 128],
                                kmT[r0:r0 + 64, h // 2, :], True, True),
                     reads=[key, ("kmT", h // 2)], writes=[("bank", gbk[h % 2])])
            gmv = gm[k][:].rearrange("p (f two) n -> p two f n", two=2)
            for par in range(2):
                P.op("dve", f_tt(gmv[:, par], C.bank[gbk[par]][:, 0:256].rearrange("p (f n) -> p f n", n=32),
                                 padd[:, t, :].unsqueeze(1).broadcast_to([128, 8, 32]), ALU.add),
                     reads=[("bank", gbk[par]), "padd"], writes=[("gm", k)])
            for h in range(16):
                P.op("dve", lambda e, k=k, h=h: e.max(out=top8[k][:, h, :], in_=gm[k][:, h, :]),
                     reads=[("gm", k)], writes=[("top8", k)])
            P.op("dve", f_tt(sel[k][:], gm[k][:], top8[k][:, :, 2:3].broadcast_to([128, 16, 32]),
                             ALU.is_ge),
                 reads=[("gm", k), ("top8", k)], writes=[("sel", k)])
            P.op("dve", f_ts(sel[k][:], sel[k][:], -1.0, None, ALU.add),
                 reads=[("sel", k)], writes=[("sel", k)])
            P.op("dve", f_tt(negpad[k][:, :, 64:96], sel[k][:],
                             past[:, t, :].unsqueeze(1).broadcast_to([128, 16, 32]), ALU.mult),
                 reads=[("sel", k), "past"], writes=[("negpad", k)])
            for hq in range(2):
                bk = 6 + hq
                tp = C.bank_bf(bk).rearrange("p (h n) -> p h n", n=128)
                for h8 in range(8):
                    h = hq * 8 + h8
                    P.op("pe", f_tr(tp[0:96, h8, :], negpad[k][:, h, :], ident[:]),
                         reads=[("negpad", k), "ident"], writes=[("bank", bk)])
                P.op("act", f_act(mstg[k][64:96, hq * 8:(hq + 1) * 8, :], tp[64:96, :, :], AF.Copy),
                     reads=[("bank", bk)], writes=[("mstg", k)])
            P.op("sp", f_dma(qaug.rearrange("h r n -> r h n")[64:96, :, t * 128:(t + 1) * 128],
                             mstg[k][64:96, :, :]),
                 reads=[("mstg", k)], writes=[("qaug", "m")], dsem=f"mst{k}")

    fm_project(C, wq, "wq", 0, hT, 0.125, "qb", store)


def _bucket_np(n):
    n = np.maximum(n, 0)
    nf = np.maximum(n, 16).astype(np.float32)
    large = 16 + (np.log(nf / np.float32(16)) / np.float32(math.log(8.0)) * np.float32(16)).astype(np.int32)
    large = np.minimum(large, 31)
    return np.where(n < 16, n, large)


def _g_index(j):
    k = np.arange(128)[:, None, None]
    q = np.arange(128)[None, None, :]
    e = (np.arange(5) - 1)[None, :, None]
    delta = e - j
    rel = q - k - 128 * delta
    idx = _bucket_np(rel)
    idx = np.where(rel >= 0, idx, 32)
    return np.broadcast_to(idx, (128, 5, 128)).astype(np.int64)


_CACHE = {}


def _get(name, fn):
    if name not in _CACHE:
        _CACHE[name] = fn()
    return _CACHE[name]


def _host_prep(x, rel_bias):
    f32 = np.float32
    bf = ml_dtypes.bfloat16
    table_ext = np.concatenate([rel_bias, np.full((1, 16), NEG, f32)], axis=0)
    per = []
    for c in range(NCORES):
        b, j = c // 4, c % 4
        xs = np.ascontiguousarray(x[b].reshape(64, 128, D)[j::4].reshape(TL, D))
        G = table_ext[_g_index(j)]
        G = np.ascontiguousarray(G.transpose(0, 3, 1, 2).reshape(128, 16, 640))
        blk = (4 * np.arange(NT) + j) // 2
        e = np.zeros((32, NT, 128), f32)
        e[blk, np.arange(NT), :] = 1.0
        past = (np.arange(32)[None, :] < blk[:, None]).astype(f32) * f32(30000.0)
        per.append({
            "x": xs, "G": G, "e_own": e.reshape(32, TL).astype(bf),
            "past30k": np.ascontiguousarray(np.broadcast_to(past[None], (128, NT, 32))),
            "b31": np.ascontiguousarray(np.broadcast_to(rel_bias[31:32, :], (128, 16))),
            "ident": np.eye(128, dtype=f32).astype(bf),
        })
    return per


def _rep(v, n=128):
    v = np.asarray(v, np.float32).reshape(1, -1)
    return np.ascontiguousarray(np.broadcast_to(v, (n, v.shape[1])))


CONST_KEYS = ("G", "b31", "ident", "e_own", "past30k")


def kernel(x, w_in_a, lam_a, subln_a, w_out_a, w_kv_shared, w_q_b, w_out_b, rel_bias,
           ln1_g, ln1_b, ln2_g, ln2_b, w_up, w_down):
    f32 = np.float32
    A = lambda v: np.ascontiguousarray(np.asarray(v, f32))
    x = A(x)
    rel_bias = A(rel_bias)
    cores = list(range(NCORES))
    per = _host_prep(x, rel_bias)
    ncA = _get("A", build_A)
    inA = [{"x": per[c]["x"], "w_in": A(w_in_a[0]), "ident": per[c]["ident"]} for c in cores]
    rA = run_bass_kernel_spmd(ncA, inA, core_ids=cores).results
    ncB = _get("B", build_B)
    prmB = {
        "lam": _rep(np.asarray(lam_a[0], f32).reshape(-1)), "subg": _rep(subln_a[0]),
        "w_out0": A(w_out_a[0]), "w_up0": A(w_up[0]), "w_down0": A(w_down[0]),
        "w_kv": A(w_kv_shared),
        "ln1_g0": _rep(ln1_g[0]), "ln1_b0": _rep(ln1_b[0]),
        "ln2_g0": _rep(ln2_g[0]), "ln2_b0": _rep(ln2_b[0]),
    }
    inB = []
    for c in cores:
        grp = [4 * (c // 4) + r for r in range(4)]
        d = {"x": per[c]["x"], "qT": rA[c]["qT"],
             "kT_all": np.stack([rA[g]["kT"] for g in grp]),
             "v_all": np.stack([rA[g]["v"] for g in grp])}
        d.update({k: per[c][k] for k in CONST_KEYS})
        d.update(prmB)
        inB.append(d)
    rB = run_bass_kernel_spmd(ncB, inB, core_ids=cores).results
    ncC = _get("C", build_C)
    prmC = {
        "w_q": A(w_q_b[0]), "w_out1": A(w_out_b[0]), "w_up1": A(w_up[1]), "w_down1": A(w_down[1]),
        "ln1_g1": _rep(ln1_g[1]), "ln1_b1": _rep(ln1_b[1]),
        "ln2_g1": _rep(ln2_g[1]), "ln2_b1": _rep(ln2_b[1]),
    }
    inC = []
    for c in cores:
        grp = [4 * (c // 4) + r for r in range(4)]
        d = {"h1": rB[c]["h1"],
             "kT1_all": np.stack([rB[g]["kT1"] for g in grp]),
             "v1_all": np.stack([rB[g]["v1"] for g in grp])}
        d.update({k: per[c][k] for k in CONST_KEYS})
        d.update(prmC)
        inC.append(d)
    rC = run_bass_kernel_spmd(ncC, inC, core_ids=cores).results
    out = np.empty((2, 64, 128, D), f32)
    for c in cores:
        b, j = c // 4, c % 4
        out[b, j::4] = np.asarray(rC[c]["out"], f32).reshape(NT, 128, D)
    return out.reshape(2, S, D)
```

```python
import math
from contextlib import ExitStack

import numpy as np
import ml_dtypes
import concourse.bass as bass
import concourse.mybir as mybir
from concourse.bass_utils import run_bass_kernel_spmd

F32 = mybir.dt.float32
BF16 = mybir.dt.bfloat16
AF = mybir.ActivationFunctionType
ALU = mybir.AluOpType
AX = mybir.AxisListType

NCORES = 8
D = 1024
S = 8192
NT = 16
TL = 2048
DFF = 4096
ALPHA = 4.0 ** 0.25
LN_EPS = 1e-5
LAM_INIT0 = 0.8 - 0.6 * math.exp(-0.3 * 0)
NEG = -30000.0
LN_SKEW = 3

ENGS = ("pe", "act", "dve", "pool", "sp")


class Op:
    __slots__ = ("fn", "deps", "dsem", "seq", "waits")

    def __init__(self, fn, deps, dsem):
        self.fn = fn
        self.deps = deps
        self.dsem = dsem
        self.seq = None
        self.waits = []


class Prog:
    def __init__(self, nc):
        self.nc = nc
        self.ops = {e: [] for e in ENGS}
        self.last_w = {}
        self.readers = {}

    def op(self, eng, fn, reads=(), writes=(), dsem=None):
        deps = set()
        for k in reads:
            w = self.last_w.get(k)
            if w is not None:
                deps.add(w)
        for k in writes:
            w = self.last_w.get(k)
            if w is not None:
                deps.add(w)
            rd = self.readers.get(k)
            if rd:
                for e2, i2 in rd.items():
                    deps.add((e2, i2))
        idx = len(self.ops[eng])
        me = (eng, idx)
        deps.discard(me)
        self.ops[eng].append(Op(fn, deps, dsem))
        for k in reads:
            self.readers.setdefault(k, {})[eng] = idx
        for k in writes:
            self.last_w[k] = me
            self.readers[k] = {}
        return me

    def barrier(self):
        deps = set()
        for e in ENGS:
            ops = self.ops[e]
            last_real = None
            seen_d = set()
            for i in range(len(ops) - 1, -1, -1):
                o = ops[i]
                if o.fn is None:
                    continue
                if o.dsem is not None:
                    if o.dsem not in seen_d:
                        seen_d.add(o.dsem)
                        deps.add((e, i))
                elif last_real is None:
                    last_real = i
                    deps.add((e, i))
        for e in ENGS:
            self.ops[e].append(Op(None, set(deps), None))

    def resolve(self):
        needed = set()
        for e in ENGS:
            for o in self.ops[e]:
                for d in o.deps:
                    if d[0] == e and e == "pe":
                        continue
                    needed.add(d)
        self.sem_names = set()
        cnt = {}
        for e in ENGS:
            for i, o in enumerate(self.ops[e]):
                if o.dsem is not None:
                    name = "d_" + o.dsem
                    cnt[name] = cnt.get(name, 0) + 16
                    o.seq = (name, cnt[name])
                    self.sem_names.add(name)
                elif (e, i) in needed:
                    name = "e_" + e
                    cnt[name] = cnt.get(name, 0) + 1
                    o.seq = (name, cnt[name])
                    self.sem_names.add(name)
        self.finals = dict(cnt)
        for e in ENGS:
            seen = {}
            for o in self.ops[e]:
                req = {}
                for d in o.deps:
                    if d[0] == e and e == "pe":
                        continue
                    name, val = self.ops[d[0]][d[1]].seq
                    if req.get(name, 0) < val:
                        req[name] = val
                for name, val in req.items():
                    if seen.get(name, 0) < val:
                        seen[name] = val
                        o.waits.append((name, val))

    def check(self):
        sem = {}
        pc = {e: 0 for e in ENGS}
        n = {e: len(self.ops[e]) for e in ENGS}
        progress = True
        while progress:
            progress = False
            for e in ENGS:
                while pc[e] < n[e]:
                    o = self.ops[e][pc[e]]
                    if all(sem.get(nm, 0) >= v for nm, v in o.waits):
                        if o.seq is not None:
                            nm = o.seq[0]
                            sem[nm] = sem.get(nm, 0) + (16 if nm.startswith("d_") else 1)
                        pc[e] += 1
                        progress = True
                    else:
                        break
        stuck = {e: (pc[e], n[e]) for e in ENGS if pc[e] < n[e]}
        if stuck:
            msg = []
            for e, (p, m) in stuck.items():
                o = self.ops[e][p]
                msg.append(f"{e}@{p}/{m} waits {[(nm, v, sem.get(nm, 0)) for nm, v in o.waits]}")
            raise RuntimeError("sync deadlock: " + "; ".join(msg))
        return {e: n[e] for e in ENGS}, len(self.sem_names)

    def emit(self):
        nc = self.nc
        self.resolve()
        print("prog ops/sems:", self.check(), flush=True)
        with ExitStack() as st:
            sems = {}
            for name in sorted(self.sem_names):
                sems[name] = st.enter_context(nc.semaphore(name))
            block = st.enter_context(nc.Block())

            def run(engname):
                def body(engine):
                    for o in self.ops[engname]:
                        if o.fn is None:
                            for (name, val) in o.waits:
                                engine.wait_ge(sems[name], val)
                            continue
                        for (name, val) in o.waits[1:]:
                            engine.wait_ge(sems[name], val)
                        ins = o.fn(engine)
                        if o.waits:
                            ins._wait_ge(sems[o.waits[0][0]], o.waits[0][1])
                        if o.seq is not None:
                            name = o.seq[0]
                            ins.then_inc(sems[name], 16 if name.startswith("d_") else 1)
                    if engname == "sp":
                        for name, val in self.finals.items():
                            if name.startswith("d_"):
                                engine.wait_ge(sems[name], val)
                return body

            block.tensor(run("pe"))
            block.scalar(run("act"))
            block.vector(run("dve"))
            block.gpsimd(run("pool"))
            block.sync(run("sp"))


def f_mm(out, lhsT, rhs, start, stop, skip=False):
    return lambda e: e.matmul(out, lhsT=lhsT, rhs=rhs, start=start, stop=stop,
                              skip_group_check=skip)


def f_tr(out, in_, ident):
    return lambda e: e.transpose(out=out, in_=in_, identity=ident)


def f_act(out, in_, func, bias=None, scale=1.0):
    if bias is None:
        return lambda e: e.activation(out=out, in_=in_, func=func, scale=scale)
    return lambda e: e.activation(out=out, in_=in_, func=func, bias=bias, scale=scale)


def f_tt(out, in0, in1, op):
    return lambda e: e.tensor_tensor(out=out, in0=in0, in1=in1, op=op)


def f_ts(out, in0, s1, s2, op0, op1=None):
    if op1 is None:
        return lambda e: e.tensor_scalar(out=out, in0=in0, scalar1=s1, scalar2=None, op0=op0)
    return lambda e: e.tensor_scalar(out=out, in0=in0, scalar1=s1, scalar2=s2, op0=op0, op1=op1)


def f_stt(out, in0, scalar, in1, op0, op1):
    return lambda e: e.scalar_tensor_tensor(out=out, in0=in0, scalar=scalar, in1=in1,
                                            op0=op0, op1=op1)


def f_copy(out, in_):
    return lambda e: e.tensor_copy(out=out, in_=in_)


def f_dma(out, in_):
    return lambda e: e.dma_start(out=out, in_=in_)


def f_memset(ap, v):
    return lambda e: e.memset(ap, v)


def _san(key):
    return "".join(ch for ch in str(key) if ch.isalnum())


class Ctx:
    def __init__(self, nc, st):
        self.nc = nc
        self.st = st
        self.P = Prog(nc)
        self.psum = st.enter_context(nc.psum_tensor("psum_all", [128, 8, 512], F32))
        self.bank = [self.psum[:, i, :] for i in range(8)]
        self.uid = 0
        self.side = "left"
        self.root = st

    def sb(self, name, shape, dt):
        return self.st.enter_context(self.nc.sbuf_tensor("s_" + name, shape, dt, side=self.side))

    def bank_bf(self, i):
        return self.bank[i].bitcast(BF16)


def load_weight_bf16(C, wt, w_ap, key, ncols, col0=0, row0=0, nchunks=8):
    P = C.P
    step = 1024
    for c in range(nchunks):
        for n0 in range(0, ncols, step):
            n1 = min(ncols, n0 + step)
            P.op("pool", f_dma(wt[:, c, n0:n1],
                               w_ap[row0 + c * 128: row0 + (c + 1) * 128, col0 + n0: col0 + n1]),
                 writes=[key], dsem=_san(key))


def make_hT_from_dram(C, x_ap, hT, ident, xt, xb, tag):
    P = C.P
    for t in range(NT):
        b = t % 2
        P.op("sp", f_dma(xt[b][:], x_ap[t * 128:(t + 1) * 128, :]),
             writes=[(tag, "xt", b)], dsem=f"{tag}xt{b}")
        P.op("act", f_act(xb[b][:], xt[b][:], AF.Copy), reads=[(tag, "xt", b)],
             writes=[(tag, "xb", b)])
        transpose_tile(C, xb[b], (tag, "xb", b), hT, t, ident, b)


def transpose_tile(C, src_bf, src_key, hT, t, ident, b, bank0=0):
    P = C.P
    bk = bank0 + b
    pT = C.bank_bf(bk).rearrange("p (c n) -> p c n", n=128)
    for c in range(8):
        P.op("pe", f_tr(pT[:, c, :], src_bf[:, c * 128:(c + 1) * 128], ident[:]),
             reads=[src_key, "ident"], writes=[("bank", bk)])
    P.op("dve", f_copy(hT[:, :, t * 128:(t + 1) * 128], pT), reads=[("bank", bk)],
         writes=[("hT", t)])


def ln_stats(C, src, src_key, scr):
    P = C.P
    k = C.uid % 4
    C.uid += 1
    stats, mv, rstd = scr["stats"][k], scr["mv"][k], scr["rstd"][k]
    kk = ("ln", k)
    P.op("dve", lambda e: e.bn_stats(out=stats[:, 0, :], in_=src[:, 0:512]),
         reads=[src_key], writes=[kk + ("st",)])
    P.op("dve", lambda e: e.bn_stats(out=stats[:, 1, :], in_=src[:, 512:1024]),
         reads=[src_key], writes=[kk + ("st",)])
    P.op("dve", lambda e: e.bn_aggr(out=mv[:], in_=stats[:].rearrange("p a s -> p (a s)")),
         reads=[kk + ("st",)], writes=[kk + ("mv",)])
    P.op("dve", f_ts(rstd[:], mv[:, 1:2], LN_EPS, None, ALU.add),
         reads=[kk + ("mv",)], writes=[kk + ("rs0",)])
    P.op("pool", f_tt(rstd[:], rstd[:], scr["mhalf"][:, 0:1], ALU.pow),
         reads=[kk + ("rs0",), "mhalf"], writes=[kk + ("rs0",)])
    return k


def ln_apply(C, k, src, src_key, dst, dst_key, g_rep, b_rep, gb_key, scr):
    P = C.P
    mv, rstd, tmp = scr["mv"][k], scr["rstd"][k], scr["tmp"][k % 2]
    kk = ("ln", k)
    kt = ("ln", "tmp", k % 2)
    P.op("dve", f_ts(tmp[:], src, mv[:, 0:1], rstd[:, 0:1], ALU.subtract, ALU.mult),
         reads=[src_key, kk + ("mv",), kk + ("rs0",)], writes=[kt])
    P.op("pool", f_tt(tmp[:], tmp[:], g_rep[:], ALU.mult),
         reads=[kt, gb_key], writes=[kt])
    P.op("pool", f_tt(dst, tmp[:], b_rep[:], ALU.add),
         reads=[kt, gb_key, "rep1"], writes=[dst_key])


def layer_norm_tile(C, src, src_key, dst, dst_key, g_rep, b_rep, gb_key, scr):
    k = ln_stats(C, src, src_key, scr)
    ln_apply(C, k, src, src_key, dst, dst_key, g_rep, b_rep, gb_key, scr)


def ln_scratch(C):
    scr = {
        "stats": [C.sb(f"ln_stats{k}", [128, 2, 6], F32) for k in range(4)],
        "mv": [C.sb(f"ln_mv{k}", [128, 2], F32) for k in range(4)],
        "rstd": [C.sb(f"ln_rstd{k}", [128, 1], F32) for k in range(4)],
        "tmp": [C.sb(f"ln_tmp{k}", [128, 1024], F32) for k in range(2)],
        "mhalf": C.sb("ln_mhalf", [128, 1], F32),
    }
    C.P.op("dve", f_memset(scr["mhalf"][:], -0.5), writes=["mhalf"])
    return scr


def h_to_hT(C, hs, t, hT, ident, hb):
    P = C.P
    b = t % 2
    P.op("act", f_act(hb[b][:], hs[:, t, :], AF.Copy), reads=[("hs", t)], writes=[("hb", b)])
    transpose_tile(C, hb[b], ("hb", b), hT, t, ident, b, bank0=4)


def mlp(C, hs, hT, w_up_ap, w_down_ap, tagp):
    P = C.P
    NS = 8
    wu = [C.sb(f"{tagp}wu{i}", [128, 8, 512], BF16) for i in range(2)]
    wd = [C.sb(f"{tagp}wd{i}", [128, 4, 1024], BF16) for i in range(2)]
    uT1 = C.sb(f"{tagp}uT", [128, 4, TL], BF16)
    uT = [uT1, uT1]
    rl = [C.sb(f"{tagp}rl{i}", [128, 512], F32) for i in range(2)]
    cnt = 0
    dcnt = 0

    def wload(s):
        b = s % 2
        load_weight_bf16(C, wu[b], w_up_ap, (tagp, "wu", b), 512, col0=s * 512)
        for fc in range(4):
            P.op("pool", f_dma(wd[b][:, fc, :],
                               w_down_ap[s * 512 + fc * 128: s * 512 + (fc + 1) * 128, :]),
                 writes=[(tagp, "wd", b)], dsem=f"{tagp}wd{b}")

    wload(0)
    for s in range(NS):
        b = s % 2
        for g in range(4):
            for fc in range(4):
                bk = 2 + (cnt % 2)
                ps = C.bank[bk]
                for c in range(8):
                    P.op("pe", f_mm(ps[:], wu[b][:, c, fc * 128:(fc + 1) * 128],
                                    hT[:, c, g * 512:(g + 1) * 512], c == 0, c == 7),
                         reads=[(tagp, "wu", b)] + [("hT", 4 * g + a) for a in range(4)],
                         writes=[("bank", bk)])
                r = rl[cnt % 2]
                P.op("act", f_act(r[:], ps[:], AF.Relu), reads=[("bank", bk)],
                     writes=[(tagp, "rl", cnt % 2)])
                P.op("pool", f_tt(uT[b][:, fc, g * 512:(g + 1) * 512], r[:], r[:], ALU.mult),
                     reads=[(tagp, "rl", cnt % 2)], writes=[(tagp, "uT", g)])
                cnt += 1
        if s + 1 < NS:
            wload(s + 1)
        for t in range(NT):
            for half in range(2):
                bk = 4 + (dcnt % 4)
                ps = C.bank[bk]
                for fc in range(4):
                    P.op("pe", f_mm(ps[:], uT[b][:, fc, t * 128:(t + 1) * 128],
                                    wd[b][:, fc, half * 512:(half + 1) * 512], fc == 0, fc == 3),
                         reads=[(tagp, "uT", t // 4), (tagp, "wd", b)], writes=[("bank", bk)])
                dst = hs[:, t, half * 512:(half + 1) * 512]
                if s == 0:
                    P.op("dve", f_stt(dst, dst, ALPHA, ps[:], ALU.mult, ALU.add),
                         reads=[("bank", bk), ("hs", t)], writes=[("hs", t)])
                else:
                    P.op("dve", f_tt(dst, dst, ps[:], ALU.add),
                         reads=[("bank", bk), ("hs", t)], writes=[("hs", t)])
                dcnt += 1


def attention(C, cfg, q_d, k_d, v_d, G_d, b31, epilogue):
    P = C.P
    H, nm, KR, dv = cfg["H"], cfg["nm"], cfg["KR"], cfg["dv"]
    dv1 = dv + 1
    per_bank = 512 // dv1 if nm == 1 else 3
    Kb = [C.sb(f"Kb{i}", [128, S], BF16) for i in range(2)]
    Vb = [C.sb(f"Vb{i}", [128, 64, dv1], BF16) for i in range(2)]
    Qb = [C.sb(f"Qb{i}", [128, TL], BF16) for i in range(2)]
    Gb = [C.sb(f"Gb{i}", [128, nm, 640], F32) for i in range(2)]
    nslots = 3 if nm == 1 else 2
    depth = nslots - 1
    nPT = depth + 2
    acc0 = 2 * nslots
    nacc_banks = -(-(4 * nm) // per_bank)
    PT = [C.sb(f"PT{s_}", [128, 2, 512], BF16) for s_ in range(nPT)]
    accsb = [C.sb(f"accsb{i}", [128, nacc_banks, 512], F32) for i in range(2)]

    def rows(m):
        return (64 * m, 64 * m + 64) if nm == 2 else (0, KR)

    def mapidx(h, m):
        return 2 * h + m if nm == 2 else h

    def acc_bank(n):
        return acc0 + n // per_bank

    def acc_ap(n):
        off = (n % per_bank) * dv1
        return C.bank[acc_bank(n)][:, off:off + dv1]

    def acc_key(n):
        return ("bank", acc_bank(n))

    def load(h):
        p = h % 2
        m0 = mapidx(h, 0)
        P.op("sp", f_dma(Gb[p][:], G_d[:, m0:m0 + nm, :]), writes=[("G", p)], dsem=f"G{p}")
        P.op("sp", f_dma(Qb[p][0:KR, :], q_d[h]), writes=[("Q", p)], dsem=f"Q{p}")
        for r in range(4):
            P.op("sp", f_dma(Kb[p][0:KR, r * TL:(r + 1) * TL], k_d[r, h]),
                 writes=[("K", p)], dsem=f"K{p}")
        for r in range(4):
            P.op("sp", f_dma(Vb[p][:, r * 16:(r + 1) * 16, :], v_d[r, h]),
                 writes=[("V", p)], dsem=f"V{p}")

    def fold_bias(h):
        p = h % 2
        for m in range(nm):
            mi = mapidx(h, m)
            P.op("dve", f_ts(Gb[p][:, m, :], Gb[p][:, m, :], b31[:, mi:mi + 1], None, ALU.subtract),
                 reads=[("G", p), "b31"], writes=[("G", p)])

    def amin_of(u, c):
        return max(0, -((-(c - 16 * u - 3)) // 4))

    def afar_of(u, c):
        return min(4, max(0, -((-(c - 16 * u + 2)) // 4)))

    def kslot(c):
        return (c % 4) * 16 + c // 4

    items = []
    for h in range(H):
        for u in range(4):
            if nm == 2:
                for c in range(16 * u + 16):
                    items.append((h, u, ((c, 0), (c, 1))))
            else:
                for cp in range(8 * u + 8):
                    items.append((h, u, ((2 * cp, 0), (2 * cp + 1, 0))))

    def emit_qk_exp(i):
        h, u, subs = items[i]
        if i == 0 or items[i - 1][0] != h:
            fold_bias(h)
        p = h % 2
        slot = i % nslots
        psl = i % nPT
        bks = (2 * slot, 2 * slot + 1)
        amins = [amin_of(u, c) for c, m in subs]
        afars = [afar_of(u, c) for c, m in subs]
        for j, (c, m) in enumerate(subs):
            r0, r1 = rows(m)
            col = kslot(c) * 128
            amin = amins[j]
            P.op("pe", f_mm(C.bank[bks[j]][:, amin * 128:512], Kb[p][r0:r1, col:col + 128],
                            Qb[p][r0:r1, (4 * u + amin) * 128:(4 * u + 4) * 128], True, True),
                 reads=[("K", p), ("Q", p)], writes=[("bank", bks[j])])
        if nm == 2:
            c = subs[0][0]
            for a in range(amins[0], afars[0]):
                s_ = c - (16 * u + 4 * a) + 1
                P.op("dve", f_tt(C.psum[:, bks[0]:bks[0] + 2, a * 128:(a + 1) * 128],
                                 C.psum[:, bks[0]:bks[0] + 2, a * 128:(a + 1) * 128],
                                 Gb[p][:, :, s_ * 128:(s_ + 1) * 128], ALU.add),
                     reads=[("bank", bks[0]), ("bank", bks[1]), ("G", p)],
                     writes=[("bank", bks[0]), ("bank", bks[1])])
        else:
            for j, (c, m) in enumerate(subs):
                for a in range(amins[j], afars[j]):
                    s_ = c - (16 * u + 4 * a) + 1
                    P.op("dve", f_tt(C.bank[bks[j]][:, a * 128:(a + 1) * 128],
                                     C.bank[bks[j]][:, a * 128:(a + 1) * 128],
                                     Gb[p][:, 0, s_ * 128:(s_ + 1) * 128], ALU.add),
                         reads=[("bank", bks[j]), ("G", p)], writes=[("bank", bks[j])])
        am = min(amins)
        P.op("act", f_act(PT[psl][:, :, am * 128:512], C.psum[:, bks[0]:bks[0] + 2, am * 128:512], AF.Exp),
             reads=[("bank", bks[0]), ("bank", bks[1])],
             writes=[("PT", psl, j, a) for j in range(2) for a in range(am, 4)])

    def emit_pv(i):
        h, u, subs = items[i]
        p = h % 2
        psl = i % nPT
        for j, (c, m) in enumerate(subs):
            amin = amin_of(u, c)
            for a in range(amin, 4):
                n = m * 4 + a
                first = (c == 0) and (n % per_bank == 0)
                last = (c == 16 * u + 4 * a + 3)
                P.op("pe", f_mm(acc_ap(n), PT[psl][:, j, a * 128:(a + 1) * 128],
                                Vb[p][:, kslot(c), :], first, last, skip=True),
                     reads=[("PT", psl, j, a), ("V", p)], writes=[acc_key(n)])

    gcnt = [0]

    def evacuate_and_epilogue(h, u):
        k = gcnt[0] % 2
        gcnt[0] += 1
        for bi in range(nacc_banks):
            nin = min(per_bank, 4 * nm - bi * per_bank)
            P.op("dve", f_copy(accsb[k][:, bi, 0:nin * dv1], C.bank[acc0 + bi][:, 0:nin * dv1]),
                 reads=[("bank", acc0 + bi)], writes=[("accsb", k, bi)])

        def sb_ap(n):
            off = (n % per_bank) * dv1
            return accsb[k][:, n // per_bank, off:off + dv1]

        def sb_key(n):
            return ("accsb", k, n // per_bank)
        epilogue(h, u, sb_ap, sb_key)

    load(0)
    if H > 1:
        load(1)
    for i0 in range(min(depth, len(items))):
        emit_qk_exp(i0)
    for i in range(len(items)):
        h, u, subs = items[i]
        nxt = items[i + 1] if i + 1 < len(items) else None
        if i + depth < len(items):
            emit_qk_exp(i + depth)
        emit_pv(i)
        if subs[1][0] == 16 * u + 15:
            evacuate_and_epilogue(h, u)
        if nxt is not None and nxt[0] != h and h + 2 < H:
            load(h + 2)


def dram_in(nc, name, shape, dt):
    return nc.dram_tensor(name, list(shape), dt, kind="ExternalInput").ap()


def dram_out(nc, name, shape, dt):
    return nc.dram_tensor(name, list(shape), dt, kind="ExternalOutput").ap()


def fm_project(C, w, wkey, col0, hT, scale, tag, store):
    P = C.P
    stg = [C.sb(f"{tag}stg{i}", [128, 8, 512], BF16) for i in range(2)]
    cnt = 0
    for g in range(4):
        b = g % 2
        for f in range(8):
            bk = 2 + (cnt % 2)
            cnt += 1
            ps = C.bank[bk]
            for c in range(8):
                P.op("pe", f_mm(ps[:], w[:, c, col0 + f * 128: col0 + (f + 1) * 128],
                                hT[:, c, g * 512:(g + 1) * 512], c == 0, c == 7),
                     reads=[wkey] + [("hT", 4 * g + a) for a in range(4)], writes=[("bank", bk)])
            P.op("act", f_act(stg[b][:, f, :], ps[:], AF.Copy, scale=scale),
                 reads=[("bank", bk)], writes=[(tag, "stg", b)])
        store(g, stg[b], (tag, "stg", b), b)
    return stg


def v_project(C, wb, wkey, col0, hT, out_d, nh, dv, tag):
    P = C.P
    dv1 = dv + 1
    stg = [C.sb(f"{tag}vstg{i}", [128, nh, 4, dv1], BF16) for i in range(2)]
    for i in range(2):
        P.op("pool", f_memset(stg[i][:], 1.0), writes=[(tag, "vstg", i)])
    hpb = 512 // dv
    nhalf = nh * dv // 512
    cnt = 0
    for g in range(4):
        b = g % 2
        for a in range(4):
            t = 4 * g + a
            for half in range(nhalf):
                bk = 4 + (cnt % 2)
                cnt += 1
                ps = C.bank[bk]
                for c in range(8):
                    P.op("pe", f_mm(ps[:], hT[:, c, t * 128:(t + 1) * 128],
                                    wb[:, c, col0 + half * 512: col0 + (half + 1) * 512],
                                    c == 0, c == 7),
                         reads=[wkey, ("hT", t)], writes=[("bank", bk)])
                P.op("dve", f_copy(stg[b][:, half * hpb:(half + 1) * hpb, a, 0:dv],
                                   ps[:].rearrange("p (h e) -> p h e", e=dv)),
                     reads=[("bank", bk)], writes=[(tag, "vstg", b)])
        P.op("sp", f_dma(out_d.rearrange("h p t e -> p h (t e)")[:, :, 4 * g * dv1:(4 * g + 4) * dv1],
                         stg[b][:].rearrange("p h a e -> p h (a e)")),
             reads=[(tag, "vstg", b)], writes=[(tag, "dram")], dsem=f"{tag}vst{b}")


def out_proj_ln(C, ostore, wo, res_ap, hs, hT, ident, g_rep, b_rep, gb_key, scr, xt, hb, oT):
    P = C.P
    pend = []
    for t in range(NT):
        b = t % 2
        pT = C.bank_bf(b).rearrange("p (c n) -> p c n", n=128)
        for c in range(8):
            P.op("pe", f_tr(pT[:, c, :], ostore[:, t, c * 128:(c + 1) * 128], ident[:]),
                 reads=[("os", t), "ident"], writes=[("bank", b)])
        P.op("dve", f_copy(oT[b][:], pT), reads=[("bank", b)], writes=[("oT", b)])
        if res_ap is not None:
            P.op("sp", f_dma(xt[:], res_ap[t * 128:(t + 1) * 128, :]),
                 writes=[("B", "xt")], dsem="Bxt")
        for half in range(2):
            bk = 2 + half
            ps = C.bank[bk]
            for c in range(8):
                P.op("pe", f_mm(ps[:], oT[b][:, c, :], wo[:, c, half * 512:(half + 1) * 512],
                                c == 0, c == 7),
                     reads=[("oT", b), "wo"], writes=[("bank", bk)])
            dst = hs[:, t, half * 512:(half + 1) * 512]
            if res_ap is None:
                P.op("dve", f_stt(dst, dst, ALPHA, ps[:], ALU.mult, ALU.add),
                     reads=[("bank", bk), ("hs", t)], writes=[("hs", t)])
            else:
                P.op("dve", f_stt(dst, xt[:, half * 512:(half + 1) * 512], ALPHA, ps[:],
                                  ALU.mult, ALU.add),
                     reads=[("bank", bk), ("B", "xt")], writes=[("hs", t)])
        pend.append((t, ln_stats(C, hs[:, t, :], ("hs", t), scr)))
        if len(pend) > LN_SKEW:
            pt, pk = pend.pop(0)
            ln_apply(C, pk, hs[:, pt, :], ("hs", pt), hs[:, pt, :], ("hs", pt), g_rep, b_rep, gb_key, scr)
            h_to_hT(C, hs, pt, hT, ident, hb)
    for pt, pk in pend:
        ln_apply(C, pk, hs[:, pt, :], ("hs", pt), hs[:, pt, :], ("hs", pt), g_rep, b_rep, gb_key, scr)
        h_to_hT(C, hs, pt, hT, ident, hb)


def ln_all_tiles(C, hs, g_rep, b_rep, gb_key, scr, after):
    pend = []
    for t in range(NT):
        pend.append((t, ln_stats(C, hs[:, t, :], ("hs", t), scr)))
        if len(pend) > LN_SKEW:
            pt, pk = pend.pop(0)
            ln_apply(C, pk, hs[:, pt, :], ("hs", pt), hs[:, pt, :], ("hs", pt), g_rep, b_rep, gb_key, scr)
            after(pt)
    for pt, pk in pend:
        ln_apply(C, pk, hs[:, pt, :], ("hs", pt), hs[:, pt, :], ("hs", pt), g_rep, b_rep, gb_key, scr)
        after(pt)


def load_rep(C, tile, ap, key):
    C.P.op("sp", f_dma(tile[:], ap), writes=[key], dsem=_san(key))


class Scope:
    def __init__(self, C):
        self.C = C

    def __enter__(self):
        self.st = ExitStack()
        self.prev = (self.C.st, self.C.side)
        self.C.st, self.C.side = self.st, "right"
        return self

    def __exit__(self, *a):
        self.C.P.barrier()
        self.st.close()
        self.C.st, self.C.side = self.prev
        return False


def build_A():
    nc = bass.Bass("TRN2", target_bir_lowering=False)
    x = dram_in(nc, "x", [TL, D], F32)
    w_in = dram_in(nc, "w_in", [D, 3 * D], F32)
    ident_d = dram_in(nc, "ident", [128, 128], BF16)
    qT = dram_out(nc, "qT", [8, 128, TL], BF16)
    kT = dram_out(nc, "kT", [8, 128, TL], BF16)
    v = dram_out(nc, "v", [8, 128, NT, 129], BF16)
    with ExitStack() as st:
        C = Ctx(nc, st)
        P = C.P
        ident = C.sb("ident", [128, 128], BF16)
        P.op("sp", f_dma(ident[:], ident_d), writes=["ident"], dsem="ident")
        phase_A(C, x, w_in, ident, qT, kT, v)
        P.emit()
    return nc


def phase_A(C, x, w_in, ident, qT, kT, v):
    P = C.P
    with Scope(C):
        wb = C.sb("wb", [128, 8, 3 * D], BF16)
        load_weight_bf16(C, wb, w_in, "wb", 3 * D)
        hT = C.sb("hTA", [128, 8, TL], BF16)
        xt = [C.sb(f"xtA{i}", [128, D], F32) for i in range(2)]
        xb = [C.sb(f"xbA{i}", [128, D], BF16) for i in range(2)]
        make_hT_from_dram(C, x, hT, ident, xt, xb, "A")

        def mkstore(out_d, tag):
            def store(g, stg, key, b):
                P.op("sp", f_dma(out_d.rearrange("h p n -> p h n")[:, :, g * 512:(g + 1) * 512], stg[:]),
                     reads=[key], writes=[(tag, "dram")], dsem=f"{tag}st{b}")
            return store
        fm_project(C, wb, "wb", 0, hT, 0.125, "q", mkstore(qT, "q"))
        fm_project(C, wb, "wb", D, hT, 1.0, "k", mkstore(kT, "k"))
        v_project(C, wb, "wb", 2 * D, hT, v, 8, 128, "v")


def build_B():
    nc = bass.Bass("TRN2", target_bir_lowering=False)
    x = dram_in(nc, "x", [TL, D], F32)
    qT = dram_in(nc, "qT", [8, 128, TL], BF16)
    kT_all = dram_in(nc, "kT_all", [4, 8, 128, TL], BF16)
    v_all = dram_in(nc, "v_all", [4, 8, 128, NT, 129], BF16)
    cst = const_inputs(nc)
    prm = {
        "lam": dram_in(nc, "lam", [128, 256], F32),
        "subg": dram_in(nc, "subg", [128, 128], F32),
        "w_out0": dram_in(nc, "w_out0", [D, D], F32),
        "w_up0": dram_in(nc, "w_up0", [D, DFF], F32),
        "w_down0": dram_in(nc, "w_down0", [DFF, D], F32),
        "w_kv": dram_in(nc, "w_kv", [D, 2 * D], F32),
    }
    for k in ("ln1_g0", "ln1_b0", "ln2_g0", "ln2_b0"):
        prm[k] = dram_in(nc, k, [128, D], F32)
    h1 = dram_out(nc, "h1", [TL, D], F32)
    kT1 = dram_out(nc, "kT1", [16, 96, TL], BF16)
    v1 = dram_out(nc, "v1", [16, 128, NT, 65], BF16)
    with ExitStack() as st:
        C = Ctx(nc, st)
        P = C.P
        cs = load_consts(C, cst)
        hs, hT, hT_stack = phase_B(C, cs, cst, prm, x, qT, kT_all, v_all, kT1, v1)
        st.callback(hT_stack.close)
        for t in range(NT):
            P.op("sp", f_dma(h1[t * 128:(t + 1) * 128, :], hs[:, t, :]), reads=[("hs", t)],
                 dsem=f"h1st{t % 2}")
        P.emit()
    return nc


def const_inputs(nc):
    return {
        "G": dram_in(nc, "G", [128, 16, 640], F32),
        "b31": dram_in(nc, "b31", [128, 16], F32),
        "ident": dram_in(nc, "ident", [128, 128], BF16),
        "e_own": dram_in(nc, "e_own", [32, TL], BF16),
        "past30k": dram_in(nc, "past30k", [128, NT, 32], F32),
    }


def load_consts(C, cst):
    P = C.P
    ident = C.sb("ident", [128, 128], BF16)
    P.op("sp", f_dma(ident[:], cst["ident"]), writes=["ident"], dsem="ident")
    b31 = C.sb("b31", [128, 16], F32)
    P.op("sp", f_dma(b31[:], cst["b31"]), writes=["b31"], dsem="b31")
    scr = ln_scratch(C)
    hb = [C.sb(f"hb{i}", [128, D], BF16) for i in range(2)]
    rep = [C.sb(f"rep{i}", [128, D], F32) for i in range(2)]
    kmT = C.sb("kmT", [128, 8, 32], BF16)
    return {"ident": ident, "b31": b31, "scr": scr, "hb": hb, "rep": rep, "kmT": kmT}


def alloc_hT(C):
    stk = ExitStack()
    hT = stk.enter_context(C.nc.sbuf_tensor("s_hT%d" % C.uid, [128, 8, TL], BF16, side="left"))
    C.uid += 1
    return hT, stk


def phase_B(C, cs, cst, prm, x, qT, kT_all, v_all, kT1, v1):
    P = C.P
    ident, b31, scr, hb, rep = cs["ident"], cs["b31"], cs["scr"], cs["hb"], cs["rep"]
    with Scope(C):
        ostore = C.sb("ostore", [128, NT, D], BF16)
        lam_sb = C.sb("lam_sb", [128, 256], F32)
        lprod = C.sb("lprod", [128, 2, 64], F32)
        lsum = C.sb("lsum", [128, 2], F32)
        lexp = C.sb("lexp", [128, 2], F32)
        nl = C.sb("nl", [128, 2], F32)
        gsub = C.sb("gsub", [128, 128], F32)
        P.op("sp", f_dma(lam_sb[:], prm["lam"]), writes=["lam_sb"], dsem="lam")
        P.op("sp", f_dma(gsub[:], prm["subg"]), writes=["gsub"], dsem="gsub")
        lv = lam_sb[:].rearrange("p (a b d) -> p a b d", a=2, b=2)
        P.op("dve", f_tt(lprod[:], lv[:, :, 0, :], lv[:, :, 1, :], ALU.mult),
             reads=["lam_sb"], writes=["lprod"])
        P.op("dve", lambda e: e.reduce_sum(out=lsum[:], in_=lprod[:], axis=AX.X),
             reads=["lprod"], writes=["lsum"])
        P.op("act", f_act(lexp[:], lsum[:], AF.Exp), reads=["lsum"], writes=["lexp"])
        P.op("dve", f_tt(nl[:, 0:1], lexp[:, 1:2], lexp[:, 0:1], ALU.subtract),
             reads=["lexp"], writes=["nl0"])
        P.op("dve", f_ts(nl[:, 1:2], nl[:, 0:1], -LAM_INIT0, None, ALU.add),
             reads=["nl0"], writes=["neglam"])
        neglam = nl[:, 1:2]
        P.op("dve", f_ts(gsub[:], gsub[:], 1.0 - LAM_INIT0, None, ALU.mult),
             reads=["gsub"], writes=["gsub"])
        ep = {k: [C.sb(f"ep_{k}{i}", shp, F32) for i in range(2)]
              for k, shp in (("rl", [128, 2]), ("c1", [128, 1]), ("t1", [128, 128]),
                             ("o", [128, 128]), ("sq", [128, 128]), ("ss", [128, 1]))}
        ecnt = [0]

        def epilogue(h, u, acc_ap, acc_key):
            for a in range(4):
                t = 4 * u + a
                k = ecnt[0] % 2
                ecnt[0] += 1
                rl, c1, t1, o, sq, ss = (ep[x_][k] for x_ in ("rl", "c1", "t1", "o", "sq", "ss"))
                a0, a1 = acc_ap(a), acc_ap(4 + a)
                k0, k1 = acc_key(a), acc_key(4 + a)
                P.op("dve", lambda e, rl=rl, a0=a0: e.reciprocal(out=rl[:, 0:1], in_=a0[:, 128:129]),
                     reads=[k0], writes=[("ep", k, "rl0")])
                P.op("dve", lambda e, rl=rl, a1=a1: e.reciprocal(out=rl[:, 1:2], in_=a1[:, 128:129]),
                     reads=[k1], writes=[("ep", k, "rl1")])
                P.op("dve", f_tt(c1[:], rl[:, 1:2], neglam, ALU.mult),
                     reads=[("ep", k, "rl1"), "neglam"], writes=[("ep", k, "c1")])
                P.op("dve", f_ts(t1[:], a1[:, 0:128], c1[:, 0:1], None, ALU.mult),
                     reads=[k1, ("ep", k, "c1")], writes=[("ep", k, "t1")])
                P.op("dve", f_stt(o[:], a0[:, 0:128], rl[:, 0:1], t1[:], ALU.mult, ALU.add),
                     reads=[k0, ("ep", k, "rl0"), ("ep", k, "t1")], writes=[("ep", k, "o")])
                P.op("pool", f_tt(sq[:], o[:], o[:], ALU.mult), reads=[("ep", k, "o")],
                     writes=[("ep", k, "sq")])
                P.op("dve", lambda e, ss=ss, sq=sq: e.reduce_sum(out=ss[:], in_=sq[:], axis=AX.X),
                     reads=[("ep", k, "sq")], writes=[("ep", k, "ss")])
                P.op("dve", f_ts(ss[:], ss[:], 1.0 / 128.0, LN_EPS, ALU.mult, ALU.add),
                     reads=[("ep", k, "ss")], writes=[("ep", k, "ss")])
                P.op("pool", f_tt(ss[:], ss[:], scr["mhalf"][:, 0:1], ALU.pow),
                     reads=[("ep", k, "ss"), "mhalf"], writes=[("ep", k, "ss")])
                P.op("dve", f_stt(ostore[:, t, h * 128:(h + 1) * 128], o[:], ss[:, 0:1], gsub[:],
                                  ALU.mult, ALU.mult),
                     reads=[("ep", k, "o"), ("ep", k, "ss"), "gsub"], writes=[("os", t)])

        with Scope(C):
            attention(C, dict(H=8, nm=2, KR=128, dv=128), qT, kT_all, v_all, cst["G"], b31, epilogue)
        C.side = "left"
        hs = C.nc.sbuf_tensor("s_hs", [128, NT, D], F32, side="left")
        hs = C.root.enter_context(hs)
        hT, hT_stack = alloc_hT(C)
        C.side = "right"
        with Scope(C):
            wo = C.sb("wo", [128, 8, D], BF16)
            load_weight_bf16(C, wo, prm["w_out0"], "wo", D)
            xt = C.sb("xtB", [128, D], F32)
            oT = [C.sb(f"oT{i}", [128, 8, 128], BF16) for i in range(2)]
            load_rep(C, rep[0], prm["ln1_g0"], "rep0")
            load_rep(C, rep[1], prm["ln1_b0"], "rep1")
            out_proj_ln(C, ostore, wo, x, hs, hT, ident, rep[0], rep[1], "rep0", scr, xt, hb, oT)
    with Scope(C):
        mlp(C, hs, hT, prm["w_up0"], prm["w_down0"], "m0")
    load_rep(C, rep[0], prm["ln2_g0"], "rep0")
    load_rep(C, rep[1], prm["ln2_b0"], "rep1")
    with Scope(C):
        wkv = C.sb("wk1", [128, 8, D], BF16)
        load_weight_bf16(C, wkv, prm["w_kv"], "wkv", D)
        wv1 = C.sb("wv1", [128, 8, D], BF16)
        load_weight_bf16(C, wv1, prm["w_kv"], "wv1", D, col0=D)
        ln_all_tiles(C, hs, rep[0], rep[1], "rep0", scr, lambda t: h_to_hT(C, hs, t, hT, ident, hb))

        def store(g, stg, key, b):
            kv = kT1.rearrange("(f two) r n -> two r f n", two=2)
            for hh in range(2):
                P.op("sp", f_dma(kv[hh, 0:64, :, g * 512:(g + 1) * 512], stg[hh * 64:(hh + 1) * 64, :, :]),
                     reads=[key], writes=[("k1dram", hh)], dsem=f"k1st{b}")
        fm_project(C, wkv, "wkv", 0, hT, 1.0, "k1", store)
        for h in range(16):
            P.op("sp", f_dma(kT1[h, 64:96, :], cst["e_own"]), writes=[("k1dramE",)], dsem="k1e")
        v_project(C, wv1, "wv1", 0, hT, v1, 16, 64, "v1")
    return hs, hT, hT_stack


def build_C(stop_after=None):
    nc = bass.Bass("TRN2", target_bir_lowering=False)
    h1 = dram_in(nc, "h1", [TL, D], F32)
    kT1_all = dram_in(nc, "kT1_all", [4, 16, 96, TL], BF16)
    v1_all = dram_in(nc, "v1_all", [4, 16, 128, NT, 65], BF16)
    cst = const_inputs(nc)
    prm = {
        "w_q": dram_in(nc, "w_q", [D, D], F32),
        "w_out1": dram_in(nc, "w_out1", [D, D], F32),
        "w_up1": dram_in(nc, "w_up1", [D, DFF], F32),
        "w_down1": dram_in(nc, "w_down1", [DFF, D], F32),
    }
    for k in ("ln1_g1", "ln1_b1", "ln2_g1", "ln2_b1"):
        prm[k] = dram_in(nc, k, [128, D], F32)
    out = dram_out(nc, "out", [TL, D], F32)
    qaug = nc.dram_tensor("qaug", [16, 96, TL], BF16,
                          kind="ExternalOutput" if stop_after else "Internal").ap()
    with ExitStack() as st:
        C = Ctx(nc, st)
        C.stop_after = stop_after
        P = C.P
        cs = load_consts(C, cst)
        hs = C.sb("hs", [128, NT, D], F32)
        hT, hT_stack = alloc_hT(C)
        for t in range(NT):
            P.op("sp", f_dma(hs[:, t, :], h1[t * 128:(t + 1) * 128, :]), writes=[("hs", t)],
                 dsem=f"hsld{t}")
            h_to_hT(C, hs, t, hT, cs["ident"], cs["hb"])
        phase_C(C, cs, cst, prm, kT1_all, v1_all, qaug, hs, hT, hT_stack, out)
        P.emit()
    return nc


def phase_C(C, cs, cst, prm, kT1_all, v1_all, qaug, hs, hT, hT_stack, out):
    P = C.P
    ident, b31, scr, hb, rep = cs["ident"], cs["b31"], cs["scr"], cs["hb"], cs["rep"]
    kmT = cs["kmT"]
    with Scope(C):
        moba_kmean(C, kT1_all, kmT)
    if getattr(C, "stop_after", None) == "kmean":
        P.op("sp", f_dma(qaug.rearrange("h r n -> r h n")[0:64, 0:8, 0:32], kmT[0:64, :, :]),
             reads=[("kmT", f) for f in range(8)], dsem="dbg1")
        P.op("sp", f_dma(qaug.rearrange("h r n -> r h n")[0:64, 8:16, 0:32], kmT[64:128, :, :]),
             reads=[("kmT", f) for f in range(8)], dsem="dbg2")
        for t in range(NT):
            P.op("sp", f_dma(out[t * 128:(t + 1) * 128, :], hs[:, t, :]), reads=[("hs", t)],
                 dsem=f"outst{t % 2}")
        hT_stack.close()
        return
    with Scope(C):
        moba_gate_qproj(C, hT, prm["w_q"], kmT, cst["past30k"], qaug, ident)
    hT_stack.close()
    hT = None
    if getattr(C, "stop_after", None) == "gate":
        for t in range(NT):
            P.op("sp", f_dma(out[t * 128:(t + 1) * 128, :], hs[:, t, :]), reads=[("hs", t)],
                 dsem=f"outst{t % 2}")
        return
    with Scope(C):
        ostore = C.sb("ostore1", [128, NT, D], BF16)
        eprl = [C.sb(f"eprl{i}", [128, 1], F32) for i in range(8)]

        def epilogue(h, u, acc_ap, acc_key):
            for a in range(4):
                t = 4 * u + a
                k = (h * 4 + u) % 2
                rl = eprl[4 * k + a]
                acc = acc_ap(a)
                P.op("dve", lambda e, rl=rl, acc=acc: e.reciprocal(out=rl[:], in_=acc[:, 64:65]),
                     reads=[acc_key(a)], writes=[("eprl", 4 * k + a)])
                P.op("dve", f_ts(ostore[:, t, h * 64:(h + 1) * 64], acc[:, 0:64], rl[:, 0:1], None,
                                 ALU.mult),
                     reads=[acc_key(a), ("eprl", 4 * k + a)], writes=[("os", t)])

        with Scope(C):
            attention(C, dict(H=16, nm=1, KR=96, dv=64), qaug, kT1_all, v1_all, cst["G"], b31, epilogue)
        hT, hT_stack2 = alloc_hT(C)
        C.root.callback(hT_stack2.close)
        with Scope(C):
            wo = C.sb("wo1", [128, 8, D], BF16)
            load_weight_bf16(C, wo, prm["w_out1"], "wo", D)
            oT = [C.sb(f"oT1{i}", [128, 8, 128], BF16) for i in range(2)]
            load_rep(C, rep[0], prm["ln1_g1"], "rep0")
            load_rep(C, rep[1], prm["ln1_b1"], "rep1")
            out_proj_ln(C, ostore, wo, None, hs, hT, ident, rep[0], rep[1], "rep0", scr, None, hb, oT)
    with Scope(C):
        mlp(C, hs, hT, prm["w_up1"], prm["w_down1"], "m1")
    load_rep(C, rep[0], prm["ln2_g1"], "rep0")
    load_rep(C, rep[1], prm["ln2_b1"], "rep1")
    ln_all_tiles(C, hs, rep[0], rep[1], "rep0", scr,
                 lambda t: P.op("sp", f_dma(out[t * 128:(t + 1) * 128, :], hs[:, t, :]),
                                reads=[("hs", t)], dsem=f"outst{t % 2}"))


def moba_kmean(C, kT1_all, kmT):
    P = C.P
    Kp = [C.sb(f"Kp{i}", [128, S], BF16) for i in range(2)]
    ksum = [C.sb(f"ksum{i}", [128, 64], F32) for i in range(2)]
    kmf = [C.sb(f"kmf{i}", [128, 16, 2], F32) for i in range(2)]
    for f in range(8):
        b = f % 2
        for r in range(4):
            for hh in range(2):
                P.op("sp", f_dma(Kp[b][hh * 64:(hh + 1) * 64, r * TL:(r + 1) * TL],
                                 kT1_all[r, 2 * f + hh, 0:64, :]),
                     writes=[("Kp", b)], dsem=f"Kp{b}")
        P.op("dve", lambda e, b=b: e.reduce_sum(out=ksum[b][:],
                                                in_=Kp[b][:].rearrange("p (s q) -> p s q", q=128),
                                                axis=AX.X),
             reads=[("Kp", b)], writes=[("ksum", b)])
        kv = ksum[b][:].rearrange("p (r t) -> p r t", t=16)
        for par in range(2):
            P.op("dve", f_tt(kmf[b][:, :, par], kv[:, 2 * par, :], kv[:, 2 * par + 1, :], ALU.add),
                 reads=[("ksum", b)], writes=[("kmf", b, par)])
        P.op("dve", f_ts(kmT[:, f, :], kmf[b][:].rearrange("p t two -> p (t two)"), 1.0 / 256.0,
                         None, ALU.mult),
             reads=[("kmf", b, 0), ("kmf", b, 1)], writes=[("kmT", f)])


def moba_gate_qproj(C, hT, w_q, kmT, past_d, qaug, ident):
    P = C.P
    wq = C.sb("wq", [128, 8, D], BF16)
    load_weight_bf16(C, wq, w_q, "wq", D)
    past = C.sb("past", [128, NT, 32], F32)
    padd = C.sb("padd", [128, NT, 32], F32)
    P.op("sp", f_dma(past[:], past_d), writes=["past"], dsem="past")
    P.op("dve", f_ts(padd[:], past[:], NEG, None, ALU.add), reads=["past"], writes=["padd"])
    mstg = [C.sb(f"mstg{i}", [128, 16, 128], BF16) for i in range(2)]
    gm = [C.sb(f"gm{i}", [128, 16, 32], F32) for i in range(2)]
    top8 = [C.sb(f"top8{i}", [128, 16, 8], F32) for i in range(2)]
    sel = [C.sb(f"sel{i}", [128, 16, 32], F32) for i in range(2)]
    negpad = [C.sb(f"negpad{i}", [128, 16, 96], BF16) for i in range(2)]
    for i in range(2):
        P.op("pool", f_memset(negpad[i][:], 0.0), writes=[("negpad", i)])

    def store(g, stg, key, b):
        qv = qaug.rearrange("(f two) r n -> two r f n", two=2)
        for hh in range(2):
            P.op("sp", f_dma(qv[hh, 0:64, :, g * 512:(g + 1) * 512], stg[hh * 64:(hh + 1) * 64, :, :]),
                 reads=[key], writes=[("qaug", "q")], dsem=f"qst{b}")
        for a in range(4):
            t = 4 * g + a
            k = t % 2
            gbk = (4, 5) if k == 0 else (0, 1)
            for h in range(16):
                r0 = (h % 2) * 64
                gps = C.bank[gbk[h % 2]]
                P.op("pe", f_mm(gps[:, (h // 2) * 32:(h // 2 + 1) * 32],
                                stg[r0:r0 + 64, h // 2, a * 128:(a + 1) * 128],
                                kmT[r0:r0 + 64, h // 2, :], True, True),
                     reads=[key, ("kmT", h // 2)], writes=[("bank", gbk[h % 2])])
            gmv = gm[k][:].rearrange("p (f two) n -> p two f n", two=2)
            for par in range(2):
                P.op("dve", f_tt(gmv[:, par], C.bank[gbk[par]][:, 0:256].rearrange("p (f n) -> p f n", n=32),
                                 padd[:, t, :].unsqueeze(1).broadcast_to([128, 8, 32]), ALU.add),
                     reads=[("bank", gbk[par]), "padd"], writes=[("gm", k)])
            for h in range(16):
                P.op("dve", lambda e, k=k, h=h: e.max(out=top8[k][:, h, :], in_=gm[k][:, h, :]),
                     reads=[("gm", k)], writes=[("top8", k)])
            P.op("dve", f_tt(sel[k][:], gm[k][:], top8[k][:, :, 2:3].broadcast_to([128, 16, 32]),
                             ALU.is_ge),
                 reads=[("gm", k), ("top8", k)], writes=[("sel", k)])
            P.op("dve", f_ts(sel[k][:], sel[k][:], -1.0, None, ALU.add),
                 reads=[("sel", k)], writes=[("sel", k)])
            P.op("dve", f_tt(negpad[k][:, :, 64:96], sel[k][:],
                             past[:, t, :].unsqueeze(1).broadcast_to([128, 16, 32]), ALU.mult),
                 reads=[("sel", k), "past"], writes=[("negpad", k)])
            for hq in range(2):
                bk = 6 + hq
                tp = C.bank_bf(bk).rearrange("p (h n) -> p h n", n=128)
                for h8 in range(8):
                    h = hq * 8 + h8
                    P.op("pe", f_tr(tp[0:96, h8, :], negpad[k][:, h, :], ident[:]),
                         reads=[("negpad", k), "ident"], writes=[("bank", bk)])
                P.op("act", f_act(mstg[k][64:96, hq * 8:(hq + 1) * 8, :], tp[64:96, :, :], AF.Copy),
                     reads=[("bank", bk)], writes=[("mstg", k)])
            P.op("sp", f_dma(qaug.rearrange("h r n -> r h n")[64:96, :, t * 128:(t + 1) * 128],
                             mstg[k][64:96, :, :]),
                 reads=[("mstg", k)], writes=[("qaug", "m")], dsem=f"mst{k}")

    fm_project(C, wq, "wq", 0, hT, 0.125, "qb", store)


def _bucket_np(n):
    n = np.maximum(n, 0)
    nf = np.maximum(n, 16).astype(np.float32)
    large = 16 + (np.log(nf / np.float32(16)) / np.float32(math.log(8.0)) * np.float32(16)).astype(np.int32)
    large = np.minimum(large, 31)
    return np.where(n < 16, n, large)


def _g_index(j):
    k = np.arange(128)[:, None, None]
    q = np.arange(128)[None, None, :]
    e = (np.arange(5) - 1)[None, :, None]
    delta = e - j
    rel = q - k - 128 * delta
    idx = _bucket_np(rel)
    idx = np.where(rel >= 0, idx, 32)
    return np.broadcast_to(idx, (128, 5, 128)).astype(np.int64)


_CACHE = {}


def _get(name, fn):
    if name not in _CACHE:
        _CACHE[name] = fn()
    return _CACHE[name]


def _host_prep(x, rel_bias):
    f32 = np.float32
    bf = ml_dtypes.bfloat16
    table_ext = np.concatenate([rel_bias, np.full((1, 16), NEG, f32)], axis=0)
    per = []
    for c in range(NCORES):
        b, j = c // 4, c % 4
        xs = np.ascontiguousarray(x[b].reshape(64, 128, D)[j::4].reshape(TL, D))
        G = table_ext[_g_index(j)]
        G = np.ascontiguousarray(G.transpose(0, 3, 1, 2).reshape(128, 16, 640))
        blk = (4 * np.arange(NT) + j) // 2
        e = np.zeros((32, NT, 128), f32)
        e[blk, np.arange(NT), :] = 1.0
        past = (np.arange(32)[None, :] < blk[:, None]).astype(f32) * f32(30000.0)
        per.append({
            "x": xs, "G": G, "e_own": e.reshape(32, TL).astype(bf),
            "past30k": np.ascontiguousarray(np.broadcast_to(past[None], (128, NT, 32))),
            "b31": np.ascontiguousarray(np.broadcast_to(rel_bias[31:32, :], (128, 16))),
            "ident": np.eye(128, dtype=f32).astype(bf),
        })
    return per


def _rep(v, n=128):
    v = np.asarray(v, np.float32).reshape(1, -1)
    return np.ascontiguousarray(np.broadcast_to(v, (n, v.shape[1])))


CONST_KEYS = ("G", "b31", "ident", "e_own", "past30k")


def kernel(x, w_in_a, lam_a, subln_a, w_out_a, w_kv_shared, w_q_b, w_out_b, rel_bias,
           ln1_g, ln1_b, ln2_g, ln2_b, w_up, w_down):
    f32 = np.float32
    A = lambda v: np.ascontiguousarray(np.asarray(v, f32))
    x = A(x)
    rel_bias = A(rel_bias)
    cores = list(range(NCORES))
    per = _host_prep(x, rel_bias)
    ncA = _get("A", build_A)
    inA = [{"x": per[c]["x"], "w_in": A(w_in_a[0]), "ident": per[c]["ident"]} for c in cores]
    rA = run_bass_kernel_spmd(ncA, inA, core_ids=cores).results
    ncB = _get("B", build_B)
    prmB = {
        "lam": _rep(np.asarray(lam_a[0], f32).reshape(-1)), "subg": _rep(subln_a[0]),
        "w_out0": A(w_out_a[0]), "w_up0": A(w_up[0]), "w_down0": A(w_down[0]),
        "w_kv": A(w_kv_shared),
        "ln1_g0": _rep(ln1_g[0]), "ln1_b0": _rep(ln1_b[0]),
        "ln2_g0": _rep(ln2_g[0]), "ln2_b0": _rep(ln2_b[0]),
    }
    inB = []
    for c in cores:
        grp = [4 * (c // 4) + r for r in range(4)]
        d = {"x": per[c]["x"], "qT": rA[c]["qT"],
             "kT_all": np.stack([rA[g]["kT"] for g in grp]),
             "v_all": np.stack([rA[g]["v"] for g in grp])}
        d.update({k: per[c][k] for k in CONST_KEYS})
        d.update(prmB)
        inB.append(d)
    rB = run_bass_kernel_spmd(ncB, inB, core_ids=cores).results
    ncC = _get("C", build_C)
    prmC = {
        "w_q": A(w_q_b[0]), "w_out1": A(w_out_b[0]), "w_up1": A(w_up[1]), "w_down1": A(w_down[1]),
        "ln1_g1": _rep(ln1_g[1]), "ln1_b1": _rep(ln1_b[1]),
        "ln2_g1": _rep(ln2_g[1]), "ln2_b1": _rep(ln2_b[1]),
    }
    inC = []
    for c in cores:
        grp = [4 * (c // 4) + r for r in range(4)]
        d = {"h1": rB[c]["h1"],
             "kT1_all": np.stack([rB[g]["kT1"] for g in grp]),
             "v1_all": np.stack([rB[g]["v1"] for g in grp])}
        d.update({k: per[c][k] for k in CONST_KEYS})
        d.update(prmC)
        inC.append(d)
    rC = run_bass_kernel_spmd(ncC, inC, core_ids=cores).results
    out = np.empty((2, 64, 128, D), f32)
    for c in cores:
        b, j = c // 4, c % 4
        out[b, j::4] = np.asarray(rC[c]["out"], f32).reshape(NT, 128, D)
    return out.reshape(2, S, D)
```
